# Optimizing a Trainium2 kernel written in Bass

```python
import math
import jax, jax.numpy as jnp
from jax import lax
import numpy as np

D_MODEL = 1024
BATCH = 2
SEQ = 8192
DEPTH = 1
DEC_BATCH = 128
DEC_SEQ = 1
PAST_LEN = 8192
PAGE_SIZE = 128

A_HEADS = 4
A_DK = 128
A_DV = 128
A_WIDTH = A_HEADS * A_DV
A_QKV = 3 * A_WIDTH
CONV_W = 4
CHUNK = 64
B_HEADS = 8
B_KV_HEADS = 2
B_HD = 64
B_GROUP = B_HEADS // B_KV_HEADS
B_WIDTH = B_HEADS * B_HD
B_KV_WIDTH = B_KV_HEADS * B_HD
WINDOW = 128
BLOCK = 128
ROPE_THETA = 10000.0
MIX_WIDTH = A_WIDTH + B_WIDTH

DEEPNORM_ALPHA = (2 * DEPTH) ** 0.25
DEEPNORM_BETA = (8 * DEPTH) ** -0.25
LN_EPS = 1e-5
RMS_EPS = 1e-6
L2_EPS = 1e-6

OFF_A_QKV = 0
OFF_A_Z = OFF_A_QKV + A_QKV
OFF_A_BETA = OFF_A_Z + A_WIDTH
OFF_A_DECAY = OFF_A_BETA + A_HEADS
OFF_B_Q = OFF_A_DECAY + A_HEADS
OFF_B_K = OFF_B_Q + B_WIDTH
OFF_B_V = OFF_B_K + B_KV_WIDTH
OFF_B_Z = OFF_B_V + B_KV_WIDTH
PROJ_COLS = OFF_B_Z + B_WIDTH

kernel_name = "hymba_gdn_swa_sink_decode_step"


def layer_norm(x, g, b):
    xf = x.astype(jnp.float32)
    mu = jnp.mean(xf, -1, keepdims=True)
    var = jnp.mean(jnp.square(xf - mu), -1, keepdims=True)
    return ((xf - mu) * lax.rsqrt(var + LN_EPS) * g.astype(jnp.float32) + b.astype(jnp.float32)).astype(x.dtype)


def rms_norm(x, w):
    xf = x.astype(jnp.float32)
    return xf * lax.rsqrt(jnp.mean(xf * xf, -1, keepdims=True) + RMS_EPS) * w.astype(jnp.float32)


def l2_normalize(x):
    xf = x.astype(jnp.float32)
    return xf * lax.rsqrt(jnp.sum(xf * xf, -1, keepdims=True) + L2_EPS)


def rotary(x, pos):
    half = x.shape[-1] // 2
    inv = 1.0 / (ROPE_THETA ** (jnp.arange(half, dtype=jnp.float32) / half))
    ang = pos.astype(jnp.float32)[:, None] * inv[None, :]
    cos = jnp.cos(ang)[None, :, None, :]
    sin = jnp.sin(ang)[None, :, None, :]
    xf = x.astype(jnp.float32)
    x1, x2 = xf[..., :half], xf[..., half:]
    return jnp.concatenate([x1 * cos - x2 * sin, x2 * cos + x1 * sin], -1)


def causal_conv(buf, u, w):
    xp = jnp.concatenate([buf.astype(u.dtype), u], axis=1)
    t = u.shape[1]
    acc = xp[:, 0:t] * w[0]
    for j in range(1, CONV_W):
        acc = acc + xp[:, j:j + t] * w[j]
    return jax.nn.silu(acc), xp[:, -(CONV_W - 1):]


def gated_delta_chunked(q, k, v, g, beta, s0):
    bsz, t, h, dk = q.shape
    pad = (-t) % CHUNK
    if pad:
        pw4 = ((0, 0), (0, pad), (0, 0), (0, 0))
        pw3 = ((0, 0), (0, pad), (0, 0))
        q, k, v = jnp.pad(q, pw4), jnp.pad(k, pw4), jnp.pad(v, pw4)
        g, beta = jnp.pad(g, pw3), jnp.pad(beta, pw3)
    n = (t + pad) // CHUNK

    def split4(a):
        return a.reshape(bsz, n, CHUNK, h, a.shape[-1]).transpose(1, 0, 3, 2, 4)

    def split3(a):
        return a.reshape(bsz, n, CHUNK, h).transpose(1, 0, 3, 2)

    q, k, v = split4(q), split4(k), split4(v)
    g, beta = split3(g), split3(beta)
    G = jnp.cumsum(g, axis=-1)
    diff = G[..., :, None] - G[..., None, :]
    idx = jnp.arange(CHUNK)
    incl = idx[:, None] >= idx[None, :]
    strict = idx[:, None] > idx[None, :]
    dec_strict = jnp.exp(jnp.where(strict, diff, -jnp.inf))
    dec_incl = jnp.exp(jnp.where(incl, diff, -jnp.inf))
    eye = jnp.eye(CHUNK, dtype=q.dtype)
    lower = eye + beta[..., :, None] * jnp.einsum('nbhtd,nbhid->nbhti', k, k) * dec_strict
    rhs = jnp.concatenate([(beta * jnp.exp(G))[..., None] * k, beta[..., None] * v], -1)
    sol = lax.linalg.triangular_solve(lower, rhs, left_side=True, lower=True, unit_diagonal=True)
    w_mat, u_tilde = sol[..., :dk], sol[..., dk:]
    a_qk = jnp.einsum('nbhtd,nbhid->nbhti', q, k) * dec_incl
    q_dec = jnp.exp(G)[..., None] * q
    k_dec = jnp.exp(G[..., -1:] - G)[..., None] * k
    g_last = jnp.exp(G[..., -1])

    def step(s, xs):
        w_c, ut_c, aqk_c, qd_c, kd_c, gl_c = xs
        u = ut_c - jnp.einsum('bhcd,bhde->bhce', w_c, s)
        o = jnp.einsum('bhcd,bhde->bhce', qd_c, s) + jnp.einsum('bhti,bhie->bhte', aqk_c, u)
        s = gl_c[..., None, None] * s + jnp.einsum('bhcd,bhce->bhde', kd_c, u)
        return s, o

    s_fin, o = lax.scan(step, s0, (w_mat, u_tilde, a_qk, q_dec, k_dec, g_last))
    o = o.transpose(1, 0, 3, 2, 4).reshape(bsz, n * CHUNK, h, v.shape[-1])[:, :t]
    return o, s_fin


def gated_delta_recurrent(q, k, v, g, beta, s0):
    def step(s, xs):
        q_t, k_t, v_t, g_t, b_t = xs
        s = jnp.exp(g_t)[..., None, None] * s
        pred = jnp.einsum('bhd,bhde->bhe', k_t, s)
        s = s + jnp.einsum('bhd,bhe->bhde', k_t, b_t[..., None] * (v_t - pred))
        o = jnp.einsum('bhd,bhde->bhe', q_t, s)
        return s, o

    xs = (q.swapaxes(0, 1), k.swapaxes(0, 1), v.swapaxes(0, 1), g.swapaxes(0, 1), beta.swapaxes(0, 1))
    s_fin, o = lax.scan(step, s0, xs)
    return o.swapaxes(0, 1), s_fin


def sink_softmax(s, mask, sink):
    s = jnp.where(mask, s, -jnp.inf)
    m = jnp.maximum(jnp.max(s, -1, keepdims=True), sink)
    p = jnp.exp(s - m)
    return p / (jnp.sum(p, -1, keepdims=True) + jnp.exp(sink - m))


def swa_prompt(q, k, v, sinks):
    bsz, t = q.shape[:2]
    nb = t // BLOCK
    qb = q.astype(jnp.float32).reshape(bsz, nb, BLOCK, B_KV_HEADS, B_GROUP, B_HD)

    def band(a):
        ab = a.astype(jnp.float32).reshape(bsz, nb, BLOCK, B_KV_HEADS, B_HD)
        prev = jnp.concatenate([jnp.zeros_like(ab[:, :1]), ab[:, :-1]], axis=1)
        return jnp.concatenate([prev, ab], axis=2)

    kb, vb = band(k), band(v)
    s = jnp.einsum('bnqkgd,bnskd->bnkgqs', qb, kb)
    a = jnp.arange(BLOCK)[:, None]
    j = jnp.arange(2 * BLOCK)[None, :]
    rel = a + BLOCK - j
    band_mask = (rel >= 0) & (rel <= WINDOW)
    valid = band_mask[None] & ((jnp.arange(nb)[:, None, None] > 0) | (j[None] >= BLOCK))
    sink = sinks.astype(jnp.float32).reshape(B_KV_HEADS, B_GROUP)[:, :, None, None]
    p = sink_softmax(s, valid[None, :, None, None], sink)
    o = jnp.einsum('bnkgqs,bnskd->bnqkgd', p, vb)
    return o.reshape(bsz, t, B_WIDTH)


def swa_sample(q, k, v, win_k, win_v, sinks):
    bsz, t = q.shape[:2]
    kk = jnp.concatenate([win_k.astype(k.dtype), k], axis=1)
    vv = jnp.concatenate([win_v.astype(v.dtype), v], axis=1)
    qg = q.astype(jnp.float32).reshape(bsz, t, B_KV_HEADS, B_GROUP, B_HD)
    s = jnp.einsum('bqkgd,bskd->bkgqs', qg, kk.astype(jnp.float32))
    a = jnp.arange(t)[:, None]
    j = jnp.arange(WINDOW + t)[None, :]
    rel = WINDOW + a - j
    mask = (rel >= 0) & (rel <= WINDOW)
    sink = sinks.astype(jnp.float32).reshape(B_KV_HEADS, B_GROUP)[:, :, None, None]
    p = sink_softmax(s, mask, sink)
    o = jnp.einsum('bkgqs,bskd->bqkgd', p, vv.astype(jnp.float32))
    return o.reshape(bsz, t, B_WIDTH), kk[:, -WINDOW:], vv[:, -WINDOW:]


def hybrid_layer(x, c, pos, conv_buf, s0, win_k, win_v, is_prompt,
                 w_ada, b_ada, w_in, conv_w, a_log, dt_bias, norm_a, sinks, w_out, ln_g, ln_b):
    bsz, t, _ = x.shape
    mod = (c @ w_ada + b_ada)[:, None, :]
    shift, scale, gate = jnp.split(mod, 3, axis=-1)
    h = x * (1 + scale) + shift
    proj = h @ w_in

    qkv, new_conv = causal_conv(conv_buf, proj[..., OFF_A_QKV:OFF_A_Z], conv_w)
    qa, ka, va = jnp.split(qkv, 3, axis=-1)
    qa = l2_normalize(qa.reshape(bsz, t, A_HEADS, A_DK)) * (A_DK ** -0.5)
    ka = l2_normalize(ka.reshape(bsz, t, A_HEADS, A_DK))
    va = va.reshape(bsz, t, A_HEADS, A_DV).astype(jnp.float32)
    za = proj[..., OFF_A_Z:OFF_A_BETA]
    beta = jax.nn.sigmoid(proj[..., OFF_A_BETA:OFF_A_DECAY].astype(jnp.float32))
    g = -jnp.exp(a_log.astype(jnp.float32)) * jax.nn.softplus(
        proj[..., OFF_A_DECAY:OFF_B_Q].astype(jnp.float32) + dt_bias.astype(jnp.float32))
    s0f = s0.astype(jnp.float32)
    if is_prompt:
        oa, s_new = gated_delta_chunked(qa, ka, va, g, beta, s0f)
    else:
        oa, s_new = gated_delta_recurrent(qa, ka, va, g, beta, s0f)
    oa = rms_norm(oa, norm_a).astype(x.dtype).reshape(bsz, t, A_WIDTH) * jax.nn.silu(za)

    qb = (rotary(proj[..., OFF_B_Q:OFF_B_K].reshape(bsz, t, B_HEADS, B_HD), pos) * (B_HD ** -0.5)).astype(x.dtype)
    kb = rotary(proj[..., OFF_B_K:OFF_B_V].reshape(bsz, t, B_KV_HEADS, B_HD), pos).astype(x.dtype)
    vb = proj[..., OFF_B_V:OFF_B_Z].reshape(bsz, t, B_KV_HEADS, B_HD)
    zb = proj[..., OFF_B_Z:PROJ_COLS]
    if is_prompt:
        ob = swa_prompt(qb, kb, vb, sinks)
        new_k, new_v = kb[:, -WINDOW:], vb[:, -WINDOW:]
    else:
        ob, new_k, new_v = swa_sample(qb, kb, vb, win_k, win_v, sinks)
    ob = ob.astype(x.dtype) * jax.nn.silu(zb)

    mix = jnp.concatenate([oa, ob], axis=-1) @ w_out
    y = layer_norm(DEEPNORM_ALPHA * x + (1 + gate) * mix, ln_g, ln_b)
    return y, new_conv, s_new.astype(s0.dtype), new_k, new_v


def setup_inputs(seed: int = 0) -> dict:
    key = jax.random.key(seed)
    ks = jax.random.split(key, 20)
    f32 = jnp.float32

    def nrm(k, shape, s):
        return jax.random.normal(k, shape, f32) * s

    x_prompt = nrm(ks[0], (BATCH, SEQ, D_MODEL), 1.0)
    x_sample = nrm(ks[1], (DEC_BATCH, DEC_SEQ, D_MODEL), 1.0)
    state_conv = nrm(ks[2], (DEPTH, DEC_BATCH, CONV_W - 1, A_QKV), 1.0)
    state_delta = nrm(ks[3], (DEPTH, DEC_BATCH, A_HEADS, A_DK, A_DV), 0.3)
    cache_swa_k = nrm(ks[4], (DEPTH, DEC_BATCH, WINDOW, B_KV_HEADS, B_HD), 1.0)
    cache_swa_v = nrm(ks[5], (DEPTH, DEC_BATCH, WINDOW, B_KV_HEADS, B_HD), 1.0)
    c_prompt = nrm(ks[6], (BATCH, D_MODEL), 1.0)
    c_sample = nrm(ks[7], (DEC_BATCH, D_MODEL), 1.0)
    w_ada = nrm(ks[8], (DEPTH, D_MODEL, 3 * D_MODEL), 0.1 * D_MODEL ** -0.5)
    b_ada = nrm(ks[9], (DEPTH, 3 * D_MODEL), 0.01)
    col_scale = (jnp.ones((PROJ_COLS,), f32)
                 .at[OFF_A_QKV + 2 * A_WIDTH:OFF_A_Z].set(DEEPNORM_BETA)
                 .at[OFF_B_V:OFF_B_Z].set(DEEPNORM_BETA))
    w_in = nrm(ks[10], (DEPTH, D_MODEL, PROJ_COLS), D_MODEL ** -0.5) * col_scale
    conv_w = nrm(ks[11], (DEPTH, CONV_W, A_QKV), CONV_W ** -0.5)
    a_log = jnp.log(jax.random.uniform(ks[12], (DEPTH, A_HEADS), f32, 1.0, 16.0))
    dt = jnp.exp(jax.random.uniform(ks[13], (DEPTH, A_HEADS), f32, math.log(1e-3), math.log(1e-1)))
    dt_bias = dt + jnp.log(-jnp.expm1(-dt))
    norm_a = 1.0 + nrm(ks[14], (DEPTH, A_DV), 0.05)
    sinks = nrm(ks[15], (DEPTH, B_HEADS), 0.5)
    w_out = nrm(ks[16], (DEPTH, MIX_WIDTH, D_MODEL), MIX_WIDTH ** -0.5 * DEEPNORM_BETA)
    ln_g = 1.0 + nrm(ks[17], (DEPTH, D_MODEL), 0.05)
    ln_b = nrm(ks[18], (DEPTH, D_MODEL), 0.02)
    return {"x_prompt": x_prompt, "x_sample": x_sample,
            "state_conv": state_conv, "state_delta": state_delta,
            "cache_swa_k": cache_swa_k, "cache_swa_v": cache_swa_v,
            "c_prompt": c_prompt, "c_sample": c_sample,
            "w_ada": w_ada, "b_ada": b_ada, "w_in": w_in, "conv_w": conv_w,
            "a_log": a_log, "dt_bias": dt_bias, "norm_a": norm_a, "sinks": sinks,
            "w_out": w_out, "ln_g": ln_g, "ln_b": ln_b}


def reference(x_prompt, x_sample, state_conv, state_delta, cache_swa_k, cache_swa_v,
              c_prompt, c_sample, w_ada, b_ada, w_in, conv_w, a_log, dt_bias, norm_a,
              sinks, w_out, ln_g, ln_b):
    pos_p = jnp.arange(SEQ)
    pos_s = PAST_LEN + jnp.arange(DEC_SEQ)
    yp, ys = x_prompt, x_sample
    conv_p, delta_p, kp_l, vp_l = [], [], [], []
    conv_s, delta_s, ks_l, vs_l = [], [], [], []
    for l in range(DEPTH):
        lp = (w_ada[l], b_ada[l], w_in[l], conv_w[l], a_log[l], dt_bias[l], norm_a[l],
              sinks[l], w_out[l], ln_g[l], ln_b[l])
        zero_conv = jnp.zeros((BATCH, CONV_W - 1, A_QKV), x_prompt.dtype)
        zero_state = jnp.zeros((BATCH, A_HEADS, A_DK, A_DV), x_prompt.dtype)
        yp, cp, sp, kp, vp = hybrid_layer(yp, c_prompt, pos_p, zero_conv, zero_state, None, None, True, *lp)
        ys, cs, ss, kss, vss = hybrid_layer(ys, c_sample, pos_s, state_conv[l], state_delta[l],
                                            cache_swa_k[l], cache_swa_v[l], False, *lp)
        conv_p.append(cp); delta_p.append(sp); kp_l.append(kp); vp_l.append(vp)
        conv_s.append(cs); delta_s.append(ss); ks_l.append(kss); vs_l.append(vss)
    new_conv_p = jnp.stack(conv_p, 0)
    new_delta_p = jnp.stack(delta_p, 0)
    new_swa_k_p = jnp.stack(kp_l, 0)
    new_swa_v_p = jnp.stack(vp_l, 0)
    new_conv_s = jnp.stack(conv_s, 0)
    new_delta_s = jnp.stack(delta_s, 0)
    new_swa_k_s = jnp.stack(ks_l, 0)
    new_swa_v_s = jnp.stack(vs_l, 0)
    return (yp, ys, new_conv_p, new_delta_p, new_swa_k_p, new_swa_v_p,
            new_conv_s, new_delta_s, new_swa_k_s, new_swa_v_s)
```

```python
import contextlib
import numpy as np
import concourse.bass as bass
import concourse.mybir as mybir
from concourse.bass_utils import run_bass_kernel_spmd

F32 = mybir.dt.float32
BF16 = mybir.dt.bfloat16
ACT = mybir.ActivationFunctionType
ALU = mybir.AluOpType
AX = mybir.AxisListType

D = 1024
NTILES = 64
NS = 16
ALPHA = 2.0 ** 0.25
NEG = -1.0e30
LN_EPS = 1e-5
RMS_EPS = 1e-6
L2_EPS = 1e-6
WT_COLS = 1800
PERM = [0, 4, 1, 5, 2, 6, 3, 7]


class KB:
    def __init__(self, nc):
        self.nc = nc
        self.es = contextlib.ExitStack()
        self.eng = {"pe": nc.tensor, "dve": nc.vector, "act": nc.scalar, "pool": nc.gpsimd, "sp": nc.sync}
        self.sem = {}
        self.cnt = {}
        for e in self.eng:
            self.sem[e] = self.es.enter_context(nc.semaphore("sem_" + e))
            self.cnt[e] = 0
        self.waited = {}
        self.last_write = {}
        self.readers = {}
        self.ntensors = 0
        self.limit = None
        self.nops = 0

    def sb(self, name, shape, dt=F32):
        return self.es.enter_context(self.nc.sbuf_tensor(name, list(shape), dt))

    def ps(self, name, shape=(128, 512), dt=F32):
        return self.es.enter_context(self.nc.psum_tensor(name, list(shape), dt))

    def _deps(self, reads, writes, nowaw=False):
        deps = set()
        for t in reads:
            if t in self.last_write:
                deps.add(self.last_write[t])
        for t in writes:
            if t in self.last_write and not nowaw:
                deps.add(self.last_write[t])
            for r in self.readers.get(t, ()):
                deps.add(r)
        return deps

    def _wait(self, e, deps):
        for (src, val) in sorted(deps, key=lambda x: str(x)):
            if self.waited.get((e, src), 0) < val:
                self.eng[e].wait_ge(self.sem[src], val)
                self.waited[(e, src)] = val

    def _record(self, key, reads, writes):
        for t in writes:
            self.last_write[t] = key
            self.readers[t] = set()
        for t in reads:
            if t not in writes:
                self.readers.setdefault(t, set()).add(key)

    def op(self, e, fn, reads=(), writes=()):
        self.nops += 1
        if self.limit is not None and self.nops > self.limit:
            return
        reads = [r.name if hasattr(r, "name") else r for r in reads]
        writes = [w.name if hasattr(w, "name") else w for w in writes]
        writes = list(writes) + [r for r in reads if r.startswith("pb") and r not in writes]
        self._wait(e, self._deps(reads, writes))
        inst = fn(self.eng[e])
        self.cnt[e] += 1
        inst.then_inc(self.sem[e], 1)
        self._record((e, self.cnt[e]), reads, writes)

    def dma(self, e, out, in_, reads=(), writes=(), semkey=None, nowaw=False, **kw):
        reads = [r.name if hasattr(r, "name") else r for r in reads]
        writes = [w.name if hasattr(w, "name") else w for w in writes]
        self.nops += 1
        if self.limit is not None and self.nops > self.limit:
            return
        if semkey not in self.sem:
            self.sem[semkey] = self.es.enter_context(self.nc.semaphore("semd_" + str(semkey)))
            self.cnt[semkey] = 0
        self._wait(e, self._deps(reads, writes, nowaw))
        inst = self.eng[e].dma_start(out=out, in_=in_, **kw)
        self.cnt[semkey] += 16
        inst.then_inc(self.sem[semkey], 16)
        self._record((semkey, self.cnt[semkey]), reads, writes)

    def barrier(self):
        for e in self.eng:
            for src, val in self.cnt.items():
                if val > 0 and self.waited.get((e, src), 0) < val:
                    self.eng[e].wait_ge(self.sem[src], val)
                    self.waited[(e, src)] = val
        self.last_write = {}
        self.readers = {}

    def finish(self, e="sp"):
        for src, val in self.cnt.items():
            if val > 0 and self.waited.get((e, src), 0) < val:
                self.eng[e].wait_ge(self.sem[src], val)
                self.waited[(e, src)] = val
        self.es.close()


def bc(ap, shape):
    return ap.to_broadcast(list(shape))


def build_program(ntiles=NTILES, do_sample=True, limit=None):
    nc = bass.Bass("TRN2", target_bir_lowering=False)
    k = KB(nc)
    k.limit = limit
    NT = ntiles
    T = NT * 128

    def din(name, shape):
        return nc.dram_tensor(name, list(shape), F32, kind="ExternalInput").ap()

    def dout(name, shape):
        return nc.dram_tensor(name, list(shape), F32, kind="ExternalOutput").ap()

    x_d = din("x", [T, D])
    c_d = din("c", [1, D])
    wada_d = din("w_ada", [D, 3 * D])
    bada_d = din("b_ada", [1, 3 * D])
    wf_d = din("w_f", [D, 1536])
    wt_d = din("w_t", [D, WT_COLS])
    wo_d = din("w_o", [D, D])
    convw_d = din("conv_w", [4, 1536])
    alog_d = din("a_log", [1, 4])
    dtb_d = din("dt_bias", [1, 4])
    norma_d = din("norm_a", [1, 128])
    sinks_d = din("sinks_p", [1, 8])
    lng_d = din("ln_g", [1, D])
    lnb_d = din("ln_b", [1, D])
    cosk_d = din("cosk", [T, 32])
    sink_d = din("sink", [T, 32])

    y_d = dout("y", [T, D])
    convp_d = dout("conv_p", [3, 1536])
    deltap_d = dout("delta_p", [4, 128, 128])
    swak_d = dout("swa_k_p", [128, 128])
    swav_d = dout("swa_v_p", [128, 128])

    if do_sample:
        xs_d = din("xs", [NS, D])
        cs_d = din("cs", [NS, D])
        sconv_d = din("s_conv", [NS, 3, 1536])
        sdelta_d = din("s_delta", [NS, 4, 128, 128])
        sk_d = din("s_k", [NS, 128, 128])
        sv_d = din("s_v", [NS, 128, 128])
        coss_d = din("cos_s", [1, 32])
        sins_d = din("sin_s", [1, 32])
        ys_d = dout("ys", [NS, D])
        convs_d = dout("conv_s", [NS, 3, 1536])
        deltas_d = dout("delta_s", [NS, 4, 128, 128])
        swaks_d = dout("swa_k_s", [NS, 128, 128])
        swavs_d = dout("swa_v_s", [NS, 128, 128])

    ident = k.sb("ident", [128, 128])
    U = k.sb("U", [128, 128])
    ones = k.sb("ones", [128, 128])
    onesb = k.sb("onesb", [128, 128], BF16)
    negA = k.sb("negA", [128, 128])
    negB = k.sb("negB", [128, 128])
    swam = k.sb("swam", [128, 256])
    swam0 = k.sb("swam0", [128, 256])

    k.op("pool", lambda g: g.memset(ones[:], 1.0), writes=[ones])
    k.op("pool", lambda g: g.memset(onesb[:], 1.0), writes=[onesb])
    k.op("pool", lambda g: g.affine_select(out=ident[:], in_=ones[:], pattern=[[-1, 128]], compare_op=ALU.is_equal,
                                            fill=0.0, base=0, channel_multiplier=1), reads=[ones], writes=[ident])
    k.op("pool", lambda g: g.affine_select(out=U[:], in_=ones[:], pattern=[[1, 128]], compare_op=ALU.is_ge,
                                            fill=0.0, base=0, channel_multiplier=-1), reads=[ones], writes=[U])
    zer = k.sb("zer", [128, 256])
    k.op("pool", lambda g: g.memset(zer[:], 0.0), writes=[zer])
    k.op("pool", lambda g: g.affine_select(out=negA[:], in_=zer[:, 0:128], pattern=[[-1, 128]], compare_op=ALU.is_ge,
                                            fill=NEG, base=-1, channel_multiplier=1), reads=[zer], writes=[negA])
    k.op("pool", lambda g: g.affine_select(out=negB[:], in_=zer[:, 0:128], pattern=[[1, 128]], compare_op=ALU.is_ge,
                                            fill=NEG, base=0, channel_multiplier=-1), reads=[zer], writes=[negB])
    swamt = k.sb("swamt", [128, 256])
    k.op("pool", lambda g: g.affine_select(out=swamt[:], in_=zer[:], pattern=[[1, 256]], compare_op=ALU.is_ge,
                                            fill=NEG, base=0, channel_multiplier=-1), reads=[zer], writes=[swamt])
    k.op("pool", lambda g: g.affine_select(out=swam[:], in_=swamt[:], pattern=[[-1, 256]], compare_op=ALU.is_ge,
                                            fill=NEG, base=128, channel_multiplier=1), reads=[swamt], writes=[swam])
    k.op("pool", lambda g: g.memset(swam0[:, 0:128], NEG), writes=[swam0])
    k.op("pool", lambda g: g.tensor_copy(out=swam0[:, 128:256], in_=swam[:, 128:256]), reads=[swam], writes=[swam0])

    PB = [k.ps("pb%d" % i) for i in range(8)]
    def load_bc(name, src, n, parts=128):
        t = k.sb(name, [parts, n])
        k.dma("sp", t[:], src.partition_broadcast(parts), writes=[t], semkey="ld_" + name)
        return t

    lng_bc = load_bc("lng_bc", lng_d[0], D)
    lnb_bc = load_bc("lnb_bc", lnb_d[0], D)
    norma_bc = load_bc("norma_bc", norma_d[0], 128)
    sinks_bc = load_bc("sinks_bc", sinks_d[0], 8)
    alog_bc = load_bc("alog_bc", alog_d[0], 4)
    dtb_bc = load_bc("dtb_bc", dtb_d[0], 4)
    cwT = k.sb("cwT", [128, 48])
    bada_fm = k.sb("bada_fm", [128, 24])
    cT = k.sb("cT", [128, 8])
    rowst = k.sb("rowst", [80, 128])
    k.dma("sp", rowst[0:48, :], convw_d.rearrange("j (c p) -> (j c) p", p=128), writes=[rowst], semkey="ld_rowst", nowaw=True)
    k.dma("sp", rowst[48:72, :], bada_d[0].rearrange("(j p) -> j p", p=128), writes=[rowst], semkey="ld_rowst", nowaw=True)
    k.dma("sp", rowst[72:80, :], c_d[0].rearrange("(j p) -> j p", p=128), writes=[rowst], semkey="ld_rowst", nowaw=True)
    cst = [k.sb("cst%d" % i, [128, 2, 32]) for i in range(2)]
    k.op("pe", lambda p: p.transpose(out=PB[0][:, 0:80], in_=rowst[:, :], identity=ident[0:80, 0:80]), reads=[rowst, ident], writes=[PB[0]])
    k.op("dve", lambda v: v.tensor_copy(out=cwT[:], in_=PB[0][:, 0:48]), reads=[PB[0]], writes=[cwT])
    k.op("dve", lambda v: v.tensor_copy(out=bada_fm[:], in_=PB[0][:, 48:72]), reads=[PB[0]], writes=[bada_fm])
    k.op("dve", lambda v: v.tensor_copy(out=cT[:], in_=PB[0][:, 72:80]), reads=[PB[0]], writes=[cT])
    ea = k.sb("ea", [128, 4])
    k.op("act", lambda a: a.activation(out=ea[:], in_=alog_bc[:], func=ACT.Exp), reads=[alog_bc], writes=[ea])
    k.op("dve", lambda v: v.tensor_scalar(out=ea[:], in0=ea[:], scalar1=-1.0, scalar2=None, op0=ALU.mult),
         reads=[ea], writes=[ea])

    Wf = k.sb("Wf", [128, 8, 1536], BF16)
    Wt = k.sb("Wt", [128, 8, WT_COLS], BF16)
    Wo = k.sb("Wo", [128, 8, D], BF16)
    mod_fm = k.sb("mod_fm", [128, 16])
    g1bc = k.sb("g1bc", [128, D])
    k1s = contextlib.ExitStack()
    if do_sample:
        csT = k1s.enter_context(nc.sbuf_tensor("csT", [128, 8, NS], F32))
        mod_s = k1s.enter_context(nc.sbuf_tensor("mod_s", [NS, 3 * D], F32))
    k2 = contextlib.ExitStack()
    def sb2(name, shape, dt=F32):
        return k2.enter_context(nc.sbuf_tensor(name, list(shape), dt))
    stg = [sb2("stg%d" % i, [128, 1800]) for i in range(2)]
    bada_bc = sb2("bada_bc", [128, D])
    k.dma("sp", bada_bc[:], bada_d[0, 2 * D:3 * D].partition_broadcast(128), writes=[bada_bc], semkey="ld_bada_bc")
    si = 0
    cast_engs = ["dve", "pool", "act"]

    def load_cast(dst, src_d, ncols):
        nonlocal si
        per = 2048 // ncols if ncols <= 2048 else 0
        for kc in range(8):
            s = stg[si % 2]
            k.dma("sp", s[:, 0:ncols], src_d[kc * 128:(kc + 1) * 128, :], writes=[s], semkey="ld_" + s.name)
            e = cast_engs[si % 3]
            if e == "act":
                k.op(e, lambda a, s=s, kc=kc: a.copy(out=dst[:, kc, :], in_=s[:, 0:ncols]), reads=[s], writes=[dst])
            else:
                k.op(e, lambda v, s=s, kc=kc: v.tensor_copy(out=dst[:, kc, :], in_=s[:, 0:ncols]), reads=[s], writes=[dst])
            si += 1

    load_cast(Wf, wf_d, 1536)
    load_cast(Wt, wt_d, WT_COLS)
    load_cast(Wo, wo_d, D)


    wa = [sb2("wa%d" % i, [128, 8, 512]) for i in range(1)]
    c_bcT = sb2("c_bcT", [128, 8, 128])
    k.op("pool", lambda g: g.tensor_copy(out=c_bcT[:], in_=bc(cT[:].rearrange("p (a b) -> p a b", b=1), [128, 8, 128])),
         reads=[cT], writes=[c_bcT])
    if do_sample:
        cs_sb = sb2("cs_sb", [NS, D])
        k.dma("sp", cs_sb[:], cs_d[:, :], writes=[cs_sb], semkey="ld_cs")
        for kc in range(8):
            k.op("pe", lambda p, kc=kc: p.transpose(out=PB[0][:, kc * NS:(kc + 1) * NS], in_=cs_sb[:, kc * 128:(kc + 1) * 128],
                                                    identity=ident[0:NS, 0:NS]), reads=[cs_sb, ident], writes=[PB[0]])
        k.op("dve", lambda v: v.tensor_copy(out=csT[:].rearrange("p a b -> p (a b)"), in_=PB[0][:, 0:8 * NS]),
             reads=[PB[0]], writes=[csT])
        bada_s = sb2("bada_s", [NS, 3 * D])
        k.dma("sp", bada_s[:], bada_d[0].partition_broadcast(NS), writes=[bada_s], semkey="ld_bada_s")
    for j in range(6):
        w = wa[0]
        for kc in range(8):
            k.dma("sp", w[:, kc, :], wada_d[kc * 128:(kc + 1) * 128, j * 512:(j + 1) * 512], writes=[w],
                  semkey="ld_" + w.name, nowaw=True)
        if j < 4:
            for sub in range(4):
                col = j * 4 + sub
                for kc in range(8):
                    k.op("pe", lambda p, kc=kc, sub=sub, col=col, w=w: p.matmul(
                        PB[1][:, col:col + 1], lhsT=w[:, kc, sub * 128:(sub + 1) * 128], rhs=cT[:, kc:kc + 1],
                        start=(kc == 0), stop=(kc == 7)), reads=[w, cT], writes=[PB[1]])
        else:
            for kc in range(8):
                k.op("pe", lambda p, kc=kc, w=w: p.matmul(PB[2 + (j - 4)][:, :], lhsT=c_bcT[:, kc, :], rhs=w[:, kc, :],
                                                          start=(kc == 0), stop=(kc == 7)),
                     reads=[w, c_bcT], writes=[PB[2 + (j - 4)]])
        if do_sample:
            for kc in range(8):
                k.op("pe", lambda p, kc=kc, w=w: p.matmul(PB[4 + j % 2][0:NS, :], lhsT=csT[:, kc, :], rhs=w[:, kc, :],
                                                          start=(kc == 0), stop=(kc == 7)),
                     reads=[w, csT], writes=[PB[4 + j % 2]])
            k.op("dve", lambda v, j=j: v.tensor_tensor(out=mod_s[:, j * 512:(j + 1) * 512], in0=PB[4 + j % 2][0:NS, :],
                                                       in1=bada_s[:, j * 512:(j + 1) * 512], op=ALU.add),
                 reads=[PB[4 + j % 2], bada_s], writes=[mod_s])
    k.op("dve", lambda v: v.tensor_tensor(out=mod_fm[:], in0=PB[1][:, 0:16], in1=bada_fm[:, 0:16], op=ALU.add),
         reads=[PB[1], bada_fm], writes=[mod_fm])
    k.op("dve", lambda v: v.tensor_scalar(out=mod_fm[:, 8:16], in0=mod_fm[:, 8:16], scalar1=1.0, scalar2=None, op0=ALU.add),
         reads=[mod_fm], writes=[mod_fm])
    for j in range(2):
        k.op("dve", lambda v, j=j: v.scalar_tensor_tensor(out=g1bc[:, j * 512:(j + 1) * 512], in0=PB[2 + j][:, :], scalar=1.0,
                                                           in1=bada_bc[:, j * 512:(j + 1) * 512], op0=ALU.add, op1=ALU.add),
             reads=[PB[2 + j], bada_bc], writes=[g1bc])


    if do_sample:
        k.barrier()
        k2.close()
        k2 = contextlib.ExitStack()
        P16 = NS
        sinkcol_d = din("sinks_col", [128, 1])
        xs = sb2("xs_sb", [P16, D]); hs = sb2("hs", [P16, D]); mix_s = sb2("mix_s", [P16, D])
        hsT = sb2("hsT", [128, 8, P16], BF16)
        pqkv = sb2("pqkv", [P16, 1536])
        qkv_s = sb2("qkv_s", [P16, 12, 128])
        zAs_s = sb2("zAs_s", [P16, 512]); zBs_s = sb2("zBs_s", [P16, 512])
        qr_s = sb2("qr_s", [P16, 512]); kr_s = sb2("kr_s", [P16, 128]); v_s = sb2("v_s", [P16, 128])
        vsb = sb2("vsb", [P16, 128], BF16)
        bd_s = sb2("bd_s", [P16, 8]); bdt_s = sb2("bdt_s", [P16, 8])
        beta_s = sb2("beta_s", [P16, 4]); nbeta_s = sb2("nbeta_s", [P16, 4]); g_s = sb2("g_s", [P16, 4]); eg_s = sb2("eg_s", [P16, 4])
        cs16 = sb2("cs16", [P16, 2, 32])
        k.dma("sp", cs16[:, 0, :], coss_d[0].partition_broadcast(P16), writes=[cs16], semkey="ld_cs16", nowaw=True)
        k.dma("sp", cs16[:, 1, :], sins_d[0].partition_broadcast(P16), writes=[cs16], semkey="ld_cs16", nowaw=True)
        k.dma("sp", xs[:], xs_d[:, :], writes=[xs], semkey="ld_xs")
        k.op("dve", lambda v: v.scalar_tensor_tensor(out=hs[:], in0=mod_s[:, D:2 * D], scalar=1.0, in1=xs[:], op0=ALU.add, op1=ALU.mult),
             reads=[mod_s, xs], writes=[hs])
        k.op("dve", lambda v: v.tensor_tensor(out=hs[:], in0=hs[:], in1=mod_s[:, 0:D], op=ALU.add), reads=[hs, mod_s], writes=[hs])
        for kc in range(8):
            k.op("pe", lambda p, kc=kc: p.transpose(out=PB[0][:, kc * P16:(kc + 1) * P16], in_=hs[:, kc * 128:(kc + 1) * 128],
                                                    identity=ident[0:P16, 0:P16]), reads=[hs, ident], writes=[PB[0]])
        k.op("dve", lambda v: v.tensor_copy(out=hsT[:].rearrange("p a b -> p (a b)"), in_=PB[0][:, 0:8 * P16]), reads=[PB[0]], writes=[hsT])
        for j in range(3):
            for kc in range(8):
                k.op("pe", lambda p, j=j, kc=kc: p.matmul(PB[1 + j][0:P16, :], lhsT=hsT[:, kc, :], rhs=Wf[:, kc, j * 512:(j + 1) * 512],
                                                          start=(kc == 0), stop=(kc == 7)), reads=[hsT, Wf], writes=[PB[1 + j]])
        offs = [(0, 512), (512, 512), (1024, 512), (1536, 264)]
        for j, (o, w_) in enumerate(offs):
            for kc in range(8):
                k.op("pe", lambda p, j=j, o=o, w_=w_, kc=kc: p.matmul(PB[4 + j][0:P16, 0:w_], lhsT=hsT[:, kc, :], rhs=Wt[:, kc, o:o + w_],
                                                                      start=(kc == 0), stop=(kc == 7)), reads=[hsT, Wt], writes=[PB[4 + j]])
        for j in range(3):
            k.op("dve", lambda v, j=j: v.tensor_copy(out=pqkv[:, j * 512:(j + 1) * 512], in_=PB[1 + j][0:P16, :]), reads=[PB[1 + j]], writes=[pqkv])
        k.op("act", lambda a: a.activation(out=zAs_s[:], in_=PB[4][0:P16, :], func=ACT.Silu), reads=[PB[4]], writes=[zAs_s])
        k.op("act", lambda a: a.activation(out=zBs_s[:], in_=PB[6][0:P16, :], func=ACT.Silu), reads=[PB[6]], writes=[zBs_s])
        rts = [sb2("rts%d" % i, [P16, 8, 32]) for i in range(4)]
        q3 = PB[5][0:P16, :].rearrange("p (a b) -> p a b", a=8)
        qr3 = qr_s[:].rearrange("p (a b) -> p a b", a=8)
        cq = bc(cs16[:, 0:1, :], [P16, 8, 32]); sq_ = bc(cs16[:, 1:2, :], [P16, 8, 32])
        k.op("dve", lambda v: v.tensor_tensor(out=rts[0][:], in0=q3[:, :, 0:32], in1=cq, op=ALU.mult), reads=[PB[5], cs16], writes=[rts[0]])
        k.op("dve", lambda v: v.tensor_tensor(out=rts[1][:], in0=q3[:, :, 32:64], in1=sq_, op=ALU.mult), reads=[PB[5], cs16], writes=[rts[1]])
        k.op("dve", lambda v: v.tensor_tensor(out=rts[2][:], in0=q3[:, :, 32:64], in1=cq, op=ALU.mult), reads=[PB[5], cs16], writes=[rts[2]])
        k.op("dve", lambda v: v.tensor_tensor(out=rts[3][:], in0=q3[:, :, 0:32], in1=sq_, op=ALU.mult), reads=[PB[5], cs16], writes=[rts[3]])
        k.op("dve", lambda v: v.tensor_tensor(out=qr3[:, :, 0:32], in0=rts[0][:], in1=rts[1][:], op=ALU.subtract), reads=[rts[0], rts[1]], writes=[qr_s])
        k.op("dve", lambda v: v.tensor_tensor(out=qr3[:, :, 32:64], in0=rts[2][:], in1=rts[3][:], op=ALU.add), reads=[rts[2], rts[3]], writes=[qr_s])
        k.op("dve", lambda v: v.tensor_scalar(out=qr_s[:], in0=qr_s[:], scalar1=0.125, scalar2=None, op0=ALU.mult), reads=[qr_s], writes=[qr_s])
        k3 = PB[7][0:P16, 0:128].rearrange("p (a b) -> p a b", a=2)
        kr3 = kr_s[:].rearrange("p (a b) -> p a b", a=2)
        ck = bc(cs16[:, 0:1, :], [P16, 2, 32]); sk_ = bc(cs16[:, 1:2, :], [P16, 2, 32])
        k.op("dve", lambda v: v.tensor_tensor(out=rts[0][:, 0:2, :], in0=k3[:, :, 0:32], in1=ck, op=ALU.mult), reads=[PB[7], cs16], writes=[rts[0]])
        k.op("dve", lambda v: v.tensor_tensor(out=rts[1][:, 0:2, :], in0=k3[:, :, 32:64], in1=sk_, op=ALU.mult), reads=[PB[7], cs16], writes=[rts[1]])
        k.op("dve", lambda v: v.tensor_tensor(out=rts[2][:, 0:2, :], in0=k3[:, :, 32:64], in1=ck, op=ALU.mult), reads=[PB[7], cs16], writes=[rts[2]])
        k.op("dve", lambda v: v.tensor_tensor(out=rts[3][:, 0:2, :], in0=k3[:, :, 0:32], in1=sk_, op=ALU.mult), reads=[PB[7], cs16], writes=[rts[3]])
        k.op("dve", lambda v: v.tensor_tensor(out=kr3[:, :, 0:32], in0=rts[0][:, 0:2, :], in1=rts[1][:, 0:2, :], op=ALU.subtract), reads=[rts[0], rts[1]], writes=[kr_s])
        k.op("dve", lambda v: v.tensor_tensor(out=kr3[:, :, 32:64], in0=rts[2][:, 0:2, :], in1=rts[3][:, 0:2, :], op=ALU.add), reads=[rts[2], rts[3]], writes=[kr_s])
        k.op("dve", lambda v: v.tensor_copy(out=v_s[:], in_=PB[7][0:P16, 128:256]), reads=[PB[7]], writes=[v_s])
        k.op("dve", lambda v: v.tensor_copy(out=vsb[:], in_=PB[7][0:P16, 128:256]), reads=[PB[7]], writes=[vsb])
        k.op("dve", lambda v: v.tensor_copy(out=bd_s[:], in_=PB[7][0:P16, 256:264]), reads=[PB[7]], writes=[bd_s])
        k.op("act", lambda a: a.activation(out=bdt_s[:, 0:4], in_=bd_s[:, 0:4], func=ACT.Exp, scale=-1.0), reads=[bd_s], writes=[bdt_s])
        k.op("dve", lambda v: v.tensor_scalar(out=bdt_s[:, 0:4], in0=bdt_s[:, 0:4], scalar1=1.0, scalar2=None, op0=ALU.add), reads=[bdt_s], writes=[bdt_s])
        k.op("dve", lambda v: v.reciprocal(out=beta_s[:], in_=bdt_s[:, 0:4]), reads=[bdt_s], writes=[beta_s])
        k.op("dve", lambda v: v.tensor_scalar(out=nbeta_s[:], in0=beta_s[:], scalar1=-1.0, scalar2=None, op0=ALU.mult), reads=[beta_s], writes=[nbeta_s])
        k.op("dve", lambda v: v.tensor_tensor(out=bd_s[:, 4:8], in0=bd_s[:, 4:8], in1=dtb_bc[0:P16, :], op=ALU.add), reads=[bd_s, dtb_bc], writes=[bd_s])
        k.op("act", lambda a: a.activation(out=bdt_s[:, 4:8], in_=bd_s[:, 4:8], func=ACT.Exp), reads=[bd_s], writes=[bdt_s])
        k.op("act", lambda a: a.activation(out=bdt_s[:, 4:8], in_=bdt_s[:, 4:8], func=ACT.Ln, bias=1.0, scale=1.0), reads=[bdt_s], writes=[bdt_s])
        k.op("dve", lambda v: v.tensor_tensor(out=g_s[:], in0=bdt_s[:, 4:8], in1=ea[0:P16, :], op=ALU.mult), reads=[bdt_s, ea], writes=[g_s])
        k.op("act", lambda a: a.activation(out=eg_s[:], in_=g_s[:], func=ACT.Exp), reads=[g_s], writes=[eg_s])
        k3s = contextlib.ExitStack()
        def sb3(name, shape, dt=F32):
            return k3s.enter_context(nc.sbuf_tensor(name, list(shape), dt))
        xp4 = sb3("xp4", [P16, 4, 1536]); cwb = sb3("cwb", [P16, 4, 1536]); tmpc = xp4
        acc_s = sb3("acc_s", [P16, 1536])
        k.dma("sp", xp4[:, 0:3, :], sconv_d[:, :, :], writes=[xp4], semkey="ld_xp4", nowaw=True)
        k.dma("sp", cwb[:].rearrange("p a b -> p (a b)"), convw_d.rearrange("a b -> (a b)").partition_broadcast(P16), writes=[cwb], semkey="ld_cwb")
        k.op("act", lambda a: a.copy(out=xp4[:, 3, :], in_=pqkv[:]), reads=[pqkv], writes=[xp4])
        k.dma("sp", convs_d[:, :, :], xp4[:, 1:4, :], reads=[xp4], semkey="st_smisc")
        k.op("dve", lambda v: v.tensor_tensor(out=tmpc[:], in0=xp4[:], in1=cwb[:], op=ALU.mult), reads=[xp4, cwb], writes=[tmpc])
        k.op("dve", lambda v: v.tensor_reduce(out=acc_s[:], in_=tmpc[:].rearrange("p j c -> p c j"), axis=AX.X, op=ALU.add), reads=[tmpc], writes=[acc_s])
        k.op("act", lambda a: a.activation(out=qkv_s[:].rearrange("p a b -> p (a b)"), in_=acc_s[:], func=ACT.Silu), reads=[acc_s], writes=[qkv_s])
        sqs = sb3("sqs", [P16, 8, 128]); sss = sb3("sss", [P16, 8])
        k.op("dve", lambda v: v.tensor_tensor(out=sqs[:], in0=qkv_s[:, 0:8, :], in1=qkv_s[:, 0:8, :], op=ALU.mult), reads=[qkv_s], writes=[sqs])
        k.op("dve", lambda v: v.tensor_reduce(out=sss[:], in_=sqs[:], axis=AX.X, op=ALU.add), reads=[sqs], writes=[sss])
        k.op("act", lambda a: a.activation(out=sss[:], in_=sss[:], func=ACT.Ln, bias=L2_EPS, scale=1.0), reads=[sss], writes=[sss])
        k.op("act", lambda a: a.activation(out=sss[:, 0:4], in_=sss[:, 0:4], func=ACT.Exp, bias=float(-0.5 * np.log(128.0)), scale=-0.5), reads=[sss], writes=[sss])
        k.op("act", lambda a: a.activation(out=sss[:, 4:8], in_=sss[:, 4:8], func=ACT.Exp, scale=-0.5), reads=[sss], writes=[sss])
        k.op("dve", lambda v: v.tensor_tensor(out=qkv_s[:, 0:8, :], in0=qkv_s[:, 0:8, :], in1=bc(sss[:].rearrange("p (a b) -> p a b", b=1), [P16, 8, 128]), op=ALU.mult),
             reads=[qkv_s, sss], writes=[qkv_s])
        k.barrier()
        k3s.close()
        k3s = contextlib.ExitStack()
        Ssb = sb3("Ssb", [128, P16 * 4, 128])
        k.dma("sp", Ssb[:], sdelta_d.rearrange("b h k v -> k (b h) v"), writes=[Ssb], semkey="ld_Ssb")
        qkT_s = sb3("qkT_s", [128, 8, P16])
        for c in range(8):
            k.op("pe", lambda p, c=c: p.transpose(out=PB[0][:, c * P16:(c + 1) * P16], in_=qkv_s[:, c, :], identity=ident[0:P16, 0:P16]),
                 reads=[qkv_s, ident], writes=[PB[0]])
        k.op("dve", lambda v: v.tensor_copy(out=qkT_s[:].rearrange("p a b -> p (a b)"), in_=PB[0][:, 0:8 * P16]), reads=[PB[0]], writes=[qkT_s])
        dmask = sb3("dmask", [P16, P16, 128])
        k.op("pool", lambda g: g.tensor_copy(out=dmask[:], in_=bc(ident[0:P16, 0:P16].rearrange("p (a b) -> p a b", b=1), [P16, P16, 128])),
             reads=[ident], writes=[dmask])
        egm = sb3("egm", [P16, P16, 4]); egbc = sb3("egbc", [128, P16 * 4])
        k.op("dve", lambda v: v.tensor_tensor(out=egm[:], in0=bc(eg_s[:].rearrange("p (a b) -> p a b", a=1), [P16, P16, 4]),
                                              in1=bc(ident[0:P16, 0:P16].rearrange("p (a b) -> p a b", b=1), [P16, P16, 4]), op=ALU.mult),
             reads=[eg_s, ident], writes=[egm])
        k.op("pe", lambda p: p.matmul(PB[1][:, 0:P16 * 4], lhsT=ones[0:P16, :], rhs=egm[:].rearrange("p a b -> p (a b)"), start=True, stop=True),
             reads=[ones, egm], writes=[PB[1]])
        k.op("dve", lambda v: v.tensor_copy(out=egbc[:], in_=PB[1][:, 0:P16 * 4]), reads=[PB[1]], writes=[egbc])
        pred = sb3("pred", [P16, 4, 128]); qS = sb3("qS", [P16, 4, 128]); tmpd = sb3("tmpd", [P16, P16, 128])
        dd = sb3("dd", [P16, 4, 128]); Dm = sb3("Dm", [P16, P16, 128]); o_s = sb3("o_s", [P16, 4, 128])
        qk_s = sb3("qk_s", [P16, 4]); qkt = sb3("qkt", [P16, 4, 128])
        k.op("dve", lambda v: v.tensor_tensor(out=qkt[:], in0=qkv_s[:, 0:4, :], in1=qkv_s[:, 4:8, :], op=ALU.mult), reads=[qkv_s], writes=[qkt])
        k.op("dve", lambda v: v.tensor_reduce(out=qk_s[:], in_=qkt[:], axis=AX.X, op=ALU.add), reads=[qkt], writes=[qk_s])
        for h in range(4):
            for which, dst in ((4, pred), (0, qS)):
                banks = [PB[2], PB[3], PB[4], PB[5]] if which == 4 else [PB[6], PB[7], PB[0], PB[1]]
                for b in range(P16):
                    pb = banks[b // 4]
                    k.op("pe", lambda p, b=b, pb=pb, which=which, h=h: p.matmul(pb[0:P16, (b % 4) * 128:(b % 4 + 1) * 128], lhsT=qkT_s[:, which + h, :],
                                                                               rhs=Ssb[:, b * 4 + h, :], start=True, stop=True),
                         reads=[qkT_s, Ssb], writes=[pb])
                for j in range(4):
                    k.op("dve", lambda v, j=j, banks=banks: v.tensor_tensor(out=tmpd[:, 4 * j:4 * j + 4, :], in0=banks[j][0:P16, :].rearrange("p (a b) -> p a b", a=4),
                                                                            in1=dmask[:, 4 * j:4 * j + 4, :], op=ALU.mult), reads=[banks[j], dmask], writes=[tmpd])
                k.op("dve", lambda v, dst=dst, h=h: v.tensor_reduce(out=dst[:, h, :], in_=tmpd[:].rearrange("p b v -> p v b"), axis=AX.X, op=ALU.add),
                     reads=[tmpd], writes=[dst])
            k.op("dve", lambda v, h=h: v.scalar_tensor_tensor(out=dd[:, h, :], in0=pred[:, h, :], scalar=eg_s[:, h:h + 1], in1=qkv_s[:, 8 + h, :],
                                                               op0=ALU.mult, op1=ALU.subtract), reads=[pred, eg_s, qkv_s], writes=[dd])
            k.op("dve", lambda v, h=h: v.tensor_scalar(out=dd[:, h, :], in0=dd[:, h, :], scalar1=nbeta_s[:, h:h + 1], scalar2=None, op0=ALU.mult),
                 reads=[dd, nbeta_s], writes=[dd])
            k.op("dve", lambda v, h=h: v.tensor_scalar(out=o_s[:, h, :], in0=dd[:, h, :], scalar1=qk_s[:, h:h + 1], scalar2=None, op0=ALU.mult),
                 reads=[dd, qk_s], writes=[o_s])
            k.op("dve", lambda v, h=h: v.scalar_tensor_tensor(out=o_s[:, h, :], in0=qS[:, h, :], scalar=eg_s[:, h:h + 1], in1=o_s[:, h, :],
                                                               op0=ALU.mult, op1=ALU.add), reads=[qS, eg_s, o_s], writes=[o_s])
            k.op("dve", lambda v, h=h: v.tensor_tensor(out=Dm[:], in0=bc(dd[:, h:h + 1, :], [P16, P16, 128]), in1=dmask[:], op=ALU.mult),
                 reads=[dd, dmask], writes=[Dm])
            banks = [PB[2], PB[3], PB[4], PB[5]]
            for b in range(P16):
                pb = banks[b // 4]
                k.op("pe", lambda p, b=b, pb=pb, h=h: p.matmul(pb[:, (b % 4) * 128:(b % 4 + 1) * 128], lhsT=qkv_s[:, 4 + h, :], rhs=Dm[:, b, :],
                                                               start=True, stop=True), reads=[qkv_s, Dm], writes=[pb])
            for b in range(P16):
                pb = banks[b // 4]
                k.op("dve", lambda v, b=b, pb=pb, h=h: v.scalar_tensor_tensor(out=Ssb[:, b * 4 + h, :], in0=Ssb[:, b * 4 + h, :], scalar=egbc[:, b * 4 + h:b * 4 + h + 1],
                                                                              in1=pb[:, (b % 4) * 128:(b % 4 + 1) * 128], op0=ALU.mult, op1=ALU.add),
                     reads=[Ssb, egbc, pb], writes=[Ssb])
        k.dma("sp", deltas_d.rearrange("b h k v -> k (b h) v"), Ssb[:], reads=[Ssb], semkey="st_smisc")
        oss_s = sb3("oss_s", [P16, 4])
        k.op("dve", lambda v: v.tensor_tensor(out=qkt[:], in0=o_s[:], in1=o_s[:], op=ALU.mult), reads=[o_s], writes=[qkt])
        k.op("dve", lambda v: v.tensor_reduce(out=oss_s[:], in_=qkt[:], axis=AX.X, op=ALU.add), reads=[qkt], writes=[oss_s])
        k.op("act", lambda a: a.activation(out=oss_s[:], in_=oss_s[:], func=ACT.Ln, bias=RMS_EPS, scale=1.0 / 128.0), reads=[oss_s], writes=[oss_s])
        k.op("act", lambda a: a.activation(out=oss_s[:], in_=oss_s[:], func=ACT.Exp, scale=-0.5), reads=[oss_s], writes=[oss_s])
        k.op("dve", lambda v: v.tensor_tensor(out=o_s[:], in0=o_s[:], in1=bc(oss_s[:].rearrange("p (a b) -> p a b", b=1), [P16, 4, 128]), op=ALU.mult),
             reads=[o_s, oss_s], writes=[o_s])
        k.op("dve", lambda v: v.tensor_tensor(out=o_s[:], in0=o_s[:], in1=bc(norma_bc[0:P16, :].rearrange("p (a b) -> p a b", a=1), [P16, 4, 128]), op=ALU.mult),
             reads=[o_s, norma_bc], writes=[o_s])
        k.op("dve", lambda v: v.tensor_tensor(out=mix_s[:, 0:512], in0=o_s[:].rearrange("p a b -> p (a b)"), in1=zAs_s[:], op=ALU.mult),
             reads=[o_s, zAs_s], writes=[mix_s])
        k.barrier()
        k3s.close()
        k3s = contextlib.ExitStack()
        Kc = sb3("Kc", [128, P16, 128]); Vc = sb3("Vc", [128, P16, 128])
        KcT = sb3("KcT", [128, P16, 128], BF16); VcB = sb3("VcB", [128, P16, 128], BF16)
        k.dma("sp", Kc[:], sk_d.rearrange("b s c -> s b c"), writes=[Kc], semkey="ld_Kc")
        k.dma("sp", Vc[:], sv_d.rearrange("b s c -> s b c"), writes=[Vc], semkey="ld_Vc")
        k.dma("sp", swaks_d[:, 0:127, :], sk_d[:, 1:128, :], semkey="st_smisc")
        k.dma("sp", swavs_d[:, 0:127, :], sv_d[:, 1:128, :], semkey="st_smisc")
        k.dma("sp", swaks_d[:, 127, :], kr_s[:], reads=[kr_s], semkey="st_smisc")
        k.dma("sp", swavs_d[:, 127, :], v_s[:], reads=[v_s], semkey="st_smisc")
        k.op("pool", lambda g: g.tensor_copy(out=VcB[:], in_=Vc[:]), reads=[Vc], writes=[VcB])
        for b in range(P16):
            pb = [PB[2], PB[3], PB[4], PB[5]][b // 4]
            k.op("pe", lambda p, b=b, pb=pb: p.transpose(out=pb[:, (b % 4) * 128:(b % 4 + 1) * 128], in_=Kc[:, b, :], identity=ident[:]),
                 reads=[Kc, ident], writes=[pb])
        for j in range(4):
            pb = [PB[2], PB[3], PB[4], PB[5]][j]
            k.op("act", lambda a, j=j, pb=pb: a.copy(out=KcT[:, 4 * j:4 * j + 4, :], in_=pb[:, :].rearrange("p (a b) -> p a b", a=4)), reads=[pb], writes=[KcT])
        qT_s = sb3("qT_s", [128, 4, P16]); Aq = sb3("Aq", [128, P16, 2, 4], BF16)
        knT = sb3("knT", [128, P16], BF16); zBT = sb3("zBT", [128, 4, P16])
        for c in range(4):
            k.op("pe", lambda p, c=c: p.transpose(out=PB[6][:, c * P16:(c + 1) * P16], in_=qr_s[:, c * 128:(c + 1) * 128], identity=ident[0:P16, 0:P16]),
                 reads=[qr_s, ident], writes=[PB[6]])
        k.op("pe", lambda p: p.transpose(out=PB[6][:, 4 * P16:5 * P16], in_=kr_s[:], identity=ident[0:P16, 0:P16]), reads=[kr_s, ident], writes=[PB[6]])
        for c in range(4):
            k.op("pe", lambda p, c=c: p.transpose(out=PB[6][:, (5 + c) * P16:(6 + c) * P16], in_=zBs_s[:, c * 128:(c + 1) * 128], identity=ident[0:P16, 0:P16]),
                 reads=[zBs_s, ident], writes=[PB[6]])
        k.op("dve", lambda v: v.tensor_copy(out=qT_s[:].rearrange("p a b -> p (a b)"), in_=PB[6][:, 0:4 * P16]), reads=[PB[6]], writes=[qT_s])
        k.op("dve", lambda v: v.tensor_copy(out=knT[:], in_=PB[6][:, 4 * P16:5 * P16]), reads=[PB[6]], writes=[knT])
        k.op("dve", lambda v: v.tensor_copy(out=zBT[:].rearrange("p a b -> p (a b)"), in_=PB[6][:, 5 * P16:9 * P16]), reads=[PB[6]], writes=[zBT])
        k.op("pool", lambda g: g.memset(Aq[:], 0.0), writes=[Aq])
        k.op("dve", lambda v: v.tensor_copy(out=Aq[0:64, :, 0, :], in_=qT_s[0:64, :, :].rearrange("p c b -> p b c")), reads=[qT_s], writes=[Aq])
        k.op("dve", lambda v: v.tensor_copy(out=Aq[64:128, :, 1, :], in_=qT_s[64:128, :, :].rearrange("p c b -> p b c")), reads=[qT_s], writes=[Aq])
        for b in range(P16):
            k.op("pe", lambda p, b=b: p.matmul(PB[7][:, b * 8:(b + 1) * 8], lhsT=KcT[:, b, :], rhs=Aq[:, b, :, :].rearrange("p a b -> p (a b)"),
                                               start=True, stop=True), reads=[KcT, Aq], writes=[PB[7]])
        STs = sb3("STs", [128, 128])
        k.op("dve", lambda v: v.tensor_copy(out=STs[:], in_=PB[7][:, 0:128]), reads=[PB[7]], writes=[STs])
        k.op("pe", lambda p: p.transpose(out=PB[0][:, 0:128], in_=STs[:], identity=ident[:]), reads=[STs, ident], writes=[PB[0]])
        k.op("pe", lambda p: p.matmul(PB[0][:, 128:128 + P16], lhsT=Aq[:].rearrange("p a b c -> p (a b c)"), rhs=knT[:], start=True, stop=True),
             reads=[Aq, knT], writes=[PB[0]])
        M2 = sb3("M2", [128, P16]); M2t = sb3("M2t", [128, P16])
        k.op("pool", lambda g: g.affine_select(out=M2t[:], in_=ones[:, 0:P16], pattern=[[-8, P16]], compare_op=ALU.is_ge, fill=0.0, base=0, channel_multiplier=1),
             reads=[ones], writes=[M2t])
        k.op("pool", lambda g: g.affine_select(out=M2[:], in_=M2t[:], pattern=[[8, P16]], compare_op=ALU.is_ge, fill=0.0, base=7, channel_multiplier=-1),
             reads=[M2t], writes=[M2])
        sm = sb3("sm", [128, 16]); tmps = sb3("tmps", [128, P16])
        sinkcol = sb3("sinkcol", [128, 1])
        k.dma("sp", sinkcol[:], sinkcol_d[:, :], writes=[sinkcol], semkey="ld_sinkcol")
        k.op("dve", lambda v: v.tensor_tensor(out=tmps[:], in0=PB[0][:, 128:128 + P16], in1=M2[:], op=ALU.mult), reads=[PB[0], M2], writes=[tmps])
        k.op("dve", lambda v: v.tensor_reduce(out=sm[:, 0:1], in_=tmps[:], axis=AX.X, op=ALU.add), reads=[tmps], writes=[sm])
        k.op("dve", lambda v: v.tensor_reduce(out=sm[:, 1:2], in_=PB[0][:, 0:128], axis=AX.X, op=ALU.max), reads=[PB[0]], writes=[sm])
        k.op("dve", lambda v: v.tensor_tensor(out=sm[:, 1:2], in0=sm[:, 1:2], in1=sm[:, 0:1], op=ALU.max), reads=[sm], writes=[sm])
        k.op("dve", lambda v: v.tensor_tensor(out=sm[:, 1:2], in0=sm[:, 1:2], in1=sinkcol[:], op=ALU.max), reads=[sm, sinkcol], writes=[sm])
        k.op("dve", lambda v: v.tensor_scalar(out=sm[:, 2:3], in0=sm[:, 1:2], scalar1=-1.0, scalar2=None, op0=ALU.mult), reads=[sm], writes=[sm])
        Ps = sb3("Ps", [128, 128])
        k.op("act", lambda a: a.activation(out=Ps[:], in_=PB[0][:, 0:128], func=ACT.Exp, bias=sm[:, 2:3], scale=1.0, accum_out=sm[:, 3:4]),
             reads=[PB[0], sm], writes=[Ps, sm])
        k.op("act", lambda a: a.activation(out=sm[:, 4:5], in_=sm[:, 0:1], func=ACT.Exp, bias=sm[:, 2:3], scale=1.0), reads=[sm], writes=[sm])
        k.op("act", lambda a: a.activation(out=sm[:, 5:6], in_=sinkcol[:], func=ACT.Exp, bias=sm[:, 2:3], scale=1.0), reads=[sm, sinkcol], writes=[sm])
        k.op("dve", lambda v: v.tensor_tensor(out=sm[:, 6:7], in0=sm[:, 3:4], in1=sm[:, 4:5], op=ALU.add), reads=[sm], writes=[sm])
        k.op("dve", lambda v: v.tensor_tensor(out=sm[:, 6:7], in0=sm[:, 6:7], in1=sm[:, 5:6], op=ALU.add), reads=[sm], writes=[sm])
        k.op("dve", lambda v: v.reciprocal(out=sm[:, 7:8], in_=sm[:, 6:7]), reads=[sm], writes=[sm])
        k.op("dve", lambda v: v.tensor_scalar(out=Ps[:], in0=Ps[:], scalar1=sm[:, 7:8], scalar2=None, op0=ALU.mult), reads=[Ps, sm], writes=[Ps])
        k.op("dve", lambda v: v.tensor_tensor(out=sm[:, 8:9], in0=sm[:, 4:5], in1=sm[:, 7:8], op=ALU.mult), reads=[sm], writes=[sm])
        PsT = sb3("PsT", [128, 128], BF16); Wd = sb3("Wd", [128, P16]); Wn = sb3("Wn", [P16, 128], BF16)
        k.op("pe", lambda p: p.transpose(out=PB[1][:, 0:128], in_=Ps[:], identity=ident[:]), reads=[Ps, ident], writes=[PB[1]])
        k.op("act", lambda a: a.copy(out=PsT[:], in_=PB[1][:, 0:128]), reads=[PB[1]], writes=[PsT])
        k.op("dve", lambda v: v.tensor_scalar(out=Wd[:], in0=M2[:], scalar1=sm[:, 8:9], scalar2=None, op0=ALU.mult), reads=[M2, sm], writes=[Wd])
        k.op("pe", lambda p: p.transpose(out=PB[1][0:P16, 128:256], in_=Wd[:], identity=ident[:]), reads=[Wd, ident], writes=[PB[1]])
        k.op("act", lambda a: a.copy(out=Wn[:], in_=PB[1][0:P16, 128:256]), reads=[PB[1]], writes=[Wn])
        for b in range(P16):
            k.op("pe", lambda p, b=b: p.matmul(PB[2][:, b * 8:(b + 1) * 8], lhsT=VcB[:, b, :], rhs=PsT[:, b * 8:(b + 1) * 8], start=True, stop=False),
                 reads=[VcB, PsT], writes=[PB[2]])
            k.op("pe", lambda p, b=b: p.matmul(PB[2][:, b * 8:(b + 1) * 8], lhsT=vsb[:], rhs=Wn[:, b * 8:(b + 1) * 8], start=False, stop=True),
                 reads=[vsb, Wn], writes=[PB[2]])
        obT = sb3("obT", [128, 4, P16])
        OT4 = PB[2][:, 0:128].rearrange("p (b k c) -> p b k c", b=P16, k=2)
        k.op("dve", lambda v: v.tensor_copy(out=obT[0:64, :, :].rearrange("p c b -> p b c"), in_=OT4[0:64, :, 0, :]), reads=[PB[2]], writes=[obT])
        k.op("dve", lambda v: v.tensor_copy(out=obT[64:128, :, :].rearrange("p c b -> p b c"), in_=OT4[64:128, :, 1, :]), reads=[PB[2]], writes=[obT])
        mixT_s = sb3("mixT_s", [128, 8, P16], BF16)
        k.op("dve", lambda v: v.tensor_tensor(out=mixT_s[:, 4:8, :], in0=obT[:], in1=zBT[:], op=ALU.mult), reads=[obT, zBT], writes=[mixT_s])
        for c in range(4):
            k.op("pe", lambda p, c=c: p.transpose(out=PB[3][:, c * P16:(c + 1) * P16], in_=mix_s[:, c * 128:(c + 1) * 128], identity=ident[0:P16, 0:P16]),
                 reads=[mix_s, ident], writes=[PB[3]])
        k.op("dve", lambda v: v.tensor_copy(out=mixT_s[:, 0:4, :].rearrange("p a b -> p (a b)"), in_=PB[3][:, 0:4 * P16]), reads=[PB[3]], writes=[mixT_s])
        for j in range(2):
            for kc in range(8):
                k.op("pe", lambda p, j=j, kc=kc: p.matmul(PB[4 + j][0:P16, :], lhsT=mixT_s[:, kc, :], rhs=Wo[:, kc, j * 512:(j + 1) * 512],
                                                          start=(kc == 0), stop=(kc == 7)), reads=[mixT_s, Wo], writes=[PB[4 + j]])
        ypre_s = sb3("ypre_s", [P16, D]); yo_s = sb3("yo_s", [P16, D]); st_s = sb3("st_s", [P16, 8])
        for j in range(2):
            k.op("dve", lambda v, j=j: v.scalar_tensor_tensor(out=ypre_s[:, j * 512:(j + 1) * 512], in0=mod_s[:, 2 * D + j * 512:2 * D + (j + 1) * 512], scalar=1.0,
                                                               in1=PB[4 + j][0:P16, :], op0=ALU.add, op1=ALU.mult), reads=[mod_s, PB[4 + j]], writes=[ypre_s])
        k.op("dve", lambda v: v.scalar_tensor_tensor(out=ypre_s[:], in0=xs[:], scalar=ALPHA, in1=ypre_s[:], op0=ALU.mult, op1=ALU.add),
             reads=[xs, ypre_s], writes=[ypre_s])
        k.op("act", lambda a: a.activation(out=yo_s[:], in_=ypre_s[:], func=ACT.Identity, accum_out=st_s[:, 0:1]), reads=[ypre_s], writes=[yo_s, st_s])
        k.op("act", lambda a: a.activation(out=yo_s[:], in_=ypre_s[:], func=ACT.Square, accum_out=st_s[:, 1:2]), reads=[ypre_s], writes=[yo_s, st_s])
        k.op("dve", lambda v: v.tensor_scalar(out=st_s[:, 2:3], in0=st_s[:, 0:1], scalar1=1.0 / D, scalar2=None, op0=ALU.mult), reads=[st_s], writes=[st_s])
        k.op("dve", lambda v: v.tensor_tensor(out=st_s[:, 3:4], in0=st_s[:, 2:3], in1=st_s[:, 2:3], op=ALU.mult), reads=[st_s], writes=[st_s])
        k.op("dve", lambda v: v.scalar_tensor_tensor(out=st_s[:, 4:5], in0=st_s[:, 1:2], scalar=1.0 / D, in1=st_s[:, 3:4], op0=ALU.mult, op1=ALU.subtract),
             reads=[st_s], writes=[st_s])
        k.op("act", lambda a: a.activation(out=st_s[:, 5:6], in_=st_s[:, 4:5], func=ACT.Ln, bias=LN_EPS, scale=1.0), reads=[st_s], writes=[st_s])
        k.op("act", lambda a: a.activation(out=st_s[:, 5:6], in_=st_s[:, 5:6], func=ACT.Exp, scale=-0.5), reads=[st_s], writes=[st_s])
        k.op("dve", lambda v: v.scalar_tensor_tensor(out=st_s[:, 6:7], in0=st_s[:, 2:3], scalar=-1.0, in1=st_s[:, 5:6], op0=ALU.mult, op1=ALU.mult),
             reads=[st_s], writes=[st_s])
        k.op("act", lambda a: a.activation(out=yo_s[:], in_=ypre_s[:], func=ACT.Identity, bias=st_s[:, 6:7], scale=st_s[:, 5:6]), reads=[ypre_s, st_s], writes=[yo_s])
        k.op("dve", lambda v: v.tensor_tensor(out=yo_s[:], in0=yo_s[:], in1=lng_bc[0:P16, :], op=ALU.mult), reads=[yo_s, lng_bc], writes=[yo_s])
        k.op("dve", lambda v: v.tensor_tensor(out=yo_s[:], in0=yo_s[:], in1=lnb_bc[0:P16, :], op=ALU.add), reads=[yo_s, lnb_bc], writes=[yo_s])
        k.dma("sp", ys_d[:, :], yo_s[:], reads=[yo_s], semkey="st_smisc")
        k.barrier()
        k3s.close()

    k.barrier()
    k2.close()
    k1s.close()
    identb = k.sb("identb", [128, 128], BF16); Ub = k.sb("Ub", [128, 128], BF16)
    k.op("pool", lambda g: g.tensor_copy(out=identb[:], in_=ident[:]), reads=[ident], writes=[identb])
    k.op("pool", lambda g: g.tensor_copy(out=Ub[:], in_=U[:]), reads=[U], writes=[Ub])
    Xt = [k.sb("Xt%d" % i, [128, D]) for i in range(2)]
    Xb = k.sb("Xb", [128, D], BF16)
    hT = k.sb("hT", [128, 8, 128], BF16)
    pre = k.sb("pre", [128, 12, 131])
    k.op("pool", lambda g: g.memset(pre[:], 0.0), writes=[pre])
    cm = [k.sb("cm%d" % i, [128, 12, 128]) for i in range(2)]
    qkvs = k.sb("qkvs", [128, 12, 128])
    qkb = k.sb("qkb", [128, 12, 128], BF16)
    sqb = k.sb("sqb", [128, 8, 128], BF16)
    lnss = k.sb("lnss", [128, 8, 128])
    rn = lnss
    zAs = k.sb("zAs", [128, 512]); zBs = k.sb("zBs", [128, 512])
    qrb = k.sb("qrb", [128, 512], BF16); kr = k.sb("kr", [128, 128])
    rt = [k.sb("rt%d" % i, [128, 8, 32]) for i in range(4)]
    vB = [k.sb("vB%d" % i, [128, 128], BF16) for i in range(2)]
    vB32 = k.sb("vB32", [128, 128])
    kTb = [k.sb("kTb%d" % i, [128, 128], BF16) for i in range(2)]
    k.op("pool", lambda g: g.memset(kTb[1][:], 0.0), writes=[kTb[1]])
    k.op("pool", lambda g: g.memset(vB[1][:], 0.0), writes=[vB[1]])
    qTb = k.sb("qTb", [128, 4, 128], BF16)
    bd = k.sb("bd", [128, 8]); bdt = k.sb("bdt", [128, 8])
    beta = k.sb("beta", [128, 4]); negbeta = k.sb("negbeta", [128, 4]); gg = k.sb("gg", [128, 4])
    gsp = k.sb("gsp", [128, 8], BF16); gtmp = k.sb("gtmp", [128, 4])
    gUh = k.sb("gUh", [128, 4, 128], BF16); gUl = k.sb("gUl", [128, 4, 128], BF16)
    Gc = k.sb("Gc", [128, 4]); negGc = k.sb("negGc", [128, 4]); eG = k.sb("eG", [128, 4]); beG = k.sb("beG", [128, 4])
    edG = k.sb("edG", [128, 4]); egl = k.sb("egl", [128, 4]); dG = k.sb("dG", [128, 4]); Glb = k.sb("Glb", [128, 4])
    tmp1 = k.sb("tmp1", [128, 4, 128]); tmp2 = k.sb("tmp2", [128, 4, 128])
    E1 = tmp1; E2 = tmp2
    eGbc = k.sb("eGbc", [128, 4, 128]); qdT = k.sb("qdT", [128, 4, 128], BF16)
    AqkT = k.sb("AqkT", [128, 4, 128], BF16)
    Y0f = k.sb("Y0f", [128, 4, 128])
    XRh = [k.sb("XRh%d" % i, [128, 4, 256], BF16) for i in range(2)]
    XRl = [k.sb("XRl%d" % i, [128, 4, 256], BF16) for i in range(2)]
    Yh = [k.sb("Yh%d" % i, [128, 4, 128], BF16) for i in range(2)]
    Yl = [k.sb("Yl%d" % i, [128, 4, 128], BF16) for i in range(2)]
    Rf = k.sb("Rf", [128, 4, 128])
    kbt = k.sb("kbt", [128, 4, 128], BF16); kdt = k.sb("kdt", [128, 4, 128], BF16); bvt = k.sb("bvt", [128, 4, 128], BF16)
    wT = k.sb("wT", [128, 4, 128], BF16); ut = k.sb("ut", [128, 4, 128]); uub = k.sb("uub", [128, 4, 128], BF16)
    S = k.sb("S", [128, 4, 128]); Sb = k.sb("Sb", [128, 4, 128], BF16)
    k.op("pool", lambda g: g.memset(S[:], 0.0), writes=[S])
    k.op("pool", lambda g: g.memset(Sb[:], 0.0), writes=[Sb])
    oss = k.sb("oss", [128, 4]); orr = k.sb("orr", [128, 4]); ojunk = k.sb("ojunk", [128, 128])
    nz = k.sb("nz", [128, 4, 128])
    mixb = k.sb("mixb", [128, D], BF16); obt = k.sb("obt", [128, 512])
    mixT = k.sb("mixT", [128, 8, 128], BF16)
    SC = k.sb("SC", [128, 8, 256]); Pb = k.sb("Pb", [128, 8, 256], BF16)
    PTb = k.sb("PTb", [128, 16, 128], BF16)
    mx = k.sb("mx", [128, 8]); negm = k.sb("negm", [128, 8]); rs = k.sb("rs", [128, 8]); es_ = k.sb("es_", [128, 8])
    rden = k.sb("rden", [128, 8])
    SCf = SC[:].rearrange("p a b -> p (a b)")
    ypre = SCf[:, 0:D]
    st = k.sb("st", [128, 8])
    Yt = [SCf[:, D:2 * D]]

    def v3(ap, a):
        return ap.rearrange("p (a b) -> p a b", a=a)

    def pbf(pb):
        return pb[:, :].bitcast(BF16)

    def mm(out, lhsT, rhs, start, stop, reads, writes):
        k.op("pe", lambda p: p.matmul(out, lhsT=lhsT, rhs=rhs, start=start, stop=stop), reads=reads, writes=writes)

    def tr(out, in_, idn, reads, writes):
        k.op("pe", lambda p: p.transpose(out=out, in_=in_, identity=idn), reads=reads + [idn], writes=writes)

    for n in range(NT):
        X = Xt[n % 2]
        k.dma("sp", X[:], x_d[n * 128:(n + 1) * 128, :], writes=[X], semkey="ld_" + X.name)
        cs_t = cst[n % 2]
        k.dma("sp", cs_t[:, 0, :], cosk_d[n * 128:(n + 1) * 128, :], writes=[cs_t], semkey="ld_" + cs_t.name, nowaw=True)
        k.dma("sp", cs_t[:, 1, :], sink_d[n * 128:(n + 1) * 128, :], writes=[cs_t], semkey="ld_" + cs_t.name, nowaw=True)
        k.op("pool", lambda g: g.tensor_copy(out=Xb[:], in_=X[:]), reads=[X], writes=[Xb])
        P0b = pbf(PB[0])
        for kc in range(8):
            tr(P0b[:, kc * 128:(kc + 1) * 128], Xb[:, kc * 128:(kc + 1) * 128], identb[:], [Xb], [PB[0]])
        for kc in range(8):
            k.op("act" if kc % 2 == 0 else "dve",
                 (lambda a, kc=kc: a.activation(out=hT[:, kc, :], in_=P0b[:, kc * 128:(kc + 1) * 128], func=ACT.Identity,
                                                bias=mod_fm[:, kc:kc + 1], scale=mod_fm[:, 8 + kc:9 + kc])) if kc % 2 == 0 else
                 (lambda v, kc=kc: v.tensor_scalar(out=hT[:, kc, :], in0=P0b[:, kc * 128:(kc + 1) * 128], scalar1=mod_fm[:, 8 + kc:9 + kc],
                                                   scalar2=mod_fm[:, kc:kc + 1], op0=ALU.mult, op1=ALU.add)),
                 reads=[PB[0], mod_fm], writes=[hT])
        pc = pre
        k.op("pool", lambda g: g.tensor_copy(out=pc[:, :, 0:3], in_=pc[:, :, 128:131]), reads=[pc], writes=[pc])
        for c in range(12):
            for kc in range(8):
                mm(PB[2 + c // 4][:, (c % 4) * 128:(c % 4 + 1) * 128], Wf[:, kc, c * 128:(c + 1) * 128], hT[:, kc, :],
                   kc == 0, kc == 7, [Wf, hT], [PB[2 + c // 4]])
        for j in range(3):
            if j != 1:
                k.op("act", lambda a, j=j: a.copy(out=pc[:, j * 4:(j + 1) * 4, 3:131], in_=v3(PB[2 + j][:, :], 4)), reads=[PB[2 + j]], writes=[pc])
            else:
                k.op("dve", lambda v, j=j: v.tensor_copy(out=pc[:, j * 4:(j + 1) * 4, 3:131], in_=v3(PB[2 + j][:, :], 4)), reads=[PB[2 + j]], writes=[pc])
        k.op("dve", lambda v: v.tensor_tensor(out=cm[0][:], in0=pc[:, :, 0:128], in1=bc(cwT[:, 0:12].rearrange("p (a b) -> p a b", b=1), [128, 12, 128]), op=ALU.mult),
             reads=[pc, cwT], writes=[cm[0]])
        for j in range(1, 4):
            k.op("pool", lambda g, j=j: g.tensor_tensor(out=cm[1][:], in0=pc[:, :, j:j + 128], in1=bc(cwT[:, j * 12:(j + 1) * 12].rearrange("p (a b) -> p a b", b=1), [128, 12, 128]), op=ALU.mult),
                 reads=[pc, cwT], writes=[cm[1]])
            k.op("dve", lambda v: v.tensor_tensor(out=cm[0][:], in0=cm[0][:], in1=cm[1][:], op=ALU.add), reads=[cm[0], cm[1]], writes=[cm[0]])
        k.op("act", lambda a: a.activation(out=qkvs[:], in_=cm[0][:], func=ACT.Silu), reads=[cm[0]], writes=[qkvs])
        offs = [(0, 512), (512, 512), (1024, 512), (1536, 264)]
        for j, (o, w_) in enumerate(offs):
            pb = PB[5 + j] if j < 3 else PB[0]
            for kc in range(8):
                mm(pb[:, 0:w_], hT[:, kc, :], Wt[:, kc, o:o + w_], kc == 0, kc == 7, [hT, Wt], [pb])
        PzA, PqB, PzB, Pk = PB[5], PB[6], PB[7], PB[0]
        k.op("act", lambda a: a.activation(out=zAs[:], in_=PzA[:, :], func=ACT.Silu), reads=[PzA], writes=[zAs])
        k.op("act", lambda a: a.activation(out=zBs[:], in_=PzB[:, :], func=ACT.Silu), reads=[PzB], writes=[zBs])
        k.op("pool", lambda g: g.tensor_tensor(out=sqb[:], in0=qkvs[:, 0:8, :], in1=qkvs[:, 0:8, :], op=ALU.mult), reads=[qkvs], writes=[sqb])
        for j in range(2):
            mm(PB[1 + j][:, :], onesb[:], sqb[:, j * 4:(j + 1) * 4, :].rearrange("p a b -> p (a b)"), True, True, [onesb, sqb], [PB[1 + j]])
        for j in range(2):
            k.op("act", lambda a, j=j: a.activation(out=lnss[:, j * 4:(j + 1) * 4, :].rearrange("p a b -> p (a b)"), in_=PB[1 + j][:, :],
                                                    func=ACT.Ln, bias=L2_EPS, scale=1.0), reads=[PB[1 + j]], writes=[lnss])
        k.op("act", lambda a: a.activation(out=rn[:, 0:4, :], in_=lnss[:, 0:4, :], func=ACT.Exp, bias=float(-0.5 * np.log(128.0)), scale=-0.5), reads=[lnss], writes=[rn])
        k.op("act", lambda a: a.activation(out=rn[:, 4:8, :], in_=lnss[:, 4:8, :], func=ACT.Exp, scale=-0.5), reads=[lnss], writes=[rn])
        k.op("dve", lambda v: v.tensor_tensor(out=qkvs[:, 0:8, :], in0=qkvs[:, 0:8, :], in1=rn[:], op=ALU.mult), reads=[qkvs, rn], writes=[qkvs])
        k.op("pool", lambda g: g.tensor_copy(out=qkb[:], in_=qkvs[:]), reads=[qkvs], writes=[qkb])
        q3 = v3(PqB[:, :], 8)
        qr3 = v3(qrb[:], 8)
        cq = bc(cs_t[:, 0:1, :], [128, 8, 32]); sq_ = bc(cs_t[:, 1:2, :], [128, 8, 32])
        k.op("dve", lambda v: v.tensor_tensor(out=rt[0][:], in0=q3[:, :, 0:32], in1=cq, op=ALU.mult), reads=[PqB, cs_t], writes=[rt[0]])
        k.op("dve", lambda v: v.tensor_tensor(out=rt[1][:], in0=q3[:, :, 32:64], in1=sq_, op=ALU.mult), reads=[PqB, cs_t], writes=[rt[1]])
        k.op("dve", lambda v: v.tensor_tensor(out=rt[2][:], in0=q3[:, :, 32:64], in1=cq, op=ALU.mult), reads=[PqB, cs_t], writes=[rt[2]])
        k.op("dve", lambda v: v.tensor_tensor(out=rt[3][:], in0=q3[:, :, 0:32], in1=sq_, op=ALU.mult), reads=[PqB, cs_t], writes=[rt[3]])
        k.op("pool", lambda g: g.tensor_tensor(out=qr3[:, :, 0:32], in0=rt[0][:], in1=rt[1][:], op=ALU.subtract), reads=[rt[0], rt[1]], writes=[qrb])
        k.op("pool", lambda g: g.tensor_tensor(out=qr3[:, :, 32:64], in0=rt[2][:], in1=rt[3][:], op=ALU.add), reads=[rt[2], rt[3]], writes=[qrb])
        k3 = v3(Pk[:, 0:128], 2)
        kr3 = v3(kr[:], 2)
        ck = bc(cs_t[:, 0:1, :], [128, 2, 32]); sk_ = bc(cs_t[:, 1:2, :], [128, 2, 32])
        k.op("dve", lambda v: v.tensor_tensor(out=rt[0][:, 0:2, :], in0=k3[:, :, 0:32], in1=ck, op=ALU.mult), reads=[Pk, cs_t], writes=[rt[0]])
        k.op("dve", lambda v: v.tensor_tensor(out=rt[1][:, 0:2, :], in0=k3[:, :, 32:64], in1=sk_, op=ALU.mult), reads=[Pk, cs_t], writes=[rt[1]])
        k.op("dve", lambda v: v.tensor_tensor(out=rt[2][:, 0:2, :], in0=k3[:, :, 32:64], in1=ck, op=ALU.mult), reads=[Pk, cs_t], writes=[rt[2]])
        k.op("dve", lambda v: v.tensor_tensor(out=rt[3][:, 0:2, :], in0=k3[:, :, 0:32], in1=sk_, op=ALU.mult), reads=[Pk, cs_t], writes=[rt[3]])
        k.op("pool", lambda g: g.tensor_tensor(out=kr3[:, :, 0:32], in0=rt[0][:, 0:2, :], in1=rt[1][:, 0:2, :], op=ALU.subtract), reads=[rt[0], rt[1]], writes=[kr])
        k.op("pool", lambda g: g.tensor_tensor(out=kr3[:, :, 32:64], in0=rt[2][:, 0:2, :], in1=rt[3][:, 0:2, :], op=ALU.add), reads=[rt[2], rt[3]], writes=[kr])
        vcur = vB[n % 2]; vprev = vB[(n + 1) % 2]
        k.op("dve", lambda v: v.tensor_copy(out=vcur[:], in_=Pk[:, 128:256]), reads=[Pk], writes=[vcur])
        if n == NT - 1:
            k.op("dve", lambda v: v.tensor_copy(out=vB32[:], in_=Pk[:, 128:256]), reads=[Pk], writes=[vB32])
        k.op("dve", lambda v: v.tensor_copy(out=bd[:], in_=Pk[:, 256:264]), reads=[Pk], writes=[bd])
        k.op("act", lambda a: a.activation(out=bdt[:, 0:4], in_=bd[:, 0:4], func=ACT.Exp, scale=-1.0), reads=[bd], writes=[bdt])
        k.op("dve", lambda v: v.tensor_scalar(out=bdt[:, 0:4], in0=bdt[:, 0:4], scalar1=1.0, scalar2=None, op0=ALU.add), reads=[bdt], writes=[bdt])
        k.op("dve", lambda v: v.reciprocal(out=beta[:], in_=bdt[:, 0:4]), reads=[bdt], writes=[beta])
        k.op("dve", lambda v: v.tensor_scalar(out=negbeta[:], in0=beta[:], scalar1=-1.0, scalar2=None, op0=ALU.mult), reads=[beta], writes=[negbeta])
        k.op("dve", lambda v: v.tensor_tensor(out=bd[:, 4:8], in0=bd[:, 4:8], in1=dtb_bc[:], op=ALU.add), reads=[bd, dtb_bc], writes=[bd])
        k.op("act", lambda a: a.activation(out=bdt[:, 4:8], in_=bd[:, 4:8], func=ACT.Exp), reads=[bd], writes=[bdt])
        k.op("act", lambda a: a.activation(out=bdt[:, 4:8], in_=bdt[:, 4:8], func=ACT.Ln, bias=1.0, scale=1.0), reads=[bdt], writes=[bdt])
        k.op("dve", lambda v: v.tensor_tensor(out=gg[:], in0=bdt[:, 4:8], in1=ea[:], op=ALU.mult), reads=[bdt, ea], writes=[gg])

        k.op("dve", lambda v: v.tensor_copy(out=gsp[:, 0:4], in_=gg[:]), reads=[gg], writes=[gsp])
        k.op("dve", lambda v: v.tensor_tensor(out=gtmp[:], in0=gg[:], in1=gsp[:, 0:4], op=ALU.subtract), reads=[gg, gsp], writes=[gtmp])
        k.op("dve", lambda v: v.tensor_copy(out=gsp[:, 4:8], in_=gtmp[:]), reads=[gtmp], writes=[gsp])
        Ub4 = bc(Ub[:].rearrange("p (a b) -> p a b", a=1), [128, 4, 128])
        k.op("pool", lambda g: g.tensor_tensor(out=gUh[:], in0=Ub4, in1=bc(gsp[:, 0:4].rearrange("p (a b) -> p a b", b=1), [128, 4, 128]), op=ALU.mult),
             reads=[Ub, gsp], writes=[gUh])
        k.op("pool", lambda g: g.tensor_tensor(out=gUl[:], in0=Ub4, in1=bc(gsp[:, 4:8].rearrange("p (a b) -> p a b", b=1), [128, 4, 128]), op=ALU.mult),
             reads=[Ub, gsp], writes=[gUl])
        PG, PK_, PQ = PB[1], PB[2], PB[3]
        mm(PB[4][:, 0:8], Ub[:], gsp[:], True, True, [Ub, gsp], [PB[4]])
        mm(PB[4][:, 8:16], onesb[:], gsp[:], True, True, [onesb, gsp], [PB[4]])
        mm(PG[:, :], onesb[:], gUh[:].rearrange("p a b -> p (a b)"), True, False, [onesb, gUh], [PG])
        mm(PG[:, :], onesb[:], gUl[:].rearrange("p a b -> p (a b)"), False, True, [onesb, gUl], [PG])
        k.op("dve", lambda v: v.tensor_copy(out=gtmp[:], in_=PB[4][:, 0:4]), reads=[PB[4]], writes=[gtmp])
        k.op("dve", lambda v: v.tensor_tensor(out=Gc[:], in0=gtmp[:], in1=PB[4][:, 4:8], op=ALU.add), reads=[gtmp, PB[4]], writes=[Gc])
        k.op("dve", lambda v: v.tensor_copy(out=gtmp[:], in_=PB[4][:, 8:12]), reads=[PB[4]], writes=[gtmp])
        k.op("dve", lambda v: v.tensor_tensor(out=Glb[:], in0=gtmp[:], in1=PB[4][:, 12:16], op=ALU.add), reads=[gtmp, PB[4]], writes=[Glb])
        k.op("dve", lambda v: v.tensor_scalar(out=negGc[:], in0=Gc[:], scalar1=-1.0, scalar2=None, op0=ALU.mult), reads=[Gc], writes=[negGc])
        k.op("dve", lambda v: v.tensor_tensor(out=dG[:], in0=Glb[:], in1=Gc[:], op=ALU.subtract), reads=[Glb, Gc], writes=[dG])
        k.op("act", lambda a: a.activation(out=eG[:], in_=Gc[:], func=ACT.Exp), reads=[Gc], writes=[eG])
        k.op("act", lambda a: a.activation(out=edG[:], in_=dG[:], func=ACT.Exp), reads=[dG], writes=[edG])
        k.op("act", lambda a: a.activation(out=egl[:], in_=Glb[:], func=ACT.Exp), reads=[Glb], writes=[egl])
        k.op("dve", lambda v: v.tensor_tensor(out=beG[:], in0=eG[:], in1=beta[:], op=ALU.mult), reads=[eG, beta], writes=[beG])
        PG3 = v3(PG[:, :], 4)
        k.op("dve", lambda v: v.scalar_tensor_tensor(out=tmp1[:], in0=PG3, scalar=-1.0, in1=bc(negA[:].rearrange("p (a b) -> p a b", a=1), [128, 4, 128]),
                                                     op0=ALU.mult, op1=ALU.add), reads=[PG, negA], writes=[tmp1])
        k.op("dve", lambda v: v.tensor_tensor(out=tmp2[:], in0=PG3, in1=bc(negB[:].rearrange("p (a b) -> p a b", a=1), [128, 4, 128]), op=ALU.add),
             reads=[PG, negB], writes=[tmp2])
        k.op("act", lambda a: a.activation(out=eGbc[:], in_=PG3, func=ACT.Exp), reads=[PG], writes=[eGbc])
        for h in range(4):
            k.op("act", lambda a, h=h: a.activation(out=E1[:, h, :], in_=tmp1[:, h, :], func=ACT.Exp, bias=Gc[:, h:h + 1], scale=1.0), reads=[tmp1, Gc], writes=[E1])
            k.op("act", lambda a, h=h: a.activation(out=E2[:, h, :], in_=tmp2[:, h, :], func=ACT.Exp, bias=negGc[:, h:h + 1], scale=1.0), reads=[tmp2, negGc], writes=[E2])
        for h in range(4):
            mm(PK_[:, h * 128:(h + 1) * 128], qkb[:, 4 + h, :], qkb[:, 4 + h, :], True, True, [qkb], [PK_])
        for h in range(4):
            mm(PQ[:, h * 128:(h + 1) * 128], qkb[:, 4 + h, :], qkb[:, h, :], True, True, [qkb], [PQ])
        for h in range(4):
            k.op("dve", lambda v, h=h: v.scalar_tensor_tensor(out=Y0f[:, h, :], in0=PK_[:, h * 128:(h + 1) * 128], scalar=negbeta[:, h:h + 1],
                                                               in1=E1[:, h, :], op0=ALU.mult, op1=ALU.mult), reads=[PK_, negbeta, E1], writes=[Y0f])
        k.op("dve", lambda v: v.tensor_tensor(out=AqkT[:], in0=v3(PQ[:, :], 4), in1=E2[:], op=ALU.mult), reads=[PQ, E2], writes=[AqkT])
        k.op("pool", lambda g: g.tensor_tensor(out=qdT[:], in0=qkvs[:, 0:4, :], in1=eGbc[:], op=ALU.mult), reads=[qkvs, eGbc], writes=[qdT])
        k.op("act", lambda a: a.copy(out=Yh[0][:], in_=Y0f[:]), reads=[Y0f], writes=[Yh[0]])
        k.op("pool", lambda g: g.tensor_tensor(out=Yl[0][:], in0=Y0f[:], in1=Yh[0][:], op=ALU.subtract), reads=[Y0f, Yh[0]], writes=[Yl[0]])
        P5b = pbf(PB[5])
        for h in range(4):
            tr(P5b[:, h * 128:(h + 1) * 128], Yh[0][:, h, :], identb[:], [Yh[0]], [PB[5]])
            tr(P5b[:, (4 + h) * 128:(5 + h) * 128], Yl[0][:, h, :], identb[:], [Yl[0]], [PB[5]])
        k.op("act", lambda a: a.copy(out=XRh[0][:, :, 0:128], in_=v3(P5b[:, 0:512], 4)), reads=[PB[5]], writes=[XRh[0]])
        k.op("dve", lambda v: v.tensor_copy(out=XRl[0][:, :, 0:128], in_=v3(P5b[:, 512:1024], 4)), reads=[PB[5]], writes=[XRl[0]])
        id4 = bc(ident[:].rearrange("p (a b) -> p a b", a=1), [128, 4, 128])
        k.op("pool", lambda g: g.tensor_copy(out=Rf[:], in_=id4), reads=[ident], writes=[Rf])
        k.op("pool", lambda g: g.tensor_copy(out=XRh[0][:, :, 128:256], in_=id4), reads=[ident], writes=[XRh[0]])
        k.op("pool", lambda g: g.memset(XRl[0][:, :, 128:256], 0.0), writes=[XRl[0]])
        for lev in range(7):
            cur = lev % 2; nxt = (lev + 1) % 2
            xh, xl, yh, yl = XRh[cur], XRl[cur], Yh[cur], Yl[cur]
            xhn, xln, yhn, yln = XRh[nxt], XRl[nxt], Yh[nxt], Yl[nxt]
            base = 2 if cur == 0 else 5
            PA = [PB[base], PB[base + 1]]
            PBy = PB[base + 2] if lev % 2 == 0 else PB[0]
            last = (lev == 6)
            for h in range(4):
                pa = PA[h // 2]
                if not last:
                    o_ = pa[:, (h % 2) * 256:(h % 2 + 1) * 256]
                    r_h = xh[:, h, :]; r_l = xl[:, h, :]
                else:
                    o_ = pa[:, (h % 2) * 256 + 128:(h % 2 + 1) * 256]
                    r_h = xh[:, h, 128:256]; r_l = xl[:, h, 128:256]
                mm(o_, yh[:, h, :], r_h, True, False, [yh, xh], [pa])
                mm(o_, yh[:, h, :], r_l, False, False, [yh, xl], [pa])
                mm(o_, yl[:, h, :], r_h, False, True, [yl, xh], [pa])
            if not last:
                for h in range(4):
                    o_ = PBy[:, h * 128:(h + 1) * 128]
                    mm(o_, xh[:, h, 0:128], yh[:, h, :], True, False, [xh, yh], [PBy])
                    mm(o_, xh[:, h, 0:128], yl[:, h, :], False, False, [xh, yl], [PBy])
                    mm(o_, xl[:, h, 0:128], yh[:, h, :], False, True, [xl, yh], [PBy])
            for half in range(2):
                k.op("dve", lambda v, half=half: v.tensor_tensor(out=Rf[:, 2 * half:2 * half + 2, :], in0=Rf[:, 2 * half:2 * half + 2, :],
                                                                  in1=v3(PA[half][:, :], 2)[:, :, 128:256], op=ALU.add), reads=[Rf, PA[half]], writes=[Rf])
            k.op("act", lambda a: a.copy(out=xhn[:, :, 128:256], in_=Rf[:]), reads=[Rf], writes=[xhn])
            if not last:
                k.op("pool", lambda g: g.tensor_tensor(out=xln[:, :, 128:256], in0=Rf[:], in1=xhn[:, :, 128:256], op=ALU.subtract), reads=[Rf, xhn], writes=[xln])
                for half in range(2):
                    k.op("act", lambda a, half=half: a.copy(out=xhn[:, 2 * half:2 * half + 2, 0:128], in_=v3(PA[half][:, :], 2)[:, :, 0:128]),
                         reads=[PA[half]], writes=[xhn])
                    k.op("dve", lambda v, half=half: v.tensor_tensor(out=xln[:, 2 * half:2 * half + 2, 0:128], in0=v3(PA[half][:, :], 2)[:, :, 0:128],
                                                                      in1=xhn[:, 2 * half:2 * half + 2, 0:128], op=ALU.subtract), reads=[PA[half], xhn], writes=[xln])
                k.op("act", lambda a: a.copy(out=yhn[:], in_=v3(PBy[:, :], 4)), reads=[PBy], writes=[yhn])
                k.op("dve", lambda v: v.tensor_tensor(out=yln[:], in0=v3(PBy[:, :], 4), in1=yhn[:], op=ALU.subtract), reads=[PBy, yhn], writes=[yln])
        Rh = XRh[1]
        P1b = pbf(PB[1])
        for h in range(4):
            tr(P1b[:, h * 128:(h + 1) * 128], qkb[:, 4 + h, :], identb[:], [qkb], [PB[1]])
        for h in range(4):
            tr(P1b[:, (4 + h) * 128:(5 + h) * 128], qkb[:, 8 + h, :], identb[:], [qkb], [PB[1]])
        for h in range(4):
            k.op("dve", lambda v, h=h: v.tensor_scalar(out=kbt[:, h, :], in0=P1b[:, h * 128:(h + 1) * 128], scalar1=beG[:, h:h + 1], scalar2=None, op0=ALU.mult),
                 reads=[PB[1], beG], writes=[kbt])
            k.op("act", lambda a, h=h: a.activation(out=kdt[:, h, :], in_=P1b[:, h * 128:(h + 1) * 128], func=ACT.Identity, scale=edG[:, h:h + 1]),
                 reads=[PB[1], edG], writes=[kdt])
            k.op("act", lambda a, h=h: a.activation(out=bvt[:, h, :], in_=P1b[:, (4 + h) * 128:(5 + h) * 128], func=ACT.Identity, scale=beta[:, h:h + 1]),
                 reads=[PB[1], beta], writes=[bvt])
        PW, PU = PB[2], PB[3]
        for h in range(4):
            mm(PW[:, h * 128:(h + 1) * 128], kbt[:, h, :], Rh[:, h, 128:256], True, True, [kbt, Rh], [PW])
        for h in range(4):
            mm(PU[:, h * 128:(h + 1) * 128], Rh[:, h, 128:256], bvt[:, h, :], True, True, [Rh, bvt], [PU])
        k.op("act", lambda a: a.copy(out=wT[:], in_=v3(PW[:, :], 4)), reads=[PW], writes=[wT])
        k.op("dve", lambda v: v.tensor_copy(out=ut[:], in_=v3(PU[:, :], 4)), reads=[PU], writes=[ut])
        PWS = PB[4]
        for h in range(4):
            mm(PWS[:, h * 128:(h + 1) * 128], wT[:, h, :], Sb[:, h, :], True, True, [wT, Sb], [PWS])
        k.op("dve", lambda v: v.tensor_tensor(out=uub[:], in0=ut[:], in1=v3(PWS[:, :], 4), op=ALU.subtract), reads=[ut, PWS], writes=[uub])
        PO, PSn = PB[6], PB[7]
        for h in range(4):
            mm(PO[:, h * 128:(h + 1) * 128], qdT[:, h, :], Sb[:, h, :], True, False, [qdT, Sb], [PO])
            mm(PO[:, h * 128:(h + 1) * 128], AqkT[:, h, :], uub[:, h, :], False, True, [AqkT, uub], [PO])
        for h in range(4):
            mm(PSn[:, h * 128:(h + 1) * 128], kdt[:, h, :], uub[:, h, :], True, True, [kdt, uub], [PSn])
        for h in range(4):
            k.op("dve", lambda v, h=h: v.scalar_tensor_tensor(out=S[:, h, :], in0=S[:, h, :], scalar=egl[:, h:h + 1], in1=PSn[:, h * 128:(h + 1) * 128],
                                                               op0=ALU.mult, op1=ALU.add), reads=[S, egl, PSn], writes=[S])
        k.op("pool", lambda g: g.tensor_copy(out=Sb[:], in_=S[:]), reads=[S], writes=[Sb])
        for h in range(4):
            k.op("act", lambda a, h=h: a.activation(out=ojunk[:], in_=PO[:, h * 128:(h + 1) * 128], func=ACT.Square, accum_out=oss[:, h:h + 1]),
                 reads=[PO], writes=[ojunk, oss])
        k.op("act", lambda a: a.activation(out=orr[:], in_=oss[:], func=ACT.Ln, bias=RMS_EPS, scale=1.0 / 128.0), reads=[oss], writes=[orr])
        k.op("act", lambda a: a.activation(out=orr[:], in_=orr[:], func=ACT.Exp, scale=-0.5), reads=[orr], writes=[orr])
        k.op("pool", lambda g: g.tensor_tensor(out=nz[:], in0=v3(zAs[:], 4), in1=bc(norma_bc[:].rearrange("p (a b) -> p a b", a=1), [128, 4, 128]), op=ALU.mult),
             reads=[zAs, norma_bc], writes=[nz])
        for h in range(4):
            k.op("dve", lambda v, h=h: v.scalar_tensor_tensor(out=mixb[:, h * 128:(h + 1) * 128], in0=PO[:, h * 128:(h + 1) * 128], scalar=orr[:, h:h + 1],
                                                               in1=nz[:, h, :], op0=ALU.mult, op1=ALU.mult), reads=[PO, orr, nz], writes=[mixb])

        kcur = kTb[n % 2]; kprev = kTb[(n + 1) % 2]
        P0b = pbf(PB[0])
        for c in range(4):
            tr(P0b[:, c * 128:(c + 1) * 128], qrb[:, c * 128:(c + 1) * 128], identb[:], [qrb], [PB[0]])
        k.op("act", lambda a: a.activation(out=qTb[:], in_=v3(P0b[:, 0:512], 4), func=ACT.Identity, scale=0.125), reads=[PB[0]], writes=[qTb])
        tr(PB[1][:, 0:128], kr[:], ident[:], [kr], [PB[1]])
        k.op("dve", lambda v: v.tensor_copy(out=kcur[:], in_=PB[1][:, 0:128]), reads=[PB[1]], writes=[kcur])
        PSC = [PB[2], PB[3], PB[4], PB[5]]
        for c in range(4):
            for kv in range(2):
                slot = c * 2 + kv
                pb = PSC[slot // 2]
                o = (slot % 2) * 256
                mm(pb[:, o:o + 128], qTb[64 * kv:64 * kv + 64, c, :], kprev[64 * kv:64 * kv + 64, :], True, True, [qTb, kprev], [pb])
                mm(pb[:, o + 128:o + 256], qTb[64 * kv:64 * kv + 64, c, :], kcur[64 * kv:64 * kv + 64, :], True, True, [qTb, kcur], [pb])
        msk = swam0 if n == 0 else swam
        for j in range(4):
            k.op("dve", lambda v, j=j: v.tensor_tensor(out=SC[:, 2 * j:2 * j + 2, :], in0=v3(PSC[j][:, :], 2),
                                                       in1=bc(msk[:].rearrange("p (a b) -> p a b", a=1), [128, 2, 256]), op=ALU.add), reads=[PSC[j], msk], writes=[SC])
        k.op("dve", lambda v: v.tensor_reduce(out=mx[:], in_=SC[:], axis=AX.X, op=ALU.max), reads=[SC], writes=[mx])
        k.op("dve", lambda v: v.tensor_tensor(out=mx[:], in0=mx[:], in1=sinks_bc[:], op=ALU.max), reads=[mx, sinks_bc], writes=[mx])
        k.op("dve", lambda v: v.tensor_scalar(out=negm[:], in0=mx[:], scalar1=-1.0, scalar2=None, op0=ALU.mult), reads=[mx], writes=[negm])
        for s_ in range(8):
            k.op("act", lambda a, s_=s_: a.activation(out=Pb[:, s_, :], in_=SC[:, s_, :], func=ACT.Exp, bias=negm[:, s_:s_ + 1], scale=1.0,
                                                      accum_out=rs[:, s_:s_ + 1]), reads=[SC, negm], writes=[Pb, rs])
        k.op("dve", lambda v: v.tensor_tensor(out=es_[:], in0=sinks_bc[:], in1=mx[:], op=ALU.subtract), reads=[sinks_bc, mx], writes=[es_])
        k.op("act", lambda a: a.activation(out=es_[:], in_=es_[:], func=ACT.Exp), reads=[es_], writes=[es_])
        k.op("dve", lambda v: v.tensor_tensor(out=es_[:], in0=es_[:], in1=rs[:], op=ALU.add), reads=[es_, rs], writes=[es_])
        k.op("dve", lambda v: v.reciprocal(out=rden[:], in_=es_[:]), reads=[es_], writes=[rden])
        PPT = [PB[6], PB[7]]
        for s_ in range(8):
            for half in range(2):
                i16 = s_ * 2 + half
                pb = PPT[i16 // 8]
                o = (i16 % 8) * 128
                tr(pbf(pb)[:, o:o + 128], Pb[:, s_, half * 128:(half + 1) * 128], identb[:], [Pb], [pb])
        k.op("act", lambda a: a.copy(out=PTb[:, 0:8, :], in_=v3(pbf(PPT[0]), 8)), reads=[PPT[0]], writes=[PTb])
        k.op("dve", lambda v: v.tensor_copy(out=PTb[:, 8:16, :], in_=v3(pbf(PPT[1]), 8)), reads=[PPT[1]], writes=[PTb])
        PV = PB[2]
        for c in range(4):
            for kv in range(2):
                slot = c * 2 + kv
                mm(PV[:, slot * 64:(slot + 1) * 64], PTb[:, 2 * slot, :], vprev[:, 64 * kv:64 * kv + 64], True, False, [PTb, vprev], [PV])
                mm(PV[:, slot * 64:(slot + 1) * 64], PTb[:, 2 * slot + 1, :], vcur[:, 64 * kv:64 * kv + 64], False, True, [PTb, vcur], [PV])
        k.op("dve", lambda v: v.tensor_tensor(out=v3(obt[:], 8), in0=v3(PV[:, :], 8), in1=bc(rden[:].rearrange("p (a b) -> p a b", b=1), [128, 8, 64]), op=ALU.mult),
             reads=[PV, rden], writes=[obt])
        k.op("pool", lambda g: g.tensor_tensor(out=mixb[:, 512:1024], in0=obt[:], in1=zBs[:], op=ALU.mult), reads=[obt, zBs], writes=[mixb])

        P3b = pbf(PB[3])
        for kc in range(8):
            tr(P3b[:, kc * 128:(kc + 1) * 128], mixb[:, kc * 128:(kc + 1) * 128], identb[:], [mixb], [PB[3]])
        k.op("act", lambda a: a.copy(out=mixT[:, 0:4, :], in_=v3(P3b[:, 0:512], 4)), reads=[PB[3]], writes=[mixT])
        k.op("dve", lambda v: v.tensor_copy(out=mixT[:, 4:8, :], in_=v3(P3b[:, 512:1024], 4)), reads=[PB[3]], writes=[mixT])
        for j in range(2):
            for kc in range(8):
                mm(PB[5 + j][:, :], mixT[:, kc, :], Wo[:, kc, j * 512:(j + 1) * 512], kc == 0, kc == 7, [mixT, Wo], [PB[5 + j]])
        for j in range(2):
            k.op("dve", lambda v, j=j: v.tensor_tensor(out=ypre[:, j * 512:(j + 1) * 512], in0=PB[5 + j][:, :], in1=g1bc[:, j * 512:(j + 1) * 512], op=ALU.mult),
                 reads=[PB[5 + j], g1bc], writes=[ypre])
        k.op("dve", lambda g: g.scalar_tensor_tensor(out=ypre[:], in0=X[:], scalar=ALPHA, in1=ypre[:], op0=ALU.mult, op1=ALU.add), reads=[X, ypre], writes=[ypre])
        Yo = Yt[0]
        yjunk = Yo
        k.op("act", lambda a: a.activation(out=yjunk[:], in_=ypre[:], func=ACT.Identity, accum_out=st[:, 0:1]), reads=[ypre], writes=[yjunk, st])
        k.op("act", lambda a: a.activation(out=yjunk[:], in_=ypre[:], func=ACT.Square, accum_out=st[:, 1:2]), reads=[ypre], writes=[yjunk, st])
        k.op("dve", lambda v: v.tensor_scalar(out=st[:, 2:3], in0=st[:, 0:1], scalar1=1.0 / D, scalar2=None, op0=ALU.mult), reads=[st], writes=[st])
        k.op("dve", lambda v: v.tensor_tensor(out=st[:, 3:4], in0=st[:, 2:3], in1=st[:, 2:3], op=ALU.mult), reads=[st], writes=[st])
        k.op("dve", lambda v: v.scalar_tensor_tensor(out=st[:, 4:5], in0=st[:, 1:2], scalar=1.0 / D, in1=st[:, 3:4], op0=ALU.mult, op1=ALU.subtract), reads=[st], writes=[st])
        k.op("act", lambda a: a.activation(out=st[:, 5:6], in_=st[:, 4:5], func=ACT.Ln, bias=LN_EPS, scale=1.0), reads=[st], writes=[st])
        k.op("act", lambda a: a.activation(out=st[:, 5:6], in_=st[:, 5:6], func=ACT.Exp, scale=-0.5), reads=[st], writes=[st])
        k.op("dve", lambda v: v.scalar_tensor_tensor(out=st[:, 6:7], in0=st[:, 2:3], scalar=-1.0, in1=st[:, 5:6], op0=ALU.mult, op1=ALU.mult), reads=[st], writes=[st])
        k.op("act", lambda a: a.activation(out=yjunk[:], in_=ypre[:], func=ACT.Identity, bias=st[:, 6:7], scale=st[:, 5:6]), reads=[ypre, st], writes=[yjunk])
        k.op("pool", lambda g: g.tensor_tensor(out=yjunk[:], in0=yjunk[:], in1=lng_bc[:], op=ALU.mult), reads=[yjunk, lng_bc], writes=[yjunk])
        k.op("dve", lambda v: v.tensor_tensor(out=Yo[:], in0=yjunk[:], in1=lnb_bc[:], op=ALU.add), reads=[yjunk, lnb_bc], writes=[Yo])
        k.dma("sp", y_d[n * 128:(n + 1) * 128, :], Yo[:], reads=[Yo], semkey="st_" + Yo.name)

        if n == NT - 1:
            cpo_in = rt[0][:].rearrange("p a b -> p (a b)")[:, 0:36].rearrange("p (a b) -> p a b", a=3)
            cpo = obt[0:12, 0:384].rearrange("p (a b) -> p a b", a=3)
            k.op("dve", lambda v: v.tensor_copy(out=cpo_in[:], in_=pc[:, :, 128:131].rearrange("p c r -> p r c")), reads=[pc], writes=[cpo_in])
            for r_ in range(3):
                tr(PB[0][0:12, r_ * 128:(r_ + 1) * 128], cpo_in[:, r_, :], ident[:], [cpo_in], [PB[0]])
            k.op("dve", lambda v: v.tensor_copy(out=cpo[:].rearrange("p a b -> p (a b)"), in_=PB[0][0:12, 0:384]), reads=[PB[0]], writes=[cpo])
            k.dma("sp", convp_d.rearrange("r (c p) -> c r p", p=128), cpo[:], reads=[cpo], semkey="st_misc")
            k.dma("sp", deltap_d.rearrange("h k v -> k h v"), S[:], reads=[S], semkey="st_misc")
            k.dma("sp", swak_d[:, :], kr[:], reads=[kr], semkey="st_misc")
            k.dma("sp", swav_d[:, :], vB32[:], reads=[vB32], semkey="st_misc")

    k.finish("sp")
    nc._kb_nops = k.nops
    return nc


def rope_tables(pos):
    half = 32
    inv = (1.0 / (10000.0 ** (np.arange(half, dtype=np.float32) / np.float32(half)))).astype(np.float32)
    ang = pos.astype(np.float32)[:, None] * inv[None, :]
    return np.cos(ang).astype(np.float32), np.sin(ang).astype(np.float32)


def prep_shared(inputs):
    w_in = np.asarray(inputs["w_in"][0], np.float32)
    perm_cols = np.concatenate([np.arange(h * 64, (h + 1) * 64) for h in PERM])
    w_f = np.ascontiguousarray(w_in[:, 0:1536])
    zA = w_in[:, 1536:2048]
    beta = w_in[:, 2048:2052]
    dec = w_in[:, 2052:2056]
    qB = w_in[:, 2056:2568][:, perm_cols]
    kB = w_in[:, 2568:2696]
    vB = w_in[:, 2696:2824]
    zB = w_in[:, 2824:3336][:, perm_cols]
    w_t = np.ascontiguousarray(np.concatenate([zA, qB, zB, kB, vB, beta, dec], axis=1))
    w_out = np.asarray(inputs["w_out"][0], np.float32)
    w_o = np.ascontiguousarray(np.concatenate([w_out[0:512], w_out[512:1024][perm_cols]], axis=0))
    sh = {
        "w_ada": np.ascontiguousarray(inputs["w_ada"][0], np.float32),
        "b_ada": np.ascontiguousarray(inputs["b_ada"], np.float32).reshape(1, -1),
        "w_f": w_f, "w_t": w_t, "w_o": w_o,
        "conv_w": np.ascontiguousarray(inputs["conv_w"][0], np.float32),
        "a_log": np.ascontiguousarray(inputs["a_log"], np.float32).reshape(1, 4),
        "dt_bias": np.ascontiguousarray(inputs["dt_bias"], np.float32).reshape(1, 4),
        "norm_a": np.ascontiguousarray(inputs["norm_a"], np.float32).reshape(1, 128),
        "sinks_p": np.ascontiguousarray(np.asarray(inputs["sinks"], np.float32).reshape(8)[PERM]).reshape(1, 8),
        "ln_g": np.ascontiguousarray(inputs["ln_g"], np.float32).reshape(1, D),
        "ln_b": np.ascontiguousarray(inputs["ln_b"], np.float32).reshape(1, D),
    }
    return sh


def prep_core(inputs, sh, core, ntiles=NTILES, do_sample=True):
    b = core // 4
    T = ntiles * 128
    m = dict(sh)
    m["x"] = np.ascontiguousarray(inputs["x_prompt"][b, :T], np.float32)
    m["c"] = np.ascontiguousarray(inputs["c_prompt"][b], np.float32).reshape(1, D)
    cos, sin = rope_tables(np.arange(T))
    m["cosk"] = cos
    m["sink"] = sin
    if do_sample:
        sl = slice(core * NS, (core + 1) * NS)
        m["xs"] = np.ascontiguousarray(inputs["x_sample"][sl, 0], np.float32)
        m["cs"] = np.ascontiguousarray(inputs["c_sample"][sl], np.float32)
        m["s_conv"] = np.ascontiguousarray(inputs["state_conv"][0, sl], np.float32)
        m["s_delta"] = np.ascontiguousarray(inputs["state_delta"][0, sl], np.float32)
        m["s_k"] = np.ascontiguousarray(inputs["cache_swa_k"][0, sl], np.float32).reshape(NS, 128, 128)
        m["s_v"] = np.ascontiguousarray(inputs["cache_swa_v"][0, sl], np.float32).reshape(NS, 128, 128)
        cs_, ss_ = rope_tables(np.array([8192]))
        m["sinks_col"] = np.ascontiguousarray(np.tile(np.asarray(inputs["sinks"], np.float32).reshape(8), NS).reshape(128, 1))
        m["cos_s"] = cs_.reshape(1, 32)
        m["sin_s"] = ss_.reshape(1, 32)
    return m


_NC_CACHE = {}


DO_SAMPLE = True


def kernel(**inputs):
    if "nc" not in _NC_CACHE:
        _NC_CACHE["nc"] = build_program(NTILES, DO_SAMPLE)
    nc = _NC_CACHE["nc"]
    sh = prep_shared(inputs)
    in_maps = [prep_core(inputs, sh, c, NTILES, DO_SAMPLE) for c in range(8)]
    res = run_bass_kernel_spmd(nc, in_maps, core_ids=list(range(8))).results
    yp = np.stack([res[0]["y"], res[4]["y"]], 0).astype(np.float32)
    conv_p = np.stack([res[0]["conv_p"], res[4]["conv_p"]], 0)[None].astype(np.float32)
    delta_p = np.stack([res[0]["delta_p"], res[4]["delta_p"]], 0)[None].astype(np.float32)
    swa_k_p = np.stack([res[0]["swa_k_p"], res[4]["swa_k_p"]], 0).reshape(1, 2, 128, 2, 64).astype(np.float32)
    swa_v_p = np.stack([res[0]["swa_v_p"], res[4]["swa_v_p"]], 0).reshape(1, 2, 128, 2, 64).astype(np.float32)
    if DO_SAMPLE:
        ys = np.concatenate([r["ys"] for r in res], 0).reshape(128, 1, D).astype(np.float32)
        conv_s = np.concatenate([r["conv_s"] for r in res], 0)[None].astype(np.float32)
        delta_s = np.concatenate([r["delta_s"] for r in res], 0)[None].astype(np.float32)
        swa_k_s = np.concatenate([r["swa_k_s"] for r in res], 0).reshape(1, 128, 128, 2, 64).astype(np.float32)
        swa_v_s = np.concatenate([r["swa_v_s"] for r in res], 0).reshape(1, 128, 128, 2, 64).astype(np.float32)
    else:
        ys = np.zeros((128, 1, D), np.float32)
        conv_s = np.zeros((1, 128, 3, 1536), np.float32)
        delta_s = np.zeros((1, 128, 4, 128, 128), np.float32)
        swa_k_s = np.zeros((1, 128, 128, 2, 64), np.float32)
        swa_v_s = np.zeros((1, 128, 128, 2, 64), np.float32)
    return (yp, ys, conv_p, delta_p, swa_k_p, swa_v_p, conv_s, delta_s, swa_k_s, swa_v_s)
```

```python
import contextlib
import numpy as np
import concourse.bass as bass
import concourse.mybir as mybir
from concourse.bass_utils import run_bass_kernel_spmd

F32 = mybir.dt.float32
BF16 = mybir.dt.bfloat16
ACT = mybir.ActivationFunctionType
ALU = mybir.AluOpType
AX = mybir.AxisListType

D = 1024
NTILES = 64
NS = 16
ALPHA = 2.0 ** 0.25
NEG = -1.0e30
LN_EPS = 1e-5
RMS_EPS = 1e-6
L2_EPS = 1e-6
WT_COLS = 1800
PERM = [0, 4, 1, 5, 2, 6, 3, 7]


class KB:
    def __init__(self, nc):
        self.nc = nc
        self.es = contextlib.ExitStack()
        self.eng = {"pe": nc.tensor, "dve": nc.vector, "act": nc.scalar, "pool": nc.gpsimd, "sp": nc.sync}
        self.sem = {}
        self.cnt = {}
        for e in self.eng:
            self.sem[e] = self.es.enter_context(nc.semaphore("sem_" + e))
            self.cnt[e] = 0
        self.waited = {}
        self.last_write = {}
        self.readers = {}
        self.ntensors = 0
        self.limit = None
        self.nops = 0
        self.pe_inorder = False
        self.dry = False
        self.needed = None
        self.need_out = set()
        self.sig = {e: 0 for e in self.eng}
        self.sigval = {}

    def sb(self, name, shape, dt=F32):
        return self.es.enter_context(self.nc.sbuf_tensor(name, list(shape), dt))

    def ps(self, name, shape=(128, 512), dt=F32):
        return self.es.enter_context(self.nc.psum_tensor(name, list(shape), dt))

    def _deps(self, reads, writes, nowaw=False):
        deps = set()
        for t in reads:
            if t in self.last_write:
                deps.add(self.last_write[t])
        for t in writes:
            if t in self.last_write and not nowaw:
                deps.add(self.last_write[t])
            for r in self.readers.get(t, ()):
                deps.add(r)
        return deps

    def _semval(self, src, val):
        if src in self.eng and self.needed is not None:
            return self.sigval[(src, val)]
        return val

    def _wait(self, e, deps):
        for (src, val) in sorted(deps, key=lambda x: str(x)):
            if e == "pe" and src == "pe" and self.pe_inorder:
                continue
            if self.waited.get((e, src), 0) < val:
                if self.dry:
                    self.need_out.add((src, val))
                else:
                    self.eng[e].wait_ge(self.sem[src], self._semval(src, val))
                self.waited[(e, src)] = val

    def _record(self, key, reads, writes):
        for t in writes:
            self.last_write[t] = key
            self.readers[t] = set()
        for t in reads:
            if t not in writes:
                self.readers.setdefault(t, set()).add(key)

    def op(self, e, fn, reads=(), writes=()):
        self.nops += 1
        if self.limit is not None and self.nops > self.limit:
            return
        reads = [r.name if hasattr(r, "name") else r for r in reads]
        writes = [w.name if hasattr(w, "name") else w for w in writes]
        writes = list(writes) + [r for r in reads if r.startswith("pb") and r not in writes]
        self._wait(e, self._deps(reads, writes))
        self.cnt[e] += 1
        if not self.dry:
            inst = fn(self.eng[e])
            if self.needed is None:
                inst.then_inc(self.sem[e], 1)
            elif (e, self.cnt[e]) in self.needed:
                self.sig[e] += 1
                self.sigval[(e, self.cnt[e])] = self.sig[e]
                inst.then_inc(self.sem[e], 1)
        self._record((e, self.cnt[e]), reads, writes)

    def dma(self, e, out, in_, reads=(), writes=(), semkey=None, nowaw=False, **kw):
        reads = [r.name if hasattr(r, "name") else r for r in reads]
        writes = [w.name if hasattr(w, "name") else w for w in writes]
        self.nops += 1
        if self.limit is not None and self.nops > self.limit:
            return
        if semkey not in self.sem:
            self.sem[semkey] = self.es.enter_context(self.nc.semaphore("semd_" + str(semkey)))
            self.cnt[semkey] = 0
        self._wait(e, self._deps(reads, writes, nowaw))
        self.cnt[semkey] += 16
        if not self.dry:
            inst = self.eng[e].dma_start(out=out, in_=in_, **kw)
            inst.then_inc(self.sem[semkey], 16)
        self._record((semkey, self.cnt[semkey]), reads, writes)

    def barrier(self):
        for e in self.eng:
            for src, val in self.cnt.items():
                if val > 0 and self.waited.get((e, src), 0) < val:
                    if self.dry:
                        self.need_out.add((src, val))
                    else:
                        self.eng[e].wait_ge(self.sem[src], self._semval(src, val))
                    self.waited[(e, src)] = val
        self.last_write = {}
        self.readers = {}

    def finish(self, e="sp"):
        for src, val in self.cnt.items():
            if val > 0 and self.waited.get((e, src), 0) < val:
                if self.dry:
                    self.need_out.add((src, val))
                else:
                    self.eng[e].wait_ge(self.sem[src], self._semval(src, val))
                self.waited[(e, src)] = val
        self.es.close()


def bc(ap, shape):
    return ap.to_broadcast(list(shape))


def build_program(ntiles=NTILES, do_sample=True, limit=None):
    plan = _build(ntiles, do_sample, limit, None)
    return _build(ntiles, do_sample, limit, plan)


def _build(ntiles, do_sample, limit, plan):
    nc = bass.Bass("TRN2", target_bir_lowering=False)
    k = KB(nc)
    k.limit = limit
    if plan is None:
        k.dry = True
    else:
        k.needed = plan
    NT = ntiles
    T = NT * 128

    def din(name, shape):
        return nc.dram_tensor(name, list(shape), F32, kind="ExternalInput").ap()

    def dout(name, shape):
        return nc.dram_tensor(name, list(shape), F32, kind="ExternalOutput").ap()

    x_d = din("x", [T, D])
    c_d = din("c", [1, D])
    wada_d = din("w_ada", [D, 3 * D])
    bada_d = din("b_ada", [1, 3 * D])
    wf_d = din("w_f", [D, 1536])
    wt_d = din("w_t", [D, WT_COLS])
    wo_d = din("w_o", [D, D])
    convw_d = din("conv_w", [4, 1536])
    alog_d = din("a_log", [1, 4])
    dtb_d = din("dt_bias", [1, 4])
    norma_d = din("norm_a", [1, 128])
    sinks_d = din("sinks_p", [1, 8])
    lng_d = din("ln_g", [1, D])
    lnb_d = din("ln_b", [1, D])
    cosk_d = din("cosk", [T, 32])
    sink_d = din("sink", [T, 32])

    y_d = dout("y", [T, D])
    convp_d = dout("conv_p", [3, 1536])
    deltap_d = dout("delta_p", [4, 128, 128])
    swak_d = dout("swa_k_p", [128, 128])
    swav_d = dout("swa_v_p", [128, 128])

    if do_sample:
        xs_d = din("xs", [NS, D])
        cs_d = din("cs", [NS, D])
        sconv_d = din("s_conv", [NS, 3, 1536])
        sdelta_d = din("s_delta", [NS, 4, 128, 128])
        sk_d = din("s_k", [NS, 128, 128])
        sv_d = din("s_v", [NS, 128, 128])
        coss_d = din("cos_s", [1, 32])
        sins_d = din("sin_s", [1, 32])
        ys_d = dout("ys", [NS, D])
        convs_d = dout("conv_s", [NS, 3, 1536])
        deltas_d = dout("delta_s", [NS, 4, 128, 128])
        swaks_d = dout("swa_k_s", [NS, 128, 128])
        swavs_d = dout("swa_v_s", [NS, 128, 128])

    ident = k.sb("ident", [128, 128])
    U = k.sb("U", [128, 128])
    ones = k.sb("ones", [128, 128])
    onesb = k.sb("onesb", [128, 128], BF16)
    negA = k.sb("negA", [128, 128])
    negB = k.sb("negB", [128, 128])
    swam = k.sb("swam", [128, 256])
    swam0 = k.sb("swam0", [128, 256])

    k.op("pool", lambda g: g.memset(ones[:], 1.0), writes=[ones])
    k.op("pool", lambda g: g.memset(onesb[:], 1.0), writes=[onesb])
    k.op("pool", lambda g: g.affine_select(out=ident[:], in_=ones[:], pattern=[[-1, 128]], compare_op=ALU.is_equal,
                                            fill=0.0, base=0, channel_multiplier=1), reads=[ones], writes=[ident])
    k.op("pool", lambda g: g.affine_select(out=U[:], in_=ones[:], pattern=[[1, 128]], compare_op=ALU.is_ge,
                                            fill=0.0, base=0, channel_multiplier=-1), reads=[ones], writes=[U])
    zer = k.sb("zer", [128, 256])
    k.op("pool", lambda g: g.memset(zer[:], 0.0), writes=[zer])
    k.op("pool", lambda g: g.affine_select(out=negA[:], in_=zer[:, 0:128], pattern=[[-1, 128]], compare_op=ALU.is_ge,
                                            fill=NEG, base=-1, channel_multiplier=1), reads=[zer], writes=[negA])
    k.op("pool", lambda g: g.affine_select(out=negB[:], in_=zer[:, 0:128], pattern=[[1, 128]], compare_op=ALU.is_ge,
                                            fill=NEG, base=0, channel_multiplier=-1), reads=[zer], writes=[negB])
    swamt = k.sb("swamt", [128, 256])
    k.op("pool", lambda g: g.affine_select(out=swamt[:], in_=zer[:], pattern=[[1, 256]], compare_op=ALU.is_ge,
                                            fill=NEG, base=0, channel_multiplier=-1), reads=[zer], writes=[swamt])
    k.op("pool", lambda g: g.affine_select(out=swam[:], in_=swamt[:], pattern=[[-1, 256]], compare_op=ALU.is_ge,
                                            fill=NEG, base=128, channel_multiplier=1), reads=[swamt], writes=[swam])
    k.op("pool", lambda g: g.memset(swam0[:, 0:128], NEG), writes=[swam0])
    k.op("pool", lambda g: g.tensor_copy(out=swam0[:, 128:256], in_=swam[:, 128:256]), reads=[swam], writes=[swam0])

    PB = [k.ps("pb%d" % i) for i in range(8)]
    def load_bc(name, src, n, parts=128):
        t = k.sb(name, [parts, n])
        k.dma("sp", t[:], src.partition_broadcast(parts), writes=[t], semkey="ld_" + name)
        return t

    lng_bc = load_bc("lng_bc", lng_d[0], D)
    lnb_bc = load_bc("lnb_bc", lnb_d[0], D)
    norma_bc = load_bc("norma_bc", norma_d[0], 128)
    sinks_bc = load_bc("sinks_bc", sinks_d[0], 8)
    alog_bc = load_bc("alog_bc", alog_d[0], 4)
    dtb_bc = load_bc("dtb_bc", dtb_d[0], 4)
    cwT = k.sb("cwT", [128, 48])
    bada_fm = k.sb("bada_fm", [128, 24])
    cT = k.sb("cT", [128, 8])
    rowst = k.sb("rowst", [80, 128])
    k.dma("sp", rowst[0:48, :], convw_d.rearrange("j (c p) -> (j c) p", p=128), writes=[rowst], semkey="ld_rowst", nowaw=True)
    k.dma("sp", rowst[48:72, :], bada_d[0].rearrange("(j p) -> j p", p=128), writes=[rowst], semkey="ld_rowst", nowaw=True)
    k.dma("sp", rowst[72:80, :], c_d[0].rearrange("(j p) -> j p", p=128), writes=[rowst], semkey="ld_rowst", nowaw=True)
    cst = [k.sb("cst%d" % i, [128, 2, 32]) for i in range(2)]
    k.op("pe", lambda p: p.transpose(out=PB[0][:, 0:80], in_=rowst[:, :], identity=ident[0:80, 0:80]), reads=[rowst, ident], writes=[PB[0]])
    k.op("dve", lambda v: v.tensor_copy(out=cwT[:], in_=PB[0][:, 0:48]), reads=[PB[0]], writes=[cwT])
    k.op("dve", lambda v: v.tensor_copy(out=bada_fm[:], in_=PB[0][:, 48:72]), reads=[PB[0]], writes=[bada_fm])
    k.op("dve", lambda v: v.tensor_copy(out=cT[:], in_=PB[0][:, 72:80]), reads=[PB[0]], writes=[cT])
    ea = k.sb("ea", [128, 4])
    k.op("act", lambda a: a.activation(out=ea[:], in_=alog_bc[:], func=ACT.Exp), reads=[alog_bc], writes=[ea])
    k.op("dve", lambda v: v.tensor_scalar(out=ea[:], in0=ea[:], scalar1=-1.0, scalar2=None, op0=ALU.mult),
         reads=[ea], writes=[ea])

    Wf = k.sb("Wf", [128, 8, 1536], BF16)
    Wt = k.sb("Wt", [128, 8, WT_COLS], BF16)
    Wo = k.sb("Wo", [128, 8, D], BF16)
    mod_fm = k.sb("mod_fm", [128, 16])
    g1bc = k.sb("g1bc", [128, D])
    k1s = contextlib.ExitStack()
    if do_sample:
        csT = k1s.enter_context(nc.sbuf_tensor("csT", [128, 8, NS], F32))
        mod_s = k1s.enter_context(nc.sbuf_tensor("mod_s", [NS, 3 * D], F32))
    k2 = contextlib.ExitStack()
    def sb2(name, shape, dt=F32):
        return k2.enter_context(nc.sbuf_tensor(name, list(shape), dt))
    stg = [sb2("stg%d" % i, [128, 1800]) for i in range(2)]
    bada_bc = sb2("bada_bc", [128, D])
    k.dma("sp", bada_bc[:], bada_d[0, 2 * D:3 * D].partition_broadcast(128), writes=[bada_bc], semkey="ld_bada_bc")
    si = 0
    cast_engs = ["dve", "pool", "act"]

    def load_cast(dst, src_d, ncols):
        nonlocal si
        per = 2048 // ncols if ncols <= 2048 else 0
        for kc in range(8):
            s = stg[si % 2]
            k.dma("sp", s[:, 0:ncols], src_d[kc * 128:(kc + 1) * 128, :], writes=[s], semkey="ld_" + s.name)
            e = cast_engs[si % 3]
            if e == "act":
                k.op(e, lambda a, s=s, kc=kc: a.copy(out=dst[:, kc, :], in_=s[:, 0:ncols]), reads=[s], writes=[dst])
            else:
                k.op(e, lambda v, s=s, kc=kc: v.tensor_copy(out=dst[:, kc, :], in_=s[:, 0:ncols]), reads=[s], writes=[dst])
            si += 1

    load_cast(Wf, wf_d, 1536)
    load_cast(Wt, wt_d, WT_COLS)
    load_cast(Wo, wo_d, D)


    wa = [sb2("wa%d" % i, [128, 8, 512]) for i in range(1)]
    c_bcT = sb2("c_bcT", [128, 8, 128])
    k.op("pool", lambda g: g.tensor_copy(out=c_bcT[:], in_=bc(cT[:].rearrange("p (a b) -> p a b", b=1), [128, 8, 128])),
         reads=[cT], writes=[c_bcT])
    if do_sample:
        cs_sb = sb2("cs_sb", [NS, D])
        k.dma("sp", cs_sb[:], cs_d[:, :], writes=[cs_sb], semkey="ld_cs")
        for kc in range(8):
            k.op("pe", lambda p, kc=kc: p.transpose(out=PB[0][:, kc * NS:(kc + 1) * NS], in_=cs_sb[:, kc * 128:(kc + 1) * 128],
                                                    identity=ident[0:NS, 0:NS]), reads=[cs_sb, ident], writes=[PB[0]])
        k.op("dve", lambda v: v.tensor_copy(out=csT[:].rearrange("p a b -> p (a b)"), in_=PB[0][:, 0:8 * NS]),
             reads=[PB[0]], writes=[csT])
        bada_s = sb2("bada_s", [NS, 3 * D])
        k.dma("sp", bada_s[:], bada_d[0].partition_broadcast(NS), writes=[bada_s], semkey="ld_bada_s")
    for j in range(6):
        w = wa[0]
        for kc in range(8):
            k.dma("sp", w[:, kc, :], wada_d[kc * 128:(kc + 1) * 128, j * 512:(j + 1) * 512], writes=[w],
                  semkey="ld_" + w.name, nowaw=True)
        if j < 4:
            for sub in range(4):
                col = j * 4 + sub
                for kc in range(8):
                    k.op("pe", lambda p, kc=kc, sub=sub, col=col, w=w: p.matmul(
                        PB[1][:, col:col + 1], lhsT=w[:, kc, sub * 128:(sub + 1) * 128], rhs=cT[:, kc:kc + 1],
                        start=(kc == 0), stop=(kc == 7)), reads=[w, cT], writes=[PB[1]])
        else:
            for kc in range(8):
                k.op("pe", lambda p, kc=kc, w=w: p.matmul(PB[2 + (j - 4)][:, :], lhsT=c_bcT[:, kc, :], rhs=w[:, kc, :],
                                                          start=(kc == 0), stop=(kc == 7)),
                     reads=[w, c_bcT], writes=[PB[2 + (j - 4)]])
        if do_sample:
            for kc in range(8):
                k.op("pe", lambda p, kc=kc, w=w: p.matmul(PB[4 + j % 2][0:NS, :], lhsT=csT[:, kc, :], rhs=w[:, kc, :],
                                                          start=(kc == 0), stop=(kc == 7)),
                     reads=[w, csT], writes=[PB[4 + j % 2]])
            k.op("dve", lambda v, j=j: v.tensor_tensor(out=mod_s[:, j * 512:(j + 1) * 512], in0=PB[4 + j % 2][0:NS, :],
                                                       in1=bada_s[:, j * 512:(j + 1) * 512], op=ALU.add),
                 reads=[PB[4 + j % 2], bada_s], writes=[mod_s])
    k.op("dve", lambda v: v.tensor_tensor(out=mod_fm[:], in0=PB[1][:, 0:16], in1=bada_fm[:, 0:16], op=ALU.add),
         reads=[PB[1], bada_fm], writes=[mod_fm])
    k.op("dve", lambda v: v.tensor_scalar(out=mod_fm[:, 8:16], in0=mod_fm[:, 8:16], scalar1=1.0, scalar2=None, op0=ALU.add),
         reads=[mod_fm], writes=[mod_fm])
    for j in range(2):
        k.op("dve", lambda v, j=j: v.scalar_tensor_tensor(out=g1bc[:, j * 512:(j + 1) * 512], in0=PB[2 + j][:, :], scalar=1.0,
                                                           in1=bada_bc[:, j * 512:(j + 1) * 512], op0=ALU.add, op1=ALU.add),
             reads=[PB[2 + j], bada_bc], writes=[g1bc])


    if do_sample:
        k.barrier()
        k2.close()
        k2 = contextlib.ExitStack()
        P16 = NS
        sinkcol_d = din("sinks_col", [128, 1])
        xs = sb2("xs_sb", [P16, D]); hs = sb2("hs", [P16, D]); mix_s = sb2("mix_s", [P16, D])
        hsT = sb2("hsT", [128, 8, P16], BF16)
        pqkv = sb2("pqkv", [P16, 1536])
        qkv_s = sb2("qkv_s", [P16, 12, 128])
        zAs_s = sb2("zAs_s", [P16, 512]); zBs_s = sb2("zBs_s", [P16, 512])
        qr_s = sb2("qr_s", [P16, 512]); kr_s = sb2("kr_s", [P16, 128]); v_s = sb2("v_s", [P16, 128])
        vsb = sb2("vsb", [P16, 128], BF16)
        bd_s = sb2("bd_s", [P16, 8]); bdt_s = sb2("bdt_s", [P16, 8])
        beta_s = sb2("beta_s", [P16, 4]); nbeta_s = sb2("nbeta_s", [P16, 4]); g_s = sb2("g_s", [P16, 4]); eg_s = sb2("eg_s", [P16, 4])
        cs16 = sb2("cs16", [P16, 2, 32])
        k.dma("sp", cs16[:, 0, :], coss_d[0].partition_broadcast(P16), writes=[cs16], semkey="ld_cs16", nowaw=True)
        k.dma("sp", cs16[:, 1, :], sins_d[0].partition_broadcast(P16), writes=[cs16], semkey="ld_cs16", nowaw=True)
        k.dma("sp", xs[:], xs_d[:, :], writes=[xs], semkey="ld_xs")
        k.op("dve", lambda v: v.scalar_tensor_tensor(out=hs[:], in0=mod_s[:, D:2 * D], scalar=1.0, in1=xs[:], op0=ALU.add, op1=ALU.mult),
             reads=[mod_s, xs], writes=[hs])
        k.op("dve", lambda v: v.tensor_tensor(out=hs[:], in0=hs[:], in1=mod_s[:, 0:D], op=ALU.add), reads=[hs, mod_s], writes=[hs])
        for kc in range(8):
            k.op("pe", lambda p, kc=kc: p.transpose(out=PB[0][:, kc * P16:(kc + 1) * P16], in_=hs[:, kc * 128:(kc + 1) * 128],
                                                    identity=ident[0:P16, 0:P16]), reads=[hs, ident], writes=[PB[0]])
        k.op("dve", lambda v: v.tensor_copy(out=hsT[:].rearrange("p a b -> p (a b)"), in_=PB[0][:, 0:8 * P16]), reads=[PB[0]], writes=[hsT])
        for j in range(3):
            for kc in range(8):
                k.op("pe", lambda p, j=j, kc=kc: p.matmul(PB[1 + j][0:P16, :], lhsT=hsT[:, kc, :], rhs=Wf[:, kc, j * 512:(j + 1) * 512],
                                                          start=(kc == 0), stop=(kc == 7)), reads=[hsT, Wf], writes=[PB[1 + j]])
        offs = [(0, 512), (512, 512), (1024, 512), (1536, 264)]
        for j, (o, w_) in enumerate(offs):
            for kc in range(8):
                k.op("pe", lambda p, j=j, o=o, w_=w_, kc=kc: p.matmul(PB[4 + j][0:P16, 0:w_], lhsT=hsT[:, kc, :], rhs=Wt[:, kc, o:o + w_],
                                                                      start=(kc == 0), stop=(kc == 7)), reads=[hsT, Wt], writes=[PB[4 + j]])
        for j in range(3):
            k.op("dve", lambda v, j=j: v.tensor_copy(out=pqkv[:, j * 512:(j + 1) * 512], in_=PB[1 + j][0:P16, :]), reads=[PB[1 + j]], writes=[pqkv])
        k.op("act", lambda a: a.activation(out=zAs_s[:], in_=PB[4][0:P16, :], func=ACT.Silu), reads=[PB[4]], writes=[zAs_s])
        k.op("act", lambda a: a.activation(out=zBs_s[:], in_=PB[6][0:P16, :], func=ACT.Silu), reads=[PB[6]], writes=[zBs_s])
        rts = [sb2("rts%d" % i, [P16, 8, 32]) for i in range(4)]
        q3 = PB[5][0:P16, :].rearrange("p (a b) -> p a b", a=8)
        qr3 = qr_s[:].rearrange("p (a b) -> p a b", a=8)
        cq = bc(cs16[:, 0:1, :], [P16, 8, 32]); sq_ = bc(cs16[:, 1:2, :], [P16, 8, 32])
        k.op("dve", lambda v: v.tensor_tensor(out=rts[0][:], in0=q3[:, :, 0:32], in1=cq, op=ALU.mult), reads=[PB[5], cs16], writes=[rts[0]])
        k.op("dve", lambda v: v.tensor_tensor(out=rts[1][:], in0=q3[:, :, 32:64], in1=sq_, op=ALU.mult), reads=[PB[5], cs16], writes=[rts[1]])
        k.op("dve", lambda v: v.tensor_tensor(out=rts[2][:], in0=q3[:, :, 32:64], in1=cq, op=ALU.mult), reads=[PB[5], cs16], writes=[rts[2]])
        k.op("dve", lambda v: v.tensor_tensor(out=rts[3][:], in0=q3[:, :, 0:32], in1=sq_, op=ALU.mult), reads=[PB[5], cs16], writes=[rts[3]])
        k.op("dve", lambda v: v.tensor_tensor(out=qr3[:, :, 0:32], in0=rts[0][:], in1=rts[1][:], op=ALU.subtract), reads=[rts[0], rts[1]], writes=[qr_s])
        k.op("dve", lambda v: v.tensor_tensor(out=qr3[:, :, 32:64], in0=rts[2][:], in1=rts[3][:], op=ALU.add), reads=[rts[2], rts[3]], writes=[qr_s])
        k.op("dve", lambda v: v.tensor_scalar(out=qr_s[:], in0=qr_s[:], scalar1=0.125, scalar2=None, op0=ALU.mult), reads=[qr_s], writes=[qr_s])
        k3 = PB[7][0:P16, 0:128].rearrange("p (a b) -> p a b", a=2)
        kr3 = kr_s[:].rearrange("p (a b) -> p a b", a=2)
        ck = bc(cs16[:, 0:1, :], [P16, 2, 32]); sk_ = bc(cs16[:, 1:2, :], [P16, 2, 32])
        k.op("dve", lambda v: v.tensor_tensor(out=rts[0][:, 0:2, :], in0=k3[:, :, 0:32], in1=ck, op=ALU.mult), reads=[PB[7], cs16], writes=[rts[0]])
        k.op("dve", lambda v: v.tensor_tensor(out=rts[1][:, 0:2, :], in0=k3[:, :, 32:64], in1=sk_, op=ALU.mult), reads=[PB[7], cs16], writes=[rts[1]])
        k.op("dve", lambda v: v.tensor_tensor(out=rts[2][:, 0:2, :], in0=k3[:, :, 32:64], in1=ck, op=ALU.mult), reads=[PB[7], cs16], writes=[rts[2]])
        k.op("dve", lambda v: v.tensor_tensor(out=rts[3][:, 0:2, :], in0=k3[:, :, 0:32], in1=sk_, op=ALU.mult), reads=[PB[7], cs16], writes=[rts[3]])
        k.op("dve", lambda v: v.tensor_tensor(out=kr3[:, :, 0:32], in0=rts[0][:, 0:2, :], in1=rts[1][:, 0:2, :], op=ALU.subtract), reads=[rts[0], rts[1]], writes=[kr_s])
        k.op("dve", lambda v: v.tensor_tensor(out=kr3[:, :, 32:64], in0=rts[2][:, 0:2, :], in1=rts[3][:, 0:2, :], op=ALU.add), reads=[rts[2], rts[3]], writes=[kr_s])
        k.op("dve", lambda v: v.tensor_copy(out=v_s[:], in_=PB[7][0:P16, 128:256]), reads=[PB[7]], writes=[v_s])
        k.op("dve", lambda v: v.tensor_copy(out=vsb[:], in_=PB[7][0:P16, 128:256]), reads=[PB[7]], writes=[vsb])
        k.op("dve", lambda v: v.tensor_copy(out=bd_s[:], in_=PB[7][0:P16, 256:264]), reads=[PB[7]], writes=[bd_s])
        k.op("act", lambda a: a.activation(out=bdt_s[:, 0:4], in_=bd_s[:, 0:4], func=ACT.Exp, scale=-1.0), reads=[bd_s], writes=[bdt_s])
        k.op("dve", lambda v: v.tensor_scalar(out=bdt_s[:, 0:4], in0=bdt_s[:, 0:4], scalar1=1.0, scalar2=None, op0=ALU.add), reads=[bdt_s], writes=[bdt_s])
        k.op("dve", lambda v: v.reciprocal(out=beta_s[:], in_=bdt_s[:, 0:4]), reads=[bdt_s], writes=[beta_s])
        k.op("dve", lambda v: v.tensor_scalar(out=nbeta_s[:], in0=beta_s[:], scalar1=-1.0, scalar2=None, op0=ALU.mult), reads=[beta_s], writes=[nbeta_s])
        k.op("dve", lambda v: v.tensor_tensor(out=bd_s[:, 4:8], in0=bd_s[:, 4:8], in1=dtb_bc[0:P16, :], op=ALU.add), reads=[bd_s, dtb_bc], writes=[bd_s])
        k.op("act", lambda a: a.activation(out=bdt_s[:, 4:8], in_=bd_s[:, 4:8], func=ACT.Exp), reads=[bd_s], writes=[bdt_s])
        k.op("act", lambda a: a.activation(out=bdt_s[:, 4:8], in_=bdt_s[:, 4:8], func=ACT.Ln, bias=1.0, scale=1.0), reads=[bdt_s], writes=[bdt_s])
        k.op("dve", lambda v: v.tensor_tensor(out=g_s[:], in0=bdt_s[:, 4:8], in1=ea[0:P16, :], op=ALU.mult), reads=[bdt_s, ea], writes=[g_s])
        k.op("act", lambda a: a.activation(out=eg_s[:], in_=g_s[:], func=ACT.Exp), reads=[g_s], writes=[eg_s])
        k3s = contextlib.ExitStack()
        def sb3(name, shape, dt=F32):
            return k3s.enter_context(nc.sbuf_tensor(name, list(shape), dt))
        xp4 = sb3("xp4", [P16, 4, 1536]); cwb = sb3("cwb", [P16, 4, 1536]); tmpc = xp4
        acc_s = sb3("acc_s", [P16, 1536])
        k.dma("sp", xp4[:, 0:3, :], sconv_d[:, :, :], writes=[xp4], semkey="ld_xp4", nowaw=True)
        k.dma("sp", cwb[:].rearrange("p a b -> p (a b)"), convw_d.rearrange("a b -> (a b)").partition_broadcast(P16), writes=[cwb], semkey="ld_cwb")
        k.op("act", lambda a: a.copy(out=xp4[:, 3, :], in_=pqkv[:]), reads=[pqkv], writes=[xp4])
        k.dma("sp", convs_d[:, :, :], xp4[:, 1:4, :], reads=[xp4], semkey="st_smisc")
        k.op("dve", lambda v: v.tensor_tensor(out=tmpc[:], in0=xp4[:], in1=cwb[:], op=ALU.mult), reads=[xp4, cwb], writes=[tmpc])
        k.op("dve", lambda v: v.tensor_reduce(out=acc_s[:], in_=tmpc[:].rearrange("p j c -> p c j"), axis=AX.X, op=ALU.add), reads=[tmpc], writes=[acc_s])
        k.op("act", lambda a: a.activation(out=qkv_s[:].rearrange("p a b -> p (a b)"), in_=acc_s[:], func=ACT.Silu), reads=[acc_s], writes=[qkv_s])
        sqs = sb3("sqs", [P16, 8, 128]); sss = sb3("sss", [P16, 8])
        k.op("dve", lambda v: v.tensor_tensor(out=sqs[:], in0=qkv_s[:, 0:8, :], in1=qkv_s[:, 0:8, :], op=ALU.mult), reads=[qkv_s], writes=[sqs])
        k.op("dve", lambda v: v.tensor_reduce(out=sss[:], in_=sqs[:], axis=AX.X, op=ALU.add), reads=[sqs], writes=[sss])
        k.op("act", lambda a: a.activation(out=sss[:], in_=sss[:], func=ACT.Ln, bias=L2_EPS, scale=1.0), reads=[sss], writes=[sss])
        k.op("act", lambda a: a.activation(out=sss[:, 0:4], in_=sss[:, 0:4], func=ACT.Exp, bias=float(-0.5 * np.log(128.0)), scale=-0.5), reads=[sss], writes=[sss])
        k.op("act", lambda a: a.activation(out=sss[:, 4:8], in_=sss[:, 4:8], func=ACT.Exp, scale=-0.5), reads=[sss], writes=[sss])
        k.op("dve", lambda v: v.tensor_tensor(out=qkv_s[:, 0:8, :], in0=qkv_s[:, 0:8, :], in1=bc(sss[:].rearrange("p (a b) -> p a b", b=1), [P16, 8, 128]), op=ALU.mult),
             reads=[qkv_s, sss], writes=[qkv_s])
        k.barrier()
        k3s.close()
        k3s = contextlib.ExitStack()
        Ssb = sb3("Ssb", [128, P16 * 4, 128])
        k.dma("sp", Ssb[:], sdelta_d.rearrange("b h k v -> k (b h) v"), writes=[Ssb], semkey="ld_Ssb")
        qkT_s = sb3("qkT_s", [128, 8, P16])
        for c in range(8):
            k.op("pe", lambda p, c=c: p.transpose(out=PB[0][:, c * P16:(c + 1) * P16], in_=qkv_s[:, c, :], identity=ident[0:P16, 0:P16]),
                 reads=[qkv_s, ident], writes=[PB[0]])
        k.op("dve", lambda v: v.tensor_copy(out=qkT_s[:].rearrange("p a b -> p (a b)"), in_=PB[0][:, 0:8 * P16]), reads=[PB[0]], writes=[qkT_s])
        dmask = sb3("dmask", [P16, P16, 128])
        k.op("pool", lambda g: g.tensor_copy(out=dmask[:], in_=bc(ident[0:P16, 0:P16].rearrange("p (a b) -> p a b", b=1), [P16, P16, 128])),
             reads=[ident], writes=[dmask])
        egm = sb3("egm", [P16, P16, 4]); egbc = sb3("egbc", [128, P16 * 4])
        k.op("dve", lambda v: v.tensor_tensor(out=egm[:], in0=bc(eg_s[:].rearrange("p (a b) -> p a b", a=1), [P16, P16, 4]),
                                              in1=bc(ident[0:P16, 0:P16].rearrange("p (a b) -> p a b", b=1), [P16, P16, 4]), op=ALU.mult),
             reads=[eg_s, ident], writes=[egm])
        k.op("pe", lambda p: p.matmul(PB[1][:, 0:P16 * 4], lhsT=ones[0:P16, :], rhs=egm[:].rearrange("p a b -> p (a b)"), start=True, stop=True),
             reads=[ones, egm], writes=[PB[1]])
        k.op("dve", lambda v: v.tensor_copy(out=egbc[:], in_=PB[1][:, 0:P16 * 4]), reads=[PB[1]], writes=[egbc])
        pred = sb3("pred", [P16, 4, 128]); qS = sb3("qS", [P16, 4, 128]); tmpd = sb3("tmpd", [P16, P16, 128])
        dd = sb3("dd", [P16, 4, 128]); Dm = sb3("Dm", [P16, P16, 128]); o_s = sb3("o_s", [P16, 4, 128])
        qk_s = sb3("qk_s", [P16, 4]); qkt = sb3("qkt", [P16, 4, 128])
        k.op("dve", lambda v: v.tensor_tensor(out=qkt[:], in0=qkv_s[:, 0:4, :], in1=qkv_s[:, 4:8, :], op=ALU.mult), reads=[qkv_s], writes=[qkt])
        k.op("dve", lambda v: v.tensor_reduce(out=qk_s[:], in_=qkt[:], axis=AX.X, op=ALU.add), reads=[qkt], writes=[qk_s])
        for h in range(4):
            for which, dst in ((4, pred), (0, qS)):
                banks = [PB[2], PB[3], PB[4], PB[5]] if which == 4 else [PB[6], PB[7], PB[0], PB[1]]
                for b in range(P16):
                    pb = banks[b // 4]
                    k.op("pe", lambda p, b=b, pb=pb, which=which, h=h: p.matmul(pb[0:P16, (b % 4) * 128:(b % 4 + 1) * 128], lhsT=qkT_s[:, which + h, :],
                                                                               rhs=Ssb[:, b * 4 + h, :], start=True, stop=True),
                         reads=[qkT_s, Ssb], writes=[pb])
                for j in range(4):
                    k.op("dve", lambda v, j=j, banks=banks: v.tensor_tensor(out=tmpd[:, 4 * j:4 * j + 4, :], in0=banks[j][0:P16, :].rearrange("p (a b) -> p a b", a=4),
                                                                            in1=dmask[:, 4 * j:4 * j + 4, :], op=ALU.mult), reads=[banks[j], dmask], writes=[tmpd])
                k.op("dve", lambda v, dst=dst, h=h: v.tensor_reduce(out=dst[:, h, :], in_=tmpd[:].rearrange("p b v -> p v b"), axis=AX.X, op=ALU.add),
                     reads=[tmpd], writes=[dst])
            k.op("dve", lambda v, h=h: v.scalar_tensor_tensor(out=dd[:, h, :], in0=pred[:, h, :], scalar=eg_s[:, h:h + 1], in1=qkv_s[:, 8 + h, :],
                                                               op0=ALU.mult, op1=ALU.subtract), reads=[pred, eg_s, qkv_s], writes=[dd])
            k.op("dve", lambda v, h=h: v.tensor_scalar(out=dd[:, h, :], in0=dd[:, h, :], scalar1=nbeta_s[:, h:h + 1], scalar2=None, op0=ALU.mult),
                 reads=[dd, nbeta_s], writes=[dd])
            k.op("dve", lambda v, h=h: v.tensor_scalar(out=o_s[:, h, :], in0=dd[:, h, :], scalar1=qk_s[:, h:h + 1], scalar2=None, op0=ALU.mult),
                 reads=[dd, qk_s], writes=[o_s])
            k.op("dve", lambda v, h=h: v.scalar_tensor_tensor(out=o_s[:, h, :], in0=qS[:, h, :], scalar=eg_s[:, h:h + 1], in1=o_s[:, h, :],
                                                               op0=ALU.mult, op1=ALU.add), reads=[qS, eg_s, o_s], writes=[o_s])
            k.op("dve", lambda v, h=h: v.tensor_tensor(out=Dm[:], in0=bc(dd[:, h:h + 1, :], [P16, P16, 128]), in1=dmask[:], op=ALU.mult),
                 reads=[dd, dmask], writes=[Dm])
            banks = [PB[2], PB[3], PB[4], PB[5]]
            for b in range(P16):
                pb = banks[b // 4]
                k.op("pe", lambda p, b=b, pb=pb, h=h: p.matmul(pb[:, (b % 4) * 128:(b % 4 + 1) * 128], lhsT=qkv_s[:, 4 + h, :], rhs=Dm[:, b, :],
                                                               start=True, stop=True), reads=[qkv_s, Dm], writes=[pb])
            for b in range(P16):
                pb = banks[b // 4]
                k.op("dve", lambda v, b=b, pb=pb, h=h: v.scalar_tensor_tensor(out=Ssb[:, b * 4 + h, :], in0=Ssb[:, b * 4 + h, :], scalar=egbc[:, b * 4 + h:b * 4 + h + 1],
                                                                              in1=pb[:, (b % 4) * 128:(b % 4 + 1) * 128], op0=ALU.mult, op1=ALU.add),
                     reads=[Ssb, egbc, pb], writes=[Ssb])
        k.dma("sp", deltas_d.rearrange("b h k v -> k (b h) v"), Ssb[:], reads=[Ssb], semkey="st_smisc")
        oss_s = sb3("oss_s", [P16, 4])
        k.op("dve", lambda v: v.tensor_tensor(out=qkt[:], in0=o_s[:], in1=o_s[:], op=ALU.mult), reads=[o_s], writes=[qkt])
        k.op("dve", lambda v: v.tensor_reduce(out=oss_s[:], in_=qkt[:], axis=AX.X, op=ALU.add), reads=[qkt], writes=[oss_s])
        k.op("act", lambda a: a.activation(out=oss_s[:], in_=oss_s[:], func=ACT.Ln, bias=RMS_EPS, scale=1.0 / 128.0), reads=[oss_s], writes=[oss_s])
        k.op("act", lambda a: a.activation(out=oss_s[:], in_=oss_s[:], func=ACT.Exp, scale=-0.5), reads=[oss_s], writes=[oss_s])
        k.op("dve", lambda v: v.tensor_tensor(out=o_s[:], in0=o_s[:], in1=bc(oss_s[:].rearrange("p (a b) -> p a b", b=1), [P16, 4, 128]), op=ALU.mult),
             reads=[o_s, oss_s], writes=[o_s])
        k.op("dve", lambda v: v.tensor_tensor(out=o_s[:], in0=o_s[:], in1=bc(norma_bc[0:P16, :].rearrange("p (a b) -> p a b", a=1), [P16, 4, 128]), op=ALU.mult),
             reads=[o_s, norma_bc], writes=[o_s])
        k.op("dve", lambda v: v.tensor_tensor(out=mix_s[:, 0:512], in0=o_s[:].rearrange("p a b -> p (a b)"), in1=zAs_s[:], op=ALU.mult),
             reads=[o_s, zAs_s], writes=[mix_s])
        k.barrier()
        k3s.close()
        k3s = contextlib.ExitStack()
        Kc = sb3("Kc", [128, P16, 128]); Vc = sb3("Vc", [128, P16, 128])
        KcT = sb3("KcT", [128, P16, 128], BF16); VcB = sb3("VcB", [128, P16, 128], BF16)
        k.dma("sp", Kc[:], sk_d.rearrange("b s c -> s b c"), writes=[Kc], semkey="ld_Kc")
        k.dma("sp", Vc[:], sv_d.rearrange("b s c -> s b c"), writes=[Vc], semkey="ld_Vc")
        k.dma("sp", swaks_d[:, 0:127, :], sk_d[:, 1:128, :], semkey="st_smisc")
        k.dma("sp", swavs_d[:, 0:127, :], sv_d[:, 1:128, :], semkey="st_smisc")
        k.dma("sp", swaks_d[:, 127, :], kr_s[:], reads=[kr_s], semkey="st_smisc")
        k.dma("sp", swavs_d[:, 127, :], v_s[:], reads=[v_s], semkey="st_smisc")
        k.op("pool", lambda g: g.tensor_copy(out=VcB[:], in_=Vc[:]), reads=[Vc], writes=[VcB])
        for b in range(P16):
            pb = [PB[2], PB[3], PB[4], PB[5]][b // 4]
            k.op("pe", lambda p, b=b, pb=pb: p.transpose(out=pb[:, (b % 4) * 128:(b % 4 + 1) * 128], in_=Kc[:, b, :], identity=ident[:]),
                 reads=[Kc, ident], writes=[pb])
        for j in range(4):
            pb = [PB[2], PB[3], PB[4], PB[5]][j]
            k.op("act", lambda a, j=j, pb=pb: a.copy(out=KcT[:, 4 * j:4 * j + 4, :], in_=pb[:, :].rearrange("p (a b) -> p a b", a=4)), reads=[pb], writes=[KcT])
        qT_s = sb3("qT_s", [128, 4, P16]); Aq = sb3("Aq", [128, P16, 2, 4], BF16)
        knT = sb3("knT", [128, P16], BF16); zBT = sb3("zBT", [128, 4, P16])
        for c in range(4):
            k.op("pe", lambda p, c=c: p.transpose(out=PB[6][:, c * P16:(c + 1) * P16], in_=qr_s[:, c * 128:(c + 1) * 128], identity=ident[0:P16, 0:P16]),
                 reads=[qr_s, ident], writes=[PB[6]])
        k.op("pe", lambda p: p.transpose(out=PB[6][:, 4 * P16:5 * P16], in_=kr_s[:], identity=ident[0:P16, 0:P16]), reads=[kr_s, ident], writes=[PB[6]])
        for c in range(4):
            k.op("pe", lambda p, c=c: p.transpose(out=PB[6][:, (5 + c) * P16:(6 + c) * P16], in_=zBs_s[:, c * 128:(c + 1) * 128], identity=ident[0:P16, 0:P16]),
                 reads=[zBs_s, ident], writes=[PB[6]])
        k.op("dve", lambda v: v.tensor_copy(out=qT_s[:].rearrange("p a b -> p (a b)"), in_=PB[6][:, 0:4 * P16]), reads=[PB[6]], writes=[qT_s])
        k.op("dve", lambda v: v.tensor_copy(out=knT[:], in_=PB[6][:, 4 * P16:5 * P16]), reads=[PB[6]], writes=[knT])
        k.op("dve", lambda v: v.tensor_copy(out=zBT[:].rearrange("p a b -> p (a b)"), in_=PB[6][:, 5 * P16:9 * P16]), reads=[PB[6]], writes=[zBT])
        k.op("pool", lambda g: g.memset(Aq[:], 0.0), writes=[Aq])
        k.op("dve", lambda v: v.tensor_copy(out=Aq[0:64, :, 0, :], in_=qT_s[0:64, :, :].rearrange("p c b -> p b c")), reads=[qT_s], writes=[Aq])
        k.op("dve", lambda v: v.tensor_copy(out=Aq[64:128, :, 1, :], in_=qT_s[64:128, :, :].rearrange("p c b -> p b c")), reads=[qT_s], writes=[Aq])
        for b in range(P16):
            k.op("pe", lambda p, b=b: p.matmul(PB[7][:, b * 8:(b + 1) * 8], lhsT=KcT[:, b, :], rhs=Aq[:, b, :, :].rearrange("p a b -> p (a b)"),
                                               start=True, stop=True), reads=[KcT, Aq], writes=[PB[7]])
        STs = sb3("STs", [128, 128])
        k.op("dve", lambda v: v.tensor_copy(out=STs[:], in_=PB[7][:, 0:128]), reads=[PB[7]], writes=[STs])
        k.op("pe", lambda p: p.transpose(out=PB[0][:, 0:128], in_=STs[:], identity=ident[:]), reads=[STs, ident], writes=[PB[0]])
        k.op("pe", lambda p: p.matmul(PB[0][:, 128:128 + P16], lhsT=Aq[:].rearrange("p a b c -> p (a b c)"), rhs=knT[:], start=True, stop=True),
             reads=[Aq, knT], writes=[PB[0]])
        M2 = sb3("M2", [128, P16]); M2t = sb3("M2t", [128, P16])
        k.op("pool", lambda g: g.affine_select(out=M2t[:], in_=ones[:, 0:P16], pattern=[[-8, P16]], compare_op=ALU.is_ge, fill=0.0, base=0, channel_multiplier=1),
             reads=[ones], writes=[M2t])
        k.op("pool", lambda g: g.affine_select(out=M2[:], in_=M2t[:], pattern=[[8, P16]], compare_op=ALU.is_ge, fill=0.0, base=7, channel_multiplier=-1),
             reads=[M2t], writes=[M2])
        sm = sb3("sm", [128, 16]); tmps = sb3("tmps", [128, P16])
        sinkcol = sb3("sinkcol", [128, 1])
        k.dma("sp", sinkcol[:], sinkcol_d[:, :], writes=[sinkcol], semkey="ld_sinkcol")
        k.op("dve", lambda v: v.tensor_tensor(out=tmps[:], in0=PB[0][:, 128:128 + P16], in1=M2[:], op=ALU.mult), reads=[PB[0], M2], writes=[tmps])
        k.op("dve", lambda v: v.tensor_reduce(out=sm[:, 0:1], in_=tmps[:], axis=AX.X, op=ALU.add), reads=[tmps], writes=[sm])
        k.op("dve", lambda v: v.tensor_reduce(out=sm[:, 1:2], in_=PB[0][:, 0:128], axis=AX.X, op=ALU.max), reads=[PB[0]], writes=[sm])
        k.op("dve", lambda v: v.tensor_tensor(out=sm[:, 1:2], in0=sm[:, 1:2], in1=sm[:, 0:1], op=ALU.max), reads=[sm], writes=[sm])
        k.op("dve", lambda v: v.tensor_tensor(out=sm[:, 1:2], in0=sm[:, 1:2], in1=sinkcol[:], op=ALU.max), reads=[sm, sinkcol], writes=[sm])
        k.op("dve", lambda v: v.tensor_scalar(out=sm[:, 2:3], in0=sm[:, 1:2], scalar1=-1.0, scalar2=None, op0=ALU.mult), reads=[sm], writes=[sm])
        Ps = sb3("Ps", [128, 128])
        k.op("act", lambda a: a.activation(out=Ps[:], in_=PB[0][:, 0:128], func=ACT.Exp, bias=sm[:, 2:3], scale=1.0, accum_out=sm[:, 3:4]),
             reads=[PB[0], sm], writes=[Ps, sm])
        k.op("act", lambda a: a.activation(out=sm[:, 4:5], in_=sm[:, 0:1], func=ACT.Exp, bias=sm[:, 2:3], scale=1.0), reads=[sm], writes=[sm])
        k.op("act", lambda a: a.activation(out=sm[:, 5:6], in_=sinkcol[:], func=ACT.Exp, bias=sm[:, 2:3], scale=1.0), reads=[sm, sinkcol], writes=[sm])
        k.op("dve", lambda v: v.tensor_tensor(out=sm[:, 6:7], in0=sm[:, 3:4], in1=sm[:, 4:5], op=ALU.add), reads=[sm], writes=[sm])
        k.op("dve", lambda v: v.tensor_tensor(out=sm[:, 6:7], in0=sm[:, 6:7], in1=sm[:, 5:6], op=ALU.add), reads=[sm], writes=[sm])
        k.op("dve", lambda v: v.reciprocal(out=sm[:, 7:8], in_=sm[:, 6:7]), reads=[sm], writes=[sm])
        k.op("dve", lambda v: v.tensor_scalar(out=Ps[:], in0=Ps[:], scalar1=sm[:, 7:8], scalar2=None, op0=ALU.mult), reads=[Ps, sm], writes=[Ps])
        k.op("dve", lambda v: v.tensor_tensor(out=sm[:, 8:9], in0=sm[:, 4:5], in1=sm[:, 7:8], op=ALU.mult), reads=[sm], writes=[sm])
        PsT = sb3("PsT", [128, 128], BF16); Wd = sb3("Wd", [128, P16]); Wn = sb3("Wn", [P16, 128], BF16)
        k.op("pe", lambda p: p.transpose(out=PB[1][:, 0:128], in_=Ps[:], identity=ident[:]), reads=[Ps, ident], writes=[PB[1]])
        k.op("act", lambda a: a.copy(out=PsT[:], in_=PB[1][:, 0:128]), reads=[PB[1]], writes=[PsT])
        k.op("dve", lambda v: v.tensor_scalar(out=Wd[:], in0=M2[:], scalar1=sm[:, 8:9], scalar2=None, op0=ALU.mult), reads=[M2, sm], writes=[Wd])
        k.op("pe", lambda p: p.transpose(out=PB[1][0:P16, 128:256], in_=Wd[:], identity=ident[:]), reads=[Wd, ident], writes=[PB[1]])
        k.op("act", lambda a: a.copy(out=Wn[:], in_=PB[1][0:P16, 128:256]), reads=[PB[1]], writes=[Wn])
        for b in range(P16):
            k.op("pe", lambda p, b=b: p.matmul(PB[2][:, b * 8:(b + 1) * 8], lhsT=VcB[:, b, :], rhs=PsT[:, b * 8:(b + 1) * 8], start=True, stop=False),
                 reads=[VcB, PsT], writes=[PB[2]])
            k.op("pe", lambda p, b=b: p.matmul(PB[2][:, b * 8:(b + 1) * 8], lhsT=vsb[:], rhs=Wn[:, b * 8:(b + 1) * 8], start=False, stop=True),
                 reads=[vsb, Wn], writes=[PB[2]])
        obT = sb3("obT", [128, 4, P16])
        OT4 = PB[2][:, 0:128].rearrange("p (b k c) -> p b k c", b=P16, k=2)
        k.op("dve", lambda v: v.tensor_copy(out=obT[0:64, :, :].rearrange("p c b -> p b c"), in_=OT4[0:64, :, 0, :]), reads=[PB[2]], writes=[obT])
        k.op("dve", lambda v: v.tensor_copy(out=obT[64:128, :, :].rearrange("p c b -> p b c"), in_=OT4[64:128, :, 1, :]), reads=[PB[2]], writes=[obT])
        mixT_s = sb3("mixT_s", [128, 8, P16], BF16)
        k.op("dve", lambda v: v.tensor_tensor(out=mixT_s[:, 4:8, :], in0=obT[:], in1=zBT[:], op=ALU.mult), reads=[obT, zBT], writes=[mixT_s])
        for c in range(4):
            k.op("pe", lambda p, c=c: p.transpose(out=PB[3][:, c * P16:(c + 1) * P16], in_=mix_s[:, c * 128:(c + 1) * 128], identity=ident[0:P16, 0:P16]),
                 reads=[mix_s, ident], writes=[PB[3]])
        k.op("dve", lambda v: v.tensor_copy(out=mixT_s[:, 0:4, :].rearrange("p a b -> p (a b)"), in_=PB[3][:, 0:4 * P16]), reads=[PB[3]], writes=[mixT_s])
        for j in range(2):
            for kc in range(8):
                k.op("pe", lambda p, j=j, kc=kc: p.matmul(PB[4 + j][0:P16, :], lhsT=mixT_s[:, kc, :], rhs=Wo[:, kc, j * 512:(j + 1) * 512],
                                                          start=(kc == 0), stop=(kc == 7)), reads=[mixT_s, Wo], writes=[PB[4 + j]])
        ypre_s = sb3("ypre_s", [P16, D]); yo_s = sb3("yo_s", [P16, D]); st_s = sb3("st_s", [P16, 8])
        for j in range(2):
            k.op("dve", lambda v, j=j: v.scalar_tensor_tensor(out=ypre_s[:, j * 512:(j + 1) * 512], in0=mod_s[:, 2 * D + j * 512:2 * D + (j + 1) * 512], scalar=1.0,
                                                               in1=PB[4 + j][0:P16, :], op0=ALU.add, op1=ALU.mult), reads=[mod_s, PB[4 + j]], writes=[ypre_s])
        k.op("dve", lambda v: v.scalar_tensor_tensor(out=ypre_s[:], in0=xs[:], scalar=ALPHA, in1=ypre_s[:], op0=ALU.mult, op1=ALU.add),
             reads=[xs, ypre_s], writes=[ypre_s])
        k.op("act", lambda a: a.activation(out=yo_s[:], in_=ypre_s[:], func=ACT.Identity, accum_out=st_s[:, 0:1]), reads=[ypre_s], writes=[yo_s, st_s])
        k.op("act", lambda a: a.activation(out=yo_s[:], in_=ypre_s[:], func=ACT.Square, accum_out=st_s[:, 1:2]), reads=[ypre_s], writes=[yo_s, st_s])
        k.op("dve", lambda v: v.tensor_scalar(out=st_s[:, 2:3], in0=st_s[:, 0:1], scalar1=1.0 / D, scalar2=None, op0=ALU.mult), reads=[st_s], writes=[st_s])
        k.op("dve", lambda v: v.tensor_tensor(out=st_s[:, 3:4], in0=st_s[:, 2:3], in1=st_s[:, 2:3], op=ALU.mult), reads=[st_s], writes=[st_s])
        k.op("dve", lambda v: v.scalar_tensor_tensor(out=st_s[:, 4:5], in0=st_s[:, 1:2], scalar=1.0 / D, in1=st_s[:, 3:4], op0=ALU.mult, op1=ALU.subtract),
             reads=[st_s], writes=[st_s])
        k.op("act", lambda a: a.activation(out=st_s[:, 5:6], in_=st_s[:, 4:5], func=ACT.Ln, bias=LN_EPS, scale=1.0), reads=[st_s], writes=[st_s])
        k.op("act", lambda a: a.activation(out=st_s[:, 5:6], in_=st_s[:, 5:6], func=ACT.Exp, scale=-0.5), reads=[st_s], writes=[st_s])
        k.op("dve", lambda v: v.scalar_tensor_tensor(out=st_s[:, 6:7], in0=st_s[:, 2:3], scalar=-1.0, in1=st_s[:, 5:6], op0=ALU.mult, op1=ALU.mult),
             reads=[st_s], writes=[st_s])
        k.op("act", lambda a: a.activation(out=yo_s[:], in_=ypre_s[:], func=ACT.Identity, bias=st_s[:, 6:7], scale=st_s[:, 5:6]), reads=[ypre_s, st_s], writes=[yo_s])
        k.op("dve", lambda v: v.tensor_tensor(out=yo_s[:], in0=yo_s[:], in1=lng_bc[0:P16, :], op=ALU.mult), reads=[yo_s, lng_bc], writes=[yo_s])
        k.op("dve", lambda v: v.tensor_tensor(out=yo_s[:], in0=yo_s[:], in1=lnb_bc[0:P16, :], op=ALU.add), reads=[yo_s, lnb_bc], writes=[yo_s])
        k.dma("sp", ys_d[:, :], yo_s[:], reads=[yo_s], semkey="st_smisc")
        k.barrier()
        k3s.close()

    k.barrier()
    k2.close()
    k1s.close()
    identb = k.sb("identb", [128, 128], BF16); Ub = k.sb("Ub", [128, 128], BF16)
    k.op("pool", lambda g: g.tensor_copy(out=identb[:], in_=ident[:]), reads=[ident], writes=[identb])
    k.op("pool", lambda g: g.tensor_copy(out=Ub[:], in_=U[:]), reads=[U], writes=[Ub])
    Xt = [k.sb("Xt%d" % i, [128, D]) for i in range(2)]
    Xb = k.sb("Xb", [128, D], BF16)
    hT = k.sb("hT", [128, 8, 128], BF16)
    pre = k.sb("pre", [128, 12, 131])
    k.op("pool", lambda g: g.memset(pre[:], 0.0), writes=[pre])
    cm = [k.sb("cm%d" % i, [128, 12, 128]) for i in range(2)]
    qkvs = k.sb("qkvs", [128, 12, 128])
    qkb = k.sb("qkb", [128, 12, 128], BF16)
    sqb = k.sb("sqb", [128, 8, 128], BF16)
    lnss = k.sb("lnss", [128, 8, 128])
    rn = lnss
    zAs = k.sb("zAs", [128, 512]); zBs = k.sb("zBs", [128, 512])
    qrb = k.sb("qrb", [128, 512], BF16); kr = k.sb("kr", [128, 128])
    rt = [k.sb("rt%d" % i, [128, 8, 32]) for i in range(4)]
    vB = [k.sb("vB%d" % i, [128, 128], BF16) for i in range(2)]
    vB32 = k.sb("vB32", [128, 128])
    kTb = [k.sb("kTb%d" % i, [128, 128], BF16) for i in range(2)]
    k.op("pool", lambda g: g.memset(kTb[1][:], 0.0), writes=[kTb[1]])
    k.op("pool", lambda g: g.memset(vB[1][:], 0.0), writes=[vB[1]])
    qTb = k.sb("qTb", [128, 4, 128], BF16)
    bd = k.sb("bd", [128, 8]); bdt = k.sb("bdt", [128, 8])
    beta = k.sb("beta", [128, 4]); negbeta = k.sb("negbeta", [128, 4]); gg = k.sb("gg", [128, 4])
    gsp = k.sb("gsp", [128, 8], BF16); gtmp = k.sb("gtmp", [128, 4])
    gUh = k.sb("gUh", [128, 4, 128], BF16); gUl = k.sb("gUl", [128, 4, 128], BF16)
    Gc = k.sb("Gc", [128, 4]); negGc = k.sb("negGc", [128, 4]); eG = k.sb("eG", [128, 4]); beG = k.sb("beG", [128, 4])
    edG = k.sb("edG", [128, 4]); egl = k.sb("egl", [128, 4]); dG = k.sb("dG", [128, 4]); Glb = k.sb("Glb", [128, 4])
    tmp1 = k.sb("tmp1", [128, 4, 128]); tmp2 = k.sb("tmp2", [128, 4, 128])
    E1 = tmp1; E2 = tmp2
    eGbc = k.sb("eGbc", [128, 4, 128]); qdT = k.sb("qdT", [128, 4, 128], BF16)
    AqkT = k.sb("AqkT", [128, 4, 128], BF16)
    Y0f = k.sb("Y0f", [128, 4, 128])
    XRh = [k.sb("XRh%d" % i, [128, 4, 256], BF16) for i in range(2)]
    XRl = [k.sb("XRl%d" % i, [128, 4, 256], BF16) for i in range(2)]
    Yh = [k.sb("Yh%d" % i, [128, 4, 128], BF16) for i in range(2)]
    Yl = [k.sb("Yl%d" % i, [128, 4, 128], BF16) for i in range(2)]
    Rf = k.sb("Rf", [128, 4, 128])
    kbt = k.sb("kbt", [128, 4, 128], BF16); kdt = k.sb("kdt", [128, 4, 128], BF16); bvt = k.sb("bvt", [128, 4, 128], BF16)
    wT = k.sb("wT", [128, 4, 128], BF16); ut = k.sb("ut", [128, 4, 128]); uub = k.sb("uub", [128, 4, 128], BF16)
    S = k.sb("S", [128, 4, 128]); Sb = k.sb("Sb", [128, 4, 128], BF16)
    k.op("pool", lambda g: g.memset(S[:], 0.0), writes=[S])
    k.op("pool", lambda g: g.memset(Sb[:], 0.0), writes=[Sb])
    oss = k.sb("oss", [128, 4]); orr = k.sb("orr", [128, 4])
    nz = k.sb("nz", [128, 4, 128])
    mixb = k.sb("mixb", [128, D], BF16); obt = k.sb("obt", [128, 512])
    ojunk = tmp1[:, 0, :]
    mixT = k.sb("mixT", [128, 8, 128], BF16)
    SC = k.sb("SC", [128, 8, 256]); Pb = k.sb("Pb", [128, 8, 256], BF16)
    PTb = k.sb("PTb", [128, 16, 128], BF16)
    mx = k.sb("mx", [128, 8]); negm = k.sb("negm", [128, 8]); rs = k.sb("rs", [128, 8]); es_ = k.sb("es_", [128, 8])
    rden = k.sb("rden", [128, 8])
    SCf = SC[:].rearrange("p a b -> p (a b)")
    ypre = SCf[:, 0:D]
    st = k.sb("st", [128, 8])
    Yt = [SCf[:, D:2 * D]]

    def v3(ap, a):
        return ap.rearrange("p (a b) -> p a b", a=a)

    def pbf(pb):
        return pb[:, :].bitcast(BF16)

    def mm(out, lhsT, rhs, start, stop, reads, writes):
        k.op("pe", lambda p: p.matmul(out, lhsT=lhsT, rhs=rhs, start=start, stop=stop), reads=reads, writes=writes)

    def tr(out, in_, idn, reads, writes):
        k.op("pe", lambda p: p.transpose(out=out, in_=in_, identity=idn), reads=reads + [idn], writes=writes)

    k.barrier()
    k.pe_inorder = True
    krb = k.sb("krb", [128, 128], BF16)
    def P1(n):
        X = Xt[n % 2]
        yield
        k.dma("sp", X[:], x_d[n * 128:(n + 1) * 128, :], writes=[X], semkey="ld_" + X.name)
        cs_t = cst[n % 2]
        yield
        k.dma("sp", cs_t[:, 0, :], cosk_d[n * 128:(n + 1) * 128, :], writes=[cs_t], semkey="ld_" + cs_t.name, nowaw=True)
        yield
        k.dma("sp", cs_t[:, 1, :], sink_d[n * 128:(n + 1) * 128, :], writes=[cs_t], semkey="ld_" + cs_t.name, nowaw=True)
        yield
        k.op("pool", lambda g: g.tensor_copy(out=Xb[:], in_=X[:]), reads=[X], writes=[Xb])
        P0b = pbf(PB[0])
        yield
        for kc in range(8):
            tr(P0b[:, kc * 128:(kc + 1) * 128], Xb[:, kc * 128:(kc + 1) * 128], identb[:], [Xb], [PB[0]])
        yield
        for kc in range(8):
            k.op("act" if kc % 2 == 0 else "dve",
                 (lambda a, kc=kc: a.activation(out=hT[:, kc, :], in_=P0b[:, kc * 128:(kc + 1) * 128], func=ACT.Identity,
                                                bias=mod_fm[:, kc:kc + 1], scale=mod_fm[:, 8 + kc:9 + kc])) if kc % 2 == 0 else
                 (lambda v, kc=kc: v.tensor_scalar(out=hT[:, kc, :], in0=P0b[:, kc * 128:(kc + 1) * 128], scalar1=mod_fm[:, 8 + kc:9 + kc],
                                                   scalar2=mod_fm[:, kc:kc + 1], op0=ALU.mult, op1=ALU.add)),
                 reads=[PB[0], mod_fm], writes=[hT])
        pc = pre
        yield
        k.op("pool", lambda g: g.tensor_copy(out=pc[:, :, 0:3], in_=pc[:, :, 128:131]), reads=[pc], writes=[pc])
        yield
        for c in range(12):
            for kc in range(8):
                mm(PB[2 + c // 4][:, (c % 4) * 128:(c % 4 + 1) * 128], Wf[:, kc, c * 128:(c + 1) * 128], hT[:, kc, :],
                   kc == 0, kc == 7, [Wf, hT], [PB[2 + c // 4]])
        yield
        for j in range(3):
            if j != 1:
                k.op("act", lambda a, j=j: a.copy(out=pc[:, j * 4:(j + 1) * 4, 3:131], in_=v3(PB[2 + j][:, :], 4)), reads=[PB[2 + j]], writes=[pc])
            else:
                k.op("dve", lambda v, j=j: v.tensor_copy(out=pc[:, j * 4:(j + 1) * 4, 3:131], in_=v3(PB[2 + j][:, :], 4)), reads=[PB[2 + j]], writes=[pc])
        yield
        k.op("dve", lambda v: v.tensor_tensor(out=cm[0][:], in0=pc[:, :, 0:128], in1=bc(cwT[:, 0:12].rearrange("p (a b) -> p a b", b=1), [128, 12, 128]), op=ALU.mult),
             reads=[pc, cwT], writes=[cm[0]])
        yield
        for j in range(1, 4):
            k.op("pool", lambda g, j=j: g.tensor_tensor(out=cm[1][:], in0=pc[:, :, j:j + 128], in1=bc(cwT[:, j * 12:(j + 1) * 12].rearrange("p (a b) -> p a b", b=1), [128, 12, 128]), op=ALU.mult),
                 reads=[pc, cwT], writes=[cm[1]])
            k.op("dve", lambda v: v.tensor_tensor(out=cm[0][:], in0=cm[0][:], in1=cm[1][:], op=ALU.add), reads=[cm[0], cm[1]], writes=[cm[0]])
        yield
        k.op("act", lambda a: a.activation(out=qkvs[:], in_=cm[0][:], func=ACT.Silu), reads=[cm[0]], writes=[qkvs])
        offs = [(0, 512), (512, 512), (1024, 512), (1536, 264)]
        yield
        for j, (o, w_) in enumerate(offs):
            pb = [PB[5], PB[1], PB[0], PB[2]][j]
            for kc in range(8):
                mm(pb[:, 0:w_], hT[:, kc, :], Wt[:, kc, o:o + w_], kc == 0, kc == 7, [hT, Wt], [pb])
        PzA, PqB, PzB, Pk = PB[5], PB[1], PB[0], PB[2]
        yield
        k.op("act", lambda a: a.activation(out=zAs[:], in_=PzA[:, :], func=ACT.Silu), reads=[PzA], writes=[zAs])
        yield
        k.op("act", lambda a: a.activation(out=zBs[:], in_=PzB[:, :], func=ACT.Silu), reads=[PzB], writes=[zBs])
        yield
        k.op("pool", lambda g: g.tensor_tensor(out=sqb[:], in0=qkvs[:, 0:8, :], in1=qkvs[:, 0:8, :], op=ALU.mult), reads=[qkvs], writes=[sqb])
        yield
        for j in range(2):
            mm(PB[3 + j][:, :], onesb[:], sqb[:, j * 4:(j + 1) * 4, :].rearrange("p a b -> p (a b)"), True, True, [onesb, sqb], [PB[3 + j]])
        yield
        for j in range(2):
            k.op("act", lambda a, j=j: a.activation(out=lnss[:, j * 4:(j + 1) * 4, :].rearrange("p a b -> p (a b)"), in_=PB[3 + j][:, :],
                                                    func=ACT.Ln, bias=L2_EPS, scale=1.0), reads=[PB[3 + j]], writes=[lnss])
        yield
        k.op("act", lambda a: a.activation(out=rn[:, 0:4, :], in_=lnss[:, 0:4, :], func=ACT.Exp, bias=float(-0.5 * np.log(128.0)), scale=-0.5), reads=[lnss], writes=[rn])
        yield
        k.op("act", lambda a: a.activation(out=rn[:, 4:8, :], in_=lnss[:, 4:8, :], func=ACT.Exp, scale=-0.5), reads=[lnss], writes=[rn])
        yield
        k.op("dve", lambda v: v.tensor_tensor(out=qkvs[:, 0:8, :], in0=qkvs[:, 0:8, :], in1=rn[:], op=ALU.mult), reads=[qkvs, rn], writes=[qkvs])
        yield
        k.op("pool", lambda g: g.tensor_copy(out=qkb[:], in_=qkvs[:]), reads=[qkvs], writes=[qkb])
        q3 = v3(PqB[:, :], 8)
        qr3 = v3(qrb[:], 8)
        cq = bc(cs_t[:, 0:1, :], [128, 8, 32]); sq_ = bc(cs_t[:, 1:2, :], [128, 8, 32])
        yield
        k.op("dve", lambda v: v.tensor_tensor(out=rt[0][:], in0=q3[:, :, 0:32], in1=cq, op=ALU.mult), reads=[PqB, cs_t], writes=[rt[0]])
        yield
        k.op("dve", lambda v: v.tensor_tensor(out=rt[1][:], in0=q3[:, :, 32:64], in1=sq_, op=ALU.mult), reads=[PqB, cs_t], writes=[rt[1]])
        yield
        k.op("dve", lambda v: v.tensor_tensor(out=rt[2][:], in0=q3[:, :, 32:64], in1=cq, op=ALU.mult), reads=[PqB, cs_t], writes=[rt[2]])
        yield
        k.op("dve", lambda v: v.tensor_tensor(out=rt[3][:], in0=q3[:, :, 0:32], in1=sq_, op=ALU.mult), reads=[PqB, cs_t], writes=[rt[3]])
        yield
        k.op("pool", lambda g: g.tensor_tensor(out=qr3[:, :, 0:32], in0=rt[0][:], in1=rt[1][:], op=ALU.subtract), reads=[rt[0], rt[1]], writes=[qrb])
        yield
        k.op("pool", lambda g: g.tensor_tensor(out=qr3[:, :, 32:64], in0=rt[2][:], in1=rt[3][:], op=ALU.add), reads=[rt[2], rt[3]], writes=[qrb])
        k3 = v3(Pk[:, 0:128], 2)
        kr3 = v3(kr[:], 2)
        ck = bc(cs_t[:, 0:1, :], [128, 2, 32]); sk_ = bc(cs_t[:, 1:2, :], [128, 2, 32])
        yield
        k.op("dve", lambda v: v.tensor_tensor(out=rt[0][:, 0:2, :], in0=k3[:, :, 0:32], in1=ck, op=ALU.mult), reads=[Pk, cs_t], writes=[rt[0]])
        yield
        k.op("dve", lambda v: v.tensor_tensor(out=rt[1][:, 0:2, :], in0=k3[:, :, 32:64], in1=sk_, op=ALU.mult), reads=[Pk, cs_t], writes=[rt[1]])
        yield
        k.op("dve", lambda v: v.tensor_tensor(out=rt[2][:, 0:2, :], in0=k3[:, :, 32:64], in1=ck, op=ALU.mult), reads=[Pk, cs_t], writes=[rt[2]])
        yield
        k.op("dve", lambda v: v.tensor_tensor(out=rt[3][:, 0:2, :], in0=k3[:, :, 0:32], in1=sk_, op=ALU.mult), reads=[Pk, cs_t], writes=[rt[3]])
        yield
        k.op("pool", lambda g: g.tensor_tensor(out=kr3[:, :, 0:32], in0=rt[0][:, 0:2, :], in1=rt[1][:, 0:2, :], op=ALU.subtract), reads=[rt[0], rt[1]], writes=[kr])
        yield
        k.op("pool", lambda g: g.tensor_tensor(out=kr3[:, :, 32:64], in0=rt[2][:, 0:2, :], in1=rt[3][:, 0:2, :], op=ALU.add), reads=[rt[2], rt[3]], writes=[kr])
        vcur = vB[n % 2]; vprev = vB[(n + 1) % 2]
        yield
        k.op("dve", lambda v: v.tensor_copy(out=vcur[:], in_=Pk[:, 128:256]), reads=[Pk], writes=[vcur])
        if n == NT - 1:
            k.op("dve", lambda v: v.tensor_copy(out=vB32[:], in_=Pk[:, 128:256]), reads=[Pk], writes=[vB32])
        yield
        k.op("dve", lambda v: v.tensor_copy(out=bd[:], in_=Pk[:, 256:264]), reads=[Pk], writes=[bd])
        yield
        k.op("act", lambda a: a.activation(out=bdt[:, 0:4], in_=bd[:, 0:4], func=ACT.Exp, scale=-1.0), reads=[bd], writes=[bdt])
        yield
        k.op("dve", lambda v: v.tensor_scalar(out=bdt[:, 0:4], in0=bdt[:, 0:4], scalar1=1.0, scalar2=None, op0=ALU.add), reads=[bdt], writes=[bdt])
        yield
        k.op("dve", lambda v: v.reciprocal(out=beta[:], in_=bdt[:, 0:4]), reads=[bdt], writes=[beta])
        yield
        k.op("dve", lambda v: v.tensor_scalar(out=negbeta[:], in0=beta[:], scalar1=-1.0, scalar2=None, op0=ALU.mult), reads=[beta], writes=[negbeta])
        yield
        k.op("dve", lambda v: v.tensor_tensor(out=bd[:, 4:8], in0=bd[:, 4:8], in1=dtb_bc[:], op=ALU.add), reads=[bd, dtb_bc], writes=[bd])
        yield
        k.op("act", lambda a: a.activation(out=bdt[:, 4:8], in_=bd[:, 4:8], func=ACT.Exp), reads=[bd], writes=[bdt])
        yield
        k.op("act", lambda a: a.activation(out=bdt[:, 4:8], in_=bdt[:, 4:8], func=ACT.Ln, bias=1.0, scale=1.0), reads=[bdt], writes=[bdt])
        yield
        k.op("dve", lambda v: v.tensor_tensor(out=gg[:], in0=bdt[:, 4:8], in1=ea[:], op=ALU.mult), reads=[bdt, ea], writes=[gg])

        yield
        k.op("dve", lambda v: v.tensor_copy(out=gsp[:, 0:4], in_=gg[:]), reads=[gg], writes=[gsp])
        yield
        k.op("dve", lambda v: v.tensor_tensor(out=gtmp[:], in0=gg[:], in1=gsp[:, 0:4], op=ALU.subtract), reads=[gg, gsp], writes=[gtmp])
        yield
        k.op("dve", lambda v: v.tensor_copy(out=gsp[:, 4:8], in_=gtmp[:]), reads=[gtmp], writes=[gsp])
        Ub4 = bc(Ub[:].rearrange("p (a b) -> p a b", a=1), [128, 4, 128])
        yield
        k.op("pool", lambda g: g.tensor_tensor(out=gUh[:], in0=Ub4, in1=bc(gsp[:, 0:4].rearrange("p (a b) -> p a b", b=1), [128, 4, 128]), op=ALU.mult),
             reads=[Ub, gsp], writes=[gUh])
        yield
        k.op("pool", lambda g: g.tensor_tensor(out=gUl[:], in0=Ub4, in1=bc(gsp[:, 4:8].rearrange("p (a b) -> p a b", b=1), [128, 4, 128]), op=ALU.mult),
             reads=[Ub, gsp], writes=[gUl])
        PG, PK_, PQ = PB[3], PB[4], PB[5]
        yield
        mm(PB[1][:, 0:8], Ub[:], gsp[:], True, True, [Ub, gsp], [PB[1]])
        yield
        mm(PB[1][:, 8:16], onesb[:], gsp[:], True, True, [onesb, gsp], [PB[1]])
        yield
        mm(PG[:, :], onesb[:], gUh[:].rearrange("p a b -> p (a b)"), True, False, [onesb, gUh], [PG])
        yield
        mm(PG[:, :], onesb[:], gUl[:].rearrange("p a b -> p (a b)"), False, True, [onesb, gUl], [PG])
        yield
        k.op("dve", lambda v: v.tensor_copy(out=gtmp[:], in_=PB[1][:, 0:4]), reads=[PB[1]], writes=[gtmp])
        yield
        k.op("dve", lambda v: v.tensor_tensor(out=Gc[:], in0=gtmp[:], in1=PB[1][:, 4:8], op=ALU.add), reads=[gtmp, PB[1]], writes=[Gc])
        yield
        k.op("dve", lambda v: v.tensor_copy(out=gtmp[:], in_=PB[1][:, 8:12]), reads=[PB[1]], writes=[gtmp])
        yield
        k.op("dve", lambda v: v.tensor_tensor(out=Glb[:], in0=gtmp[:], in1=PB[1][:, 12:16], op=ALU.add), reads=[gtmp, PB[1]], writes=[Glb])
        yield
        k.op("dve", lambda v: v.tensor_scalar(out=negGc[:], in0=Gc[:], scalar1=-1.0, scalar2=None, op0=ALU.mult), reads=[Gc], writes=[negGc])
        yield
        k.op("dve", lambda v: v.tensor_tensor(out=dG[:], in0=Glb[:], in1=Gc[:], op=ALU.subtract), reads=[Glb, Gc], writes=[dG])
        yield
        k.op("act", lambda a: a.activation(out=eG[:], in_=Gc[:], func=ACT.Exp), reads=[Gc], writes=[eG])
        yield
        k.op("act", lambda a: a.activation(out=edG[:], in_=dG[:], func=ACT.Exp), reads=[dG], writes=[edG])
        yield
        k.op("act", lambda a: a.activation(out=egl[:], in_=Glb[:], func=ACT.Exp), reads=[Glb], writes=[egl])
        yield
        k.op("dve", lambda v: v.tensor_tensor(out=beG[:], in0=eG[:], in1=beta[:], op=ALU.mult), reads=[eG, beta], writes=[beG])
        PG3 = v3(PG[:, :], 4)
        yield
        k.op("dve", lambda v: v.scalar_tensor_tensor(out=tmp1[:], in0=PG3, scalar=-1.0, in1=bc(negA[:].rearrange("p (a b) -> p a b", a=1), [128, 4, 128]),
                                                     op0=ALU.mult, op1=ALU.add), reads=[PG, negA], writes=[tmp1])
        yield
        k.op("dve", lambda v: v.tensor_tensor(out=tmp2[:], in0=PG3, in1=bc(negB[:].rearrange("p (a b) -> p a b", a=1), [128, 4, 128]), op=ALU.add),
             reads=[PG, negB], writes=[tmp2])
        yield
        k.op("act", lambda a: a.activation(out=eGbc[:], in_=PG3, func=ACT.Exp), reads=[PG], writes=[eGbc])
        yield
        for h in range(4):
            k.op("act", lambda a, h=h: a.activation(out=E1[:, h, :], in_=tmp1[:, h, :], func=ACT.Exp, bias=Gc[:, h:h + 1], scale=1.0), reads=[tmp1, Gc], writes=[E1])
            k.op("act", lambda a, h=h: a.activation(out=E2[:, h, :], in_=tmp2[:, h, :], func=ACT.Exp, bias=negGc[:, h:h + 1], scale=1.0), reads=[tmp2, negGc], writes=[E2])
        yield
        for h in range(4):
            mm(PK_[:, h * 128:(h + 1) * 128], qkb[:, 4 + h, :], qkb[:, 4 + h, :], True, True, [qkb], [PK_])
        yield
        for h in range(4):
            mm(PQ[:, h * 128:(h + 1) * 128], qkb[:, 4 + h, :], qkb[:, h, :], True, True, [qkb], [PQ])
        yield
        for h in range(4):
            k.op("dve", lambda v, h=h: v.scalar_tensor_tensor(out=Y0f[:, h, :], in0=PK_[:, h * 128:(h + 1) * 128], scalar=negbeta[:, h:h + 1],
                                                               in1=E1[:, h, :], op0=ALU.mult, op1=ALU.mult), reads=[PK_, negbeta, E1], writes=[Y0f])
        yield
        k.op("dve", lambda v: v.tensor_tensor(out=AqkT[:], in0=v3(PQ[:, :], 4), in1=E2[:], op=ALU.mult), reads=[PQ, E2], writes=[AqkT])
        yield
        k.op("pool", lambda g: g.tensor_tensor(out=qdT[:], in0=qkvs[:, 0:4, :], in1=eGbc[:], op=ALU.mult), reads=[qkvs, eGbc], writes=[qdT])
        yield
        k.op("act", lambda a: a.copy(out=Yh[0][:], in_=Y0f[:]), reads=[Y0f], writes=[Yh[0]])
        yield
        k.op("pool", lambda g: g.tensor_tensor(out=Yl[0][:], in0=Y0f[:], in1=Yh[0][:], op=ALU.subtract), reads=[Y0f, Yh[0]], writes=[Yl[0]])
        P5b = pbf(PB[0])
        yield
        for h in range(4):
            tr(P5b[:, h * 128:(h + 1) * 128], Yh[0][:, h, :], identb[:], [Yh[0]], [PB[0]])
            tr(P5b[:, (4 + h) * 128:(5 + h) * 128], Yl[0][:, h, :], identb[:], [Yl[0]], [PB[0]])
        yield
        k.op("act", lambda a: a.copy(out=XRh[0][:, :, 0:128], in_=v3(P5b[:, 0:512], 4)), reads=[PB[0]], writes=[XRh[0]])
        yield
        k.op("dve", lambda v: v.tensor_copy(out=XRl[0][:, :, 0:128], in_=v3(P5b[:, 512:1024], 4)), reads=[PB[0]], writes=[XRl[0]])
        id4 = bc(ident[:].rearrange("p (a b) -> p a b", a=1), [128, 4, 128])
        yield
        k.op("pool", lambda g: g.tensor_copy(out=Rf[:], in_=id4), reads=[ident], writes=[Rf])
        yield
        k.op("pool", lambda g: g.tensor_copy(out=XRh[0][:, :, 128:256], in_=id4), reads=[ident], writes=[XRh[0]])
        yield
        k.op("pool", lambda g: g.memset(XRl[0][:, :, 128:256], 0.0), writes=[XRl[0]])
        P1b = pbf(PB[1])
        yield
        for h in range(4):
            tr(P1b[:, h * 128:(h + 1) * 128], qkb[:, 4 + h, :], identb[:], [qkb], [PB[1]])
        yield
        for h in range(4):
            tr(P1b[:, (4 + h) * 128:(5 + h) * 128], qkb[:, 8 + h, :], identb[:], [qkb], [PB[1]])
        yield
        for h in range(4):
            k.op("dve", lambda v, h=h: v.tensor_scalar(out=kbt[:, h, :], in0=P1b[:, h * 128:(h + 1) * 128], scalar1=beG[:, h:h + 1], scalar2=None, op0=ALU.mult),
                 reads=[PB[1], beG], writes=[kbt])
            k.op("act", lambda a, h=h: a.activation(out=kdt[:, h, :], in_=P1b[:, h * 128:(h + 1) * 128], func=ACT.Identity, scale=edG[:, h:h + 1]),
                 reads=[PB[1], edG], writes=[kdt])
            k.op("act", lambda a, h=h: a.activation(out=bvt[:, h, :], in_=P1b[:, (4 + h) * 128:(5 + h) * 128], func=ACT.Identity, scale=beta[:, h:h + 1]),
                 reads=[PB[1], beta], writes=[bvt])
        yield

    def G(n):
        yield
        for lev in range(7):
            cur = lev % 2; nxt = (lev + 1) % 2
            xh, xl, yh, yl = XRh[cur], XRl[cur], Yh[cur], Yl[cur]
            xhn, xln, yhn, yln = XRh[nxt], XRl[nxt], Yh[nxt], Yl[nxt]
            PA = [PB[5], PB[6]]
            PBy = PB[7]
            last = (lev == 6)
            for h in range(4):
                pa = PA[h // 2]
                if not last:
                    o_ = pa[:, (h % 2) * 256:(h % 2 + 1) * 256]
                    r_h = xh[:, h, :]; r_l = xl[:, h, :]
                else:
                    o_ = pa[:, (h % 2) * 256 + 128:(h % 2 + 1) * 256]
                    r_h = xh[:, h, 128:256]; r_l = xl[:, h, 128:256]
                mm(o_, yh[:, h, :], r_h, True, False, [yh, xh], [pa])
                mm(o_, yh[:, h, :], r_l, False, False, [yh, xl], [pa])
                mm(o_, yl[:, h, :], r_h, False, True, [yl, xh], [pa])
            yield
            if not last:
                for h in range(4):
                    o_ = PBy[:, h * 128:(h + 1) * 128]
                    mm(o_, xh[:, h, 0:128], yh[:, h, :], True, False, [xh, yh], [PBy])
                    mm(o_, xh[:, h, 0:128], yl[:, h, :], False, False, [xh, yl], [PBy])
                    mm(o_, xl[:, h, 0:128], yh[:, h, :], False, True, [xl, yh], [PBy])
            yield
            for half in range(2):
                k.op("dve", lambda v, half=half: v.tensor_tensor(out=Rf[:, 2 * half:2 * half + 2, :], in0=Rf[:, 2 * half:2 * half + 2, :],
                                                                  in1=v3(PA[half][:, :], 2)[:, :, 128:256], op=ALU.add), reads=[Rf, PA[half]], writes=[Rf])
            k.op("act", lambda a: a.copy(out=xhn[:, :, 128:256], in_=Rf[:]), reads=[Rf], writes=[xhn])
            if not last:
                k.op("pool", lambda g: g.tensor_tensor(out=xln[:, :, 128:256], in0=Rf[:], in1=xhn[:, :, 128:256], op=ALU.subtract), reads=[Rf, xhn], writes=[xln])
                for half in range(2):
                    k.op("act", lambda a, half=half: a.copy(out=xhn[:, 2 * half:2 * half + 2, 0:128], in_=v3(PA[half][:, :], 2)[:, :, 0:128]),
                         reads=[PA[half]], writes=[xhn])
                    k.op("dve", lambda v, half=half: v.tensor_tensor(out=xln[:, 2 * half:2 * half + 2, 0:128], in0=v3(PA[half][:, :], 2)[:, :, 0:128],
                                                                      in1=xhn[:, 2 * half:2 * half + 2, 0:128], op=ALU.subtract), reads=[PA[half], xhn], writes=[xln])
                k.op("act", lambda a: a.copy(out=yhn[:], in_=v3(PBy[:, :], 4)), reads=[PBy], writes=[yhn])
                k.op("dve", lambda v: v.tensor_tensor(out=yln[:], in0=v3(PBy[:, :], 4), in1=yhn[:], op=ALU.subtract), reads=[PBy, yhn], writes=[yln])
        Rh = XRh[1]
        PW, PU = PB[5], PB[6]
        yield
        for h in range(4):
            mm(PW[:, h * 128:(h + 1) * 128], kbt[:, h, :], Rh[:, h, 128:256], True, True, [kbt, Rh], [PW])
        yield
        for h in range(4):
            mm(PU[:, h * 128:(h + 1) * 128], Rh[:, h, 128:256], bvt[:, h, :], True, True, [Rh, bvt], [PU])
        yield
        k.op("act", lambda a: a.copy(out=wT[:], in_=v3(PW[:, :], 4)), reads=[PW], writes=[wT])
        yield
        k.op("dve", lambda v: v.tensor_copy(out=ut[:], in_=v3(PU[:, :], 4)), reads=[PU], writes=[ut])
        PWS = PB[7]
        yield
        for h in range(4):
            mm(PWS[:, h * 128:(h + 1) * 128], wT[:, h, :], Sb[:, h, :], True, True, [wT, Sb], [PWS])
        yield
        k.op("dve", lambda v: v.tensor_tensor(out=uub[:], in0=ut[:], in1=v3(PWS[:, :], 4), op=ALU.subtract), reads=[ut, PWS], writes=[uub])
        PO, PSn = PB[5], PB[6]
        yield
        for h in range(4):
            mm(PO[:, h * 128:(h + 1) * 128], qdT[:, h, :], Sb[:, h, :], True, False, [qdT, Sb], [PO])
            mm(PO[:, h * 128:(h + 1) * 128], AqkT[:, h, :], uub[:, h, :], False, True, [AqkT, uub], [PO])
        yield
        for h in range(4):
            mm(PSn[:, h * 128:(h + 1) * 128], kdt[:, h, :], uub[:, h, :], True, True, [kdt, uub], [PSn])
        yield
        for h in range(4):
            k.op("dve", lambda v, h=h: v.scalar_tensor_tensor(out=S[:, h, :], in0=S[:, h, :], scalar=egl[:, h:h + 1], in1=PSn[:, h * 128:(h + 1) * 128],
                                                               op0=ALU.mult, op1=ALU.add), reads=[S, egl, PSn], writes=[S])
        yield
        k.op("pool", lambda g: g.tensor_copy(out=Sb[:], in_=S[:]), reads=[S], writes=[Sb])
        yield
        for h in range(4):
            k.op("act", lambda a, h=h: a.activation(out=ojunk[:], in_=PO[:, h * 128:(h + 1) * 128], func=ACT.Square, accum_out=oss[:, h:h + 1]),
                 reads=[PO], writes=[ojunk, oss])
        yield
        k.op("act", lambda a: a.activation(out=orr[:], in_=oss[:], func=ACT.Ln, bias=RMS_EPS, scale=1.0 / 128.0), reads=[oss], writes=[orr])
        yield
        k.op("act", lambda a: a.activation(out=orr[:], in_=orr[:], func=ACT.Exp, scale=-0.5), reads=[orr], writes=[orr])
        yield
        k.op("pool", lambda g: g.tensor_tensor(out=nz[:], in0=v3(zAs[:], 4), in1=bc(norma_bc[:].rearrange("p (a b) -> p a b", a=1), [128, 4, 128]), op=ALU.mult),
             reads=[zAs, norma_bc], writes=[nz])
        yield
        for h in range(4):
            k.op("dve", lambda v, h=h: v.scalar_tensor_tensor(out=mixb[:, h * 128:(h + 1) * 128], in0=PO[:, h * 128:(h + 1) * 128], scalar=orr[:, h:h + 1],
                                                               in1=nz[:, h, :], op0=ALU.mult, op1=ALU.mult), reads=[PO, orr, nz], writes=[mixb])

        yield

    def W(n):
        vcur = vB[n % 2]; vprev = vB[(n + 1) % 2]
        kcur = kTb[n % 2]; kprev = kTb[(n + 1) % 2]
        P0b = pbf(PB[0])
        yield
        for c in range(4):
            tr(P0b[:, c * 128:(c + 1) * 128], qrb[:, c * 128:(c + 1) * 128], identb[:], [qrb], [PB[0]])
        yield
        k.op("act", lambda a: a.activation(out=qTb[:], in_=v3(P0b[:, 0:512], 4), func=ACT.Identity, scale=0.125), reads=[PB[0]], writes=[qTb])
        yield
        k.op("pool", lambda g: g.tensor_copy(out=krb[:], in_=kr[:]), reads=[kr], writes=[krb])
        yield
        tr(pbf(PB[0])[:, 512:640], krb[:], identb[:], [krb], [PB[0]])
        yield
        k.op("dve", lambda v: v.tensor_copy(out=kcur[:], in_=pbf(PB[0])[:, 512:640]), reads=[PB[0]], writes=[kcur])
        PSC = [PB[1], PB[2], PB[3], PB[4]]
        yield
        for c in range(4):
            for kv in range(2):
                pb = PSC[kv * 2 + c // 2]
                o = (c % 2) * 256
                mm(pb[:, o:o + 128], qTb[64 * kv:64 * kv + 64, c, :], kprev[64 * kv:64 * kv + 64, :], True, True, [qTb, kprev], [pb])
                mm(pb[:, o + 128:o + 256], qTb[64 * kv:64 * kv + 64, c, :], kcur[64 * kv:64 * kv + 64, :], True, True, [qTb, kcur], [pb])
        msk = swam0 if n == 0 else swam
        yield
        for j in range(4):
            k.op("dve", lambda v, j=j: v.tensor_tensor(out=SC[:, 2 * j:2 * j + 2, :], in0=v3(PSC[j][:, :], 2),
                                                       in1=bc(msk[:].rearrange("p (a b) -> p a b", a=1), [128, 2, 256]), op=ALU.add), reads=[PSC[j], msk], writes=[SC])
        yield
        k.op("dve", lambda v: v.tensor_reduce(out=mx[:], in_=SC[:], axis=AX.X, op=ALU.max), reads=[SC], writes=[mx])
        yield
        k.op("dve", lambda v: v.tensor_tensor(out=mx[:], in0=mx[:], in1=sinks_bc[:], op=ALU.max), reads=[mx, sinks_bc], writes=[mx])
        yield
        k.op("dve", lambda v: v.tensor_scalar(out=negm[:], in0=mx[:], scalar1=-1.0, scalar2=None, op0=ALU.mult), reads=[mx], writes=[negm])
        yield
        for s_ in range(8):
            k.op("act", lambda a, s_=s_: a.activation(out=Pb[:, s_, :], in_=SC[:, s_, :], func=ACT.Exp, bias=negm[:, s_:s_ + 1], scale=1.0,
                                                      accum_out=rs[:, s_:s_ + 1]), reads=[SC, negm], writes=[Pb, rs])
        yield
        k.op("dve", lambda v: v.tensor_tensor(out=es_[:], in0=sinks_bc[:], in1=mx[:], op=ALU.subtract), reads=[sinks_bc, mx], writes=[es_])
        yield
        k.op("act", lambda a: a.activation(out=es_[:], in_=es_[:], func=ACT.Exp), reads=[es_], writes=[es_])
        yield
        k.op("dve", lambda v: v.tensor_tensor(out=es_[:], in0=es_[:], in1=rs[:], op=ALU.add), reads=[es_, rs], writes=[es_])
        yield
        k.op("dve", lambda v: v.reciprocal(out=es_[:], in_=es_[:]), reads=[es_], writes=[es_])
        yield
        k.op("dve", lambda v: v.tensor_copy(out=rden[:].rearrange("p (c k) -> p k c", k=2), in_=es_[:].rearrange("p (k c) -> p k c", k=2)),
             reads=[es_], writes=[rden])
        PPT = [PB[1], PB[2]]
        yield
        for s_ in range(8):
            for half in range(2):
                i16 = s_ * 2 + half
                pb = PPT[i16 // 8]
                o = (i16 % 8) * 128
                tr(pbf(pb)[:, o:o + 128], Pb[:, s_, half * 128:(half + 1) * 128], identb[:], [Pb], [pb])
        yield
        k.op("act", lambda a: a.copy(out=PTb[:, 0:8, :], in_=v3(pbf(PPT[0]), 8)), reads=[PPT[0]], writes=[PTb])
        yield
        k.op("dve", lambda v: v.tensor_copy(out=PTb[:, 8:16, :], in_=v3(pbf(PPT[1]), 8)), reads=[PPT[1]], writes=[PTb])
        PV = PB[3]
        yield
        for c in range(4):
            for kv in range(2):
                slot = c * 2 + kv
                sp_ = kv * 4 + c
                mm(PV[:, slot * 64:(slot + 1) * 64], PTb[:, 2 * sp_, :], vprev[:, 64 * kv:64 * kv + 64], True, False, [PTb, vprev], [PV])
                mm(PV[:, slot * 64:(slot + 1) * 64], PTb[:, 2 * sp_ + 1, :], vcur[:, 64 * kv:64 * kv + 64], False, True, [PTb, vcur], [PV])
        yield
        k.op("dve", lambda v: v.tensor_tensor(out=v3(obt[:], 8), in0=v3(PV[:, :], 8), in1=bc(rden[:].rearrange("p (a b) -> p a b", b=1), [128, 8, 64]), op=ALU.mult),
             reads=[PV, rden], writes=[obt])
        yield
        k.op("pool", lambda g: g.tensor_tensor(out=mixb[:, 512:1024], in0=obt[:], in1=zBs[:], op=ALU.mult), reads=[obt, zBs], writes=[mixb])

        yield

    def E(n):
        X = Xt[n % 2]; pc = pre
        P3b = pbf(PB[6])
        yield
        for kc in range(8):
            tr(P3b[:, kc * 128:(kc + 1) * 128], mixb[:, kc * 128:(kc + 1) * 128], identb[:], [mixb], [PB[6]])
        yield
        k.op("act", lambda a: a.copy(out=mixT[:, 0:4, :], in_=v3(P3b[:, 0:512], 4)), reads=[PB[6]], writes=[mixT])
        yield
        k.op("dve", lambda v: v.tensor_copy(out=mixT[:, 4:8, :], in_=v3(P3b[:, 512:1024], 4)), reads=[PB[6]], writes=[mixT])
        yield
        for j in range(2):
            for kc in range(8):
                mm(PB[6 + j][:, :], mixT[:, kc, :], Wo[:, kc, j * 512:(j + 1) * 512], kc == 0, kc == 7, [mixT, Wo], [PB[6 + j]])
        yield
        for j in range(2):
            k.op("dve", lambda v, j=j: v.tensor_tensor(out=ypre[:, j * 512:(j + 1) * 512], in0=PB[6 + j][:, :], in1=g1bc[:, j * 512:(j + 1) * 512], op=ALU.mult),
                 reads=[PB[6 + j], g1bc], writes=[ypre])
        yield
        k.op("dve", lambda g: g.scalar_tensor_tensor(out=ypre[:], in0=X[:], scalar=ALPHA, in1=ypre[:], op0=ALU.mult, op1=ALU.add), reads=[X, ypre], writes=[ypre])
        Yo = Yt[0]
        yjunk = Yo
        yield
        k.op("act", lambda a: a.activation(out=yjunk[:], in_=ypre[:], func=ACT.Identity, accum_out=st[:, 0:1]), reads=[ypre], writes=[yjunk, st])
        yield
        k.op("act", lambda a: a.activation(out=yjunk[:], in_=ypre[:], func=ACT.Square, accum_out=st[:, 1:2]), reads=[ypre], writes=[yjunk, st])
        yield
        k.op("dve", lambda v: v.tensor_scalar(out=st[:, 2:3], in0=st[:, 0:1], scalar1=1.0 / D, scalar2=None, op0=ALU.mult), reads=[st], writes=[st])
        yield
        k.op("dve", lambda v: v.tensor_tensor(out=st[:, 3:4], in0=st[:, 2:3], in1=st[:, 2:3], op=ALU.mult), reads=[st], writes=[st])
        yield
        k.op("dve", lambda v: v.scalar_tensor_tensor(out=st[:, 4:5], in0=st[:, 1:2], scalar=1.0 / D, in1=st[:, 3:4], op0=ALU.mult, op1=ALU.subtract), reads=[st], writes=[st])
        yield
        k.op("act", lambda a: a.activation(out=st[:, 5:6], in_=st[:, 4:5], func=ACT.Ln, bias=LN_EPS, scale=1.0), reads=[st], writes=[st])
        yield
        k.op("act", lambda a: a.activation(out=st[:, 5:6], in_=st[:, 5:6], func=ACT.Exp, scale=-0.5), reads=[st], writes=[st])
        yield
        k.op("dve", lambda v: v.scalar_tensor_tensor(out=st[:, 6:7], in0=st[:, 2:3], scalar=-1.0, in1=st[:, 5:6], op0=ALU.mult, op1=ALU.mult), reads=[st], writes=[st])
        yield
        k.op("act", lambda a: a.activation(out=yjunk[:], in_=ypre[:], func=ACT.Identity, bias=st[:, 6:7], scale=st[:, 5:6]), reads=[ypre, st], writes=[yjunk])
        yield
        k.op("pool", lambda g: g.tensor_tensor(out=yjunk[:], in0=yjunk[:], in1=lng_bc[:], op=ALU.mult), reads=[yjunk, lng_bc], writes=[yjunk])
        yield
        k.op("dve", lambda v: v.tensor_tensor(out=Yo[:], in0=yjunk[:], in1=lnb_bc[:], op=ALU.add), reads=[yjunk, lnb_bc], writes=[Yo])
        yield
        k.dma("sp", y_d[n * 128:(n + 1) * 128, :], Yo[:], reads=[Yo], semkey="st_" + Yo.name)

        if n == NT - 1:
            k.barrier()
            k.pe_inorder = False
            cpo_in = rt[0][:].rearrange("p a b -> p (a b)")[:, 0:36].rearrange("p (a b) -> p a b", a=3)
            cpo = obt[0:12, 0:384].rearrange("p (a b) -> p a b", a=3)
            k.op("dve", lambda v: v.tensor_copy(out=cpo_in[:], in_=pc[:, :, 128:131].rearrange("p c r -> p r c")), reads=[pc], writes=[cpo_in])
            for r_ in range(3):
                tr(PB[6][0:12, r_ * 128:(r_ + 1) * 128], cpo_in[:, r_, :], ident[:], [cpo_in], [PB[6]])
            k.op("dve", lambda v: v.tensor_copy(out=cpo[:].rearrange("p a b -> p (a b)"), in_=PB[6][0:12, 0:384]), reads=[PB[6]], writes=[cpo])
            k.dma("sp", convp_d.rearrange("r (c p) -> c r p", p=128), cpo[:], reads=[cpo], semkey="st_misc")
            k.dma("sp", deltap_d.rearrange("h k v -> k h v"), S[:], reads=[S], semkey="st_misc")
            k.dma("sp", swak_d[:, :], kr[:], reads=[kr], semkey="st_misc")
            k.dma("sp", swav_d[:, :], vB32[:], reads=[vB32], semkey="st_misc")

        yield

    def merge(*gens):
        gens = [g_ for g_ in gens if g_ is not None]
        while gens:
            for g_ in list(gens):
                try:
                    next(g_)
                except StopIteration:
                    gens.remove(g_)

    merge(P1(0))
    for n in range(NT):
        merge(G(n), W(n))
        merge(E(n), P1(n + 1) if n + 1 < NT else None)

    k.finish("sp")
    if k.dry:
        return k.need_out
    nc._kb_nops = k.nops
    nc._kb_sig = dict(k.sig)
    nc._kb_cnt = {e: k.cnt[e] for e in k.eng}
    return nc


def rope_tables(pos):
    half = 32
    inv = (1.0 / (10000.0 ** (np.arange(half, dtype=np.float32) / np.float32(half)))).astype(np.float32)
    ang = pos.astype(np.float32)[:, None] * inv[None, :]
    return np.cos(ang).astype(np.float32), np.sin(ang).astype(np.float32)


def prep_shared(inputs):
    w_in = np.asarray(inputs["w_in"][0], np.float32)
    perm_cols = np.concatenate([np.arange(h * 64, (h + 1) * 64) for h in PERM])
    w_f = np.ascontiguousarray(w_in[:, 0:1536])
    zA = w_in[:, 1536:2048]
    beta = w_in[:, 2048:2052]
    dec = w_in[:, 2052:2056]
    qB = w_in[:, 2056:2568][:, perm_cols]
    kB = w_in[:, 2568:2696]
    vB = w_in[:, 2696:2824]
    zB = w_in[:, 2824:3336][:, perm_cols]
    w_t = np.ascontiguousarray(np.concatenate([zA, qB, zB, kB, vB, beta, dec], axis=1))
    w_out = np.asarray(inputs["w_out"][0], np.float32)
    w_o = np.ascontiguousarray(np.concatenate([w_out[0:512], w_out[512:1024][perm_cols]], axis=0))
    sh = {
        "w_ada": np.ascontiguousarray(inputs["w_ada"][0], np.float32),
        "b_ada": np.ascontiguousarray(inputs["b_ada"], np.float32).reshape(1, -1),
        "w_f": w_f, "w_t": w_t, "w_o": w_o,
        "conv_w": np.ascontiguousarray(inputs["conv_w"][0], np.float32),
        "a_log": np.ascontiguousarray(inputs["a_log"], np.float32).reshape(1, 4),
        "dt_bias": np.ascontiguousarray(inputs["dt_bias"], np.float32).reshape(1, 4),
        "norm_a": np.ascontiguousarray(inputs["norm_a"], np.float32).reshape(1, 128),
        "sinks_p": np.ascontiguousarray(np.asarray(inputs["sinks"], np.float32).reshape(8)).reshape(1, 8),
        "ln_g": np.ascontiguousarray(inputs["ln_g"], np.float32).reshape(1, D),
        "ln_b": np.ascontiguousarray(inputs["ln_b"], np.float32).reshape(1, D),
    }
    return sh


def prep_core(inputs, sh, core, ntiles=NTILES, do_sample=True):
    b = core // 4
    T = ntiles * 128
    m = dict(sh)
    m["x"] = np.ascontiguousarray(inputs["x_prompt"][b, :T], np.float32)
    m["c"] = np.ascontiguousarray(inputs["c_prompt"][b], np.float32).reshape(1, D)
    cos, sin = rope_tables(np.arange(T))
    m["cosk"] = cos
    m["sink"] = sin
    if do_sample:
        sl = slice(core * NS, (core + 1) * NS)
        m["xs"] = np.ascontiguousarray(inputs["x_sample"][sl, 0], np.float32)
        m["cs"] = np.ascontiguousarray(inputs["c_sample"][sl], np.float32)
        m["s_conv"] = np.ascontiguousarray(inputs["state_conv"][0, sl], np.float32)
        m["s_delta"] = np.ascontiguousarray(inputs["state_delta"][0, sl], np.float32)
        m["s_k"] = np.ascontiguousarray(inputs["cache_swa_k"][0, sl], np.float32).reshape(NS, 128, 128)
        m["s_v"] = np.ascontiguousarray(inputs["cache_swa_v"][0, sl], np.float32).reshape(NS, 128, 128)
        cs_, ss_ = rope_tables(np.array([8192]))
        m["sinks_col"] = np.ascontiguousarray(np.tile(np.asarray(inputs["sinks"], np.float32).reshape(8), NS).reshape(128, 1))
        m["cos_s"] = cs_.reshape(1, 32)
        m["sin_s"] = ss_.reshape(1, 32)
    return m


_NC_CACHE = {}


DO_SAMPLE = True


def kernel(**inputs):
    if "nc" not in _NC_CACHE:
        _NC_CACHE["nc"] = build_program(NTILES, DO_SAMPLE)
    nc = _NC_CACHE["nc"]
    sh = prep_shared(inputs)
    in_maps = [prep_core(inputs, sh, c, NTILES, DO_SAMPLE) for c in range(8)]
    res = run_bass_kernel_spmd(nc, in_maps, core_ids=list(range(8))).results
    yp = np.stack([res[0]["y"], res[4]["y"]], 0).astype(np.float32)
    conv_p = np.stack([res[0]["conv_p"], res[4]["conv_p"]], 0)[None].astype(np.float32)
    delta_p = np.stack([res[0]["delta_p"], res[4]["delta_p"]], 0)[None].astype(np.float32)
    swa_k_p = np.stack([res[0]["swa_k_p"], res[4]["swa_k_p"]], 0).reshape(1, 2, 128, 2, 64).astype(np.float32)
    swa_v_p = np.stack([res[0]["swa_v_p"], res[4]["swa_v_p"]], 0).reshape(1, 2, 128, 2, 64).astype(np.float32)
    if DO_SAMPLE:
        ys = np.concatenate([r["ys"] for r in res], 0).reshape(128, 1, D).astype(np.float32)
        conv_s = np.concatenate([r["conv_s"] for r in res], 0)[None].astype(np.float32)
        delta_s = np.concatenate([r["delta_s"] for r in res], 0)[None].astype(np.float32)
        swa_k_s = np.concatenate([r["swa_k_s"] for r in res], 0).reshape(1, 128, 128, 2, 64).astype(np.float32)
        swa_v_s = np.concatenate([r["swa_v_s"] for r in res], 0).reshape(1, 128, 128, 2, 64).astype(np.float32)
    else:
        ys = np.zeros((128, 1, D), np.float32)
        conv_s = np.zeros((1, 128, 3, 1536), np.float32)
        delta_s = np.zeros((1, 128, 4, 128, 128), np.float32)
        swa_k_s = np.zeros((1, 128, 128, 2, 64), np.float32)
        swa_v_s = np.zeros((1, 128, 128, 2, 64), np.float32)
    return (yp, ys, conv_p, delta_p, swa_k_p, swa_v_p, conv_s, delta_s, swa_k_s, swa_v_s)
```

```python
import contextlib
import numpy as np
import concourse.bass as bass
import concourse.mybir as mybir
from concourse.bass_utils import run_bass_kernel_spmd

F32 = mybir.dt.float32
BF16 = mybir.dt.bfloat16
ACT = mybir.ActivationFunctionType
ALU = mybir.AluOpType
AX = mybir.AxisListType

D = 1024
NTILES = 64
NS = 16
ALPHA = 2.0 ** 0.25
NEG = -1.0e30
LN_EPS = 1e-5
RMS_EPS = 1e-6
L2_EPS = 1e-6
WT_COLS = 1800
PERM = [0, 4, 1, 5, 2, 6, 3, 7]


class KB:
    def __init__(self, nc):
        self.nc = nc
        self.es = contextlib.ExitStack()
        self.eng = {"pe": nc.tensor, "dve": nc.vector, "act": nc.scalar, "pool": nc.gpsimd, "sp": nc.sync}
        self.sem = {}
        self.cnt = {}
        for e in self.eng:
            self.sem[e] = self.es.enter_context(nc.semaphore("sem_" + e))
            self.cnt[e] = 0
        self.waited = {}
        self.last_write = {}
        self.readers = {}
        self.ntensors = 0
        self.limit = None
        self.nops = 0
        self.pe_inorder = False
        self.t_eng = {e: 0.0 for e in self.eng}
        self.t_fin = {}
        self.stream = None
        self.t_stream = {}
        self.dry = False
        self.needed = None
        self.need_out = set()
        self.sig = {e: 0 for e in self.eng}
        self.sigval = {}

    def sb(self, name, shape, dt=F32):
        return self.es.enter_context(self.nc.sbuf_tensor(name, list(shape), dt))

    def ps(self, name, shape=(128, 512), dt=F32):
        return self.es.enter_context(self.nc.psum_tensor(name, list(shape), dt))

    def _deps(self, reads, writes, nowaw=False):
        deps = set()
        for t in reads:
            if t in self.last_write:
                deps.add(self.last_write[t])
        for t in writes:
            if t in self.last_write and not nowaw:
                deps.add(self.last_write[t])
            for r in self.readers.get(t, ()):
                deps.add(r)
        return deps

    def _semval(self, src, val):
        if src in self.eng and self.needed is not None:
            return self.sigval[(src, val)]
        return val

    def _wait(self, e, deps):
        for (src, val) in sorted(deps, key=lambda x: str(x)):
            if e == "pe" and src == "pe" and self.pe_inorder:
                continue
            if self.waited.get((e, src), 0) < val:
                if self.dry:
                    self.need_out.add((src, val))
                else:
                    self.eng[e].wait_ge(self.sem[src], self._semval(src, val))
                self.waited[(e, src)] = val

    def _record(self, key, reads, writes):
        for t in writes:
            self.last_write[t] = key
            self.readers[t] = set()
        for t in reads:
            if t not in writes:
                self.readers.setdefault(t, set()).add(key)

    COST = {"pe": 0.2, "dve": 0.45, "act": 0.5, "pool": 1.0, "sp": 0.1}

    def _model(self, e, key, deps, cost):
        t0 = self.t_eng.get(e, 0.0)
        for d in deps:
            t0 = max(t0, self.t_fin.get(d, 0.0) + (0.0 if d[0] == e else 0.15))
        t1 = t0 + cost
        self.t_eng[e] = t1
        self.t_fin[key] = t1
        if self.stream is not None:
            self.t_stream[self.stream] = max(self.t_stream.get(self.stream, 0.0), t1)

    def op(self, e, fn, reads=(), writes=(), cost=None):
        self.nops += 1
        if self.limit is not None and self.nops > self.limit:
            return
        reads = [r.name if hasattr(r, "name") else r for r in reads]
        writes = [w.name if hasattr(w, "name") else w for w in writes]
        writes = list(writes) + [r for r in reads if r.startswith("pb") and r not in writes]
        deps = self._deps(reads, writes)
        self._wait(e, deps)
        self.cnt[e] += 1
        self._model(e, (e, self.cnt[e]), deps, self.COST[e] if cost is None else cost)
        if not self.dry:
            inst = fn(self.eng[e])
            if self.needed is None:
                inst.then_inc(self.sem[e], 1)
            elif (e, self.cnt[e]) in self.needed:
                self.sig[e] += 1
                self.sigval[(e, self.cnt[e])] = self.sig[e]
                inst.then_inc(self.sem[e], 1)
        self._record((e, self.cnt[e]), reads, writes)

    def dma(self, e, out, in_, reads=(), writes=(), semkey=None, nowaw=False, **kw):
        reads = [r.name if hasattr(r, "name") else r for r in reads]
        writes = [w.name if hasattr(w, "name") else w for w in writes]
        self.nops += 1
        if self.limit is not None and self.nops > self.limit:
            return
        if semkey not in self.sem:
            self.sem[semkey] = self.es.enter_context(self.nc.semaphore("semd_" + str(semkey)))
            self.cnt[semkey] = 0
        deps = self._deps(reads, writes, nowaw)
        self._wait(e, deps)
        self.cnt[semkey] += 16
        self._model(e, (semkey, self.cnt[semkey]), deps, 2.0)
        if not self.dry:
            inst = self.eng[e].dma_start(out=out, in_=in_, **kw)
            inst.then_inc(self.sem[semkey], 16)
        self._record((semkey, self.cnt[semkey]), reads, writes)

    def barrier(self):
        for e in self.eng:
            for src, val in self.cnt.items():
                if val > 0 and self.waited.get((e, src), 0) < val:
                    if self.dry:
                        self.need_out.add((src, val))
                    else:
                        self.eng[e].wait_ge(self.sem[src], self._semval(src, val))
                    self.waited[(e, src)] = val
        self.last_write = {}
        self.readers = {}

    def finish(self, e="sp"):
        for src, val in self.cnt.items():
            if val > 0 and self.waited.get((e, src), 0) < val:
                if self.dry:
                    self.need_out.add((src, val))
                else:
                    self.eng[e].wait_ge(self.sem[src], self._semval(src, val))
                self.waited[(e, src)] = val
        self.es.close()


def bc(ap, shape):
    return ap.to_broadcast(list(shape))


def build_program(ntiles=NTILES, do_sample=True, limit=None):
    plan = _build(ntiles, do_sample, limit, None)
    return _build(ntiles, do_sample, limit, plan)


def _build(ntiles, do_sample, limit, plan):
    nc = bass.Bass("TRN2", target_bir_lowering=False)
    k = KB(nc)
    k.limit = limit
    if plan is None:
        k.dry = True
    else:
        k.needed = plan
    NT = ntiles
    T = NT * 128

    def din(name, shape):
        return nc.dram_tensor(name, list(shape), F32, kind="ExternalInput").ap()

    def dout(name, shape):
        return nc.dram_tensor(name, list(shape), F32, kind="ExternalOutput").ap()

    x_d = din("x", [T, D])
    c_d = din("c", [1, D])
    wada_d = din("w_ada", [D, 3 * D])
    bada_d = din("b_ada", [1, 3 * D])
    wf_d = din("w_f", [D, 1536])
    wt_d = din("w_t", [D, WT_COLS])
    wo_d = din("w_o", [D, D])
    convw_d = din("conv_w", [4, 1536])
    alog_d = din("a_log", [1, 4])
    dtb_d = din("dt_bias", [1, 4])
    norma_d = din("norm_a", [1, 128])
    sinks_d = din("sinks_p", [1, 8])
    lng_d = din("ln_g", [1, D])
    lnb_d = din("ln_b", [1, D])
    cosk_d = din("cosk", [T, 32])
    sink_d = din("sink", [T, 32])

    y_d = dout("y", [T, D])
    convp_d = dout("conv_p", [3, 1536])
    deltap_d = dout("delta_p", [4, 128, 128])
    swak_d = dout("swa_k_p", [128, 128])
    swav_d = dout("swa_v_p", [128, 128])

    if do_sample:
        xs_d = din("xs", [NS, D])
        cs_d = din("cs", [NS, D])
        sconv_d = din("s_conv", [NS, 3, 1536])
        sdelta_d = din("s_delta", [NS, 4, 128, 128])
        sk_d = din("s_k", [NS, 128, 128])
        sv_d = din("s_v", [NS, 128, 128])
        coss_d = din("cos_s", [1, 32])
        sins_d = din("sin_s", [1, 32])
        ys_d = dout("ys", [NS, D])
        convs_d = dout("conv_s", [NS, 3, 1536])
        deltas_d = dout("delta_s", [NS, 4, 128, 128])
        swaks_d = dout("swa_k_s", [NS, 128, 128])
        swavs_d = dout("swa_v_s", [NS, 128, 128])

    ident = k.sb("ident", [128, 128])
    U = k.sb("U", [128, 128])
    ones = k.sb("ones", [128, 128])
    onesb = k.sb("onesb", [128, 128], BF16)
    negA = k.sb("negA", [128, 128])
    negB = k.sb("negB", [128, 128])
    swam = k.sb("swam", [128, 256])
    swam0 = k.sb("swam0", [128, 256])

    k.op("pool", lambda g: g.memset(ones[:], 1.0), writes=[ones])
    k.op("pool", lambda g: g.memset(onesb[:], 1.0), writes=[onesb])
    k.op("pool", lambda g: g.affine_select(out=ident[:], in_=ones[:], pattern=[[-1, 128]], compare_op=ALU.is_equal,
                                            fill=0.0, base=0, channel_multiplier=1), reads=[ones], writes=[ident])
    k.op("pool", lambda g: g.affine_select(out=U[:], in_=ones[:], pattern=[[1, 128]], compare_op=ALU.is_ge,
                                            fill=0.0, base=0, channel_multiplier=-1), reads=[ones], writes=[U])
    zer = k.sb("zer", [128, 256])
    k.op("pool", lambda g: g.memset(zer[:], 0.0), writes=[zer])
    k.op("pool", lambda g: g.affine_select(out=negA[:], in_=zer[:, 0:128], pattern=[[-1, 128]], compare_op=ALU.is_ge,
                                            fill=NEG, base=-1, channel_multiplier=1), reads=[zer], writes=[negA])
    k.op("pool", lambda g: g.affine_select(out=negB[:], in_=zer[:, 0:128], pattern=[[1, 128]], compare_op=ALU.is_ge,
                                            fill=NEG, base=0, channel_multiplier=-1), reads=[zer], writes=[negB])
    swamt = k.sb("swamt", [128, 256])
    k.op("pool", lambda g: g.affine_select(out=swamt[:], in_=zer[:], pattern=[[1, 256]], compare_op=ALU.is_ge,
                                            fill=NEG, base=0, channel_multiplier=-1), reads=[zer], writes=[swamt])
    k.op("pool", lambda g: g.affine_select(out=swam[:], in_=swamt[:], pattern=[[-1, 256]], compare_op=ALU.is_ge,
                                            fill=NEG, base=128, channel_multiplier=1), reads=[swamt], writes=[swam])
    k.op("pool", lambda g: g.memset(swam0[:, 0:128], NEG), writes=[swam0])
    k.op("pool", lambda g: g.tensor_copy(out=swam0[:, 128:256], in_=swam[:, 128:256]), reads=[swam], writes=[swam0])

    PB = [k.ps("pb%d" % i) for i in range(8)]
    def load_bc(name, src, n, parts=128):
        t = k.sb(name, [parts, n])
        k.dma("sp", t[:], src.partition_broadcast(parts), writes=[t], semkey="ld_" + name)
        return t

    lng_bc = load_bc("lng_bc", lng_d[0], D)
    lnb_bc = load_bc("lnb_bc", lnb_d[0], D)
    norma_bc = load_bc("norma_bc", norma_d[0], 128)
    sinks_bc = load_bc("sinks_bc", sinks_d[0], 8)
    alog_bc = load_bc("alog_bc", alog_d[0], 4)
    dtb_bc = load_bc("dtb_bc", dtb_d[0], 4)
    cwT = k.sb("cwT", [128, 48])
    bada_fm = k.sb("bada_fm", [128, 24])
    cT = k.sb("cT", [128, 8])
    rowst = k.sb("rowst", [80, 128])
    k.dma("sp", rowst[0:48, :], convw_d.rearrange("j (c p) -> (j c) p", p=128), writes=[rowst], semkey="ld_rowst", nowaw=True)
    k.dma("sp", rowst[48:72, :], bada_d[0].rearrange("(j p) -> j p", p=128), writes=[rowst], semkey="ld_rowst", nowaw=True)
    k.dma("sp", rowst[72:80, :], c_d[0].rearrange("(j p) -> j p", p=128), writes=[rowst], semkey="ld_rowst", nowaw=True)
    cst = [k.sb("cst%d" % i, [128, 2, 32]) for i in range(2)]
    k.op("pe", lambda p: p.transpose(out=PB[0][:, 0:80], in_=rowst[:, :], identity=ident[0:80, 0:80]), reads=[rowst, ident], writes=[PB[0]])
    k.op("dve", lambda v: v.tensor_copy(out=cwT[:], in_=PB[0][:, 0:48]), reads=[PB[0]], writes=[cwT])
    k.op("dve", lambda v: v.tensor_copy(out=bada_fm[:], in_=PB[0][:, 48:72]), reads=[PB[0]], writes=[bada_fm])
    k.op("dve", lambda v: v.tensor_copy(out=cT[:], in_=PB[0][:, 72:80]), reads=[PB[0]], writes=[cT])
    ea = k.sb("ea", [128, 4])
    k.op("act", lambda a: a.activation(out=ea[:], in_=alog_bc[:], func=ACT.Exp), reads=[alog_bc], writes=[ea])
    k.op("dve", lambda v: v.tensor_scalar(out=ea[:], in0=ea[:], scalar1=-1.0, scalar2=None, op0=ALU.mult),
         reads=[ea], writes=[ea])

    Wf = k.sb("Wf", [128, 8, 1536], BF16)
    Wt = k.sb("Wt", [128, 8, WT_COLS], BF16)
    Wo = k.sb("Wo", [128, 8, D], BF16)
    mod_fm = k.sb("mod_fm", [128, 16])
    g1bc = k.sb("g1bc", [128, D])
    k1s = contextlib.ExitStack()
    if do_sample:
        csT = k1s.enter_context(nc.sbuf_tensor("csT", [128, 8, NS], F32))
        mod_s = k1s.enter_context(nc.sbuf_tensor("mod_s", [NS, 3 * D], F32))
    k2 = contextlib.ExitStack()
    def sb2(name, shape, dt=F32):
        return k2.enter_context(nc.sbuf_tensor(name, list(shape), dt))
    stg = [sb2("stg%d" % i, [128, 1800]) for i in range(2)]
    bada_bc = sb2("bada_bc", [128, D])
    k.dma("sp", bada_bc[:], bada_d[0, 2 * D:3 * D].partition_broadcast(128), writes=[bada_bc], semkey="ld_bada_bc")
    si = 0
    cast_engs = ["dve", "pool", "act"]

    def load_cast(dst, src_d, ncols):
        nonlocal si
        per = 2048 // ncols if ncols <= 2048 else 0
        for kc in range(8):
            s = stg[si % 2]
            k.dma("sp", s[:, 0:ncols], src_d[kc * 128:(kc + 1) * 128, :], writes=[s], semkey="ld_" + s.name)
            e = cast_engs[si % 3]
            if e == "act":
                k.op(e, lambda a, s=s, kc=kc: a.copy(out=dst[:, kc, :], in_=s[:, 0:ncols]), reads=[s], writes=[dst])
            else:
                k.op(e, lambda v, s=s, kc=kc: v.tensor_copy(out=dst[:, kc, :], in_=s[:, 0:ncols]), reads=[s], writes=[dst])
            si += 1

    load_cast(Wf, wf_d, 1536)
    load_cast(Wt, wt_d, WT_COLS)
    load_cast(Wo, wo_d, D)


    wa = [sb2("wa%d" % i, [128, 8, 512]) for i in range(1)]
    c_bcT = sb2("c_bcT", [128, 8, 128])
    k.op("pool", lambda g: g.tensor_copy(out=c_bcT[:], in_=bc(cT[:].rearrange("p (a b) -> p a b", b=1), [128, 8, 128])),
         reads=[cT], writes=[c_bcT])
    if do_sample:
        cs_sb = sb2("cs_sb", [NS, D])
        k.dma("sp", cs_sb[:], cs_d[:, :], writes=[cs_sb], semkey="ld_cs")
        for kc in range(8):
            k.op("pe", lambda p, kc=kc: p.transpose(out=PB[0][:, kc * NS:(kc + 1) * NS], in_=cs_sb[:, kc * 128:(kc + 1) * 128],
                                                    identity=ident[0:NS, 0:NS]), reads=[cs_sb, ident], writes=[PB[0]])
        k.op("dve", lambda v: v.tensor_copy(out=csT[:].rearrange("p a b -> p (a b)"), in_=PB[0][:, 0:8 * NS]),
             reads=[PB[0]], writes=[csT])
        bada_s = sb2("bada_s", [NS, 3 * D])
        k.dma("sp", bada_s[:], bada_d[0].partition_broadcast(NS), writes=[bada_s], semkey="ld_bada_s")
    for j in range(6):
        w = wa[0]
        for kc in range(8):
            k.dma("sp", w[:, kc, :], wada_d[kc * 128:(kc + 1) * 128, j * 512:(j + 1) * 512], writes=[w],
                  semkey="ld_" + w.name, nowaw=True)
        if j < 4:
            for sub in range(4):
                col = j * 4 + sub
                for kc in range(8):
                    k.op("pe", lambda p, kc=kc, sub=sub, col=col, w=w: p.matmul(
                        PB[1][:, col:col + 1], lhsT=w[:, kc, sub * 128:(sub + 1) * 128], rhs=cT[:, kc:kc + 1],
                        start=(kc == 0), stop=(kc == 7)), reads=[w, cT], writes=[PB[1]])
        else:
            for kc in range(8):
                k.op("pe", lambda p, kc=kc, w=w: p.matmul(PB[2 + (j - 4)][:, :], lhsT=c_bcT[:, kc, :], rhs=w[:, kc, :],
                                                          start=(kc == 0), stop=(kc == 7)),
                     reads=[w, c_bcT], writes=[PB[2 + (j - 4)]])
        if do_sample:
            for kc in range(8):
                k.op("pe", lambda p, kc=kc, w=w: p.matmul(PB[4 + j % 2][0:NS, :], lhsT=csT[:, kc, :], rhs=w[:, kc, :],
                                                          start=(kc == 0), stop=(kc == 7)),
                     reads=[w, csT], writes=[PB[4 + j % 2]])
            k.op("dve", lambda v, j=j: v.tensor_tensor(out=mod_s[:, j * 512:(j + 1) * 512], in0=PB[4 + j % 2][0:NS, :],
                                                       in1=bada_s[:, j * 512:(j + 1) * 512], op=ALU.add),
                 reads=[PB[4 + j % 2], bada_s], writes=[mod_s])
    k.op("dve", lambda v: v.tensor_tensor(out=mod_fm[:], in0=PB[1][:, 0:16], in1=bada_fm[:, 0:16], op=ALU.add),
         reads=[PB[1], bada_fm], writes=[mod_fm])
    k.op("dve", lambda v: v.tensor_scalar(out=mod_fm[:, 8:16], in0=mod_fm[:, 8:16], scalar1=1.0, scalar2=None, op0=ALU.add),
         reads=[mod_fm], writes=[mod_fm])
    for j in range(2):
        k.op("dve", lambda v, j=j: v.scalar_tensor_tensor(out=g1bc[:, j * 512:(j + 1) * 512], in0=PB[2 + j][:, :], scalar=1.0,
                                                           in1=bada_bc[:, j * 512:(j + 1) * 512], op0=ALU.add, op1=ALU.add),
             reads=[PB[2 + j], bada_bc], writes=[g1bc])


    if do_sample:
        k.barrier()
        k2.close()
        k2 = contextlib.ExitStack()
        P16 = NS
        sinkcol_d = din("sinks_col", [128, 1])
        xs = sb2("xs_sb", [P16, D]); hs = sb2("hs", [P16, D]); mix_s = sb2("mix_s", [P16, D])
        hsT = sb2("hsT", [128, 8, P16], BF16)
        pqkv = sb2("pqkv", [P16, 1536])
        qkv_s = sb2("qkv_s", [P16, 12, 128])
        zAs_s = sb2("zAs_s", [P16, 512]); zBs_s = sb2("zBs_s", [P16, 512])
        qr_s = sb2("qr_s", [P16, 512]); kr_s = sb2("kr_s", [P16, 128]); v_s = sb2("v_s", [P16, 128])
        vsb = sb2("vsb", [P16, 128], BF16)
        bd_s = sb2("bd_s", [P16, 8]); bdt_s = sb2("bdt_s", [P16, 8])
        beta_s = sb2("beta_s", [P16, 4]); nbeta_s = sb2("nbeta_s", [P16, 4]); g_s = sb2("g_s", [P16, 4]); eg_s = sb2("eg_s", [P16, 4])
        cs16 = sb2("cs16", [P16, 2, 32])
        k.dma("sp", cs16[:, 0, :], coss_d[0].partition_broadcast(P16), writes=[cs16], semkey="ld_cs16", nowaw=True)
        k.dma("sp", cs16[:, 1, :], sins_d[0].partition_broadcast(P16), writes=[cs16], semkey="ld_cs16", nowaw=True)
        k.dma("sp", xs[:], xs_d[:, :], writes=[xs], semkey="ld_xs")
        k.op("dve", lambda v: v.scalar_tensor_tensor(out=hs[:], in0=mod_s[:, D:2 * D], scalar=1.0, in1=xs[:], op0=ALU.add, op1=ALU.mult),
             reads=[mod_s, xs], writes=[hs])
        k.op("dve", lambda v: v.tensor_tensor(out=hs[:], in0=hs[:], in1=mod_s[:, 0:D], op=ALU.add), reads=[hs, mod_s], writes=[hs])
        for kc in range(8):
            k.op("pe", lambda p, kc=kc: p.transpose(out=PB[0][:, kc * P16:(kc + 1) * P16], in_=hs[:, kc * 128:(kc + 1) * 128],
                                                    identity=ident[0:P16, 0:P16]), reads=[hs, ident], writes=[PB[0]])
        k.op("dve", lambda v: v.tensor_copy(out=hsT[:].rearrange("p a b -> p (a b)"), in_=PB[0][:, 0:8 * P16]), reads=[PB[0]], writes=[hsT])
        for j in range(3):
            for kc in range(8):
                k.op("pe", lambda p, j=j, kc=kc: p.matmul(PB[1 + j][0:P16, :], lhsT=hsT[:, kc, :], rhs=Wf[:, kc, j * 512:(j + 1) * 512],
                                                          start=(kc == 0), stop=(kc == 7)), reads=[hsT, Wf], writes=[PB[1 + j]])
        offs = [(0, 512), (512, 512), (1024, 512), (1536, 264)]
        for j, (o, w_) in enumerate(offs):
            for kc in range(8):
                k.op("pe", lambda p, j=j, o=o, w_=w_, kc=kc: p.matmul(PB[4 + j][0:P16, 0:w_], lhsT=hsT[:, kc, :], rhs=Wt[:, kc, o:o + w_],
                                                                      start=(kc == 0), stop=(kc == 7)), reads=[hsT, Wt], writes=[PB[4 + j]])
        for j in range(3):
            k.op("dve", lambda v, j=j: v.tensor_copy(out=pqkv[:, j * 512:(j + 1) * 512], in_=PB[1 + j][0:P16, :]), reads=[PB[1 + j]], writes=[pqkv])
        k.op("act", lambda a: a.activation(out=zAs_s[:], in_=PB[4][0:P16, :], func=ACT.Silu), reads=[PB[4]], writes=[zAs_s])
        k.op("act", lambda a: a.activation(out=zBs_s[:], in_=PB[6][0:P16, :], func=ACT.Silu), reads=[PB[6]], writes=[zBs_s])
        rts = [sb2("rts%d" % i, [P16, 8, 32]) for i in range(4)]
        q3 = PB[5][0:P16, :].rearrange("p (a b) -> p a b", a=8)
        qr3 = qr_s[:].rearrange("p (a b) -> p a b", a=8)
        cq = bc(cs16[:, 0:1, :], [P16, 8, 32]); sq_ = bc(cs16[:, 1:2, :], [P16, 8, 32])
        k.op("dve", lambda v: v.tensor_tensor(out=rts[0][:], in0=q3[:, :, 0:32], in1=cq, op=ALU.mult), reads=[PB[5], cs16], writes=[rts[0]])
        k.op("dve", lambda v: v.tensor_tensor(out=rts[1][:], in0=q3[:, :, 32:64], in1=sq_, op=ALU.mult), reads=[PB[5], cs16], writes=[rts[1]])
        k.op("dve", lambda v: v.tensor_tensor(out=rts[2][:], in0=q3[:, :, 32:64], in1=cq, op=ALU.mult), reads=[PB[5], cs16], writes=[rts[2]])
        k.op("dve", lambda v: v.tensor_tensor(out=rts[3][:], in0=q3[:, :, 0:32], in1=sq_, op=ALU.mult), reads=[PB[5], cs16], writes=[rts[3]])
        k.op("dve", lambda v: v.tensor_tensor(out=qr3[:, :, 0:32], in0=rts[0][:], in1=rts[1][:], op=ALU.subtract), reads=[rts[0], rts[1]], writes=[qr_s])
        k.op("dve", lambda v: v.tensor_tensor(out=qr3[:, :, 32:64], in0=rts[2][:], in1=rts[3][:], op=ALU.add), reads=[rts[2], rts[3]], writes=[qr_s])
        k.op("dve", lambda v: v.tensor_scalar(out=qr_s[:], in0=qr_s[:], scalar1=0.125, scalar2=None, op0=ALU.mult), reads=[qr_s], writes=[qr_s])
        k3 = PB[7][0:P16, 0:128].rearrange("p (a b) -> p a b", a=2)
        kr3 = kr_s[:].rearrange("p (a b) -> p a b", a=2)
        ck = bc(cs16[:, 0:1, :], [P16, 2, 32]); sk_ = bc(cs16[:, 1:2, :], [P16, 2, 32])
        k.op("dve", lambda v: v.tensor_tensor(out=rts[0][:, 0:2, :], in0=k3[:, :, 0:32], in1=ck, op=ALU.mult), reads=[PB[7], cs16], writes=[rts[0]])
        k.op("dve", lambda v: v.tensor_tensor(out=rts[1][:, 0:2, :], in0=k3[:, :, 32:64], in1=sk_, op=ALU.mult), reads=[PB[7], cs16], writes=[rts[1]])
        k.op("dve", lambda v: v.tensor_tensor(out=rts[2][:, 0:2, :], in0=k3[:, :, 32:64], in1=ck, op=ALU.mult), reads=[PB[7], cs16], writes=[rts[2]])
        k.op("dve", lambda v: v.tensor_tensor(out=rts[3][:, 0:2, :], in0=k3[:, :, 0:32], in1=sk_, op=ALU.mult), reads=[PB[7], cs16], writes=[rts[3]])
        k.op("dve", lambda v: v.tensor_tensor(out=kr3[:, :, 0:32], in0=rts[0][:, 0:2, :], in1=rts[1][:, 0:2, :], op=ALU.subtract), reads=[rts[0], rts[1]], writes=[kr_s])
        k.op("dve", lambda v: v.tensor_tensor(out=kr3[:, :, 32:64], in0=rts[2][:, 0:2, :], in1=rts[3][:, 0:2, :], op=ALU.add), reads=[rts[2], rts[3]], writes=[kr_s])
        k.op("dve", lambda v: v.tensor_copy(out=v_s[:], in_=PB[7][0:P16, 128:256]), reads=[PB[7]], writes=[v_s])
        k.op("dve", lambda v: v.tensor_copy(out=vsb[:], in_=PB[7][0:P16, 128:256]), reads=[PB[7]], writes=[vsb])
        k.op("dve", lambda v: v.tensor_copy(out=bd_s[:], in_=PB[7][0:P16, 256:264]), reads=[PB[7]], writes=[bd_s])
        k.op("act", lambda a: a.activation(out=bdt_s[:, 0:4], in_=bd_s[:, 0:4], func=ACT.Exp, scale=-1.0), reads=[bd_s], writes=[bdt_s])
        k.op("dve", lambda v: v.tensor_scalar(out=bdt_s[:, 0:4], in0=bdt_s[:, 0:4], scalar1=1.0, scalar2=None, op0=ALU.add), reads=[bdt_s], writes=[bdt_s])
        k.op("dve", lambda v: v.reciprocal(out=beta_s[:], in_=bdt_s[:, 0:4]), reads=[bdt_s], writes=[beta_s])
        k.op("dve", lambda v: v.tensor_scalar(out=nbeta_s[:], in0=beta_s[:], scalar1=-1.0, scalar2=None, op0=ALU.mult), reads=[beta_s], writes=[nbeta_s])
        k.op("dve", lambda v: v.tensor_tensor(out=bd_s[:, 4:8], in0=bd_s[:, 4:8], in1=dtb_bc[0:P16, :], op=ALU.add), reads=[bd_s, dtb_bc], writes=[bd_s])
        k.op("act", lambda a: a.activation(out=bdt_s[:, 4:8], in_=bd_s[:, 4:8], func=ACT.Exp), reads=[bd_s], writes=[bdt_s])
        k.op("act", lambda a: a.activation(out=bdt_s[:, 4:8], in_=bdt_s[:, 4:8], func=ACT.Ln, bias=1.0, scale=1.0), reads=[bdt_s], writes=[bdt_s])
        k.op("dve", lambda v: v.tensor_tensor(out=g_s[:], in0=bdt_s[:, 4:8], in1=ea[0:P16, :], op=ALU.mult), reads=[bdt_s, ea], writes=[g_s])
        k.op("act", lambda a: a.activation(out=eg_s[:], in_=g_s[:], func=ACT.Exp), reads=[g_s], writes=[eg_s])
        k3s = contextlib.ExitStack()
        def sb3(name, shape, dt=F32):
            return k3s.enter_context(nc.sbuf_tensor(name, list(shape), dt))
        xp4 = sb3("xp4", [P16, 4, 1536]); cwb = sb3("cwb", [P16, 4, 1536]); tmpc = xp4
        acc_s = sb3("acc_s", [P16, 1536])
        k.dma("sp", xp4[:, 0:3, :], sconv_d[:, :, :], writes=[xp4], semkey="ld_xp4", nowaw=True)
        k.dma("sp", cwb[:].rearrange("p a b -> p (a b)"), convw_d.rearrange("a b -> (a b)").partition_broadcast(P16), writes=[cwb], semkey="ld_cwb")
        k.op("act", lambda a: a.copy(out=xp4[:, 3, :], in_=pqkv[:]), reads=[pqkv], writes=[xp4])
        k.dma("sp", convs_d[:, :, :], xp4[:, 1:4, :], reads=[xp4], semkey="st_smisc")
        k.op("dve", lambda v: v.tensor_tensor(out=tmpc[:], in0=xp4[:], in1=cwb[:], op=ALU.mult), reads=[xp4, cwb], writes=[tmpc])
        k.op("dve", lambda v: v.tensor_reduce(out=acc_s[:], in_=tmpc[:].rearrange("p j c -> p c j"), axis=AX.X, op=ALU.add), reads=[tmpc], writes=[acc_s])
        k.op("act", lambda a: a.activation(out=qkv_s[:].rearrange("p a b -> p (a b)"), in_=acc_s[:], func=ACT.Silu), reads=[acc_s], writes=[qkv_s])
        sqs = sb3("sqs", [P16, 8, 128]); sss = sb3("sss", [P16, 8])
        k.op("dve", lambda v: v.tensor_tensor(out=sqs[:], in0=qkv_s[:, 0:8, :], in1=qkv_s[:, 0:8, :], op=ALU.mult), reads=[qkv_s], writes=[sqs])
        k.op("dve", lambda v: v.tensor_reduce(out=sss[:], in_=sqs[:], axis=AX.X, op=ALU.add), reads=[sqs], writes=[sss])
        k.op("act", lambda a: a.activation(out=sss[:], in_=sss[:], func=ACT.Ln, bias=L2_EPS, scale=1.0), reads=[sss], writes=[sss])
        k.op("act", lambda a: a.activation(out=sss[:, 0:4], in_=sss[:, 0:4], func=ACT.Exp, bias=float(-0.5 * np.log(128.0)), scale=-0.5), reads=[sss], writes=[sss])
        k.op("act", lambda a: a.activation(out=sss[:, 4:8], in_=sss[:, 4:8], func=ACT.Exp, scale=-0.5), reads=[sss], writes=[sss])
        k.op("dve", lambda v: v.tensor_tensor(out=qkv_s[:, 0:8, :], in0=qkv_s[:, 0:8, :], in1=bc(sss[:].rearrange("p (a b) -> p a b", b=1), [P16, 8, 128]), op=ALU.mult),
             reads=[qkv_s, sss], writes=[qkv_s])
        k.barrier()
        k3s.close()
        k3s = contextlib.ExitStack()
        Ssb = sb3("Ssb", [128, P16 * 4, 128])
        k.dma("sp", Ssb[:], sdelta_d.rearrange("b h k v -> k (b h) v"), writes=[Ssb], semkey="ld_Ssb")
        qkT_s = sb3("qkT_s", [128, 8, P16])
        for c in range(8):
            k.op("pe", lambda p, c=c: p.transpose(out=PB[0][:, c * P16:(c + 1) * P16], in_=qkv_s[:, c, :], identity=ident[0:P16, 0:P16]),
                 reads=[qkv_s, ident], writes=[PB[0]])
        k.op("dve", lambda v: v.tensor_copy(out=qkT_s[:].rearrange("p a b -> p (a b)"), in_=PB[0][:, 0:8 * P16]), reads=[PB[0]], writes=[qkT_s])
        dmask = sb3("dmask", [P16, P16, 128])
        k.op("pool", lambda g: g.tensor_copy(out=dmask[:], in_=bc(ident[0:P16, 0:P16].rearrange("p (a b) -> p a b", b=1), [P16, P16, 128])),
             reads=[ident], writes=[dmask])
        egm = sb3("egm", [P16, P16, 4]); egbc = sb3("egbc", [128, P16 * 4])
        k.op("dve", lambda v: v.tensor_tensor(out=egm[:], in0=bc(eg_s[:].rearrange("p (a b) -> p a b", a=1), [P16, P16, 4]),
                                              in1=bc(ident[0:P16, 0:P16].rearrange("p (a b) -> p a b", b=1), [P16, P16, 4]), op=ALU.mult),
             reads=[eg_s, ident], writes=[egm])
        k.op("pe", lambda p: p.matmul(PB[1][:, 0:P16 * 4], lhsT=ones[0:P16, :], rhs=egm[:].rearrange("p a b -> p (a b)"), start=True, stop=True),
             reads=[ones, egm], writes=[PB[1]])
        k.op("dve", lambda v: v.tensor_copy(out=egbc[:], in_=PB[1][:, 0:P16 * 4]), reads=[PB[1]], writes=[egbc])
        pred = sb3("pred", [P16, 4, 128]); qS = sb3("qS", [P16, 4, 128]); tmpd = sb3("tmpd", [P16, P16, 128])
        dd = sb3("dd", [P16, 4, 128]); Dm = sb3("Dm", [P16, P16, 128]); o_s = sb3("o_s", [P16, 4, 128])
        qk_s = sb3("qk_s", [P16, 4]); qkt = sb3("qkt", [P16, 4, 128])
        k.op("dve", lambda v: v.tensor_tensor(out=qkt[:], in0=qkv_s[:, 0:4, :], in1=qkv_s[:, 4:8, :], op=ALU.mult), reads=[qkv_s], writes=[qkt])
        k.op("dve", lambda v: v.tensor_reduce(out=qk_s[:], in_=qkt[:], axis=AX.X, op=ALU.add), reads=[qkt], writes=[qk_s])
        for h in range(4):
            for which, dst in ((4, pred), (0, qS)):
                banks = [PB[2], PB[3], PB[4], PB[5]] if which == 4 else [PB[6], PB[7], PB[0], PB[1]]
                for b in range(P16):
                    pb = banks[b // 4]
                    k.op("pe", lambda p, b=b, pb=pb, which=which, h=h: p.matmul(pb[0:P16, (b % 4) * 128:(b % 4 + 1) * 128], lhsT=qkT_s[:, which + h, :],
                                                                               rhs=Ssb[:, b * 4 + h, :], start=True, stop=True),
                         reads=[qkT_s, Ssb], writes=[pb])
                for j in range(4):
                    k.op("dve", lambda v, j=j, banks=banks: v.tensor_tensor(out=tmpd[:, 4 * j:4 * j + 4, :], in0=banks[j][0:P16, :].rearrange("p (a b) -> p a b", a=4),
                                                                            in1=dmask[:, 4 * j:4 * j + 4, :], op=ALU.mult), reads=[banks[j], dmask], writes=[tmpd])
                k.op("dve", lambda v, dst=dst, h=h: v.tensor_reduce(out=dst[:, h, :], in_=tmpd[:].rearrange("p b v -> p v b"), axis=AX.X, op=ALU.add),
                     reads=[tmpd], writes=[dst])
            k.op("dve", lambda v, h=h: v.scalar_tensor_tensor(out=dd[:, h, :], in0=pred[:, h, :], scalar=eg_s[:, h:h + 1], in1=qkv_s[:, 8 + h, :],
                                                               op0=ALU.mult, op1=ALU.subtract), reads=[pred, eg_s, qkv_s], writes=[dd])
            k.op("dve", lambda v, h=h: v.tensor_scalar(out=dd[:, h, :], in0=dd[:, h, :], scalar1=nbeta_s[:, h:h + 1], scalar2=None, op0=ALU.mult),
                 reads=[dd, nbeta_s], writes=[dd])
            k.op("dve", lambda v, h=h: v.tensor_scalar(out=o_s[:, h, :], in0=dd[:, h, :], scalar1=qk_s[:, h:h + 1], scalar2=None, op0=ALU.mult),
                 reads=[dd, qk_s], writes=[o_s])
            k.op("dve", lambda v, h=h: v.scalar_tensor_tensor(out=o_s[:, h, :], in0=qS[:, h, :], scalar=eg_s[:, h:h + 1], in1=o_s[:, h, :],
                                                               op0=ALU.mult, op1=ALU.add), reads=[qS, eg_s, o_s], writes=[o_s])
            k.op("dve", lambda v, h=h: v.tensor_tensor(out=Dm[:], in0=bc(dd[:, h:h + 1, :], [P16, P16, 128]), in1=dmask[:], op=ALU.mult),
                 reads=[dd, dmask], writes=[Dm])
            banks = [PB[2], PB[3], PB[4], PB[5]]
            for b in range(P16):
                pb = banks[b // 4]
                k.op("pe", lambda p, b=b, pb=pb, h=h: p.matmul(pb[:, (b % 4) * 128:(b % 4 + 1) * 128], lhsT=qkv_s[:, 4 + h, :], rhs=Dm[:, b, :],
                                                               start=True, stop=True), reads=[qkv_s, Dm], writes=[pb])
            for b in range(P16):
                pb = banks[b // 4]
                k.op("dve", lambda v, b=b, pb=pb, h=h: v.scalar_tensor_tensor(out=Ssb[:, b * 4 + h, :], in0=Ssb[:, b * 4 + h, :], scalar=egbc[:, b * 4 + h:b * 4 + h + 1],
                                                                              in1=pb[:, (b % 4) * 128:(b % 4 + 1) * 128], op0=ALU.mult, op1=ALU.add),
                     reads=[Ssb, egbc, pb], writes=[Ssb])
        k.dma("sp", deltas_d.rearrange("b h k v -> k (b h) v"), Ssb[:], reads=[Ssb], semkey="st_smisc")
        oss_s = sb3("oss_s", [P16, 4])
        k.op("dve", lambda v: v.tensor_tensor(out=qkt[:], in0=o_s[:], in1=o_s[:], op=ALU.mult), reads=[o_s], writes=[qkt])
        k.op("dve", lambda v: v.tensor_reduce(out=oss_s[:], in_=qkt[:], axis=AX.X, op=ALU.add), reads=[qkt], writes=[oss_s])
        k.op("act", lambda a: a.activation(out=oss_s[:], in_=oss_s[:], func=ACT.Ln, bias=RMS_EPS, scale=1.0 / 128.0), reads=[oss_s], writes=[oss_s])
        k.op("act", lambda a: a.activation(out=oss_s[:], in_=oss_s[:], func=ACT.Exp, scale=-0.5), reads=[oss_s], writes=[oss_s])
        k.op("dve", lambda v: v.tensor_tensor(out=o_s[:], in0=o_s[:], in1=bc(oss_s[:].rearrange("p (a b) -> p a b", b=1), [P16, 4, 128]), op=ALU.mult),
             reads=[o_s, oss_s], writes=[o_s])
        k.op("dve", lambda v: v.tensor_tensor(out=o_s[:], in0=o_s[:], in1=bc(norma_bc[0:P16, :].rearrange("p (a b) -> p a b", a=1), [P16, 4, 128]), op=ALU.mult),
             reads=[o_s, norma_bc], writes=[o_s])
        k.op("dve", lambda v: v.tensor_tensor(out=mix_s[:, 0:512], in0=o_s[:].rearrange("p a b -> p (a b)"), in1=zAs_s[:], op=ALU.mult),
             reads=[o_s, zAs_s], writes=[mix_s])
        k.barrier()
        k3s.close()
        k3s = contextlib.ExitStack()
        Kc = sb3("Kc", [128, P16, 128]); Vc = sb3("Vc", [128, P16, 128])
        KcT = sb3("KcT", [128, P16, 128], BF16); VcB = sb3("VcB", [128, P16, 128], BF16)
        k.dma("sp", Kc[:], sk_d.rearrange("b s c -> s b c"), writes=[Kc], semkey="ld_Kc")
        k.dma("sp", Vc[:], sv_d.rearrange("b s c -> s b c"), writes=[Vc], semkey="ld_Vc")
        k.dma("sp", swaks_d[:, 0:127, :], sk_d[:, 1:128, :], semkey="st_smisc")
        k.dma("sp", swavs_d[:, 0:127, :], sv_d[:, 1:128, :], semkey="st_smisc")
        k.dma("sp", swaks_d[:, 127, :], kr_s[:], reads=[kr_s], semkey="st_smisc")
        k.dma("sp", swavs_d[:, 127, :], v_s[:], reads=[v_s], semkey="st_smisc")
        k.op("pool", lambda g: g.tensor_copy(out=VcB[:], in_=Vc[:]), reads=[Vc], writes=[VcB])
        for b in range(P16):
            pb = [PB[2], PB[3], PB[4], PB[5]][b // 4]
            k.op("pe", lambda p, b=b, pb=pb: p.transpose(out=pb[:, (b % 4) * 128:(b % 4 + 1) * 128], in_=Kc[:, b, :], identity=ident[:]),
                 reads=[Kc, ident], writes=[pb])
        for j in range(4):
            pb = [PB[2], PB[3], PB[4], PB[5]][j]
            k.op("act", lambda a, j=j, pb=pb: a.copy(out=KcT[:, 4 * j:4 * j + 4, :], in_=pb[:, :].rearrange("p (a b) -> p a b", a=4)), reads=[pb], writes=[KcT])
        qT_s = sb3("qT_s", [128, 4, P16]); Aq = sb3("Aq", [128, P16, 2, 4], BF16)
        knT = sb3("knT", [128, P16], BF16); zBT = sb3("zBT", [128, 4, P16])
        for c in range(4):
            k.op("pe", lambda p, c=c: p.transpose(out=PB[6][:, c * P16:(c + 1) * P16], in_=qr_s[:, c * 128:(c + 1) * 128], identity=ident[0:P16, 0:P16]),
                 reads=[qr_s, ident], writes=[PB[6]])
        k.op("pe", lambda p: p.transpose(out=PB[6][:, 4 * P16:5 * P16], in_=kr_s[:], identity=ident[0:P16, 0:P16]), reads=[kr_s, ident], writes=[PB[6]])
        for c in range(4):
            k.op("pe", lambda p, c=c: p.transpose(out=PB[6][:, (5 + c) * P16:(6 + c) * P16], in_=zBs_s[:, c * 128:(c + 1) * 128], identity=ident[0:P16, 0:P16]),
                 reads=[zBs_s, ident], writes=[PB[6]])
        k.op("dve", lambda v: v.tensor_copy(out=qT_s[:].rearrange("p a b -> p (a b)"), in_=PB[6][:, 0:4 * P16]), reads=[PB[6]], writes=[qT_s])
        k.op("dve", lambda v: v.tensor_copy(out=knT[:], in_=PB[6][:, 4 * P16:5 * P16]), reads=[PB[6]], writes=[knT])
        k.op("dve", lambda v: v.tensor_copy(out=zBT[:].rearrange("p a b -> p (a b)"), in_=PB[6][:, 5 * P16:9 * P16]), reads=[PB[6]], writes=[zBT])
        k.op("pool", lambda g: g.memset(Aq[:], 0.0), writes=[Aq])
        k.op("dve", lambda v: v.tensor_copy(out=Aq[0:64, :, 0, :], in_=qT_s[0:64, :, :].rearrange("p c b -> p b c")), reads=[qT_s], writes=[Aq])
        k.op("dve", lambda v: v.tensor_copy(out=Aq[64:128, :, 1, :], in_=qT_s[64:128, :, :].rearrange("p c b -> p b c")), reads=[qT_s], writes=[Aq])
        for b in range(P16):
            k.op("pe", lambda p, b=b: p.matmul(PB[7][:, b * 8:(b + 1) * 8], lhsT=KcT[:, b, :], rhs=Aq[:, b, :, :].rearrange("p a b -> p (a b)"),
                                               start=True, stop=True), reads=[KcT, Aq], writes=[PB[7]])
        STs = sb3("STs", [128, 128])
        k.op("dve", lambda v: v.tensor_copy(out=STs[:], in_=PB[7][:, 0:128]), reads=[PB[7]], writes=[STs])
        k.op("pe", lambda p: p.transpose(out=PB[0][:, 0:128], in_=STs[:], identity=ident[:]), reads=[STs, ident], writes=[PB[0]])
        k.op("pe", lambda p: p.matmul(PB[0][:, 128:128 + P16], lhsT=Aq[:].rearrange("p a b c -> p (a b c)"), rhs=knT[:], start=True, stop=True),
             reads=[Aq, knT], writes=[PB[0]])
        M2 = sb3("M2", [128, P16]); M2t = sb3("M2t", [128, P16])
        k.op("pool", lambda g: g.affine_select(out=M2t[:], in_=ones[:, 0:P16], pattern=[[-8, P16]], compare_op=ALU.is_ge, fill=0.0, base=0, channel_multiplier=1),
             reads=[ones], writes=[M2t])
        k.op("pool", lambda g: g.affine_select(out=M2[:], in_=M2t[:], pattern=[[8, P16]], compare_op=ALU.is_ge, fill=0.0, base=7, channel_multiplier=-1),
             reads=[M2t], writes=[M2])
        sm = sb3("sm", [128, 16]); tmps = sb3("tmps", [128, P16])
        sinkcol = sb3("sinkcol", [128, 1])
        k.dma("sp", sinkcol[:], sinkcol_d[:, :], writes=[sinkcol], semkey="ld_sinkcol")
        k.op("dve", lambda v: v.tensor_tensor(out=tmps[:], in0=PB[0][:, 128:128 + P16], in1=M2[:], op=ALU.mult), reads=[PB[0], M2], writes=[tmps])
        k.op("dve", lambda v: v.tensor_reduce(out=sm[:, 0:1], in_=tmps[:], axis=AX.X, op=ALU.add), reads=[tmps], writes=[sm])
        k.op("dve", lambda v: v.tensor_reduce(out=sm[:, 1:2], in_=PB[0][:, 0:128], axis=AX.X, op=ALU.max), reads=[PB[0]], writes=[sm])
        k.op("dve", lambda v: v.tensor_tensor(out=sm[:, 1:2], in0=sm[:, 1:2], in1=sm[:, 0:1], op=ALU.max), reads=[sm], writes=[sm])
        k.op("dve", lambda v: v.tensor_tensor(out=sm[:, 1:2], in0=sm[:, 1:2], in1=sinkcol[:], op=ALU.max), reads=[sm, sinkcol], writes=[sm])
        k.op("dve", lambda v: v.tensor_scalar(out=sm[:, 2:3], in0=sm[:, 1:2], scalar1=-1.0, scalar2=None, op0=ALU.mult), reads=[sm], writes=[sm])
        Ps = sb3("Ps", [128, 128])
        k.op("act", lambda a: a.activation(out=Ps[:], in_=PB[0][:, 0:128], func=ACT.Exp, bias=sm[:, 2:3], scale=1.0, accum_out=sm[:, 3:4]),
             reads=[PB[0], sm], writes=[Ps, sm])
        k.op("act", lambda a: a.activation(out=sm[:, 4:5], in_=sm[:, 0:1], func=ACT.Exp, bias=sm[:, 2:3], scale=1.0), reads=[sm], writes=[sm])
        k.op("act", lambda a: a.activation(out=sm[:, 5:6], in_=sinkcol[:], func=ACT.Exp, bias=sm[:, 2:3], scale=1.0), reads=[sm, sinkcol], writes=[sm])
        k.op("dve", lambda v: v.tensor_tensor(out=sm[:, 6:7], in0=sm[:, 3:4], in1=sm[:, 4:5], op=ALU.add), reads=[sm], writes=[sm])
        k.op("dve", lambda v: v.tensor_tensor(out=sm[:, 6:7], in0=sm[:, 6:7], in1=sm[:, 5:6], op=ALU.add), reads=[sm], writes=[sm])
        k.op("dve", lambda v: v.reciprocal(out=sm[:, 7:8], in_=sm[:, 6:7]), reads=[sm], writes=[sm])
        k.op("dve", lambda v: v.tensor_scalar(out=Ps[:], in0=Ps[:], scalar1=sm[:, 7:8], scalar2=None, op0=ALU.mult), reads=[Ps, sm], writes=[Ps])
        k.op("dve", lambda v: v.tensor_tensor(out=sm[:, 8:9], in0=sm[:, 4:5], in1=sm[:, 7:8], op=ALU.mult), reads=[sm], writes=[sm])
        PsT = sb3("PsT", [128, 128], BF16); Wd = sb3("Wd", [128, P16]); Wn = sb3("Wn", [P16, 128], BF16)
        k.op("pe", lambda p: p.transpose(out=PB[1][:, 0:128], in_=Ps[:], identity=ident[:]), reads=[Ps, ident], writes=[PB[1]])
        k.op("act", lambda a: a.copy(out=PsT[:], in_=PB[1][:, 0:128]), reads=[PB[1]], writes=[PsT])
        k.op("dve", lambda v: v.tensor_scalar(out=Wd[:], in0=M2[:], scalar1=sm[:, 8:9], scalar2=None, op0=ALU.mult), reads=[M2, sm], writes=[Wd])
        k.op("pe", lambda p: p.transpose(out=PB[1][0:P16, 128:256], in_=Wd[:], identity=ident[:]), reads=[Wd, ident], writes=[PB[1]])
        k.op("act", lambda a: a.copy(out=Wn[:], in_=PB[1][0:P16, 128:256]), reads=[PB[1]], writes=[Wn])
        for b in range(P16):
            k.op("pe", lambda p, b=b: p.matmul(PB[2][:, b * 8:(b + 1) * 8], lhsT=VcB[:, b, :], rhs=PsT[:, b * 8:(b + 1) * 8], start=True, stop=False),
                 reads=[VcB, PsT], writes=[PB[2]])
            k.op("pe", lambda p, b=b: p.matmul(PB[2][:, b * 8:(b + 1) * 8], lhsT=vsb[:], rhs=Wn[:, b * 8:(b + 1) * 8], start=False, stop=True),
                 reads=[vsb, Wn], writes=[PB[2]])
        obT = sb3("obT", [128, 4, P16])
        OT4 = PB[2][:, 0:128].rearrange("p (b k c) -> p b k c", b=P16, k=2)
        k.op("dve", lambda v: v.tensor_copy(out=obT[0:64, :, :].rearrange("p c b -> p b c"), in_=OT4[0:64, :, 0, :]), reads=[PB[2]], writes=[obT])
        k.op("dve", lambda v: v.tensor_copy(out=obT[64:128, :, :].rearrange("p c b -> p b c"), in_=OT4[64:128, :, 1, :]), reads=[PB[2]], writes=[obT])
        mixT_s = sb3("mixT_s", [128, 8, P16], BF16)
        k.op("dve", lambda v: v.tensor_tensor(out=mixT_s[:, 4:8, :], in0=obT[:], in1=zBT[:], op=ALU.mult), reads=[obT, zBT], writes=[mixT_s])
        for c in range(4):
            k.op("pe", lambda p, c=c: p.transpose(out=PB[3][:, c * P16:(c + 1) * P16], in_=mix_s[:, c * 128:(c + 1) * 128], identity=ident[0:P16, 0:P16]),
                 reads=[mix_s, ident], writes=[PB[3]])
        k.op("dve", lambda v: v.tensor_copy(out=mixT_s[:, 0:4, :].rearrange("p a b -> p (a b)"), in_=PB[3][:, 0:4 * P16]), reads=[PB[3]], writes=[mixT_s])
        for j in range(2):
            for kc in range(8):
                k.op("pe", lambda p, j=j, kc=kc: p.matmul(PB[4 + j][0:P16, :], lhsT=mixT_s[:, kc, :], rhs=Wo[:, kc, j * 512:(j + 1) * 512],
                                                          start=(kc == 0), stop=(kc == 7)), reads=[mixT_s, Wo], writes=[PB[4 + j]])
        ypre_s = sb3("ypre_s", [P16, D]); yo_s = sb3("yo_s", [P16, D]); st_s = sb3("st_s", [P16, 8])
        for j in range(2):
            k.op("dve", lambda v, j=j: v.scalar_tensor_tensor(out=ypre_s[:, j * 512:(j + 1) * 512], in0=mod_s[:, 2 * D + j * 512:2 * D + (j + 1) * 512], scalar=1.0,
                                                               in1=PB[4 + j][0:P16, :], op0=ALU.add, op1=ALU.mult), reads=[mod_s, PB[4 + j]], writes=[ypre_s])
        k.op("dve", lambda v: v.scalar_tensor_tensor(out=ypre_s[:], in0=xs[:], scalar=ALPHA, in1=ypre_s[:], op0=ALU.mult, op1=ALU.add),
             reads=[xs, ypre_s], writes=[ypre_s])
        k.op("act", lambda a: a.activation(out=yo_s[:], in_=ypre_s[:], func=ACT.Identity, accum_out=st_s[:, 0:1]), reads=[ypre_s], writes=[yo_s, st_s])
        k.op("act", lambda a: a.activation(out=yo_s[:], in_=ypre_s[:], func=ACT.Square, accum_out=st_s[:, 1:2]), reads=[ypre_s], writes=[yo_s, st_s])
        k.op("dve", lambda v: v.tensor_scalar(out=st_s[:, 2:3], in0=st_s[:, 0:1], scalar1=1.0 / D, scalar2=None, op0=ALU.mult), reads=[st_s], writes=[st_s])
        k.op("dve", lambda v: v.tensor_tensor(out=st_s[:, 3:4], in0=st_s[:, 2:3], in1=st_s[:, 2:3], op=ALU.mult), reads=[st_s], writes=[st_s])
        k.op("dve", lambda v: v.scalar_tensor_tensor(out=st_s[:, 4:5], in0=st_s[:, 1:2], scalar=1.0 / D, in1=st_s[:, 3:4], op0=ALU.mult, op1=ALU.subtract),
             reads=[st_s], writes=[st_s])
        k.op("act", lambda a: a.activation(out=st_s[:, 5:6], in_=st_s[:, 4:5], func=ACT.Ln, bias=LN_EPS, scale=1.0), reads=[st_s], writes=[st_s])
        k.op("act", lambda a: a.activation(out=st_s[:, 5:6], in_=st_s[:, 5:6], func=ACT.Exp, scale=-0.5), reads=[st_s], writes=[st_s])
        k.op("dve", lambda v: v.scalar_tensor_tensor(out=st_s[:, 6:7], in0=st_s[:, 2:3], scalar=-1.0, in1=st_s[:, 5:6], op0=ALU.mult, op1=ALU.mult),
             reads=[st_s], writes=[st_s])
        k.op("act", lambda a: a.activation(out=yo_s[:], in_=ypre_s[:], func=ACT.Identity, bias=st_s[:, 6:7], scale=st_s[:, 5:6]), reads=[ypre_s, st_s], writes=[yo_s])
        k.op("dve", lambda v: v.tensor_tensor(out=yo_s[:], in0=yo_s[:], in1=lng_bc[0:P16, :], op=ALU.mult), reads=[yo_s, lng_bc], writes=[yo_s])
        k.op("dve", lambda v: v.tensor_tensor(out=yo_s[:], in0=yo_s[:], in1=lnb_bc[0:P16, :], op=ALU.add), reads=[yo_s, lnb_bc], writes=[yo_s])
        k.dma("sp", ys_d[:, :], yo_s[:], reads=[yo_s], semkey="st_smisc")
        k.barrier()
        k3s.close()

    k.barrier()
    k2.close()
    k1s.close()
    identb = k.sb("identb", [128, 128], BF16); Ub = k.sb("Ub", [128, 128], BF16)
    k.op("pool", lambda g: g.tensor_copy(out=identb[:], in_=ident[:]), reads=[ident], writes=[identb])
    k.op("pool", lambda g: g.tensor_copy(out=Ub[:], in_=U[:]), reads=[U], writes=[Ub])
    Xt = [k.sb("Xt%d" % i, [128, D]) for i in range(2)]
    Xb = k.sb("Xb", [128, D], BF16)
    hT = k.sb("hT", [128, 8, 128], BF16)
    pre = k.sb("pre", [128, 12, 131])
    k.op("pool", lambda g: g.memset(pre[:], 0.0), writes=[pre])
    cm = [k.sb("cm%d" % i, [128, 12, 128]) for i in range(2)]
    qkvs = cm[0]
    qkb = k.sb("qkb", [128, 12, 128], BF16)
    sqb = k.sb("sqb", [128, 8, 128], BF16)
    lnss = k.sb("lnss", [128, 8, 128])
    rn = lnss
    zAs = k.sb("zAs", [128, 512]); zBs2 = [k.sb("zBs%d" % i, [128, 512]) for i in range(2)]
    qrb2 = [k.sb("qrb%d" % i, [128, 512], BF16) for i in range(2)]; kr2 = [k.sb("kr%d" % i, [128, 128]) for i in range(2)]
    rt = [k.sb("rt%d" % i, [128, 8, 32]) for i in range(4)]
    vB = [k.sb("vB%d" % i, [128, 128], BF16) for i in range(3)]
    vB32 = k.sb("vB32", [128, 128])
    kTb = [k.sb("kTb%d" % i, [128, 128], BF16) for i in range(2)]
    k.op("pool", lambda g: g.memset(kTb[1][:], 0.0), writes=[kTb[1]])
    k.op("pool", lambda g: g.memset(vB[2][:], 0.0), writes=[vB[2]])
    qTb = k.sb("qTb", [128, 4, 128], BF16)
    bd = k.sb("bd", [128, 8]); bdt = k.sb("bdt", [128, 8])
    beta = k.sb("beta", [128, 4]); negbeta = k.sb("negbeta", [128, 4]); gg = k.sb("gg", [128, 4])
    gsp = k.sb("gsp", [128, 8], BF16); gtmp = k.sb("gtmp", [128, 4])
    gUh = k.sb("gUh", [128, 4, 128], BF16); gUl = k.sb("gUl", [128, 4, 128], BF16)
    Gc = k.sb("Gc", [128, 4]); negGc = k.sb("negGc", [128, 4]); eG = k.sb("eG", [128, 4]); beG = k.sb("beG", [128, 4])
    edG = k.sb("edG", [128, 4]); egl = k.sb("egl", [128, 4]); dG = k.sb("dG", [128, 4]); Glb = k.sb("Glb", [128, 4])
    tmp1 = k.sb("tmp1", [128, 4, 128]); tmp2 = k.sb("tmp2", [128, 4, 128])
    E1 = tmp1; E2 = tmp2
    eGbc = k.sb("eGbc", [128, 4, 128]); qdT = k.sb("qdT", [128, 4, 128], BF16)
    AqkT = k.sb("AqkT", [128, 4, 128], BF16)
    Y0f = k.sb("Y0f", [128, 4, 128])
    XRh = [[k.sb("XRh%d_%d" % (g_, i), [128, 2, 256], BF16) for i in range(2)] for g_ in range(2)]
    XRl = [[k.sb("XRl%d_%d" % (g_, i), [128, 2, 256], BF16) for i in range(2)] for g_ in range(2)]
    Yh = [[k.sb("Yh%d_%d" % (g_, i), [128, 2, 128], BF16) for i in range(2)] for g_ in range(2)]
    Yl = [[k.sb("Yl%d_%d" % (g_, i), [128, 2, 128], BF16) for i in range(2)] for g_ in range(2)]
    Rf = [k.sb("Rf%d" % g_, [128, 2, 128]) for g_ in range(2)]
    kbt = k.sb("kbt", [128, 4, 128], BF16); kdt = k.sb("kdt", [128, 4, 128], BF16); bvt = k.sb("bvt", [128, 4, 128], BF16)
    wT = [k.sb("wT%d" % g_, [128, 2, 128], BF16) for g_ in range(2)]
    ut = [k.sb("ut%d" % g_, [128, 2, 128]) for g_ in range(2)]
    uub = [k.sb("uub%d" % g_, [128, 2, 128], BF16) for g_ in range(2)]
    S = [k.sb("S%d" % g_, [128, 2, 128]) for g_ in range(2)]
    Sb = [k.sb("Sb%d" % g_, [128, 2, 128], BF16) for g_ in range(2)]
    for g_ in range(2):
        k.op("pool", lambda g, g_=g_: g.memset(S[g_][:], 0.0), writes=[S[g_]])
        k.op("pool", lambda g, g_=g_: g.memset(Sb[g_][:], 0.0), writes=[Sb[g_]])
    oss = [k.sb("oss%d" % g_, [128, 2]) for g_ in range(2)]; orr = [k.sb("orr%d" % g_, [128, 2]) for g_ in range(2)]
    nz = k.sb("nz", [128, 4, 128])
    mixb = k.sb("mixb", [128, D], BF16); obt = k.sb("obt", [128, 512])
    ojunk = [tmp1[:, 0, :], tmp2[:, 0, :]]
    mixT = k.sb("mixT", [128, 8, 128], BF16)
    SC = k.sb("SC", [128, 8, 256]); Pb = k.sb("Pb", [128, 8, 256], BF16)
    PTb = k.sb("PTb", [128, 16, 128], BF16)
    mx = k.sb("mx", [128, 8]); negm = k.sb("negm", [128, 8]); rs = k.sb("rs", [128, 8]); es_ = k.sb("es_", [128, 8])
    rden = k.sb("rden", [128, 8])
    SCf = SC[:].rearrange("p a b -> p (a b)")
    ypre = SCf[:, 0:D]
    st = k.sb("st", [128, 8])
    Yt = [SCf[:, D:2 * D]]

    def v3(ap, a):
        return ap.rearrange("p (a b) -> p a b", a=a)

    def pbf(pb):
        return pb[:, :].bitcast(BF16)

    def mm(out, lhsT, rhs, start, stop, reads, writes):
        ncol = int(np.prod(out.shape[1:]))
        k.op("pe", lambda p: p.matmul(out, lhsT=lhsT, rhs=rhs, start=start, stop=stop), reads=reads, writes=writes,
             cost=0.14 + ncol / 1200.0)

    def tr(out, in_, idn, reads, writes):
        k.op("pe", lambda p: p.transpose(out=out, in_=in_, identity=idn), reads=reads + [idn], writes=writes, cost=0.25)

    k.barrier()
    k.pe_inorder = True
    krb = k.sb("krb", [128, 128], BF16)
    def P1a(n):
        X = Xt[n % 2]
        cs_t = cst[n % 2]
        zBs = zBs2[n % 2]; qrb = qrb2[n % 2]; kr = kr2[n % 2]
        vcur = vB[n % 3]
        k.dma("sp", X[:], x_d[n * 128:(n + 1) * 128, :], writes=[X], semkey="ld_" + X.name)
        k.dma("sp", cs_t[:, 0, :], cosk_d[n * 128:(n + 1) * 128, :], writes=[cs_t], semkey="ld_" + cs_t.name, nowaw=True)
        k.dma("sp", cs_t[:, 1, :], sink_d[n * 128:(n + 1) * 128, :], writes=[cs_t], semkey="ld_" + cs_t.name, nowaw=True)
        yield
        k.op("pool", lambda g: g.tensor_copy(out=Xb[:], in_=X[:]), reads=[X], writes=[Xb])
        P3b = pbf(PB[2])
        yield
        for kc in range(8):
            tr(P3b[:, kc * 128:(kc + 1) * 128], Xb[:, kc * 128:(kc + 1) * 128], identb[:], [Xb], [PB[2]])
        yield
        for kc in range(8):
            k.op("act" if kc % 2 == 0 else "dve",
                 (lambda a, kc=kc: a.activation(out=hT[:, kc, :], in_=P3b[:, kc * 128:(kc + 1) * 128], func=ACT.Identity,
                                                bias=mod_fm[:, kc:kc + 1], scale=mod_fm[:, 8 + kc:9 + kc])) if kc % 2 == 0 else
                 (lambda v, kc=kc: v.tensor_scalar(out=hT[:, kc, :], in0=P3b[:, kc * 128:(kc + 1) * 128], scalar1=mod_fm[:, 8 + kc:9 + kc],
                                                   scalar2=mod_fm[:, kc:kc + 1], op0=ALU.mult, op1=ALU.add)),
                 reads=[PB[2], mod_fm], writes=[hT])
            if kc % 2 == 1:
                yield
        pc = pre
        k.op("pool", lambda g: g.tensor_copy(out=pc[:, :, 0:3], in_=pc[:, :, 128:131]), reads=[pc], writes=[pc])
        yield
        for j in range(3):
            pb = [PB[3], PB[2], PB[3]][j]
            for c4 in range(4):
                c = j * 4 + c4
                for kc in range(8):
                    mm(pb[:, c4 * 128:(c4 + 1) * 128], Wf[:, kc, c * 128:(c + 1) * 128], hT[:, kc, :], kc == 0, kc == 7, [Wf, hT], [pb])
                yield
            if j != 1:
                k.op("act", lambda a, j=j, pb=pb: a.copy(out=pc[:, j * 4:(j + 1) * 4, 3:131], in_=v3(pb[:, :], 4)), reads=[pb], writes=[pc])
            else:
                k.op("dve", lambda v, j=j, pb=pb: v.tensor_copy(out=pc[:, j * 4:(j + 1) * 4, 3:131], in_=v3(pb[:, :], 4)), reads=[pb], writes=[pc])
            yield
        k.op("dve", lambda v: v.tensor_tensor(out=cm[0][:], in0=pc[:, :, 0:128], in1=bc(cwT[:, 0:12].rearrange("p (a b) -> p a b", b=1), [128, 12, 128]), op=ALU.mult),
             reads=[pc, cwT], writes=[cm[0]])
        yield
        for j in range(1, 4):
            k.op("pool", lambda g, j=j: g.tensor_tensor(out=cm[1][:], in0=pc[:, :, j:j + 128], in1=bc(cwT[:, j * 12:(j + 1) * 12].rearrange("p (a b) -> p a b", b=1), [128, 12, 128]), op=ALU.mult),
                 reads=[pc, cwT], writes=[cm[1]])
            yield
            k.op("dve", lambda v: v.tensor_tensor(out=cm[0][:], in0=cm[0][:], in1=cm[1][:], op=ALU.add), reads=[cm[0], cm[1]], writes=[cm[0]])
            yield
        k.op("act", lambda a: a.activation(out=qkvs[:], in_=cm[0][:], func=ACT.Silu), reads=[cm[0]], writes=[qkvs])
        yield
        offs = [(0, 512), (512, 512), (1024, 512), (1536, 264)]

        def tproj(j, pb):
            o, w_ = offs[j]
            for kc in range(8):
                mm(pb[:, 0:w_], hT[:, kc, :], Wt[:, kc, o:o + w_], kc == 0, kc == 7, [hT, Wt], [pb])
        PzA, PqB = PB[2], PB[3]
        tproj(0, PzA)
        yield
        k.op("act", lambda a: a.activation(out=zAs[:], in_=PzA[:, :], func=ACT.Silu), reads=[PzA], writes=[zAs])
        yield
        tproj(1, PqB)
        yield
        q3 = v3(PqB[:, :], 8)
        qr3 = v3(qrb[:], 8)
        cq = bc(cs_t[:, 0:1, :], [128, 8, 32]); sq_ = bc(cs_t[:, 1:2, :], [128, 8, 32])
        k.op("dve", lambda v: v.tensor_tensor(out=rt[0][:], in0=q3[:, :, 0:32], in1=cq, op=ALU.mult), reads=[PqB, cs_t], writes=[rt[0]])
        k.op("dve", lambda v: v.tensor_tensor(out=rt[1][:], in0=q3[:, :, 32:64], in1=sq_, op=ALU.mult), reads=[PqB, cs_t], writes=[rt[1]])
        yield
        k.op("dve", lambda v: v.tensor_tensor(out=rt[2][:], in0=q3[:, :, 32:64], in1=cq, op=ALU.mult), reads=[PqB, cs_t], writes=[rt[2]])
        k.op("dve", lambda v: v.tensor_tensor(out=rt[3][:], in0=q3[:, :, 0:32], in1=sq_, op=ALU.mult), reads=[PqB, cs_t], writes=[rt[3]])
        yield
        k.op("pool", lambda g: g.tensor_tensor(out=qr3[:, :, 0:32], in0=rt[0][:], in1=rt[1][:], op=ALU.subtract), reads=[rt[0], rt[1]], writes=[qrb])
        k.op("pool", lambda g: g.tensor_tensor(out=qr3[:, :, 32:64], in0=rt[2][:], in1=rt[3][:], op=ALU.add), reads=[rt[2], rt[3]], writes=[qrb])
        yield
        PzB, Pk = PB[2], PB[3]
        tproj(2, PzB)
        yield
        k.op("act", lambda a: a.activation(out=zBs[:], in_=PzB[:, :], func=ACT.Silu), reads=[PzB], writes=[zBs])
        yield
        tproj(3, Pk)
        yield
        k3 = v3(Pk[:, 0:128], 2)
        kr3 = v3(kr[:], 2)
        ck = bc(cs_t[:, 0:1, :], [128, 2, 32]); sk_ = bc(cs_t[:, 1:2, :], [128, 2, 32])
        k.op("dve", lambda v: v.tensor_tensor(out=rt[0][:, 0:2, :], in0=k3[:, :, 0:32], in1=ck, op=ALU.mult), reads=[Pk, cs_t], writes=[rt[0]])
        k.op("dve", lambda v: v.tensor_tensor(out=rt[1][:, 0:2, :], in0=k3[:, :, 32:64], in1=sk_, op=ALU.mult), reads=[Pk, cs_t], writes=[rt[1]])
        yield
        k.op("dve", lambda v: v.tensor_tensor(out=rt[2][:, 0:2, :], in0=k3[:, :, 32:64], in1=ck, op=ALU.mult), reads=[Pk, cs_t], writes=[rt[2]])
        k.op("dve", lambda v: v.tensor_tensor(out=rt[3][:, 0:2, :], in0=k3[:, :, 0:32], in1=sk_, op=ALU.mult), reads=[Pk, cs_t], writes=[rt[3]])
        yield
        k.op("pool", lambda g: g.tensor_tensor(out=kr3[:, :, 0:32], in0=rt[0][:, 0:2, :], in1=rt[1][:, 0:2, :], op=ALU.subtract), reads=[rt[0], rt[1]], writes=[kr])
        k.op("pool", lambda g: g.tensor_tensor(out=kr3[:, :, 32:64], in0=rt[2][:, 0:2, :], in1=rt[3][:, 0:2, :], op=ALU.add), reads=[rt[2], rt[3]], writes=[kr])
        yield
        k.op("dve", lambda v: v.tensor_copy(out=vcur[:], in_=Pk[:, 128:256]), reads=[Pk], writes=[vcur])
        if n == NT - 1:
            k.op("dve", lambda v: v.tensor_copy(out=vB32[:], in_=Pk[:, 128:256]), reads=[Pk], writes=[vB32])
        k.op("dve", lambda v: v.tensor_copy(out=bd[:], in_=Pk[:, 256:264]), reads=[Pk], writes=[bd])
        yield
        k.op("pool", lambda g: g.tensor_tensor(out=sqb[:], in0=qkvs[:, 0:8, :], in1=qkvs[:, 0:8, :], op=ALU.mult), reads=[qkvs], writes=[sqb])
        yield
        for j in range(2):
            mm(PB[2 + j][:, :], onesb[:], sqb[:, j * 4:(j + 1) * 4, :].rearrange("p a b -> p (a b)"), True, True, [onesb, sqb], [PB[2 + j]])
        yield
        for j in range(2):
            k.op("act", lambda a, j=j: a.activation(out=lnss[:, j * 4:(j + 1) * 4, :].rearrange("p a b -> p (a b)"), in_=PB[2 + j][:, :],
                                                    func=ACT.Ln, bias=L2_EPS, scale=1.0), reads=[PB[2 + j]], writes=[lnss])
        yield
        k.op("act", lambda a: a.activation(out=rn[:, 0:4, :], in_=lnss[:, 0:4, :], func=ACT.Exp, bias=float(-0.5 * np.log(128.0)), scale=-0.5), reads=[lnss], writes=[rn])
        k.op("act", lambda a: a.activation(out=rn[:, 4:8, :], in_=lnss[:, 4:8, :], func=ACT.Exp, scale=-0.5), reads=[lnss], writes=[rn])
        yield
        k.op("dve", lambda v: v.tensor_tensor(out=qkvs[:, 0:8, :], in0=qkvs[:, 0:8, :], in1=rn[:], op=ALU.mult), reads=[qkvs, rn], writes=[qkvs])
        yield
        k.op("pool", lambda g: g.tensor_copy(out=qkb[:], in_=qkvs[:]), reads=[qkvs], writes=[qkb])
        yield
        k.op("act", lambda a: a.activation(out=bdt[:, 0:4], in_=bd[:, 0:4], func=ACT.Exp, scale=-1.0), reads=[bd], writes=[bdt])
        k.op("dve", lambda v: v.tensor_scalar(out=bdt[:, 0:4], in0=bdt[:, 0:4], scalar1=1.0, scalar2=None, op0=ALU.add), reads=[bdt], writes=[bdt])
        k.op("dve", lambda v: v.reciprocal(out=beta[:], in_=bdt[:, 0:4]), reads=[bdt], writes=[beta])
        k.op("dve", lambda v: v.tensor_scalar(out=negbeta[:], in0=beta[:], scalar1=-1.0, scalar2=None, op0=ALU.mult), reads=[beta], writes=[negbeta])
        yield
        k.op("dve", lambda v: v.tensor_tensor(out=bd[:, 4:8], in0=bd[:, 4:8], in1=dtb_bc[:], op=ALU.add), reads=[bd, dtb_bc], writes=[bd])
        k.op("act", lambda a: a.activation(out=bdt[:, 4:8], in_=bd[:, 4:8], func=ACT.Exp), reads=[bd], writes=[bdt])
        k.op("act", lambda a: a.activation(out=bdt[:, 4:8], in_=bdt[:, 4:8], func=ACT.Ln, bias=1.0, scale=1.0), reads=[bdt], writes=[bdt])
        k.op("dve", lambda v: v.tensor_tensor(out=gg[:], in0=bdt[:, 4:8], in1=ea[:], op=ALU.mult), reads=[bdt, ea], writes=[gg])
        yield

    def P1b(n):
        yield
        k.op("pool", lambda g: g.tensor_tensor(out=nz[:], in0=v3(zAs[:], 4), in1=bc(norma_bc[:].rearrange("p (a b) -> p a b", a=1), [128, 4, 128]), op=ALU.mult),
             reads=[zAs, norma_bc], writes=[nz])
        yield
        k.op("dve", lambda v: v.tensor_copy(out=gsp[:, 0:4], in_=gg[:]), reads=[gg], writes=[gsp])
        yield
        k.op("dve", lambda v: v.tensor_tensor(out=gtmp[:], in0=gg[:], in1=gsp[:, 0:4], op=ALU.subtract), reads=[gg, gsp], writes=[gtmp])
        yield
        k.op("dve", lambda v: v.tensor_copy(out=gsp[:, 4:8], in_=gtmp[:]), reads=[gtmp], writes=[gsp])
        Ub4 = bc(Ub[:].rearrange("p (a b) -> p a b", a=1), [128, 4, 128])
        yield
        k.op("pool", lambda g: g.tensor_tensor(out=gUh[:], in0=Ub4, in1=bc(gsp[:, 0:4].rearrange("p (a b) -> p a b", b=1), [128, 4, 128]), op=ALU.mult),
             reads=[Ub, gsp], writes=[gUh])
        yield
        k.op("pool", lambda g: g.tensor_tensor(out=gUl[:], in0=Ub4, in1=bc(gsp[:, 4:8].rearrange("p (a b) -> p a b", b=1), [128, 4, 128]), op=ALU.mult),
             reads=[Ub, gsp], writes=[gUl])
        PG, PK_, PQ = PB[3], PB[4], PB[5]
        yield
        mm(PB[1][:, 0:8], Ub[:], gsp[:], True, True, [Ub, gsp], [PB[1]])
        yield
        mm(PB[1][:, 8:16], onesb[:], gsp[:], True, True, [onesb, gsp], [PB[1]])
        yield
        mm(PG[:, :], onesb[:], gUh[:].rearrange("p a b -> p (a b)"), True, False, [onesb, gUh], [PG])
        yield
        mm(PG[:, :], onesb[:], gUl[:].rearrange("p a b -> p (a b)"), False, True, [onesb, gUl], [PG])
        yield
        k.op("dve", lambda v: v.tensor_copy(out=gtmp[:], in_=PB[1][:, 0:4]), reads=[PB[1]], writes=[gtmp])
        yield
        k.op("dve", lambda v: v.tensor_tensor(out=Gc[:], in0=gtmp[:], in1=PB[1][:, 4:8], op=ALU.add), reads=[gtmp, PB[1]], writes=[Gc])
        yield
        k.op("dve", lambda v: v.tensor_copy(out=gtmp[:], in_=PB[1][:, 8:12]), reads=[PB[1]], writes=[gtmp])
        yield
        k.op("dve", lambda v: v.tensor_tensor(out=Glb[:], in0=gtmp[:], in1=PB[1][:, 12:16], op=ALU.add), reads=[gtmp, PB[1]], writes=[Glb])
        yield
        k.op("dve", lambda v: v.tensor_scalar(out=negGc[:], in0=Gc[:], scalar1=-1.0, scalar2=None, op0=ALU.mult), reads=[Gc], writes=[negGc])
        yield
        k.op("dve", lambda v: v.tensor_tensor(out=dG[:], in0=Glb[:], in1=Gc[:], op=ALU.subtract), reads=[Glb, Gc], writes=[dG])
        yield
        k.op("act", lambda a: a.activation(out=eG[:], in_=Gc[:], func=ACT.Exp), reads=[Gc], writes=[eG])
        yield
        k.op("act", lambda a: a.activation(out=edG[:], in_=dG[:], func=ACT.Exp), reads=[dG], writes=[edG])
        yield
        k.op("act", lambda a: a.activation(out=egl[:], in_=Glb[:], func=ACT.Exp), reads=[Glb], writes=[egl])
        yield
        k.op("dve", lambda v: v.tensor_tensor(out=beG[:], in0=eG[:], in1=beta[:], op=ALU.mult), reads=[eG, beta], writes=[beG])
        PG3 = v3(PG[:, :], 4)
        yield
        k.op("dve", lambda v: v.scalar_tensor_tensor(out=tmp1[:], in0=PG3, scalar=-1.0, in1=bc(negA[:].rearrange("p (a b) -> p a b", a=1), [128, 4, 128]),
                                                     op0=ALU.mult, op1=ALU.add), reads=[PG, negA], writes=[tmp1])
        yield
        k.op("dve", lambda v: v.tensor_tensor(out=tmp2[:], in0=PG3, in1=bc(negB[:].rearrange("p (a b) -> p a b", a=1), [128, 4, 128]), op=ALU.add),
             reads=[PG, negB], writes=[tmp2])
        yield
        k.op("act", lambda a: a.activation(out=eGbc[:], in_=PG3, func=ACT.Exp), reads=[PG], writes=[eGbc])
        yield
        for h in range(4):
            k.op("act", lambda a, h=h: a.activation(out=E1[:, h, :], in_=tmp1[:, h, :], func=ACT.Exp, bias=Gc[:, h:h + 1], scale=1.0), reads=[tmp1, Gc], writes=[E1])
            k.op("act", lambda a, h=h: a.activation(out=E2[:, h, :], in_=tmp2[:, h, :], func=ACT.Exp, bias=negGc[:, h:h + 1], scale=1.0), reads=[tmp2, negGc], writes=[E2])
        yield
        for h in range(4):
            mm(PK_[:, h * 128:(h + 1) * 128], qkb[:, 4 + h, :], qkb[:, 4 + h, :], True, True, [qkb], [PK_])
        yield
        for h in range(4):
            mm(PQ[:, h * 128:(h + 1) * 128], qkb[:, 4 + h, :], qkb[:, h, :], True, True, [qkb], [PQ])
        yield
        for h in range(4):
            k.op("dve", lambda v, h=h: v.scalar_tensor_tensor(out=Y0f[:, h, :], in0=PK_[:, h * 128:(h + 1) * 128], scalar=negbeta[:, h:h + 1],
                                                               in1=E1[:, h, :], op0=ALU.mult, op1=ALU.mult), reads=[PK_, negbeta, E1], writes=[Y0f])
        yield
        k.op("dve", lambda v: v.tensor_tensor(out=AqkT[:], in0=v3(PQ[:, :], 4), in1=E2[:], op=ALU.mult), reads=[PQ, E2], writes=[AqkT])
        yield
        k.op("pool", lambda g: g.tensor_tensor(out=qdT[:], in0=qkvs[:, 0:4, :], in1=eGbc[:], op=ALU.mult), reads=[qkvs, eGbc], writes=[qdT])
        P5b = pbf(PB[0])
        id2 = bc(ident[:].rearrange("p (a b) -> p a b", a=1), [128, 2, 128])
        for g_ in range(2):
            yield
            k.op("act", lambda a, g_=g_: a.copy(out=Yh[g_][0][:], in_=Y0f[:, 2 * g_:2 * g_ + 2, :]), reads=[Y0f], writes=[Yh[g_][0]])
            yield
            k.op("pool", lambda g, g_=g_: g.tensor_tensor(out=Yl[g_][0][:], in0=Y0f[:, 2 * g_:2 * g_ + 2, :], in1=Yh[g_][0][:], op=ALU.subtract),
                 reads=[Y0f, Yh[g_][0]], writes=[Yl[g_][0]])
            yield
            for hh in range(2):
                h = 2 * g_ + hh
                tr(P5b[:, h * 128:(h + 1) * 128], Yh[g_][0][:, hh, :], identb[:], [Yh[g_][0]], [PB[0]])
                tr(P5b[:, (4 + h) * 128:(5 + h) * 128], Yl[g_][0][:, hh, :], identb[:], [Yl[g_][0]], [PB[0]])
        for g_ in range(2):
            yield
            k.op("act", lambda a, g_=g_: a.copy(out=XRh[g_][0][:, :, 0:128], in_=v3(P5b[:, g_ * 256:(g_ + 1) * 256], 2)), reads=[PB[0]], writes=[XRh[g_][0]])
            yield
            k.op("dve", lambda v, g_=g_: v.tensor_copy(out=XRl[g_][0][:, :, 0:128], in_=v3(P5b[:, 512 + g_ * 256:512 + (g_ + 1) * 256], 2)), reads=[PB[0]], writes=[XRl[g_][0]])
            yield
            k.op("pool", lambda g, g_=g_: g.tensor_copy(out=Rf[g_][:], in_=id2), reads=[ident], writes=[Rf[g_]])
            k.op("pool", lambda g, g_=g_: g.tensor_copy(out=XRh[g_][0][:, :, 128:256], in_=id2), reads=[ident], writes=[XRh[g_][0]])
            k.op("pool", lambda g, g_=g_: g.memset(XRl[g_][0][:, :, 128:256], 0.0), writes=[XRl[g_][0]])
        P1b = pbf(PB[1])
        yield
        for h in range(4):
            tr(P1b[:, h * 128:(h + 1) * 128], qkb[:, 4 + h, :], identb[:], [qkb], [PB[1]])
        yield
        for h in range(4):
            tr(P1b[:, (4 + h) * 128:(5 + h) * 128], qkb[:, 8 + h, :], identb[:], [qkb], [PB[1]])
        yield
        for h in range(4):
            k.op("dve", lambda v, h=h: v.tensor_scalar(out=kbt[:, h, :], in0=P1b[:, h * 128:(h + 1) * 128], scalar1=beG[:, h:h + 1], scalar2=None, op0=ALU.mult),
                 reads=[PB[1], beG], writes=[kbt])
            k.op("act", lambda a, h=h: a.activation(out=kdt[:, h, :], in_=P1b[:, h * 128:(h + 1) * 128], func=ACT.Identity, scale=edG[:, h:h + 1]),
                 reads=[PB[1], edG], writes=[kdt])
            k.op("act", lambda a, h=h: a.activation(out=bvt[:, h, :], in_=P1b[:, (4 + h) * 128:(5 + h) * 128], func=ACT.Identity, scale=beta[:, h:h + 1]),
                 reads=[PB[1], beta], writes=[bvt])
        yield

    def G(n, g_):
        BA, BB = PB[4 + g_], PB[6 + g_]
        for lev in range(7):
            cur = lev % 2; nxt = (lev + 1) % 2
            xh, xl, yh, yl = XRh[g_][cur], XRl[g_][cur], Yh[g_][cur], Yl[g_][cur]
            xhn, xln, yhn, yln = XRh[g_][nxt], XRl[g_][nxt], Yh[g_][nxt], Yl[g_][nxt]
            last = (lev == 6)
            yield
            for hh in range(2):
                if not last:
                    o_ = BA[:, hh * 256:(hh + 1) * 256]
                    r_h = xh[:, hh, :]; r_l = xl[:, hh, :]
                else:
                    o_ = BA[:, hh * 256 + 128:(hh + 1) * 256]
                    r_h = xh[:, hh, 128:256]; r_l = xl[:, hh, 128:256]
                mm(o_, yh[:, hh, :], r_h, True, False, [yh, xh], [BA])
                mm(o_, yh[:, hh, :], r_l, False, False, [yh, xl], [BA])
                mm(o_, yl[:, hh, :], r_h, False, True, [yl, xh], [BA])
            if not last:
                yield
                for hh in range(2):
                    o_ = BB[:, hh * 128:(hh + 1) * 128]
                    mm(o_, xh[:, hh, 0:128], yh[:, hh, :], True, False, [xh, yh], [BB])
                    mm(o_, xh[:, hh, 0:128], yl[:, hh, :], False, False, [xh, yl], [BB])
                    mm(o_, xl[:, hh, 0:128], yh[:, hh, :], False, True, [xl, yh], [BB])
            yield
            k.op("dve", lambda v: v.tensor_tensor(out=Rf[g_][:], in0=Rf[g_][:], in1=v3(BA[:, :], 2)[:, :, 128:256], op=ALU.add), reads=[Rf[g_], BA], writes=[Rf[g_]],
                 cost=0.3)
            yield
            k.op("act", lambda a: a.copy(out=xhn[:, :, 128:256], in_=Rf[g_][:]), reads=[Rf[g_]], writes=[xhn], cost=0.3)
            if not last:
                yield
                k.op("pool", lambda g: g.tensor_tensor(out=xln[:, :, 128:256], in0=Rf[g_][:], in1=xhn[:, :, 128:256], op=ALU.subtract), reads=[Rf[g_], xhn], writes=[xln],
                     cost=0.6)
                yield
                k.op("act", lambda a: a.copy(out=xhn[:, :, 0:128], in_=v3(BA[:, :], 2)[:, :, 0:128]), reads=[BA], writes=[xhn], cost=0.3)
                yield
                k.op("dve", lambda v: v.tensor_tensor(out=xln[:, :, 0:128], in0=v3(BA[:, :], 2)[:, :, 0:128], in1=xhn[:, :, 0:128], op=ALU.subtract),
                     reads=[BA, xhn], writes=[xln], cost=0.3)
                yield
                k.op("act", lambda a: a.copy(out=yhn[:], in_=v3(BB[:, 0:256], 2)), reads=[BB], writes=[yhn], cost=0.3)
                yield
                k.op("dve", lambda v: v.tensor_tensor(out=yln[:], in0=v3(BB[:, 0:256], 2), in1=yhn[:], op=ALU.subtract), reads=[BB, yhn], writes=[yln], cost=0.3)
        Rh = XRh[g_][1]
        yield
        for hh in range(2):
            h = 2 * g_ + hh
            mm(BA[:, hh * 128:(hh + 1) * 128], kbt[:, h, :], Rh[:, hh, 128:256], True, True, [kbt, Rh], [BA])
        for hh in range(2):
            h = 2 * g_ + hh
            mm(BB[:, hh * 128:(hh + 1) * 128], Rh[:, hh, 128:256], bvt[:, h, :], True, True, [Rh, bvt], [BB])
        yield
        k.op("act", lambda a: a.copy(out=wT[g_][:], in_=v3(BA[:, 0:256], 2)), reads=[BA], writes=[wT[g_]], cost=0.3)
        yield
        k.op("dve", lambda v: v.tensor_copy(out=ut[g_][:], in_=v3(BB[:, 0:256], 2)), reads=[BB], writes=[ut[g_]], cost=0.3)
        yield
        for hh in range(2):
            mm(BA[:, 256 + hh * 128:256 + (hh + 1) * 128], wT[g_][:, hh, :], Sb[g_][:, hh, :], True, True, [wT[g_], Sb[g_]], [BA])
        yield
        k.op("dve", lambda v: v.tensor_tensor(out=uub[g_][:], in0=ut[g_][:], in1=v3(BA[:, 256:512], 2), op=ALU.subtract), reads=[ut[g_], BA], writes=[uub[g_]], cost=0.3)
        yield
        for hh in range(2):
            h = 2 * g_ + hh
            mm(BB[:, 256 + hh * 128:256 + (hh + 1) * 128], qdT[:, h, :], Sb[g_][:, hh, :], True, False, [qdT, Sb[g_]], [BB])
            mm(BB[:, 256 + hh * 128:256 + (hh + 1) * 128], AqkT[:, h, :], uub[g_][:, hh, :], False, True, [AqkT, uub[g_]], [BB])
        for hh in range(2):
            h = 2 * g_ + hh
            mm(BA[:, hh * 128:(hh + 1) * 128], kdt[:, h, :], uub[g_][:, hh, :], True, True, [kdt, uub[g_]], [BA])
        yield
        for hh in range(2):
            h = 2 * g_ + hh
            k.op("dve", lambda v, hh=hh, h=h: v.scalar_tensor_tensor(out=S[g_][:, hh, :], in0=S[g_][:, hh, :], scalar=egl[:, h:h + 1], in1=BA[:, hh * 128:(hh + 1) * 128],
                                                                   op0=ALU.mult, op1=ALU.add), reads=[S[g_], egl, BA], writes=[S[g_]], cost=0.2)
        yield
        k.op("pool", lambda g: g.tensor_copy(out=Sb[g_][:], in_=S[g_][:]), reads=[S[g_]], writes=[Sb[g_]], cost=0.5)
        yield
        for hh in range(2):
            k.op("act", lambda a, hh=hh: a.activation(out=ojunk[g_][:], in_=BB[:, 256 + hh * 128:256 + (hh + 1) * 128], func=ACT.Square, accum_out=oss[g_][:, hh:hh + 1]),
                 reads=[BB], writes=[ojunk[g_], oss[g_]], cost=0.25)
        yield
        k.op("act", lambda a: a.activation(out=orr[g_][:], in_=oss[g_][:], func=ACT.Ln, bias=RMS_EPS, scale=1.0 / 128.0), reads=[oss[g_]], writes=[orr[g_]], cost=0.2)
        k.op("act", lambda a: a.activation(out=orr[g_][:], in_=orr[g_][:], func=ACT.Exp, scale=-0.5), reads=[orr[g_]], writes=[orr[g_]], cost=0.2)
        yield
        for hh in range(2):
            h = 2 * g_ + hh
            k.op("dve", lambda v, hh=hh, h=h: v.scalar_tensor_tensor(out=mixb[:, h * 128:(h + 1) * 128], in0=BB[:, 256 + hh * 128:256 + (hh + 1) * 128], scalar=orr[g_][:, hh:hh + 1],
                                                                   in1=nz[:, h, :], op0=ALU.mult, op1=ALU.mult), reads=[BB, orr[g_], nz], writes=[mixb], cost=0.2)
        yield

    def W(n):
        vcur = vB[n % 3]; vprev = vB[(n + 2) % 3]
        zBs = zBs2[n % 2]; qrb = qrb2[n % 2]; kr = kr2[n % 2]
        kcur = kTb[n % 2]; kprev = kTb[(n + 1) % 2]
        P0b = pbf(PB[0])
        yield
        for c in range(4):
            tr(P0b[:, c * 128:(c + 1) * 128], qrb[:, c * 128:(c + 1) * 128], identb[:], [qrb], [PB[0]])
        yield
        k.op("act", lambda a: a.activation(out=qTb[:], in_=v3(P0b[:, 0:512], 4), func=ACT.Identity, scale=0.125), reads=[PB[0]], writes=[qTb])
        yield
        k.op("pool", lambda g: g.tensor_copy(out=krb[:], in_=kr[:]), reads=[kr], writes=[krb])
        yield
        tr(pbf(PB[0])[:, 512:640], krb[:], identb[:], [krb], [PB[0]])
        yield
        k.op("dve", lambda v: v.tensor_copy(out=kcur[:], in_=pbf(PB[0])[:, 512:640]), reads=[PB[0]], writes=[kcur])
        msk = swam0 if n == 0 else swam
        rnd = 0
        for kv in range(2):
            for cp in range(2):
                pb = PB[1] if rnd % 2 == 0 else PB[0]
                rnd += 1
                yield
                for cc in range(2):
                    c = cp * 2 + cc
                    o = cc * 256
                    mm(pb[:, o:o + 128], qTb[64 * kv:64 * kv + 64, c, :], kprev[64 * kv:64 * kv + 64, :], True, True, [qTb, kprev], [pb])
                    mm(pb[:, o + 128:o + 256], qTb[64 * kv:64 * kv + 64, c, :], kcur[64 * kv:64 * kv + 64, :], True, True, [qTb, kcur], [pb])
                yield
                s0 = kv * 4 + cp * 2
                k.op("dve", lambda v, pb=pb, s0=s0: v.tensor_tensor(out=SC[:, s0:s0 + 2, :], in0=v3(pb[:, :], 2),
                                                                     in1=bc(msk[:].rearrange("p (a b) -> p a b", a=1), [128, 2, 256]), op=ALU.add),
                     reads=[pb, msk], writes=[SC])
        yield
        k.op("dve", lambda v: v.tensor_reduce(out=mx[:], in_=SC[:], axis=AX.X, op=ALU.max), reads=[SC], writes=[mx])
        yield
        k.op("dve", lambda v: v.tensor_tensor(out=mx[:], in0=mx[:], in1=sinks_bc[:], op=ALU.max), reads=[mx, sinks_bc], writes=[mx])
        yield
        k.op("dve", lambda v: v.tensor_scalar(out=negm[:], in0=mx[:], scalar1=-1.0, scalar2=None, op0=ALU.mult), reads=[mx], writes=[negm])
        yield
        for s_ in range(8):
            k.op("act", lambda a, s_=s_: a.activation(out=Pb[:, s_, :], in_=SC[:, s_, :], func=ACT.Exp, bias=negm[:, s_:s_ + 1], scale=1.0,
                                                      accum_out=rs[:, s_:s_ + 1]), reads=[SC, negm], writes=[Pb, rs])
        yield
        k.op("dve", lambda v: v.tensor_tensor(out=es_[:], in0=sinks_bc[:], in1=mx[:], op=ALU.subtract), reads=[sinks_bc, mx], writes=[es_])
        yield
        k.op("act", lambda a: a.activation(out=es_[:], in_=es_[:], func=ACT.Exp), reads=[es_], writes=[es_])
        yield
        k.op("dve", lambda v: v.tensor_tensor(out=es_[:], in0=es_[:], in1=rs[:], op=ALU.add), reads=[es_, rs], writes=[es_])
        yield
        k.op("dve", lambda v: v.reciprocal(out=es_[:], in_=es_[:]), reads=[es_], writes=[es_])
        yield
        k.op("dve", lambda v: v.tensor_copy(out=rden[:].rearrange("p (c k) -> p k c", k=2), in_=es_[:].rearrange("p (k c) -> p k c", k=2)),
             reads=[es_], writes=[rden])
        PPT = [PB[1], PB[0]]
        yield
        for s_ in range(8):
            for half in range(2):
                i16 = s_ * 2 + half
                pb = PPT[i16 // 8]
                o = (i16 % 8) * 128
                tr(pbf(pb)[:, o:o + 128], Pb[:, s_, half * 128:(half + 1) * 128], identb[:], [Pb], [pb])
        yield
        k.op("act", lambda a: a.copy(out=PTb[:, 0:8, :], in_=v3(pbf(PPT[0]), 8)), reads=[PPT[0]], writes=[PTb])
        yield
        k.op("dve", lambda v: v.tensor_copy(out=PTb[:, 8:16, :], in_=v3(pbf(PPT[1]), 8)), reads=[PPT[1]], writes=[PTb])
        PV = PB[1]
        yield
        for c in range(4):
            for kv in range(2):
                slot = c * 2 + kv
                sp_ = kv * 4 + c
                mm(PV[:, slot * 64:(slot + 1) * 64], PTb[:, 2 * sp_, :], vprev[:, 64 * kv:64 * kv + 64], True, False, [PTb, vprev], [PV])
                mm(PV[:, slot * 64:(slot + 1) * 64], PTb[:, 2 * sp_ + 1, :], vcur[:, 64 * kv:64 * kv + 64], False, True, [PTb, vcur], [PV])
        yield
        k.op("dve", lambda v: v.tensor_tensor(out=v3(obt[:], 8), in0=v3(PV[:, :], 8), in1=bc(rden[:].rearrange("p (a b) -> p a b", b=1), [128, 8, 64]), op=ALU.mult),
             reads=[PV, rden], writes=[obt])
        yield
        k.op("pool", lambda g: g.tensor_tensor(out=mixb[:, 512:1024], in0=obt[:], in1=zBs[:], op=ALU.mult), reads=[obt, zBs], writes=[mixb])

        yield

    def E(n):
        X = Xt[n % 2]; pc = pre; kr = kr2[n % 2]
        P3b = pbf(PB[6])
        yield
        for kc in range(8):
            tr(P3b[:, kc * 128:(kc + 1) * 128], mixb[:, kc * 128:(kc + 1) * 128], identb[:], [mixb], [PB[6]])
        yield
        k.op("act", lambda a: a.copy(out=mixT[:, 0:4, :], in_=v3(P3b[:, 0:512], 4)), reads=[PB[6]], writes=[mixT])
        yield
        k.op("dve", lambda v: v.tensor_copy(out=mixT[:, 4:8, :], in_=v3(P3b[:, 512:1024], 4)), reads=[PB[6]], writes=[mixT])
        yield
        for j in range(2):
            for kc in range(8):
                mm(PB[6 + j][:, :], mixT[:, kc, :], Wo[:, kc, j * 512:(j + 1) * 512], kc == 0, kc == 7, [mixT, Wo], [PB[6 + j]])
        yield
        for j in range(2):
            k.op("dve", lambda v, j=j: v.tensor_tensor(out=ypre[:, j * 512:(j + 1) * 512], in0=PB[6 + j][:, :], in1=g1bc[:, j * 512:(j + 1) * 512], op=ALU.mult),
                 reads=[PB[6 + j], g1bc], writes=[ypre])
        yield
        k.op("dve", lambda g: g.scalar_tensor_tensor(out=ypre[:], in0=X[:], scalar=ALPHA, in1=ypre[:], op0=ALU.mult, op1=ALU.add), reads=[X, ypre], writes=[ypre])
        Yo = Yt[0]
        yjunk = Yo
        yield
        k.op("act", lambda a: a.activation(out=yjunk[:], in_=ypre[:], func=ACT.Identity, accum_out=st[:, 0:1]), reads=[ypre], writes=[yjunk, st])
        yield
        k.op("act", lambda a: a.activation(out=yjunk[:], in_=ypre[:], func=ACT.Square, accum_out=st[:, 1:2]), reads=[ypre], writes=[yjunk, st])
        yield
        k.op("dve", lambda v: v.tensor_scalar(out=st[:, 2:3], in0=st[:, 0:1], scalar1=1.0 / D, scalar2=None, op0=ALU.mult), reads=[st], writes=[st])
        yield
        k.op("dve", lambda v: v.tensor_tensor(out=st[:, 3:4], in0=st[:, 2:3], in1=st[:, 2:3], op=ALU.mult), reads=[st], writes=[st])
        yield
        k.op("dve", lambda v: v.scalar_tensor_tensor(out=st[:, 4:5], in0=st[:, 1:2], scalar=1.0 / D, in1=st[:, 3:4], op0=ALU.mult, op1=ALU.subtract), reads=[st], writes=[st])
        yield
        k.op("act", lambda a: a.activation(out=st[:, 5:6], in_=st[:, 4:5], func=ACT.Ln, bias=LN_EPS, scale=1.0), reads=[st], writes=[st])
        yield
        k.op("act", lambda a: a.activation(out=st[:, 5:6], in_=st[:, 5:6], func=ACT.Exp, scale=-0.5), reads=[st], writes=[st])
        yield
        k.op("dve", lambda v: v.scalar_tensor_tensor(out=st[:, 6:7], in0=st[:, 2:3], scalar=-1.0, in1=st[:, 5:6], op0=ALU.mult, op1=ALU.mult), reads=[st], writes=[st])
        yield
        k.op("act", lambda a: a.activation(out=yjunk[:], in_=ypre[:], func=ACT.Identity, bias=st[:, 6:7], scale=st[:, 5:6]), reads=[ypre, st], writes=[yjunk])
        yield
        k.op("pool", lambda g: g.tensor_tensor(out=yjunk[:], in0=yjunk[:], in1=lng_bc[:], op=ALU.mult), reads=[yjunk, lng_bc], writes=[yjunk])
        yield
        k.op("dve", lambda v: v.tensor_tensor(out=Yo[:], in0=yjunk[:], in1=lnb_bc[:], op=ALU.add), reads=[yjunk, lnb_bc], writes=[Yo])
        yield
        k.dma("sp", y_d[n * 128:(n + 1) * 128, :], Yo[:], reads=[Yo], semkey="st_" + Yo.name)

        if n == NT - 1:
            k.barrier()
            k.pe_inorder = False
            cpo_in = rt[0][:].rearrange("p a b -> p (a b)")[:, 0:36].rearrange("p (a b) -> p a b", a=3)
            cpo = obt[0:12, 0:384].rearrange("p (a b) -> p a b", a=3)
            k.op("dve", lambda v: v.tensor_copy(out=cpo_in[:], in_=pc[:, :, 128:131].rearrange("p c r -> p r c")), reads=[pc], writes=[cpo_in])
            for r_ in range(3):
                tr(PB[6][0:12, r_ * 128:(r_ + 1) * 128], cpo_in[:, r_, :], ident[:], [cpo_in], [PB[6]])
            k.op("dve", lambda v: v.tensor_copy(out=cpo[:].rearrange("p a b -> p (a b)"), in_=PB[6][0:12, 0:384]), reads=[PB[6]], writes=[cpo])
            k.dma("sp", convp_d.rearrange("r (c p) -> c r p", p=128), cpo[:], reads=[cpo], semkey="st_misc")
            for g_ in range(2):
                k.dma("sp", deltap_d[2 * g_:2 * g_ + 2].rearrange("h k v -> k h v"), S[g_][:], reads=[S[g_]], semkey="st_misc")
            k.dma("sp", swak_d[:, :], kr[:], reads=[kr], semkey="st_misc")
            k.dma("sp", swav_d[:, :], vB32[:], reads=[vB32], semkey="st_misc")

        yield

    def merge(*gens):
        gens = [[g_, "s%d" % i] for i, g_ in enumerate(gens) if g_ is not None]
        base = max(k.t_eng.values()) if False else min(k.t_eng.values())
        for _, lab in gens:
            k.t_stream[lab] = base
        while gens:
            gens.sort(key=lambda x: k.t_stream[x[1]])
            g_, lab = gens[0]
            k.stream = lab
            try:
                next(g_)
            except StopIteration:
                gens.pop(0)
        k.stream = None

    merge(P1a(0))
    merge(P1b(0))
    for n in range(NT):
        merge(G(n, 0), G(n, 1), W(n), P1a(n + 1) if n + 1 < NT else None)
        merge(E(n), P1b(n + 1) if n + 1 < NT else None)

    k.finish("sp")
    if k.dry:
        return k.need_out
    nc._kb_nops = k.nops
    nc._kb_sig = dict(k.sig)
    nc._kb_cnt = {e: k.cnt[e] for e in k.eng}
    return nc


def rope_tables(pos):
    half = 32
    inv = (1.0 / (10000.0 ** (np.arange(half, dtype=np.float32) / np.float32(half)))).astype(np.float32)
    ang = pos.astype(np.float32)[:, None] * inv[None, :]
    return np.cos(ang).astype(np.float32), np.sin(ang).astype(np.float32)


def prep_shared(inputs):
    w_in = np.asarray(inputs["w_in"][0], np.float32)
    perm_cols = np.concatenate([np.arange(h * 64, (h + 1) * 64) for h in PERM])
    w_f = np.ascontiguousarray(w_in[:, 0:1536])
    zA = w_in[:, 1536:2048]
    beta = w_in[:, 2048:2052]
    dec = w_in[:, 2052:2056]
    qB = w_in[:, 2056:2568][:, perm_cols]
    kB = w_in[:, 2568:2696]
    vB = w_in[:, 2696:2824]
    zB = w_in[:, 2824:3336][:, perm_cols]
    w_t = np.ascontiguousarray(np.concatenate([zA, qB, zB, kB, vB, beta, dec], axis=1))
    w_out = np.asarray(inputs["w_out"][0], np.float32)
    w_o = np.ascontiguousarray(np.concatenate([w_out[0:512], w_out[512:1024][perm_cols]], axis=0))
    sh = {
        "w_ada": np.ascontiguousarray(inputs["w_ada"][0], np.float32),
        "b_ada": np.ascontiguousarray(inputs["b_ada"], np.float32).reshape(1, -1),
        "w_f": w_f, "w_t": w_t, "w_o": w_o,
        "conv_w": np.ascontiguousarray(inputs["conv_w"][0], np.float32),
        "a_log": np.ascontiguousarray(inputs["a_log"], np.float32).reshape(1, 4),
        "dt_bias": np.ascontiguousarray(inputs["dt_bias"], np.float32).reshape(1, 4),
        "norm_a": np.ascontiguousarray(inputs["norm_a"], np.float32).reshape(1, 128),
        "sinks_p": np.ascontiguousarray(np.asarray(inputs["sinks"], np.float32).reshape(8)).reshape(1, 8),
        "ln_g": np.ascontiguousarray(inputs["ln_g"], np.float32).reshape(1, D),
        "ln_b": np.ascontiguousarray(inputs["ln_b"], np.float32).reshape(1, D),
    }
    return sh


def prep_core(inputs, sh, core, ntiles=NTILES, do_sample=True):
    b = core // 4
    T = ntiles * 128
    m = dict(sh)
    m["x"] = np.ascontiguousarray(inputs["x_prompt"][b, :T], np.float32)
    m["c"] = np.ascontiguousarray(inputs["c_prompt"][b], np.float32).reshape(1, D)
    cos, sin = rope_tables(np.arange(T))
    m["cosk"] = cos
    m["sink"] = sin
    if do_sample:
        sl = slice(core * NS, (core + 1) * NS)
        m["xs"] = np.ascontiguousarray(inputs["x_sample"][sl, 0], np.float32)
        m["cs"] = np.ascontiguousarray(inputs["c_sample"][sl], np.float32)
        m["s_conv"] = np.ascontiguousarray(inputs["state_conv"][0, sl], np.float32)
        m["s_delta"] = np.ascontiguousarray(inputs["state_delta"][0, sl], np.float32)
        m["s_k"] = np.ascontiguousarray(inputs["cache_swa_k"][0, sl], np.float32).reshape(NS, 128, 128)
        m["s_v"] = np.ascontiguousarray(inputs["cache_swa_v"][0, sl], np.float32).reshape(NS, 128, 128)
        cs_, ss_ = rope_tables(np.array([8192]))
        m["sinks_col"] = np.ascontiguousarray(np.tile(np.asarray(inputs["sinks"], np.float32).reshape(8), NS).reshape(128, 1))
        m["cos_s"] = cs_.reshape(1, 32)
        m["sin_s"] = ss_.reshape(1, 32)
    return m


_NC_CACHE = {}


DO_SAMPLE = True


def kernel(**inputs):
    if "nc" not in _NC_CACHE:
        _NC_CACHE["nc"] = build_program(NTILES, DO_SAMPLE)
    nc = _NC_CACHE["nc"]
    sh = prep_shared(inputs)
    in_maps = [prep_core(inputs, sh, c, NTILES, DO_SAMPLE) for c in range(8)]
    res = run_bass_kernel_spmd(nc, in_maps, core_ids=list(range(8))).results
    yp = np.stack([res[0]["y"], res[4]["y"]], 0).astype(np.float32)
    conv_p = np.stack([res[0]["conv_p"], res[4]["conv_p"]], 0)[None].astype(np.float32)
    delta_p = np.stack([res[0]["delta_p"], res[4]["delta_p"]], 0)[None].astype(np.float32)
    swa_k_p = np.stack([res[0]["swa_k_p"], res[4]["swa_k_p"]], 0).reshape(1, 2, 128, 2, 64).astype(np.float32)
    swa_v_p = np.stack([res[0]["swa_v_p"], res[4]["swa_v_p"]], 0).reshape(1, 2, 128, 2, 64).astype(np.float32)
    if DO_SAMPLE:
        ys = np.concatenate([r["ys"] for r in res], 0).reshape(128, 1, D).astype(np.float32)
        conv_s = np.concatenate([r["conv_s"] for r in res], 0)[None].astype(np.float32)
        delta_s = np.concatenate([r["delta_s"] for r in res], 0)[None].astype(np.float32)
        swa_k_s = np.concatenate([r["swa_k_s"] for r in res], 0).reshape(1, 128, 128, 2, 64).astype(np.float32)
        swa_v_s = np.concatenate([r["swa_v_s"] for r in res], 0).reshape(1, 128, 128, 2, 64).astype(np.float32)
    else:
        ys = np.zeros((128, 1, D), np.float32)
        conv_s = np.zeros((1, 128, 3, 1536), np.float32)
        delta_s = np.zeros((1, 128, 4, 128, 128), np.float32)
        swa_k_s = np.zeros((1, 128, 128, 2, 64), np.float32)
        swa_v_s = np.zeros((1, 128, 128, 2, 64), np.float32)
    return (yp, ys, conv_p, delta_p, swa_k_p, swa_v_p, conv_s, delta_s, swa_k_s, swa_v_s)
```

```python
import contextlib
import numpy as np
import concourse.bass as bass
import concourse.mybir as mybir
from concourse.bass_utils import run_bass_kernel_spmd

F32 = mybir.dt.float32
BF16 = mybir.dt.bfloat16
ACT = mybir.ActivationFunctionType
ALU = mybir.AluOpType
AX = mybir.AxisListType

D = 1024
NTILES = 64
NS = 16
ALPHA = 2.0 ** 0.25
NEG = -1.0e30
LN_EPS = 1e-5
RMS_EPS = 1e-6
L2_EPS = 1e-6
WT_COLS = 1800
PERM = [0, 4, 1, 5, 2, 6, 3, 7]


class KB:
    def __init__(self, nc):
        self.nc = nc
        self.es = contextlib.ExitStack()
        self.eng = {"pe": nc.tensor, "dve": nc.vector, "act": nc.scalar, "pool": nc.gpsimd, "sp": nc.sync}
        self.sem = {}
        self.cnt = {}
        for e in self.eng:
            self.sem[e] = self.es.enter_context(nc.semaphore("sem_" + e))
            self.cnt[e] = 0
        self.waited = {}
        self.last_write = {}
        self.readers = {}
        self.ntensors = 0
        self.limit = None
        self.nops = 0
        self.pe_inorder = False
        self.t_eng = {e: 0.0 for e in self.eng}
        self.t_fin = {}
        self.stream = None
        self.t_stream = {}
        self.dry = False
        self.needed = None
        self.need_out = set()
        self.sig = {e: 0 for e in self.eng}
        self.sigval = {}

    def sb(self, name, shape, dt=F32):
        return self.es.enter_context(self.nc.sbuf_tensor(name, list(shape), dt))

    def ps(self, name, shape=(128, 512), dt=F32):
        return self.es.enter_context(self.nc.psum_tensor(name, list(shape), dt))

    def _deps(self, reads, writes, nowaw=False):
        deps = set()
        for t in reads:
            if t in self.last_write:
                deps.add(self.last_write[t])
        for t in writes:
            if t in self.last_write and not nowaw:
                deps.add(self.last_write[t])
            for r in self.readers.get(t, ()):
                deps.add(r)
        return deps

    def _semval(self, src, val):
        if src in self.eng and self.needed is not None:
            return self.sigval[(src, val)]
        return val

    def _wait(self, e, deps):
        for (src, val) in sorted(deps, key=lambda x: str(x)):
            if e == "pe" and src == "pe" and self.pe_inorder:
                continue
            if self.waited.get((e, src), 0) < val:
                if self.dry:
                    self.need_out.add((src, val))
                else:
                    self.eng[e].wait_ge(self.sem[src], self._semval(src, val))
                self.waited[(e, src)] = val

    def _record(self, key, reads, writes):
        for t in writes:
            self.last_write[t] = key
            self.readers[t] = set()
        for t in reads:
            if t not in writes:
                self.readers.setdefault(t, set()).add(key)

    COST = {"pe": 0.2, "dve": 0.45, "act": 0.5, "pool": 1.0, "sp": 0.1}

    def _model(self, e, key, deps, cost):
        t0 = self.t_eng.get(e, 0.0)
        for d in deps:
            t0 = max(t0, self.t_fin.get(d, 0.0) + (0.0 if d[0] == e else 0.15))
        t1 = t0 + cost
        self.t_eng[e] = t1
        self.t_fin[key] = t1
        if self.stream is not None:
            self.t_stream[self.stream] = max(self.t_stream.get(self.stream, 0.0), t1)

    def op(self, e, fn, reads=(), writes=(), cost=None):
        self.nops += 1
        if self.limit is not None and self.nops > self.limit:
            return
        reads = [r.name if hasattr(r, "name") else r for r in reads]
        writes = [w.name if hasattr(w, "name") else w for w in writes]
        writes = list(writes) + [r for r in reads if r.startswith("pb") and r not in writes]
        deps = self._deps(reads, writes)
        self._wait(e, deps)
        self.cnt[e] += 1
        self._model(e, (e, self.cnt[e]), deps, self.COST[e] if cost is None else cost)
        if not self.dry:
            inst = fn(self.eng[e])
            if self.needed is None:
                inst.then_inc(self.sem[e], 1)
            elif (e, self.cnt[e]) in self.needed:
                self.sig[e] += 1
                self.sigval[(e, self.cnt[e])] = self.sig[e]
                inst.then_inc(self.sem[e], 1)
        self._record((e, self.cnt[e]), reads, writes)

    def dma(self, e, out, in_, reads=(), writes=(), semkey=None, nowaw=False, **kw):
        reads = [r.name if hasattr(r, "name") else r for r in reads]
        writes = [w.name if hasattr(w, "name") else w for w in writes]
        self.nops += 1
        if self.limit is not None and self.nops > self.limit:
            return
        if semkey not in self.sem:
            self.sem[semkey] = self.es.enter_context(self.nc.semaphore("semd_" + str(semkey)))
            self.cnt[semkey] = 0
        deps = self._deps(reads, writes, nowaw)
        self._wait(e, deps)
        self.cnt[semkey] += 16
        self._model(e, (semkey, self.cnt[semkey]), deps, 2.0)
        if not self.dry:
            inst = self.eng[e].dma_start(out=out, in_=in_, **kw)
            inst.then_inc(self.sem[semkey], 16)
        self._record((semkey, self.cnt[semkey]), reads, writes)

    def barrier(self):
        for e in self.eng:
            for src, val in self.cnt.items():
                if val > 0 and self.waited.get((e, src), 0) < val:
                    if self.dry:
                        self.need_out.add((src, val))
                    else:
                        self.eng[e].wait_ge(self.sem[src], self._semval(src, val))
                    self.waited[(e, src)] = val
        self.last_write = {}
        self.readers = {}

    def finish(self, e="sp"):
        for src, val in self.cnt.items():
            if val > 0 and self.waited.get((e, src), 0) < val:
                if self.dry:
                    self.need_out.add((src, val))
                else:
                    self.eng[e].wait_ge(self.sem[src], self._semval(src, val))
                self.waited[(e, src)] = val
        self.es.close()


def bc(ap, shape):
    return ap.to_broadcast(list(shape))


NLOCAL = 16


def build_program(ntiles=NTILES, do_sample=True, limit=None, nl=None):
    if nl is None:
        nl = NLOCAL if ntiles >= NLOCAL else ntiles
    plan = _build(ntiles, do_sample, limit, None, nl)
    return _build(ntiles, do_sample, limit, plan, nl)


def _build(ntiles, do_sample, limit, plan, NL):
    nc = bass.Bass("TRN2", target_bir_lowering=False)
    k = KB(nc)
    k.limit = limit
    if plan is None:
        k.dry = True
    else:
        k.needed = plan
    NT = ntiles
    T = NT * 128

    def din(name, shape):
        return nc.dram_tensor(name, list(shape), F32, kind="ExternalInput").ap()

    def dout(name, shape):
        return nc.dram_tensor(name, list(shape), F32, kind="ExternalOutput").ap()

    x_d = din("x", [T, D])
    c_d = din("c", [1, D])
    wada_d = din("w_ada", [D, 3 * D])
    bada_d = din("b_ada", [1, 3 * D])
    wf_d = din("w_f", [D, 1536])
    wt_d = din("w_t", [D, WT_COLS])
    wo_d = din("w_o", [D, D])
    convw_d = din("conv_w", [4, 1536])
    alog_d = din("a_log", [1, 4])
    dtb_d = din("dt_bias", [1, 4])
    norma_d = din("norm_a", [1, 128])
    sinks_d = din("sinks_p", [1, 8])
    lng_d = din("ln_g", [1, D])
    lnb_d = din("ln_b", [1, D])
    cosk_d = din("cosk", [T, 32])
    sink_d = din("sink", [T, 32])

    y_d = dout("y", [NL * 128, D])
    valid_d = din("valid", [1, NT])
    F0 = NT - NL

    def full(n):
        return n >= F0

    def swaprep(n):
        return n == F0 - 1
    convp_d = dout("conv_p", [3, 1536])
    deltap_d = dout("delta_p", [4, 128, 128])
    swak_d = dout("swa_k_p", [128, 128])
    swav_d = dout("swa_v_p", [128, 128])

    if do_sample:
        xs_d = din("xs", [NS, D])
        cs_d = din("cs", [NS, D])
        sconv_d = din("s_conv", [NS, 3, 1536])
        sdelta_d = din("s_delta", [NS, 4, 128, 128])
        sk_d = din("s_k", [NS, 128, 128])
        sv_d = din("s_v", [NS, 128, 128])
        coss_d = din("cos_s", [1, 32])
        sins_d = din("sin_s", [1, 32])
        ys_d = dout("ys", [NS, D])
        convs_d = dout("conv_s", [NS, 3, 1536])
        deltas_d = dout("delta_s", [NS, 4, 128, 128])
        swaks_d = dout("swa_k_s", [NS, 128, 128])
        swavs_d = dout("swa_v_s", [NS, 128, 128])

    ident = k.sb("ident", [128, 128])
    U = k.sb("U", [128, 128])
    ones = k.sb("ones", [128, 128])
    onesb = k.sb("onesb", [128, 128], BF16)
    negA = k.sb("negA", [128, 128])
    negB = k.sb("negB", [128, 128])
    swam = k.sb("swam", [128, 256])
    swam0 = k.sb("swam0", [128, 256])

    k.op("pool", lambda g: g.memset(ones[:], 1.0), writes=[ones])
    k.op("pool", lambda g: g.memset(onesb[:], 1.0), writes=[onesb])
    k.op("pool", lambda g: g.affine_select(out=ident[:], in_=ones[:], pattern=[[-1, 128]], compare_op=ALU.is_equal,
                                            fill=0.0, base=0, channel_multiplier=1), reads=[ones], writes=[ident])
    k.op("pool", lambda g: g.affine_select(out=U[:], in_=ones[:], pattern=[[1, 128]], compare_op=ALU.is_ge,
                                            fill=0.0, base=0, channel_multiplier=-1), reads=[ones], writes=[U])
    zer = k.sb("zer", [128, 256])
    k.op("pool", lambda g: g.memset(zer[:], 0.0), writes=[zer])
    k.op("pool", lambda g: g.affine_select(out=negA[:], in_=zer[:, 0:128], pattern=[[-1, 128]], compare_op=ALU.is_ge,
                                            fill=NEG, base=-1, channel_multiplier=1), reads=[zer], writes=[negA])
    k.op("pool", lambda g: g.affine_select(out=negB[:], in_=zer[:, 0:128], pattern=[[1, 128]], compare_op=ALU.is_ge,
                                            fill=NEG, base=0, channel_multiplier=-1), reads=[zer], writes=[negB])
    swamt = k.sb("swamt", [128, 256])
    k.op("pool", lambda g: g.affine_select(out=swamt[:], in_=zer[:], pattern=[[1, 256]], compare_op=ALU.is_ge,
                                            fill=NEG, base=0, channel_multiplier=-1), reads=[zer], writes=[swamt])
    k.op("pool", lambda g: g.affine_select(out=swam[:], in_=swamt[:], pattern=[[-1, 256]], compare_op=ALU.is_ge,
                                            fill=NEG, base=128, channel_multiplier=1), reads=[swamt], writes=[swam])
    k.op("pool", lambda g: g.memset(swam0[:, 0:128], NEG), writes=[swam0])
    k.op("pool", lambda g: g.tensor_copy(out=swam0[:, 128:256], in_=swam[:, 128:256]), reads=[swam], writes=[swam0])

    PB = [k.ps("pb%d" % i) for i in range(8)]
    def load_bc(name, src, n, parts=128):
        t = k.sb(name, [parts, n])
        k.dma("sp", t[:], src.partition_broadcast(parts), writes=[t], semkey="ld_" + name)
        return t

    lng_bc = load_bc("lng_bc", lng_d[0], D)
    lnb_bc = load_bc("lnb_bc", lnb_d[0], D)
    norma_bc = load_bc("norma_bc", norma_d[0], 128)
    sinks_bc = load_bc("sinks_bc", sinks_d[0], 8)
    valid_bc = load_bc("valid_bc", valid_d[0], NT)
    alog_bc = load_bc("alog_bc", alog_d[0], 4)
    dtb_bc = load_bc("dtb_bc", dtb_d[0], 4)
    cwT = k.sb("cwT", [128, 48])
    bada_fm = k.sb("bada_fm", [128, 24])
    cT = k.sb("cT", [128, 8])
    rowst = k.sb("rowst", [80, 128])
    k.dma("sp", rowst[0:48, :], convw_d.rearrange("j (c p) -> (j c) p", p=128), writes=[rowst], semkey="ld_rowst", nowaw=True)
    k.dma("sp", rowst[48:72, :], bada_d[0].rearrange("(j p) -> j p", p=128), writes=[rowst], semkey="ld_rowst", nowaw=True)
    k.dma("sp", rowst[72:80, :], c_d[0].rearrange("(j p) -> j p", p=128), writes=[rowst], semkey="ld_rowst", nowaw=True)
    cst = [k.sb("cst%d" % i, [128, 2, 32]) for i in range(2)]
    k.op("pe", lambda p: p.transpose(out=PB[0][:, 0:80], in_=rowst[:, :], identity=ident[0:80, 0:80]), reads=[rowst, ident], writes=[PB[0]])
    k.op("dve", lambda v: v.tensor_copy(out=cwT[:], in_=PB[0][:, 0:48]), reads=[PB[0]], writes=[cwT])
    k.op("dve", lambda v: v.tensor_copy(out=bada_fm[:], in_=PB[0][:, 48:72]), reads=[PB[0]], writes=[bada_fm])
    k.op("dve", lambda v: v.tensor_copy(out=cT[:], in_=PB[0][:, 72:80]), reads=[PB[0]], writes=[cT])
    ea = k.sb("ea", [128, 4])
    k.op("act", lambda a: a.activation(out=ea[:], in_=alog_bc[:], func=ACT.Exp), reads=[alog_bc], writes=[ea])
    k.op("dve", lambda v: v.tensor_scalar(out=ea[:], in0=ea[:], scalar1=-1.0, scalar2=None, op0=ALU.mult),
         reads=[ea], writes=[ea])

    Wf = k.sb("Wf", [128, 8, 1536], BF16)
    Wt = k.sb("Wt", [128, 8, WT_COLS], BF16)
    Wo = k.sb("Wo", [128, 8, D], BF16)
    mod_fm = k.sb("mod_fm", [128, 16])
    g1bc = k.sb("g1bc", [128, D])
    k1s = contextlib.ExitStack()
    if do_sample:
        csT = k1s.enter_context(nc.sbuf_tensor("csT", [128, 8, NS], F32))
        mod_s = k1s.enter_context(nc.sbuf_tensor("mod_s", [NS, 3 * D], F32))
    k2 = contextlib.ExitStack()
    def sb2(name, shape, dt=F32):
        return k2.enter_context(nc.sbuf_tensor(name, list(shape), dt))
    stg = [sb2("stg%d" % i, [128, 1800]) for i in range(2)]
    bada_bc = sb2("bada_bc", [128, D])
    k.dma("sp", bada_bc[:], bada_d[0, 2 * D:3 * D].partition_broadcast(128), writes=[bada_bc], semkey="ld_bada_bc")
    si = 0
    cast_engs = ["dve", "pool", "act"]

    def load_cast(dst, src_d, ncols):
        nonlocal si
        per = 2048 // ncols if ncols <= 2048 else 0
        for kc in range(8):
            s = stg[si % 2]
            k.dma("sp", s[:, 0:ncols], src_d[kc * 128:(kc + 1) * 128, :], writes=[s], semkey="ld_" + s.name)
            e = cast_engs[si % 3]
            if e == "act":
                k.op(e, lambda a, s=s, kc=kc: a.copy(out=dst[:, kc, :], in_=s[:, 0:ncols]), reads=[s], writes=[dst])
            else:
                k.op(e, lambda v, s=s, kc=kc: v.tensor_copy(out=dst[:, kc, :], in_=s[:, 0:ncols]), reads=[s], writes=[dst])
            si += 1

    load_cast(Wf, wf_d, 1536)
    load_cast(Wt, wt_d, WT_COLS)
    load_cast(Wo, wo_d, D)


    wa = [sb2("wa%d" % i, [128, 8, 512]) for i in range(1)]
    c_bcT = sb2("c_bcT", [128, 8, 128])
    k.op("pool", lambda g: g.tensor_copy(out=c_bcT[:], in_=bc(cT[:].rearrange("p (a b) -> p a b", b=1), [128, 8, 128])),
         reads=[cT], writes=[c_bcT])
    if do_sample:
        cs_sb = sb2("cs_sb", [NS, D])
        k.dma("sp", cs_sb[:], cs_d[:, :], writes=[cs_sb], semkey="ld_cs")
        for kc in range(8):
            k.op("pe", lambda p, kc=kc: p.transpose(out=PB[0][:, kc * NS:(kc + 1) * NS], in_=cs_sb[:, kc * 128:(kc + 1) * 128],
                                                    identity=ident[0:NS, 0:NS]), reads=[cs_sb, ident], writes=[PB[0]])
        k.op("dve", lambda v: v.tensor_copy(out=csT[:].rearrange("p a b -> p (a b)"), in_=PB[0][:, 0:8 * NS]),
             reads=[PB[0]], writes=[csT])
        bada_s = sb2("bada_s", [NS, 3 * D])
        k.dma("sp", bada_s[:], bada_d[0].partition_broadcast(NS), writes=[bada_s], semkey="ld_bada_s")
    for j in range(6):
        w = wa[0]
        for kc in range(8):
            k.dma("sp", w[:, kc, :], wada_d[kc * 128:(kc + 1) * 128, j * 512:(j + 1) * 512], writes=[w],
                  semkey="ld_" + w.name, nowaw=True)
        if j < 4:
            for sub in range(4):
                col = j * 4 + sub
                for kc in range(8):
                    k.op("pe", lambda p, kc=kc, sub=sub, col=col, w=w: p.matmul(
                        PB[1][:, col:col + 1], lhsT=w[:, kc, sub * 128:(sub + 1) * 128], rhs=cT[:, kc:kc + 1],
                        start=(kc == 0), stop=(kc == 7)), reads=[w, cT], writes=[PB[1]])
        else:
            for kc in range(8):
                k.op("pe", lambda p, kc=kc, w=w: p.matmul(PB[2 + (j - 4)][:, :], lhsT=c_bcT[:, kc, :], rhs=w[:, kc, :],
                                                          start=(kc == 0), stop=(kc == 7)),
                     reads=[w, c_bcT], writes=[PB[2 + (j - 4)]])
        if do_sample:
            for kc in range(8):
                k.op("pe", lambda p, kc=kc, w=w: p.matmul(PB[4 + j % 2][0:NS, :], lhsT=csT[:, kc, :], rhs=w[:, kc, :],
                                                          start=(kc == 0), stop=(kc == 7)),
                     reads=[w, csT], writes=[PB[4 + j % 2]])
            k.op("dve", lambda v, j=j: v.tensor_tensor(out=mod_s[:, j * 512:(j + 1) * 512], in0=PB[4 + j % 2][0:NS, :],
                                                       in1=bada_s[:, j * 512:(j + 1) * 512], op=ALU.add),
                 reads=[PB[4 + j % 2], bada_s], writes=[mod_s])
    k.op("dve", lambda v: v.tensor_tensor(out=mod_fm[:], in0=PB[1][:, 0:16], in1=bada_fm[:, 0:16], op=ALU.add),
         reads=[PB[1], bada_fm], writes=[mod_fm])
    k.op("dve", lambda v: v.tensor_scalar(out=mod_fm[:, 8:16], in0=mod_fm[:, 8:16], scalar1=1.0, scalar2=None, op0=ALU.add),
         reads=[mod_fm], writes=[mod_fm])
    for j in range(2):
        k.op("dve", lambda v, j=j: v.scalar_tensor_tensor(out=g1bc[:, j * 512:(j + 1) * 512], in0=PB[2 + j][:, :], scalar=1.0,
                                                           in1=bada_bc[:, j * 512:(j + 1) * 512], op0=ALU.add, op1=ALU.add),
             reads=[PB[2 + j], bada_bc], writes=[g1bc])


    if do_sample:
        k.barrier()
        k2.close()
        k2 = contextlib.ExitStack()
        P16 = NS
        sinkcol_d = din("sinks_col", [128, 1])
        xs = sb2("xs_sb", [P16, D]); hs = sb2("hs", [P16, D]); mix_s = sb2("mix_s", [P16, D])
        hsT = sb2("hsT", [128, 8, P16], BF16)
        pqkv = sb2("pqkv", [P16, 1536])
        qkv_s = sb2("qkv_s", [P16, 12, 128])
        zAs_s = sb2("zAs_s", [P16, 512]); zBs_s = sb2("zBs_s", [P16, 512])
        qr_s = sb2("qr_s", [P16, 512]); kr_s = sb2("kr_s", [P16, 128]); v_s = sb2("v_s", [P16, 128])
        vsb = sb2("vsb", [P16, 128], BF16)
        bd_s = sb2("bd_s", [P16, 8]); bdt_s = sb2("bdt_s", [P16, 8])
        beta_s = sb2("beta_s", [P16, 4]); nbeta_s = sb2("nbeta_s", [P16, 4]); g_s = sb2("g_s", [P16, 4]); eg_s = sb2("eg_s", [P16, 4])
        cs16 = sb2("cs16", [P16, 2, 32])
        k.dma("sp", cs16[:, 0, :], coss_d[0].partition_broadcast(P16), writes=[cs16], semkey="ld_cs16", nowaw=True)
        k.dma("sp", cs16[:, 1, :], sins_d[0].partition_broadcast(P16), writes=[cs16], semkey="ld_cs16", nowaw=True)
        k.dma("sp", xs[:], xs_d[:, :], writes=[xs], semkey="ld_xs")
        k.op("dve", lambda v: v.scalar_tensor_tensor(out=hs[:], in0=mod_s[:, D:2 * D], scalar=1.0, in1=xs[:], op0=ALU.add, op1=ALU.mult),
             reads=[mod_s, xs], writes=[hs])
        k.op("dve", lambda v: v.tensor_tensor(out=hs[:], in0=hs[:], in1=mod_s[:, 0:D], op=ALU.add), reads=[hs, mod_s], writes=[hs])
        for kc in range(8):
            k.op("pe", lambda p, kc=kc: p.transpose(out=PB[0][:, kc * P16:(kc + 1) * P16], in_=hs[:, kc * 128:(kc + 1) * 128],
                                                    identity=ident[0:P16, 0:P16]), reads=[hs, ident], writes=[PB[0]])
        k.op("dve", lambda v: v.tensor_copy(out=hsT[:].rearrange("p a b -> p (a b)"), in_=PB[0][:, 0:8 * P16]), reads=[PB[0]], writes=[hsT])
        for j in range(3):
            for kc in range(8):
                k.op("pe", lambda p, j=j, kc=kc: p.matmul(PB[1 + j][0:P16, :], lhsT=hsT[:, kc, :], rhs=Wf[:, kc, j * 512:(j + 1) * 512],
                                                          start=(kc == 0), stop=(kc == 7)), reads=[hsT, Wf], writes=[PB[1 + j]])
        offs = [(0, 512), (512, 512), (1024, 512), (1536, 264)]
        for j, (o, w_) in enumerate(offs):
            for kc in range(8):
                k.op("pe", lambda p, j=j, o=o, w_=w_, kc=kc: p.matmul(PB[4 + j][0:P16, 0:w_], lhsT=hsT[:, kc, :], rhs=Wt[:, kc, o:o + w_],
                                                                      start=(kc == 0), stop=(kc == 7)), reads=[hsT, Wt], writes=[PB[4 + j]])
        for j in range(3):
            k.op("dve", lambda v, j=j: v.tensor_copy(out=pqkv[:, j * 512:(j + 1) * 512], in_=PB[1 + j][0:P16, :]), reads=[PB[1 + j]], writes=[pqkv])
        k.op("act", lambda a: a.activation(out=zAs_s[:], in_=PB[4][0:P16, :], func=ACT.Silu), reads=[PB[4]], writes=[zAs_s])
        k.op("act", lambda a: a.activation(out=zBs_s[:], in_=PB[6][0:P16, :], func=ACT.Silu), reads=[PB[6]], writes=[zBs_s])
        rts = [sb2("rts%d" % i, [P16, 8, 32]) for i in range(4)]
        q3 = PB[5][0:P16, :].rearrange("p (a b) -> p a b", a=8)
        qr3 = qr_s[:].rearrange("p (a b) -> p a b", a=8)
        cq = bc(cs16[:, 0:1, :], [P16, 8, 32]); sq_ = bc(cs16[:, 1:2, :], [P16, 8, 32])
        k.op("dve", lambda v: v.tensor_tensor(out=rts[0][:], in0=q3[:, :, 0:32], in1=cq, op=ALU.mult), reads=[PB[5], cs16], writes=[rts[0]])
        k.op("dve", lambda v: v.tensor_tensor(out=rts[1][:], in0=q3[:, :, 32:64], in1=sq_, op=ALU.mult), reads=[PB[5], cs16], writes=[rts[1]])
        k.op("dve", lambda v: v.tensor_tensor(out=rts[2][:], in0=q3[:, :, 32:64], in1=cq, op=ALU.mult), reads=[PB[5], cs16], writes=[rts[2]])
        k.op("dve", lambda v: v.tensor_tensor(out=rts[3][:], in0=q3[:, :, 0:32], in1=sq_, op=ALU.mult), reads=[PB[5], cs16], writes=[rts[3]])
        k.op("dve", lambda v: v.tensor_tensor(out=qr3[:, :, 0:32], in0=rts[0][:], in1=rts[1][:], op=ALU.subtract), reads=[rts[0], rts[1]], writes=[qr_s])
        k.op("dve", lambda v: v.tensor_tensor(out=qr3[:, :, 32:64], in0=rts[2][:], in1=rts[3][:], op=ALU.add), reads=[rts[2], rts[3]], writes=[qr_s])
        k.op("dve", lambda v: v.tensor_scalar(out=qr_s[:], in0=qr_s[:], scalar1=0.125, scalar2=None, op0=ALU.mult), reads=[qr_s], writes=[qr_s])
        k3 = PB[7][0:P16, 0:128].rearrange("p (a b) -> p a b", a=2)
        kr3 = kr_s[:].rearrange("p (a b) -> p a b", a=2)
        ck = bc(cs16[:, 0:1, :], [P16, 2, 32]); sk_ = bc(cs16[:, 1:2, :], [P16, 2, 32])
        k.op("dve", lambda v: v.tensor_tensor(out=rts[0][:, 0:2, :], in0=k3[:, :, 0:32], in1=ck, op=ALU.mult), reads=[PB[7], cs16], writes=[rts[0]])
        k.op("dve", lambda v: v.tensor_tensor(out=rts[1][:, 0:2, :], in0=k3[:, :, 32:64], in1=sk_, op=ALU.mult), reads=[PB[7], cs16], writes=[rts[1]])
        k.op("dve", lambda v: v.tensor_tensor(out=rts[2][:, 0:2, :], in0=k3[:, :, 32:64], in1=ck, op=ALU.mult), reads=[PB[7], cs16], writes=[rts[2]])
        k.op("dve", lambda v: v.tensor_tensor(out=rts[3][:, 0:2, :], in0=k3[:, :, 0:32], in1=sk_, op=ALU.mult), reads=[PB[7], cs16], writes=[rts[3]])
        k.op("dve", lambda v: v.tensor_tensor(out=kr3[:, :, 0:32], in0=rts[0][:, 0:2, :], in1=rts[1][:, 0:2, :], op=ALU.subtract), reads=[rts[0], rts[1]], writes=[kr_s])
        k.op("dve", lambda v: v.tensor_tensor(out=kr3[:, :, 32:64], in0=rts[2][:, 0:2, :], in1=rts[3][:, 0:2, :], op=ALU.add), reads=[rts[2], rts[3]], writes=[kr_s])
        k.op("dve", lambda v: v.tensor_copy(out=v_s[:], in_=PB[7][0:P16, 128:256]), reads=[PB[7]], writes=[v_s])
        k.op("dve", lambda v: v.tensor_copy(out=vsb[:], in_=PB[7][0:P16, 128:256]), reads=[PB[7]], writes=[vsb])
        k.op("dve", lambda v: v.tensor_copy(out=bd_s[:], in_=PB[7][0:P16, 256:264]), reads=[PB[7]], writes=[bd_s])
        k.op("act", lambda a: a.activation(out=bdt_s[:, 0:4], in_=bd_s[:, 0:4], func=ACT.Exp, scale=-1.0), reads=[bd_s], writes=[bdt_s])
        k.op("dve", lambda v: v.tensor_scalar(out=bdt_s[:, 0:4], in0=bdt_s[:, 0:4], scalar1=1.0, scalar2=None, op0=ALU.add), reads=[bdt_s], writes=[bdt_s])
        k.op("dve", lambda v: v.reciprocal(out=beta_s[:], in_=bdt_s[:, 0:4]), reads=[bdt_s], writes=[beta_s])
        k.op("dve", lambda v: v.tensor_scalar(out=nbeta_s[:], in0=beta_s[:], scalar1=-1.0, scalar2=None, op0=ALU.mult), reads=[beta_s], writes=[nbeta_s])
        k.op("dve", lambda v: v.tensor_tensor(out=bd_s[:, 4:8], in0=bd_s[:, 4:8], in1=dtb_bc[0:P16, :], op=ALU.add), reads=[bd_s, dtb_bc], writes=[bd_s])
        k.op("act", lambda a: a.activation(out=bdt_s[:, 4:8], in_=bd_s[:, 4:8], func=ACT.Exp), reads=[bd_s], writes=[bdt_s])
        k.op("act", lambda a: a.activation(out=bdt_s[:, 4:8], in_=bdt_s[:, 4:8], func=ACT.Ln, bias=1.0, scale=1.0), reads=[bdt_s], writes=[bdt_s])
        k.op("dve", lambda v: v.tensor_tensor(out=g_s[:], in0=bdt_s[:, 4:8], in1=ea[0:P16, :], op=ALU.mult), reads=[bdt_s, ea], writes=[g_s])
        k.op("act", lambda a: a.activation(out=eg_s[:], in_=g_s[:], func=ACT.Exp), reads=[g_s], writes=[eg_s])
        k3s = contextlib.ExitStack()
        def sb3(name, shape, dt=F32):
            return k3s.enter_context(nc.sbuf_tensor(name, list(shape), dt))
        xp4 = sb3("xp4", [P16, 4, 1536]); cwb = sb3("cwb", [P16, 4, 1536]); tmpc = xp4
        acc_s = sb3("acc_s", [P16, 1536])
        k.dma("sp", xp4[:, 0:3, :], sconv_d[:, :, :], writes=[xp4], semkey="ld_xp4", nowaw=True)
        k.dma("sp", cwb[:].rearrange("p a b -> p (a b)"), convw_d.rearrange("a b -> (a b)").partition_broadcast(P16), writes=[cwb], semkey="ld_cwb")
        k.op("act", lambda a: a.copy(out=xp4[:, 3, :], in_=pqkv[:]), reads=[pqkv], writes=[xp4])
        k.dma("sp", convs_d[:, :, :], xp4[:, 1:4, :], reads=[xp4], semkey="st_smisc")
        k.op("dve", lambda v: v.tensor_tensor(out=tmpc[:], in0=xp4[:], in1=cwb[:], op=ALU.mult), reads=[xp4, cwb], writes=[tmpc])
        k.op("dve", lambda v: v.tensor_reduce(out=acc_s[:], in_=tmpc[:].rearrange("p j c -> p c j"), axis=AX.X, op=ALU.add), reads=[tmpc], writes=[acc_s])
        k.op("act", lambda a: a.activation(out=qkv_s[:].rearrange("p a b -> p (a b)"), in_=acc_s[:], func=ACT.Silu), reads=[acc_s], writes=[qkv_s])
        sqs = sb3("sqs", [P16, 8, 128]); sss = sb3("sss", [P16, 8])
        k.op("dve", lambda v: v.tensor_tensor(out=sqs[:], in0=qkv_s[:, 0:8, :], in1=qkv_s[:, 0:8, :], op=ALU.mult), reads=[qkv_s], writes=[sqs])
        k.op("dve", lambda v: v.tensor_reduce(out=sss[:], in_=sqs[:], axis=AX.X, op=ALU.add), reads=[sqs], writes=[sss])
        k.op("act", lambda a: a.activation(out=sss[:], in_=sss[:], func=ACT.Ln, bias=L2_EPS, scale=1.0), reads=[sss], writes=[sss])
        k.op("act", lambda a: a.activation(out=sss[:, 0:4], in_=sss[:, 0:4], func=ACT.Exp, bias=float(-0.5 * np.log(128.0)), scale=-0.5), reads=[sss], writes=[sss])
        k.op("act", lambda a: a.activation(out=sss[:, 4:8], in_=sss[:, 4:8], func=ACT.Exp, scale=-0.5), reads=[sss], writes=[sss])
        k.op("dve", lambda v: v.tensor_tensor(out=qkv_s[:, 0:8, :], in0=qkv_s[:, 0:8, :], in1=bc(sss[:].rearrange("p (a b) -> p a b", b=1), [P16, 8, 128]), op=ALU.mult),
             reads=[qkv_s, sss], writes=[qkv_s])
        k.barrier()
        k3s.close()
        k3s = contextlib.ExitStack()
        Ssb = sb3("Ssb", [128, P16 * 4, 128])
        k.dma("sp", Ssb[:], sdelta_d.rearrange("b h k v -> k (b h) v"), writes=[Ssb], semkey="ld_Ssb")
        qkT_s = sb3("qkT_s", [128, 8, P16])
        for c in range(8):
            k.op("pe", lambda p, c=c: p.transpose(out=PB[0][:, c * P16:(c + 1) * P16], in_=qkv_s[:, c, :], identity=ident[0:P16, 0:P16]),
                 reads=[qkv_s, ident], writes=[PB[0]])
        k.op("dve", lambda v: v.tensor_copy(out=qkT_s[:].rearrange("p a b -> p (a b)"), in_=PB[0][:, 0:8 * P16]), reads=[PB[0]], writes=[qkT_s])
        dmask = sb3("dmask", [P16, P16, 128])
        k.op("pool", lambda g: g.tensor_copy(out=dmask[:], in_=bc(ident[0:P16, 0:P16].rearrange("p (a b) -> p a b", b=1), [P16, P16, 128])),
             reads=[ident], writes=[dmask])
        egm = sb3("egm", [P16, P16, 4]); egbc = sb3("egbc", [128, P16 * 4])
        k.op("dve", lambda v: v.tensor_tensor(out=egm[:], in0=bc(eg_s[:].rearrange("p (a b) -> p a b", a=1), [P16, P16, 4]),
                                              in1=bc(ident[0:P16, 0:P16].rearrange("p (a b) -> p a b", b=1), [P16, P16, 4]), op=ALU.mult),
             reads=[eg_s, ident], writes=[egm])
        k.op("pe", lambda p: p.matmul(PB[1][:, 0:P16 * 4], lhsT=ones[0:P16, :], rhs=egm[:].rearrange("p a b -> p (a b)"), start=True, stop=True),
             reads=[ones, egm], writes=[PB[1]])
        k.op("dve", lambda v: v.tensor_copy(out=egbc[:], in_=PB[1][:, 0:P16 * 4]), reads=[PB[1]], writes=[egbc])
        pred = sb3("pred", [P16, 4, 128]); qS = sb3("qS", [P16, 4, 128]); tmpd = sb3("tmpd", [P16, P16, 128])
        dd = sb3("dd", [P16, 4, 128]); Dm = sb3("Dm", [P16, P16, 128]); o_s = sb3("o_s", [P16, 4, 128])
        qk_s = sb3("qk_s", [P16, 4]); qkt = sb3("qkt", [P16, 4, 128])
        k.op("dve", lambda v: v.tensor_tensor(out=qkt[:], in0=qkv_s[:, 0:4, :], in1=qkv_s[:, 4:8, :], op=ALU.mult), reads=[qkv_s], writes=[qkt])
        k.op("dve", lambda v: v.tensor_reduce(out=qk_s[:], in_=qkt[:], axis=AX.X, op=ALU.add), reads=[qkt], writes=[qk_s])
        for h in range(4):
            for which, dst in ((4, pred), (0, qS)):
                banks = [PB[2], PB[3], PB[4], PB[5]] if which == 4 else [PB[6], PB[7], PB[0], PB[1]]
                for b in range(P16):
                    pb = banks[b // 4]
                    k.op("pe", lambda p, b=b, pb=pb, which=which, h=h: p.matmul(pb[0:P16, (b % 4) * 128:(b % 4 + 1) * 128], lhsT=qkT_s[:, which + h, :],
                                                                               rhs=Ssb[:, b * 4 + h, :], start=True, stop=True),
                         reads=[qkT_s, Ssb], writes=[pb])
                for j in range(4):
                    k.op("dve", lambda v, j=j, banks=banks: v.tensor_tensor(out=tmpd[:, 4 * j:4 * j + 4, :], in0=banks[j][0:P16, :].rearrange("p (a b) -> p a b", a=4),
                                                                            in1=dmask[:, 4 * j:4 * j + 4, :], op=ALU.mult), reads=[banks[j], dmask], writes=[tmpd])
                k.op("dve", lambda v, dst=dst, h=h: v.tensor_reduce(out=dst[:, h, :], in_=tmpd[:].rearrange("p b v -> p v b"), axis=AX.X, op=ALU.add),
                     reads=[tmpd], writes=[dst])
            k.op("dve", lambda v, h=h: v.scalar_tensor_tensor(out=dd[:, h, :], in0=pred[:, h, :], scalar=eg_s[:, h:h + 1], in1=qkv_s[:, 8 + h, :],
                                                               op0=ALU.mult, op1=ALU.subtract), reads=[pred, eg_s, qkv_s], writes=[dd])
            k.op("dve", lambda v, h=h: v.tensor_scalar(out=dd[:, h, :], in0=dd[:, h, :], scalar1=nbeta_s[:, h:h + 1], scalar2=None, op0=ALU.mult),
                 reads=[dd, nbeta_s], writes=[dd])
            k.op("dve", lambda v, h=h: v.tensor_scalar(out=o_s[:, h, :], in0=dd[:, h, :], scalar1=qk_s[:, h:h + 1], scalar2=None, op0=ALU.mult),
                 reads=[dd, qk_s], writes=[o_s])
            k.op("dve", lambda v, h=h: v.scalar_tensor_tensor(out=o_s[:, h, :], in0=qS[:, h, :], scalar=eg_s[:, h:h + 1], in1=o_s[:, h, :],
                                                               op0=ALU.mult, op1=ALU.add), reads=[qS, eg_s, o_s], writes=[o_s])
            k.op("dve", lambda v, h=h: v.tensor_tensor(out=Dm[:], in0=bc(dd[:, h:h + 1, :], [P16, P16, 128]), in1=dmask[:], op=ALU.mult),
                 reads=[dd, dmask], writes=[Dm])
            banks = [PB[2], PB[3], PB[4], PB[5]]
            for b in range(P16):
                pb = banks[b // 4]
                k.op("pe", lambda p, b=b, pb=pb, h=h: p.matmul(pb[:, (b % 4) * 128:(b % 4 + 1) * 128], lhsT=qkv_s[:, 4 + h, :], rhs=Dm[:, b, :],
                                                               start=True, stop=True), reads=[qkv_s, Dm], writes=[pb])
            for b in range(P16):
                pb = banks[b // 4]
                k.op("dve", lambda v, b=b, pb=pb, h=h: v.scalar_tensor_tensor(out=Ssb[:, b * 4 + h, :], in0=Ssb[:, b * 4 + h, :], scalar=egbc[:, b * 4 + h:b * 4 + h + 1],
                                                                              in1=pb[:, (b % 4) * 128:(b % 4 + 1) * 128], op0=ALU.mult, op1=ALU.add),
                     reads=[Ssb, egbc, pb], writes=[Ssb])
        k.dma("sp", deltas_d.rearrange("b h k v -> k (b h) v"), Ssb[:], reads=[Ssb], semkey="st_smisc")
        oss_s = sb3("oss_s", [P16, 4])
        k.op("dve", lambda v: v.tensor_tensor(out=qkt[:], in0=o_s[:], in1=o_s[:], op=ALU.mult), reads=[o_s], writes=[qkt])
        k.op("dve", lambda v: v.tensor_reduce(out=oss_s[:], in_=qkt[:], axis=AX.X, op=ALU.add), reads=[qkt], writes=[oss_s])
        k.op("act", lambda a: a.activation(out=oss_s[:], in_=oss_s[:], func=ACT.Ln, bias=RMS_EPS, scale=1.0 / 128.0), reads=[oss_s], writes=[oss_s])
        k.op("act", lambda a: a.activation(out=oss_s[:], in_=oss_s[:], func=ACT.Exp, scale=-0.5), reads=[oss_s], writes=[oss_s])
        k.op("dve", lambda v: v.tensor_tensor(out=o_s[:], in0=o_s[:], in1=bc(oss_s[:].rearrange("p (a b) -> p a b", b=1), [P16, 4, 128]), op=ALU.mult),
             reads=[o_s, oss_s], writes=[o_s])
        k.op("dve", lambda v: v.tensor_tensor(out=o_s[:], in0=o_s[:], in1=bc(norma_bc[0:P16, :].rearrange("p (a b) -> p a b", a=1), [P16, 4, 128]), op=ALU.mult),
             reads=[o_s, norma_bc], writes=[o_s])
        k.op("dve", lambda v: v.tensor_tensor(out=mix_s[:, 0:512], in0=o_s[:].rearrange("p a b -> p (a b)"), in1=zAs_s[:], op=ALU.mult),
             reads=[o_s, zAs_s], writes=[mix_s])
        k.barrier()
        k3s.close()
        k3s = contextlib.ExitStack()
        Kc = sb3("Kc", [128, P16, 128]); Vc = sb3("Vc", [128, P16, 128])
        KcT = sb3("KcT", [128, P16, 128], BF16); VcB = sb3("VcB", [128, P16, 128], BF16)
        k.dma("sp", Kc[:], sk_d.rearrange("b s c -> s b c"), writes=[Kc], semkey="ld_Kc")
        k.dma("sp", Vc[:], sv_d.rearrange("b s c -> s b c"), writes=[Vc], semkey="ld_Vc")
        k.dma("sp", swaks_d[:, 0:127, :], sk_d[:, 1:128, :], semkey="st_smisc")
        k.dma("sp", swavs_d[:, 0:127, :], sv_d[:, 1:128, :], semkey="st_smisc")
        k.dma("sp", swaks_d[:, 127, :], kr_s[:], reads=[kr_s], semkey="st_smisc")
        k.dma("sp", swavs_d[:, 127, :], v_s[:], reads=[v_s], semkey="st_smisc")
        k.op("pool", lambda g: g.tensor_copy(out=VcB[:], in_=Vc[:]), reads=[Vc], writes=[VcB])
        for b in range(P16):
            pb = [PB[2], PB[3], PB[4], PB[5]][b // 4]
            k.op("pe", lambda p, b=b, pb=pb: p.transpose(out=pb[:, (b % 4) * 128:(b % 4 + 1) * 128], in_=Kc[:, b, :], identity=ident[:]),
                 reads=[Kc, ident], writes=[pb])
        for j in range(4):
            pb = [PB[2], PB[3], PB[4], PB[5]][j]
            k.op("act", lambda a, j=j, pb=pb: a.copy(out=KcT[:, 4 * j:4 * j + 4, :], in_=pb[:, :].rearrange("p (a b) -> p a b", a=4)), reads=[pb], writes=[KcT])
        qT_s = sb3("qT_s", [128, 4, P16]); Aq = sb3("Aq", [128, P16, 2, 4], BF16)
        knT = sb3("knT", [128, P16], BF16); zBT = sb3("zBT", [128, 4, P16])
        for c in range(4):
            k.op("pe", lambda p, c=c: p.transpose(out=PB[6][:, c * P16:(c + 1) * P16], in_=qr_s[:, c * 128:(c + 1) * 128], identity=ident[0:P16, 0:P16]),
                 reads=[qr_s, ident], writes=[PB[6]])
        k.op("pe", lambda p: p.transpose(out=PB[6][:, 4 * P16:5 * P16], in_=kr_s[:], identity=ident[0:P16, 0:P16]), reads=[kr_s, ident], writes=[PB[6]])
        for c in range(4):
            k.op("pe", lambda p, c=c: p.transpose(out=PB[6][:, (5 + c) * P16:(6 + c) * P16], in_=zBs_s[:, c * 128:(c + 1) * 128], identity=ident[0:P16, 0:P16]),
                 reads=[zBs_s, ident], writes=[PB[6]])
        k.op("dve", lambda v: v.tensor_copy(out=qT_s[:].rearrange("p a b -> p (a b)"), in_=PB[6][:, 0:4 * P16]), reads=[PB[6]], writes=[qT_s])
        k.op("dve", lambda v: v.tensor_copy(out=knT[:], in_=PB[6][:, 4 * P16:5 * P16]), reads=[PB[6]], writes=[knT])
        k.op("dve", lambda v: v.tensor_copy(out=zBT[:].rearrange("p a b -> p (a b)"), in_=PB[6][:, 5 * P16:9 * P16]), reads=[PB[6]], writes=[zBT])
        k.op("pool", lambda g: g.memset(Aq[:], 0.0), writes=[Aq])
        k.op("dve", lambda v: v.tensor_copy(out=Aq[0:64, :, 0, :], in_=qT_s[0:64, :, :].rearrange("p c b -> p b c")), reads=[qT_s], writes=[Aq])
        k.op("dve", lambda v: v.tensor_copy(out=Aq[64:128, :, 1, :], in_=qT_s[64:128, :, :].rearrange("p c b -> p b c")), reads=[qT_s], writes=[Aq])
        for b in range(P16):
            k.op("pe", lambda p, b=b: p.matmul(PB[7][:, b * 8:(b + 1) * 8], lhsT=KcT[:, b, :], rhs=Aq[:, b, :, :].rearrange("p a b -> p (a b)"),
                                               start=True, stop=True), reads=[KcT, Aq], writes=[PB[7]])
        STs = sb3("STs", [128, 128])
        k.op("dve", lambda v: v.tensor_copy(out=STs[:], in_=PB[7][:, 0:128]), reads=[PB[7]], writes=[STs])
        k.op("pe", lambda p: p.transpose(out=PB[0][:, 0:128], in_=STs[:], identity=ident[:]), reads=[STs, ident], writes=[PB[0]])
        k.op("pe", lambda p: p.matmul(PB[0][:, 128:128 + P16], lhsT=Aq[:].rearrange("p a b c -> p (a b c)"), rhs=knT[:], start=True, stop=True),
             reads=[Aq, knT], writes=[PB[0]])
        M2 = sb3("M2", [128, P16]); M2t = sb3("M2t", [128, P16])
        k.op("pool", lambda g: g.affine_select(out=M2t[:], in_=ones[:, 0:P16], pattern=[[-8, P16]], compare_op=ALU.is_ge, fill=0.0, base=0, channel_multiplier=1),
             reads=[ones], writes=[M2t])
        k.op("pool", lambda g: g.affine_select(out=M2[:], in_=M2t[:], pattern=[[8, P16]], compare_op=ALU.is_ge, fill=0.0, base=7, channel_multiplier=-1),
             reads=[M2t], writes=[M2])
        sm = sb3("sm", [128, 16]); tmps = sb3("tmps", [128, P16])
        sinkcol = sb3("sinkcol", [128, 1])
        k.dma("sp", sinkcol[:], sinkcol_d[:, :], writes=[sinkcol], semkey="ld_sinkcol")
        k.op("dve", lambda v: v.tensor_tensor(out=tmps[:], in0=PB[0][:, 128:128 + P16], in1=M2[:], op=ALU.mult), reads=[PB[0], M2], writes=[tmps])
        k.op("dve", lambda v: v.tensor_reduce(out=sm[:, 0:1], in_=tmps[:], axis=AX.X, op=ALU.add), reads=[tmps], writes=[sm])
        k.op("dve", lambda v: v.tensor_reduce(out=sm[:, 1:2], in_=PB[0][:, 0:128], axis=AX.X, op=ALU.max), reads=[PB[0]], writes=[sm])
        k.op("dve", lambda v: v.tensor_tensor(out=sm[:, 1:2], in0=sm[:, 1:2], in1=sm[:, 0:1], op=ALU.max), reads=[sm], writes=[sm])
        k.op("dve", lambda v: v.tensor_tensor(out=sm[:, 1:2], in0=sm[:, 1:2], in1=sinkcol[:], op=ALU.max), reads=[sm, sinkcol], writes=[sm])
        k.op("dve", lambda v: v.tensor_scalar(out=sm[:, 2:3], in0=sm[:, 1:2], scalar1=-1.0, scalar2=None, op0=ALU.mult), reads=[sm], writes=[sm])
        Ps = sb3("Ps", [128, 128])
        k.op("act", lambda a: a.activation(out=Ps[:], in_=PB[0][:, 0:128], func=ACT.Exp, bias=sm[:, 2:3], scale=1.0, accum_out=sm[:, 3:4]),
             reads=[PB[0], sm], writes=[Ps, sm])
        k.op("act", lambda a: a.activation(out=sm[:, 4:5], in_=sm[:, 0:1], func=ACT.Exp, bias=sm[:, 2:3], scale=1.0), reads=[sm], writes=[sm])
        k.op("act", lambda a: a.activation(out=sm[:, 5:6], in_=sinkcol[:], func=ACT.Exp, bias=sm[:, 2:3], scale=1.0), reads=[sm, sinkcol], writes=[sm])
        k.op("dve", lambda v: v.tensor_tensor(out=sm[:, 6:7], in0=sm[:, 3:4], in1=sm[:, 4:5], op=ALU.add), reads=[sm], writes=[sm])
        k.op("dve", lambda v: v.tensor_tensor(out=sm[:, 6:7], in0=sm[:, 6:7], in1=sm[:, 5:6], op=ALU.add), reads=[sm], writes=[sm])
        k.op("dve", lambda v: v.reciprocal(out=sm[:, 7:8], in_=sm[:, 6:7]), reads=[sm], writes=[sm])
        k.op("dve", lambda v: v.tensor_scalar(out=Ps[:], in0=Ps[:], scalar1=sm[:, 7:8], scalar2=None, op0=ALU.mult), reads=[Ps, sm], writes=[Ps])
        k.op("dve", lambda v: v.tensor_tensor(out=sm[:, 8:9], in0=sm[:, 4:5], in1=sm[:, 7:8], op=ALU.mult), reads=[sm], writes=[sm])
        PsT = sb3("PsT", [128, 128], BF16); Wd = sb3("Wd", [128, P16]); Wn = sb3("Wn", [P16, 128], BF16)
        k.op("pe", lambda p: p.transpose(out=PB[1][:, 0:128], in_=Ps[:], identity=ident[:]), reads=[Ps, ident], writes=[PB[1]])
        k.op("act", lambda a: a.copy(out=PsT[:], in_=PB[1][:, 0:128]), reads=[PB[1]], writes=[PsT])
        k.op("dve", lambda v: v.tensor_scalar(out=Wd[:], in0=M2[:], scalar1=sm[:, 8:9], scalar2=None, op0=ALU.mult), reads=[M2, sm], writes=[Wd])
        k.op("pe", lambda p: p.transpose(out=PB[1][0:P16, 128:256], in_=Wd[:], identity=ident[:]), reads=[Wd, ident], writes=[PB[1]])
        k.op("act", lambda a: a.copy(out=Wn[:], in_=PB[1][0:P16, 128:256]), reads=[PB[1]], writes=[Wn])
        for b in range(P16):
            k.op("pe", lambda p, b=b: p.matmul(PB[2][:, b * 8:(b + 1) * 8], lhsT=VcB[:, b, :], rhs=PsT[:, b * 8:(b + 1) * 8], start=True, stop=False),
                 reads=[VcB, PsT], writes=[PB[2]])
            k.op("pe", lambda p, b=b: p.matmul(PB[2][:, b * 8:(b + 1) * 8], lhsT=vsb[:], rhs=Wn[:, b * 8:(b + 1) * 8], start=False, stop=True),
                 reads=[vsb, Wn], writes=[PB[2]])
        obT = sb3("obT", [128, 4, P16])
        OT4 = PB[2][:, 0:128].rearrange("p (b k c) -> p b k c", b=P16, k=2)
        k.op("dve", lambda v: v.tensor_copy(out=obT[0:64, :, :].rearrange("p c b -> p b c"), in_=OT4[0:64, :, 0, :]), reads=[PB[2]], writes=[obT])
        k.op("dve", lambda v: v.tensor_copy(out=obT[64:128, :, :].rearrange("p c b -> p b c"), in_=OT4[64:128, :, 1, :]), reads=[PB[2]], writes=[obT])
        mixT_s = sb3("mixT_s", [128, 8, P16], BF16)
        k.op("dve", lambda v: v.tensor_tensor(out=mixT_s[:, 4:8, :], in0=obT[:], in1=zBT[:], op=ALU.mult), reads=[obT, zBT], writes=[mixT_s])
        for c in range(4):
            k.op("pe", lambda p, c=c: p.transpose(out=PB[3][:, c * P16:(c + 1) * P16], in_=mix_s[:, c * 128:(c + 1) * 128], identity=ident[0:P16, 0:P16]),
                 reads=[mix_s, ident], writes=[PB[3]])
        k.op("dve", lambda v: v.tensor_copy(out=mixT_s[:, 0:4, :].rearrange("p a b -> p (a b)"), in_=PB[3][:, 0:4 * P16]), reads=[PB[3]], writes=[mixT_s])
        for j in range(2):
            for kc in range(8):
                k.op("pe", lambda p, j=j, kc=kc: p.matmul(PB[4 + j][0:P16, :], lhsT=mixT_s[:, kc, :], rhs=Wo[:, kc, j * 512:(j + 1) * 512],
                                                          start=(kc == 0), stop=(kc == 7)), reads=[mixT_s, Wo], writes=[PB[4 + j]])
        ypre_s = sb3("ypre_s", [P16, D]); yo_s = sb3("yo_s", [P16, D]); st_s = sb3("st_s", [P16, 8])
        for j in range(2):
            k.op("dve", lambda v, j=j: v.scalar_tensor_tensor(out=ypre_s[:, j * 512:(j + 1) * 512], in0=mod_s[:, 2 * D + j * 512:2 * D + (j + 1) * 512], scalar=1.0,
                                                               in1=PB[4 + j][0:P16, :], op0=ALU.add, op1=ALU.mult), reads=[mod_s, PB[4 + j]], writes=[ypre_s])
        k.op("dve", lambda v: v.scalar_tensor_tensor(out=ypre_s[:], in0=xs[:], scalar=ALPHA, in1=ypre_s[:], op0=ALU.mult, op1=ALU.add),
             reads=[xs, ypre_s], writes=[ypre_s])
        k.op("act", lambda a: a.activation(out=yo_s[:], in_=ypre_s[:], func=ACT.Identity, accum_out=st_s[:, 0:1]), reads=[ypre_s], writes=[yo_s, st_s])
        k.op("act", lambda a: a.activation(out=yo_s[:], in_=ypre_s[:], func=ACT.Square, accum_out=st_s[:, 1:2]), reads=[ypre_s], writes=[yo_s, st_s])
        k.op("dve", lambda v: v.tensor_scalar(out=st_s[:, 2:3], in0=st_s[:, 0:1], scalar1=1.0 / D, scalar2=None, op0=ALU.mult), reads=[st_s], writes=[st_s])
        k.op("dve", lambda v: v.tensor_tensor(out=st_s[:, 3:4], in0=st_s[:, 2:3], in1=st_s[:, 2:3], op=ALU.mult), reads=[st_s], writes=[st_s])
        k.op("dve", lambda v: v.scalar_tensor_tensor(out=st_s[:, 4:5], in0=st_s[:, 1:2], scalar=1.0 / D, in1=st_s[:, 3:4], op0=ALU.mult, op1=ALU.subtract),
             reads=[st_s], writes=[st_s])
        k.op("act", lambda a: a.activation(out=st_s[:, 5:6], in_=st_s[:, 4:5], func=ACT.Ln, bias=LN_EPS, scale=1.0), reads=[st_s], writes=[st_s])
        k.op("act", lambda a: a.activation(out=st_s[:, 5:6], in_=st_s[:, 5:6], func=ACT.Exp, scale=-0.5), reads=[st_s], writes=[st_s])
        k.op("dve", lambda v: v.scalar_tensor_tensor(out=st_s[:, 6:7], in0=st_s[:, 2:3], scalar=-1.0, in1=st_s[:, 5:6], op0=ALU.mult, op1=ALU.mult),
             reads=[st_s], writes=[st_s])
        k.op("act", lambda a: a.activation(out=yo_s[:], in_=ypre_s[:], func=ACT.Identity, bias=st_s[:, 6:7], scale=st_s[:, 5:6]), reads=[ypre_s, st_s], writes=[yo_s])
        k.op("dve", lambda v: v.tensor_tensor(out=yo_s[:], in0=yo_s[:], in1=lng_bc[0:P16, :], op=ALU.mult), reads=[yo_s, lng_bc], writes=[yo_s])
        k.op("dve", lambda v: v.tensor_tensor(out=yo_s[:], in0=yo_s[:], in1=lnb_bc[0:P16, :], op=ALU.add), reads=[yo_s, lnb_bc], writes=[yo_s])
        k.dma("sp", ys_d[:, :], yo_s[:], reads=[yo_s], semkey="st_smisc")
        k.barrier()
        k3s.close()

    k.barrier()
    k2.close()
    k1s.close()
    identb = k.sb("identb", [128, 128], BF16); Ub = k.sb("Ub", [128, 128], BF16)
    k.op("pool", lambda g: g.tensor_copy(out=identb[:], in_=ident[:]), reads=[ident], writes=[identb])
    k.op("pool", lambda g: g.tensor_copy(out=Ub[:], in_=U[:]), reads=[U], writes=[Ub])
    Xt = [k.sb("Xt%d" % i, [128, D]) for i in range(2)]
    Xb = k.sb("Xb", [128, D], BF16)
    hT = k.sb("hT", [128, 8, 128], BF16)
    pre = k.sb("pre", [128, 12, 131])
    k.op("pool", lambda g: g.memset(pre[:], 0.0), writes=[pre])
    cm = [k.sb("cm%d" % i, [128, 12, 128]) for i in range(2)]
    qkvs = cm[0]
    qkb = k.sb("qkb", [128, 12, 128], BF16)
    sqb = k.sb("sqb", [128, 8, 128], BF16)
    lnss = k.sb("lnss", [128, 8, 128])
    rn = lnss
    zAs = k.sb("zAs", [128, 512]); zBs2 = [k.sb("zBs%d" % i, [128, 512]) for i in range(2)]
    qrb2 = [k.sb("qrb%d" % i, [128, 512], BF16) for i in range(2)]; kr2 = [k.sb("kr%d" % i, [128, 128]) for i in range(2)]
    rt = [k.sb("rt%d" % i, [128, 8, 32]) for i in range(4)]
    vB = [k.sb("vB%d" % i, [128, 128], BF16) for i in range(3)]
    vB32 = k.sb("vB32", [128, 128])
    kTb = [k.sb("kTb%d" % i, [128, 128], BF16) for i in range(2)]
    k.op("pool", lambda g: g.memset(kTb[1][:], 0.0), writes=[kTb[1]])
    k.op("pool", lambda g: g.memset(vB[2][:], 0.0), writes=[vB[2]])
    qTb = k.sb("qTb", [128, 4, 128], BF16)
    bd = k.sb("bd", [128, 8]); bdt = k.sb("bdt", [128, 8])
    beta = k.sb("beta", [128, 4]); negbeta = k.sb("negbeta", [128, 4]); gg = k.sb("gg", [128, 4])
    gsp = k.sb("gsp", [128, 8], BF16); gtmp = k.sb("gtmp", [128, 4])
    gUh = k.sb("gUh", [128, 4, 128], BF16); gUl = k.sb("gUl", [128, 4, 128], BF16)
    Gc = k.sb("Gc", [128, 4]); negGc = k.sb("negGc", [128, 4]); eG = k.sb("eG", [128, 4]); beG = k.sb("beG", [128, 4])
    edG = k.sb("edG", [128, 4]); egl = k.sb("egl", [128, 4]); dG = k.sb("dG", [128, 4]); Glb = k.sb("Glb", [128, 4])
    tmp1 = k.sb("tmp1", [128, 4, 128]); tmp2 = k.sb("tmp2", [128, 4, 128])
    E1 = tmp1; E2 = tmp2
    eGbc = k.sb("eGbc", [128, 4, 128]); qdT = k.sb("qdT", [128, 4, 128], BF16)
    AqkT = k.sb("AqkT", [128, 4, 128], BF16)
    Y0f = k.sb("Y0f", [128, 4, 128])
    XRh = [[k.sb("XRh%d_%d" % (g_, i), [128, 2, 256], BF16) for i in range(2)] for g_ in range(2)]
    XRl = [[k.sb("XRl%d_%d" % (g_, i), [128, 2, 256], BF16) for i in range(2)] for g_ in range(2)]
    Yh = [[k.sb("Yh%d_%d" % (g_, i), [128, 2, 128], BF16) for i in range(2)] for g_ in range(2)]
    Yl = [[k.sb("Yl%d_%d" % (g_, i), [128, 2, 128], BF16) for i in range(2)] for g_ in range(2)]
    Rf = [k.sb("Rf%d" % g_, [128, 2, 128]) for g_ in range(2)]
    kbt = k.sb("kbt", [128, 4, 128], BF16); kdt = k.sb("kdt", [128, 4, 128], BF16); bvt = k.sb("bvt", [128, 4, 128], BF16)
    wT = [k.sb("wT%d" % g_, [128, 2, 128], BF16) for g_ in range(2)]
    ut = [k.sb("ut%d" % g_, [128, 2, 128]) for g_ in range(2)]
    uub = [k.sb("uub%d" % g_, [128, 2, 128], BF16) for g_ in range(2)]
    S = [k.sb("S%d" % g_, [128, 2, 128]) for g_ in range(2)]
    Sb = [k.sb("Sb%d" % g_, [128, 2, 128], BF16) for g_ in range(2)]
    for g_ in range(2):
        k.op("pool", lambda g, g_=g_: g.memset(S[g_][:], 0.0), writes=[S[g_]])
        k.op("pool", lambda g, g_=g_: g.memset(Sb[g_][:], 0.0), writes=[Sb[g_]])
    oss = [k.sb("oss%d" % g_, [128, 2]) for g_ in range(2)]; orr = [k.sb("orr%d" % g_, [128, 2]) for g_ in range(2)]
    nz = k.sb("nz", [128, 4, 128])
    mixb = k.sb("mixb", [128, D], BF16); obt = k.sb("obt", [128, 512])
    ojunk = [tmp1[:, 0, :], tmp2[:, 0, :]]
    mixT = k.sb("mixT", [128, 8, 128], BF16)
    SC = k.sb("SC", [128, 8, 256]); Pb = k.sb("Pb", [128, 8, 256], BF16)
    PTb = k.sb("PTb", [128, 16, 128], BF16)
    mx = k.sb("mx", [128, 8]); negm = k.sb("negm", [128, 8]); rs = k.sb("rs", [128, 8]); es_ = k.sb("es_", [128, 8])
    rden = k.sb("rden", [128, 8])
    SCf = SC[:].rearrange("p a b -> p (a b)")
    ypre = SCf[:, 0:D]
    st = k.sb("st", [128, 8]); negpad = k.sb("negpad", [128, 1])
    Yt = [SCf[:, D:2 * D]]

    def v3(ap, a):
        return ap.rearrange("p (a b) -> p a b", a=a)

    def pbf(pb):
        return pb[:, :].bitcast(BF16)

    def mm(out, lhsT, rhs, start, stop, reads, writes):
        ncol = int(np.prod(out.shape[1:]))
        k.op("pe", lambda p: p.matmul(out, lhsT=lhsT, rhs=rhs, start=start, stop=stop), reads=reads, writes=writes,
             cost=0.14 + ncol / 1200.0)

    def tr(out, in_, idn, reads, writes):
        k.op("pe", lambda p: p.transpose(out=out, in_=in_, identity=idn), reads=reads + [idn], writes=writes, cost=0.25)

    k.barrier()
    k.pe_inorder = True
    krb = k.sb("krb", [128, 128], BF16)
    def P1a(n):
        f_ = full(n); sp_ = swaprep(n)
        c0 = 0 if f_ else 4
        X = Xt[n % 2]
        cs_t = cst[n % 2]
        zBs = zBs2[n % 2]; qrb = qrb2[n % 2]; kr = kr2[n % 2]
        vcur = vB[n % 3]
        k.dma("sp", X[:], x_d[n * 128:(n + 1) * 128, :], writes=[X], semkey="ld_" + X.name)
        if f_ or sp_:
            k.dma("sp", cs_t[:, 0, :], cosk_d[n * 128:(n + 1) * 128, :], writes=[cs_t], semkey="ld_" + cs_t.name)
            k.dma("sp", cs_t[:, 1, :], sink_d[n * 128:(n + 1) * 128, :], writes=[cs_t], semkey="ld_" + cs_t.name, nowaw=True)
        yield
        k.op("pool", lambda g: g.tensor_copy(out=Xb[:], in_=X[:]), reads=[X], writes=[Xb])
        P3b = pbf(PB[2])
        yield
        for kc in range(8):
            tr(P3b[:, kc * 128:(kc + 1) * 128], Xb[:, kc * 128:(kc + 1) * 128], identb[:], [Xb], [PB[2]])
        yield
        for kc in range(8):
            k.op("act" if kc % 2 == 0 else "dve",
                 (lambda a, kc=kc: a.activation(out=hT[:, kc, :], in_=P3b[:, kc * 128:(kc + 1) * 128], func=ACT.Identity,
                                                bias=mod_fm[:, kc:kc + 1], scale=mod_fm[:, 8 + kc:9 + kc])) if kc % 2 == 0 else
                 (lambda v, kc=kc: v.tensor_scalar(out=hT[:, kc, :], in0=P3b[:, kc * 128:(kc + 1) * 128], scalar1=mod_fm[:, 8 + kc:9 + kc],
                                                   scalar2=mod_fm[:, kc:kc + 1], op0=ALU.mult, op1=ALU.add)),
                 reads=[PB[2], mod_fm], writes=[hT])
            if kc % 2 == 1:
                yield
        pc = pre
        k.op("pool", lambda g: g.tensor_copy(out=pc[:, :, 0:3], in_=pc[:, :, 128:131]), reads=[pc], writes=[pc])
        yield
        for j in range(3):
            if j == 0 and not (f_ or sp_):
                continue
            pb = [PB[3], PB[2], PB[3]][j]
            for c4 in range(4):
                c = j * 4 + c4
                for kc in range(8):
                    mm(pb[:, c4 * 128:(c4 + 1) * 128], Wf[:, kc, c * 128:(c + 1) * 128], hT[:, kc, :], kc == 0, kc == 7, [Wf, hT], [pb])
                yield
            if j != 1:
                k.op("act", lambda a, j=j, pb=pb: a.activation(out=pc[:, j * 4:(j + 1) * 4, 3:131], in_=v3(pb[:, :], 4), func=ACT.Identity,
                                                               scale=valid_bc[:, n:n + 1]), reads=[pb, valid_bc], writes=[pc])
            else:
                k.op("dve", lambda v, j=j, pb=pb: v.tensor_scalar(out=pc[:, j * 4:(j + 1) * 4, 3:131], in0=v3(pb[:, :], 4), scalar1=valid_bc[:, n:n + 1],
                                                                  scalar2=None, op0=ALU.mult), reads=[pb, valid_bc], writes=[pc])
            yield
        nch = 12 - c0
        k.op("dve", lambda v: v.tensor_tensor(out=cm[0][:, c0:12, :], in0=pc[:, c0:12, 0:128], in1=bc(cwT[:, c0:12].rearrange("p (a b) -> p a b", b=1), [128, nch, 128]), op=ALU.mult),
             reads=[pc, cwT], writes=[cm[0]], cost=0.13 * nch)
        yield
        for j in range(1, 4):
            k.op("pool", lambda g, j=j: g.tensor_tensor(out=cm[1][:, c0:12, :], in0=pc[:, c0:12, j:j + 128], in1=bc(cwT[:, j * 12 + c0:(j + 1) * 12].rearrange("p (a b) -> p a b", b=1), [128, nch, 128]), op=ALU.mult),
                 reads=[pc, cwT], writes=[cm[1]], cost=0.25 * nch)
            yield
            k.op("dve", lambda v: v.tensor_tensor(out=cm[0][:, c0:12, :], in0=cm[0][:, c0:12, :], in1=cm[1][:, c0:12, :], op=ALU.add), reads=[cm[0], cm[1]], writes=[cm[0]],
                 cost=0.13 * nch)
            yield
        k.op("act", lambda a: a.activation(out=qkvs[:, c0:12, :], in_=cm[0][:, c0:12, :], func=ACT.Silu), reads=[cm[0]], writes=[qkvs], cost=0.12 * nch)
        yield
        offs = [(0, 512), (512, 512), (1024, 512), (1536, 264)]

        def tproj(j, pb):
            o, w_ = offs[j]
            for kc in range(8):
                mm(pb[:, 0:w_], hT[:, kc, :], Wt[:, kc, o:o + w_], kc == 0, kc == 7, [hT, Wt], [pb])
        PzA, PqB = PB[2], PB[3]
        if f_:
            tproj(0, PzA)
            yield
            k.op("act", lambda a: a.activation(out=zAs[:], in_=PzA[:, :], func=ACT.Silu), reads=[PzA], writes=[zAs])
            yield
            tproj(1, PqB)
            yield
            q3 = v3(PqB[:, :], 8)
            qr3 = v3(qrb[:], 8)
            cq = bc(cs_t[:, 0:1, :], [128, 8, 32]); sq_ = bc(cs_t[:, 1:2, :], [128, 8, 32])
            k.op("dve", lambda v: v.tensor_tensor(out=rt[0][:], in0=q3[:, :, 0:32], in1=cq, op=ALU.mult), reads=[PqB, cs_t], writes=[rt[0]])
            k.op("dve", lambda v: v.tensor_tensor(out=rt[1][:], in0=q3[:, :, 32:64], in1=sq_, op=ALU.mult), reads=[PqB, cs_t], writes=[rt[1]])
            yield
            k.op("dve", lambda v: v.tensor_tensor(out=rt[2][:], in0=q3[:, :, 32:64], in1=cq, op=ALU.mult), reads=[PqB, cs_t], writes=[rt[2]])
            k.op("dve", lambda v: v.tensor_tensor(out=rt[3][:], in0=q3[:, :, 0:32], in1=sq_, op=ALU.mult), reads=[PqB, cs_t], writes=[rt[3]])
            yield
            k.op("pool", lambda g: g.tensor_tensor(out=qr3[:, :, 0:32], in0=rt[0][:], in1=rt[1][:], op=ALU.subtract), reads=[rt[0], rt[1]], writes=[qrb])
            k.op("pool", lambda g: g.tensor_tensor(out=qr3[:, :, 32:64], in0=rt[2][:], in1=rt[3][:], op=ALU.add), reads=[rt[2], rt[3]], writes=[qrb])
            yield
        PzB, Pk = PB[2], PB[3]
        if f_:
            tproj(2, PzB)
            yield
            k.op("act", lambda a: a.activation(out=zBs[:], in_=PzB[:, :], func=ACT.Silu), reads=[PzB], writes=[zBs])
            yield
        tproj(3, Pk)
        yield
        if f_ or sp_:
            k3 = v3(Pk[:, 0:128], 2)
            kr3 = v3(kr[:], 2)
            ck = bc(cs_t[:, 0:1, :], [128, 2, 32]); sk_ = bc(cs_t[:, 1:2, :], [128, 2, 32])
            k.op("dve", lambda v: v.tensor_tensor(out=rt[0][:, 0:2, :], in0=k3[:, :, 0:32], in1=ck, op=ALU.mult), reads=[Pk, cs_t], writes=[rt[0]])
            k.op("dve", lambda v: v.tensor_tensor(out=rt[1][:, 0:2, :], in0=k3[:, :, 32:64], in1=sk_, op=ALU.mult), reads=[Pk, cs_t], writes=[rt[1]])
            yield
            k.op("dve", lambda v: v.tensor_tensor(out=rt[2][:, 0:2, :], in0=k3[:, :, 32:64], in1=ck, op=ALU.mult), reads=[Pk, cs_t], writes=[rt[2]])
            k.op("dve", lambda v: v.tensor_tensor(out=rt[3][:, 0:2, :], in0=k3[:, :, 0:32], in1=sk_, op=ALU.mult), reads=[Pk, cs_t], writes=[rt[3]])
            yield
            k.op("pool", lambda g: g.tensor_tensor(out=kr3[:, :, 0:32], in0=rt[0][:, 0:2, :], in1=rt[1][:, 0:2, :], op=ALU.subtract), reads=[rt[0], rt[1]], writes=[kr])
            k.op("pool", lambda g: g.tensor_tensor(out=kr3[:, :, 32:64], in0=rt[2][:, 0:2, :], in1=rt[3][:, 0:2, :], op=ALU.add), reads=[rt[2], rt[3]], writes=[kr])
            yield
            k.op("dve", lambda v: v.tensor_copy(out=vcur[:], in_=Pk[:, 128:256]), reads=[Pk], writes=[vcur])
            if n == NT - 1:
                k.op("dve", lambda v: v.tensor_copy(out=vB32[:], in_=Pk[:, 128:256]), reads=[Pk], writes=[vB32])
        k.op("dve", lambda v: v.tensor_copy(out=bd[:], in_=Pk[:, 256:264]), reads=[Pk], writes=[bd])
        yield
        j0 = 0 if f_ else 1
        k.op("pool", lambda g: g.tensor_tensor(out=sqb[:, c0:8, :], in0=qkvs[:, c0:8, :], in1=qkvs[:, c0:8, :], op=ALU.mult), reads=[qkvs], writes=[sqb],
             cost=0.25 * (8 - c0))
        yield
        for j in range(j0, 2):
            mm(PB[2 + j][:, :], onesb[:], sqb[:, j * 4:(j + 1) * 4, :].rearrange("p a b -> p (a b)"), True, True, [onesb, sqb], [PB[2 + j]])
        yield
        for j in range(j0, 2):
            k.op("act", lambda a, j=j: a.activation(out=lnss[:, j * 4:(j + 1) * 4, :].rearrange("p a b -> p (a b)"), in_=PB[2 + j][:, :],
                                                    func=ACT.Ln, bias=L2_EPS, scale=1.0), reads=[PB[2 + j]], writes=[lnss])
        yield
        if f_:
            k.op("act", lambda a: a.activation(out=rn[:, 0:4, :], in_=lnss[:, 0:4, :], func=ACT.Exp, bias=float(-0.5 * np.log(128.0)), scale=-0.5), reads=[lnss], writes=[rn])
        k.op("act", lambda a: a.activation(out=rn[:, 4:8, :], in_=lnss[:, 4:8, :], func=ACT.Exp, scale=-0.5), reads=[lnss], writes=[rn])
        yield
        k.op("dve", lambda v: v.tensor_tensor(out=qkvs[:, c0:8, :], in0=qkvs[:, c0:8, :], in1=rn[:, c0:8, :], op=ALU.mult), reads=[qkvs, rn], writes=[qkvs],
             cost=0.13 * (8 - c0))
        yield
        k.op("pool", lambda g: g.tensor_copy(out=qkb[:, c0:12, :], in_=qkvs[:, c0:12, :]), reads=[qkvs], writes=[qkb], cost=0.25 * nch)
        yield
        k.op("act", lambda a: a.activation(out=bdt[:, 0:4], in_=bd[:, 0:4], func=ACT.Exp, scale=-1.0), reads=[bd], writes=[bdt])
        k.op("dve", lambda v: v.tensor_scalar(out=bdt[:, 0:4], in0=bdt[:, 0:4], scalar1=1.0, scalar2=None, op0=ALU.add), reads=[bdt], writes=[bdt])
        k.op("dve", lambda v: v.reciprocal(out=beta[:], in_=bdt[:, 0:4]), reads=[bdt], writes=[beta])
        k.op("dve", lambda v: v.tensor_scalar(out=beta[:], in0=beta[:], scalar1=valid_bc[:, n:n + 1], scalar2=None, op0=ALU.mult), reads=[beta, valid_bc], writes=[beta])
        k.op("dve", lambda v: v.tensor_scalar(out=negbeta[:], in0=beta[:], scalar1=-1.0, scalar2=None, op0=ALU.mult), reads=[beta], writes=[negbeta])
        yield
        k.op("dve", lambda v: v.tensor_tensor(out=bd[:, 4:8], in0=bd[:, 4:8], in1=dtb_bc[:], op=ALU.add), reads=[bd, dtb_bc], writes=[bd])
        k.op("act", lambda a: a.activation(out=bdt[:, 4:8], in_=bd[:, 4:8], func=ACT.Exp), reads=[bd], writes=[bdt])
        k.op("act", lambda a: a.activation(out=bdt[:, 4:8], in_=bdt[:, 4:8], func=ACT.Ln, bias=1.0, scale=1.0), reads=[bdt], writes=[bdt])
        k.op("dve", lambda v: v.tensor_tensor(out=gg[:], in0=bdt[:, 4:8], in1=ea[:], op=ALU.mult), reads=[bdt, ea], writes=[gg])
        yield

    def P1b(n):
        f_ = full(n)
        yield
        if f_:
            k.op("pool", lambda g: g.tensor_tensor(out=nz[:], in0=v3(zAs[:], 4), in1=bc(norma_bc[:].rearrange("p (a b) -> p a b", a=1), [128, 4, 128]), op=ALU.mult),
                 reads=[zAs, norma_bc], writes=[nz])
        yield
        k.op("dve", lambda v: v.tensor_copy(out=gsp[:, 0:4], in_=gg[:]), reads=[gg], writes=[gsp])
        yield
        k.op("dve", lambda v: v.tensor_tensor(out=gtmp[:], in0=gg[:], in1=gsp[:, 0:4], op=ALU.subtract), reads=[gg, gsp], writes=[gtmp])
        yield
        k.op("dve", lambda v: v.tensor_copy(out=gsp[:, 4:8], in_=gtmp[:]), reads=[gtmp], writes=[gsp])
        Ub4 = bc(Ub[:].rearrange("p (a b) -> p a b", a=1), [128, 4, 128])
        yield
        k.op("pool", lambda g: g.tensor_tensor(out=gUh[:], in0=Ub4, in1=bc(gsp[:, 0:4].rearrange("p (a b) -> p a b", b=1), [128, 4, 128]), op=ALU.mult),
             reads=[Ub, gsp], writes=[gUh])
        yield
        k.op("pool", lambda g: g.tensor_tensor(out=gUl[:], in0=Ub4, in1=bc(gsp[:, 4:8].rearrange("p (a b) -> p a b", b=1), [128, 4, 128]), op=ALU.mult),
             reads=[Ub, gsp], writes=[gUl])
        PG, PK_, PQ = PB[3], PB[4], PB[5]
        yield
        mm(PB[1][:, 0:8], Ub[:], gsp[:], True, True, [Ub, gsp], [PB[1]])
        yield
        mm(PB[1][:, 8:16], onesb[:], gsp[:], True, True, [onesb, gsp], [PB[1]])
        yield
        mm(PG[:, :], onesb[:], gUh[:].rearrange("p a b -> p (a b)"), True, False, [onesb, gUh], [PG])
        yield
        mm(PG[:, :], onesb[:], gUl[:].rearrange("p a b -> p (a b)"), False, True, [onesb, gUl], [PG])
        yield
        k.op("dve", lambda v: v.tensor_copy(out=gtmp[:], in_=PB[1][:, 0:4]), reads=[PB[1]], writes=[gtmp])
        yield
        k.op("dve", lambda v: v.tensor_tensor(out=Gc[:], in0=gtmp[:], in1=PB[1][:, 4:8], op=ALU.add), reads=[gtmp, PB[1]], writes=[Gc])
        yield
        k.op("dve", lambda v: v.tensor_copy(out=gtmp[:], in_=PB[1][:, 8:12]), reads=[PB[1]], writes=[gtmp])
        yield
        k.op("dve", lambda v: v.tensor_tensor(out=Glb[:], in0=gtmp[:], in1=PB[1][:, 12:16], op=ALU.add), reads=[gtmp, PB[1]], writes=[Glb])
        yield
        k.op("dve", lambda v: v.tensor_scalar(out=negGc[:], in0=Gc[:], scalar1=-1.0, scalar2=None, op0=ALU.mult), reads=[Gc], writes=[negGc])
        yield
        k.op("dve", lambda v: v.tensor_tensor(out=dG[:], in0=Glb[:], in1=Gc[:], op=ALU.subtract), reads=[Glb, Gc], writes=[dG])
        yield
        k.op("act", lambda a: a.activation(out=eG[:], in_=Gc[:], func=ACT.Exp), reads=[Gc], writes=[eG])
        yield
        k.op("act", lambda a: a.activation(out=edG[:], in_=dG[:], func=ACT.Exp), reads=[dG], writes=[edG])
        yield
        k.op("act", lambda a: a.activation(out=egl[:], in_=Glb[:], func=ACT.Exp), reads=[Glb], writes=[egl])
        yield
        k.op("dve", lambda v: v.tensor_tensor(out=beG[:], in0=eG[:], in1=beta[:], op=ALU.mult), reads=[eG, beta], writes=[beG])
        PG3 = v3(PG[:, :], 4)
        yield
        k.op("dve", lambda v: v.scalar_tensor_tensor(out=tmp1[:], in0=PG3, scalar=-1.0, in1=bc(negA[:].rearrange("p (a b) -> p a b", a=1), [128, 4, 128]),
                                                     op0=ALU.mult, op1=ALU.add), reads=[PG, negA], writes=[tmp1])
        yield
        if f_:
            k.op("dve", lambda v: v.tensor_tensor(out=tmp2[:], in0=PG3, in1=bc(negB[:].rearrange("p (a b) -> p a b", a=1), [128, 4, 128]), op=ALU.add),
                 reads=[PG, negB], writes=[tmp2])
            yield
            k.op("act", lambda a: a.activation(out=eGbc[:], in_=PG3, func=ACT.Exp), reads=[PG], writes=[eGbc])
        yield
        for h in range(4):
            k.op("act", lambda a, h=h: a.activation(out=E1[:, h, :], in_=tmp1[:, h, :], func=ACT.Exp, bias=Gc[:, h:h + 1], scale=1.0), reads=[tmp1, Gc], writes=[E1])
            if f_:
                k.op("act", lambda a, h=h: a.activation(out=E2[:, h, :], in_=tmp2[:, h, :], func=ACT.Exp, bias=negGc[:, h:h + 1], scale=1.0), reads=[tmp2, negGc], writes=[E2])
        yield
        for h in range(4):
            mm(PK_[:, h * 128:(h + 1) * 128], qkb[:, 4 + h, :], qkb[:, 4 + h, :], True, True, [qkb], [PK_])
        yield
        for h in range(4):
            if f_:
                mm(PQ[:, h * 128:(h + 1) * 128], qkb[:, 4 + h, :], qkb[:, h, :], True, True, [qkb], [PQ])
        yield
        for h in range(4):
            k.op("dve", lambda v, h=h: v.scalar_tensor_tensor(out=Y0f[:, h, :], in0=PK_[:, h * 128:(h + 1) * 128], scalar=negbeta[:, h:h + 1],
                                                               in1=E1[:, h, :], op0=ALU.mult, op1=ALU.mult), reads=[PK_, negbeta, E1], writes=[Y0f])
        yield
        if f_:
            k.op("dve", lambda v: v.tensor_tensor(out=AqkT[:], in0=v3(PQ[:, :], 4), in1=E2[:], op=ALU.mult), reads=[PQ, E2], writes=[AqkT])
            yield
            k.op("pool", lambda g: g.tensor_tensor(out=qdT[:], in0=qkvs[:, 0:4, :], in1=eGbc[:], op=ALU.mult), reads=[qkvs, eGbc], writes=[qdT])
        P5b = pbf(PB[0])
        id2 = bc(ident[:].rearrange("p (a b) -> p a b", a=1), [128, 2, 128])
        for g_ in range(2):
            yield
            k.op("act", lambda a, g_=g_: a.copy(out=Yh[g_][0][:], in_=Y0f[:, 2 * g_:2 * g_ + 2, :]), reads=[Y0f], writes=[Yh[g_][0]])
            yield
            k.op("pool", lambda g, g_=g_: g.tensor_tensor(out=Yl[g_][0][:], in0=Y0f[:, 2 * g_:2 * g_ + 2, :], in1=Yh[g_][0][:], op=ALU.subtract),
                 reads=[Y0f, Yh[g_][0]], writes=[Yl[g_][0]])
            yield
            for hh in range(2):
                h = 2 * g_ + hh
                tr(P5b[:, h * 128:(h + 1) * 128], Yh[g_][0][:, hh, :], identb[:], [Yh[g_][0]], [PB[0]])
                tr(P5b[:, (4 + h) * 128:(5 + h) * 128], Yl[g_][0][:, hh, :], identb[:], [Yl[g_][0]], [PB[0]])
        for g_ in range(2):
            yield
            k.op("act", lambda a, g_=g_: a.copy(out=XRh[g_][0][:, :, 0:128], in_=v3(P5b[:, g_ * 256:(g_ + 1) * 256], 2)), reads=[PB[0]], writes=[XRh[g_][0]])
            yield
            k.op("dve", lambda v, g_=g_: v.tensor_copy(out=XRl[g_][0][:, :, 0:128], in_=v3(P5b[:, 512 + g_ * 256:512 + (g_ + 1) * 256], 2)), reads=[PB[0]], writes=[XRl[g_][0]])
            yield
            k.op("pool", lambda g, g_=g_: g.tensor_copy(out=Rf[g_][:], in_=id2), reads=[ident], writes=[Rf[g_]])
            k.op("pool", lambda g, g_=g_: g.tensor_copy(out=XRh[g_][0][:, :, 128:256], in_=id2), reads=[ident], writes=[XRh[g_][0]])
            k.op("pool", lambda g, g_=g_: g.memset(XRl[g_][0][:, :, 128:256], 0.0), writes=[XRl[g_][0]])
        P1b = pbf(PB[1])
        yield
        for h in range(4):
            tr(P1b[:, h * 128:(h + 1) * 128], qkb[:, 4 + h, :], identb[:], [qkb], [PB[1]])
        yield
        for h in range(4):
            tr(P1b[:, (4 + h) * 128:(5 + h) * 128], qkb[:, 8 + h, :], identb[:], [qkb], [PB[1]])
        yield
        for h in range(4):
            k.op("dve", lambda v, h=h: v.tensor_scalar(out=kbt[:, h, :], in0=P1b[:, h * 128:(h + 1) * 128], scalar1=beG[:, h:h + 1], scalar2=None, op0=ALU.mult),
                 reads=[PB[1], beG], writes=[kbt])
            k.op("act", lambda a, h=h: a.activation(out=kdt[:, h, :], in_=P1b[:, h * 128:(h + 1) * 128], func=ACT.Identity, scale=edG[:, h:h + 1]),
                 reads=[PB[1], edG], writes=[kdt])
            k.op("act", lambda a, h=h: a.activation(out=bvt[:, h, :], in_=P1b[:, (4 + h) * 128:(5 + h) * 128], func=ACT.Identity, scale=beta[:, h:h + 1]),
                 reads=[PB[1], beta], writes=[bvt])
        yield

    def G(n, g_):
        BA, BB = PB[4 + g_], PB[6 + g_]
        for lev in range(7):
            cur = lev % 2; nxt = (lev + 1) % 2
            xh, xl, yh, yl = XRh[g_][cur], XRl[g_][cur], Yh[g_][cur], Yl[g_][cur]
            xhn, xln, yhn, yln = XRh[g_][nxt], XRl[g_][nxt], Yh[g_][nxt], Yl[g_][nxt]
            last = (lev == 6)
            yield
            for hh in range(2):
                if not last:
                    o_ = BA[:, hh * 256:(hh + 1) * 256]
                    r_h = xh[:, hh, :]; r_l = xl[:, hh, :]
                else:
                    o_ = BA[:, hh * 256 + 128:(hh + 1) * 256]
                    r_h = xh[:, hh, 128:256]; r_l = xl[:, hh, 128:256]
                mm(o_, yh[:, hh, :], r_h, True, False, [yh, xh], [BA])
                mm(o_, yh[:, hh, :], r_l, False, False, [yh, xl], [BA])
                mm(o_, yl[:, hh, :], r_h, False, True, [yl, xh], [BA])
            if not last:
                yield
                for hh in range(2):
                    o_ = BB[:, hh * 128:(hh + 1) * 128]
                    mm(o_, xh[:, hh, 0:128], yh[:, hh, :], True, False, [xh, yh], [BB])
                    mm(o_, xh[:, hh, 0:128], yl[:, hh, :], False, False, [xh, yl], [BB])
                    mm(o_, xl[:, hh, 0:128], yh[:, hh, :], False, True, [xl, yh], [BB])
            yield
            k.op("dve", lambda v: v.tensor_tensor(out=Rf[g_][:], in0=Rf[g_][:], in1=v3(BA[:, :], 2)[:, :, 128:256], op=ALU.add), reads=[Rf[g_], BA], writes=[Rf[g_]],
                 cost=0.3)
            yield
            k.op("act", lambda a: a.copy(out=xhn[:, :, 128:256], in_=Rf[g_][:]), reads=[Rf[g_]], writes=[xhn], cost=0.3)
            if not last:
                yield
                k.op("pool", lambda g: g.tensor_tensor(out=xln[:, :, 128:256], in0=Rf[g_][:], in1=xhn[:, :, 128:256], op=ALU.subtract), reads=[Rf[g_], xhn], writes=[xln],
                     cost=0.6)
                yield
                k.op("act", lambda a: a.copy(out=xhn[:, :, 0:128], in_=v3(BA[:, :], 2)[:, :, 0:128]), reads=[BA], writes=[xhn], cost=0.3)
                yield
                k.op("dve", lambda v: v.tensor_tensor(out=xln[:, :, 0:128], in0=v3(BA[:, :], 2)[:, :, 0:128], in1=xhn[:, :, 0:128], op=ALU.subtract),
                     reads=[BA, xhn], writes=[xln], cost=0.3)
                yield
                k.op("act", lambda a: a.copy(out=yhn[:], in_=v3(BB[:, 0:256], 2)), reads=[BB], writes=[yhn], cost=0.3)
                yield
                k.op("dve", lambda v: v.tensor_tensor(out=yln[:], in0=v3(BB[:, 0:256], 2), in1=yhn[:], op=ALU.subtract), reads=[BB, yhn], writes=[yln], cost=0.3)
        Rh = XRh[g_][1]
        yield
        for hh in range(2):
            h = 2 * g_ + hh
            mm(BA[:, hh * 128:(hh + 1) * 128], kbt[:, h, :], Rh[:, hh, 128:256], True, True, [kbt, Rh], [BA])
        for hh in range(2):
            h = 2 * g_ + hh
            mm(BB[:, hh * 128:(hh + 1) * 128], Rh[:, hh, 128:256], bvt[:, h, :], True, True, [Rh, bvt], [BB])
        yield
        k.op("act", lambda a: a.copy(out=wT[g_][:], in_=v3(BA[:, 0:256], 2)), reads=[BA], writes=[wT[g_]], cost=0.3)
        yield
        k.op("dve", lambda v: v.tensor_copy(out=ut[g_][:], in_=v3(BB[:, 0:256], 2)), reads=[BB], writes=[ut[g_]], cost=0.3)
        yield
        for hh in range(2):
            mm(BA[:, 256 + hh * 128:256 + (hh + 1) * 128], wT[g_][:, hh, :], Sb[g_][:, hh, :], True, True, [wT[g_], Sb[g_]], [BA])
        yield
        k.op("dve", lambda v: v.tensor_tensor(out=uub[g_][:], in0=ut[g_][:], in1=v3(BA[:, 256:512], 2), op=ALU.subtract), reads=[ut[g_], BA], writes=[uub[g_]], cost=0.3)
        yield
        for hh in range(2):
            h = 2 * g_ + hh
            if full(n):
                mm(BB[:, 256 + hh * 128:256 + (hh + 1) * 128], qdT[:, h, :], Sb[g_][:, hh, :], True, False, [qdT, Sb[g_]], [BB])
                mm(BB[:, 256 + hh * 128:256 + (hh + 1) * 128], AqkT[:, h, :], uub[g_][:, hh, :], False, True, [AqkT, uub[g_]], [BB])
        for hh in range(2):
            h = 2 * g_ + hh
            mm(BA[:, hh * 128:(hh + 1) * 128], kdt[:, h, :], uub[g_][:, hh, :], True, True, [kdt, uub[g_]], [BA])
        yield
        for hh in range(2):
            h = 2 * g_ + hh
            k.op("dve", lambda v, hh=hh, h=h: v.scalar_tensor_tensor(out=S[g_][:, hh, :], in0=S[g_][:, hh, :], scalar=egl[:, h:h + 1], in1=BA[:, hh * 128:(hh + 1) * 128],
                                                                   op0=ALU.mult, op1=ALU.add), reads=[S[g_], egl, BA], writes=[S[g_]], cost=0.2)
        yield
        k.op("pool", lambda g: g.tensor_copy(out=Sb[g_][:], in_=S[g_][:]), reads=[S[g_]], writes=[Sb[g_]], cost=0.5)
        yield
        if not full(n):
            return
        for hh in range(2):
            k.op("act", lambda a, hh=hh: a.activation(out=ojunk[g_][:], in_=BB[:, 256 + hh * 128:256 + (hh + 1) * 128], func=ACT.Square, accum_out=oss[g_][:, hh:hh + 1]),
                 reads=[BB], writes=[ojunk[g_], oss[g_]], cost=0.25)
        yield
        k.op("act", lambda a: a.activation(out=orr[g_][:], in_=oss[g_][:], func=ACT.Ln, bias=RMS_EPS, scale=1.0 / 128.0), reads=[oss[g_]], writes=[orr[g_]], cost=0.2)
        k.op("act", lambda a: a.activation(out=orr[g_][:], in_=orr[g_][:], func=ACT.Exp, scale=-0.5), reads=[orr[g_]], writes=[orr[g_]], cost=0.2)
        yield
        for hh in range(2):
            h = 2 * g_ + hh
            k.op("dve", lambda v, hh=hh, h=h: v.scalar_tensor_tensor(out=mixb[:, h * 128:(h + 1) * 128], in0=BB[:, 256 + hh * 128:256 + (hh + 1) * 128], scalar=orr[g_][:, hh:hh + 1],
                                                                   in1=nz[:, h, :], op0=ALU.mult, op1=ALU.mult), reads=[BB, orr[g_], nz], writes=[mixb], cost=0.2)
        yield

    def W(n):
        vcur = vB[n % 3]; vprev = vB[(n + 2) % 3]
        zBs = zBs2[n % 2]; qrb = qrb2[n % 2]; kr = kr2[n % 2]
        kcur = kTb[n % 2]; kprev = kTb[(n + 1) % 2]
        if not (full(n) or swaprep(n)):
            return
        if swaprep(n):
            yield
            k.op("pool", lambda g: g.tensor_copy(out=krb[:], in_=kr[:]), reads=[kr], writes=[krb])
            yield
            tr(pbf(PB[0])[:, 512:640], krb[:], identb[:], [krb], [PB[0]])
            yield
            k.op("dve", lambda v: v.tensor_copy(out=kcur[:], in_=pbf(PB[0])[:, 512:640]), reads=[PB[0]], writes=[kcur])
            yield
            return
        P0b = pbf(PB[0])
        yield
        for c in range(4):
            tr(P0b[:, c * 128:(c + 1) * 128], qrb[:, c * 128:(c + 1) * 128], identb[:], [qrb], [PB[0]])
        yield
        k.op("act", lambda a: a.activation(out=qTb[:], in_=v3(P0b[:, 0:512], 4), func=ACT.Identity, scale=0.125), reads=[PB[0]], writes=[qTb])
        yield
        k.op("pool", lambda g: g.tensor_copy(out=krb[:], in_=kr[:]), reads=[kr], writes=[krb])
        yield
        tr(pbf(PB[0])[:, 512:640], krb[:], identb[:], [krb], [PB[0]])
        yield
        k.op("dve", lambda v: v.tensor_copy(out=kcur[:], in_=pbf(PB[0])[:, 512:640]), reads=[PB[0]], writes=[kcur])
        msk = swam
        rnd = 0
        for kv in range(2):
            for cp in range(2):
                pb = PB[1] if rnd % 2 == 0 else PB[0]
                rnd += 1
                yield
                for cc in range(2):
                    c = cp * 2 + cc
                    o = cc * 256
                    mm(pb[:, o:o + 128], qTb[64 * kv:64 * kv + 64, c, :], kprev[64 * kv:64 * kv + 64, :], True, True, [qTb, kprev], [pb])
                    mm(pb[:, o + 128:o + 256], qTb[64 * kv:64 * kv + 64, c, :], kcur[64 * kv:64 * kv + 64, :], True, True, [qTb, kcur], [pb])
                yield
                s0 = kv * 4 + cp * 2
                k.op("dve", lambda v, pb=pb, s0=s0: v.tensor_tensor(out=SC[:, s0:s0 + 2, :], in0=v3(pb[:, :], 2),
                                                                     in1=bc(msk[:].rearrange("p (a b) -> p a b", a=1), [128, 2, 256]), op=ALU.add),
                     reads=[pb, msk], writes=[SC])
        yield
        if n == F0:
            if F0 == 0:
                k.op("dve", lambda v: v.tensor_scalar(out=SC[:, :, 0:128], in0=SC[:, :, 0:128], scalar1=NEG, scalar2=None, op0=ALU.add), reads=[SC], writes=[SC])
            else:
                k.op("dve", lambda v: v.tensor_scalar(out=negpad[:], in0=valid_bc[:, F0 - 1:F0], scalar1=-1.0, scalar2=-NEG, op0=ALU.add, op1=ALU.mult),
                     reads=[valid_bc], writes=[negpad])
                k.op("dve", lambda v: v.tensor_scalar(out=SC[:, :, 0:128], in0=SC[:, :, 0:128], scalar1=negpad[:, 0:1], scalar2=None, op0=ALU.add),
                     reads=[SC, negpad], writes=[SC])
            yield
        k.op("dve", lambda v: v.tensor_reduce(out=mx[:], in_=SC[:], axis=AX.X, op=ALU.max), reads=[SC], writes=[mx])
        yield
        k.op("dve", lambda v: v.tensor_tensor(out=mx[:], in0=mx[:], in1=sinks_bc[:], op=ALU.max), reads=[mx, sinks_bc], writes=[mx])
        yield
        k.op("dve", lambda v: v.tensor_scalar(out=negm[:], in0=mx[:], scalar1=-1.0, scalar2=None, op0=ALU.mult), reads=[mx], writes=[negm])
        yield
        for s_ in range(8):
            k.op("act", lambda a, s_=s_: a.activation(out=Pb[:, s_, :], in_=SC[:, s_, :], func=ACT.Exp, bias=negm[:, s_:s_ + 1], scale=1.0,
                                                      accum_out=rs[:, s_:s_ + 1]), reads=[SC, negm], writes=[Pb, rs])
        yield
        k.op("dve", lambda v: v.tensor_tensor(out=es_[:], in0=sinks_bc[:], in1=mx[:], op=ALU.subtract), reads=[sinks_bc, mx], writes=[es_])
        yield
        k.op("act", lambda a: a.activation(out=es_[:], in_=es_[:], func=ACT.Exp), reads=[es_], writes=[es_])
        yield
        k.op("dve", lambda v: v.tensor_tensor(out=es_[:], in0=es_[:], in1=rs[:], op=ALU.add), reads=[es_, rs], writes=[es_])
        yield
        k.op("dve", lambda v: v.reciprocal(out=es_[:], in_=es_[:]), reads=[es_], writes=[es_])
        yield
        k.op("dve", lambda v: v.tensor_copy(out=rden[:].rearrange("p (c k) -> p k c", k=2), in_=es_[:].rearrange("p (k c) -> p k c", k=2)),
             reads=[es_], writes=[rden])
        PPT = [PB[1], PB[0]]
        yield
        for s_ in range(8):
            for half in range(2):
                i16 = s_ * 2 + half
                pb = PPT[i16 // 8]
                o = (i16 % 8) * 128
                tr(pbf(pb)[:, o:o + 128], Pb[:, s_, half * 128:(half + 1) * 128], identb[:], [Pb], [pb])
        yield
        k.op("act", lambda a: a.copy(out=PTb[:, 0:8, :], in_=v3(pbf(PPT[0]), 8)), reads=[PPT[0]], writes=[PTb])
        yield
        k.op("dve", lambda v: v.tensor_copy(out=PTb[:, 8:16, :], in_=v3(pbf(PPT[1]), 8)), reads=[PPT[1]], writes=[PTb])
        PV = PB[1]
        yield
        for c in range(4):
            for kv in range(2):
                slot = c * 2 + kv
                sp_ = kv * 4 + c
                mm(PV[:, slot * 64:(slot + 1) * 64], PTb[:, 2 * sp_, :], vprev[:, 64 * kv:64 * kv + 64], True, False, [PTb, vprev], [PV])
                mm(PV[:, slot * 64:(slot + 1) * 64], PTb[:, 2 * sp_ + 1, :], vcur[:, 64 * kv:64 * kv + 64], False, True, [PTb, vcur], [PV])
        yield
        k.op("dve", lambda v: v.tensor_tensor(out=v3(obt[:], 8), in0=v3(PV[:, :], 8), in1=bc(rden[:].rearrange("p (a b) -> p a b", b=1), [128, 8, 64]), op=ALU.mult),
             reads=[PV, rden], writes=[obt])
        yield
        k.op("pool", lambda g: g.tensor_tensor(out=mixb[:, 512:1024], in0=obt[:], in1=zBs[:], op=ALU.mult), reads=[obt, zBs], writes=[mixb])

        yield

    def E(n):
        X = Xt[n % 2]; pc = pre; kr = kr2[n % 2]
        if not full(n):
            return
        P3b = pbf(PB[6])
        yield
        for kc in range(8):
            tr(P3b[:, kc * 128:(kc + 1) * 128], mixb[:, kc * 128:(kc + 1) * 128], identb[:], [mixb], [PB[6]])
        yield
        k.op("act", lambda a: a.copy(out=mixT[:, 0:4, :], in_=v3(P3b[:, 0:512], 4)), reads=[PB[6]], writes=[mixT])
        yield
        k.op("dve", lambda v: v.tensor_copy(out=mixT[:, 4:8, :], in_=v3(P3b[:, 512:1024], 4)), reads=[PB[6]], writes=[mixT])
        yield
        for j in range(2):
            for kc in range(8):
                mm(PB[6 + j][:, :], mixT[:, kc, :], Wo[:, kc, j * 512:(j + 1) * 512], kc == 0, kc == 7, [mixT, Wo], [PB[6 + j]])
        yield
        for j in range(2):
            k.op("dve", lambda v, j=j: v.tensor_tensor(out=ypre[:, j * 512:(j + 1) * 512], in0=PB[6 + j][:, :], in1=g1bc[:, j * 512:(j + 1) * 512], op=ALU.mult),
                 reads=[PB[6 + j], g1bc], writes=[ypre])
        yield
        k.op("dve", lambda g: g.scalar_tensor_tensor(out=ypre[:], in0=X[:], scalar=ALPHA, in1=ypre[:], op0=ALU.mult, op1=ALU.add), reads=[X, ypre], writes=[ypre])
        Yo = Yt[0]
        yjunk = Yo
        yield
        k.op("act", lambda a: a.activation(out=yjunk[:], in_=ypre[:], func=ACT.Identity, accum_out=st[:, 0:1]), reads=[ypre], writes=[yjunk, st])
        yield
        k.op("act", lambda a: a.activation(out=yjunk[:], in_=ypre[:], func=ACT.Square, accum_out=st[:, 1:2]), reads=[ypre], writes=[yjunk, st])
        yield
        k.op("dve", lambda v: v.tensor_scalar(out=st[:, 2:3], in0=st[:, 0:1], scalar1=1.0 / D, scalar2=None, op0=ALU.mult), reads=[st], writes=[st])
        yield
        k.op("dve", lambda v: v.tensor_tensor(out=st[:, 3:4], in0=st[:, 2:3], in1=st[:, 2:3], op=ALU.mult), reads=[st], writes=[st])
        yield
        k.op("dve", lambda v: v.scalar_tensor_tensor(out=st[:, 4:5], in0=st[:, 1:2], scalar=1.0 / D, in1=st[:, 3:4], op0=ALU.mult, op1=ALU.subtract), reads=[st], writes=[st])
        yield
        k.op("act", lambda a: a.activation(out=st[:, 5:6], in_=st[:, 4:5], func=ACT.Ln, bias=LN_EPS, scale=1.0), reads=[st], writes=[st])
        yield
        k.op("act", lambda a: a.activation(out=st[:, 5:6], in_=st[:, 5:6], func=ACT.Exp, scale=-0.5), reads=[st], writes=[st])
        yield
        k.op("dve", lambda v: v.scalar_tensor_tensor(out=st[:, 6:7], in0=st[:, 2:3], scalar=-1.0, in1=st[:, 5:6], op0=ALU.mult, op1=ALU.mult), reads=[st], writes=[st])
        yield
        k.op("act", lambda a: a.activation(out=yjunk[:], in_=ypre[:], func=ACT.Identity, bias=st[:, 6:7], scale=st[:, 5:6]), reads=[ypre, st], writes=[yjunk])
        yield
        k.op("pool", lambda g: g.tensor_tensor(out=yjunk[:], in0=yjunk[:], in1=lng_bc[:], op=ALU.mult), reads=[yjunk, lng_bc], writes=[yjunk])
        yield
        k.op("dve", lambda v: v.tensor_tensor(out=Yo[:], in0=yjunk[:], in1=lnb_bc[:], op=ALU.add), reads=[yjunk, lnb_bc], writes=[Yo])
        yield
        k.dma("sp", y_d[(n - F0) * 128:(n - F0 + 1) * 128, :], Yo[:], reads=[Yo], semkey="st_" + Yo.name)

        if n == NT - 1:
            k.barrier()
            k.pe_inorder = False
            cpo_in = rt[0][:].rearrange("p a b -> p (a b)")[:, 0:36].rearrange("p (a b) -> p a b", a=3)
            cpo = obt[0:12, 0:384].rearrange("p (a b) -> p a b", a=3)
            k.op("dve", lambda v: v.tensor_copy(out=cpo_in[:], in_=pc[:, :, 128:131].rearrange("p c r -> p r c")), reads=[pc], writes=[cpo_in])
            for r_ in range(3):
                tr(PB[6][0:12, r_ * 128:(r_ + 1) * 128], cpo_in[:, r_, :], ident[:], [cpo_in], [PB[6]])
            k.op("dve", lambda v: v.tensor_copy(out=cpo[:].rearrange("p a b -> p (a b)"), in_=PB[6][0:12, 0:384]), reads=[PB[6]], writes=[cpo])
            k.dma("sp", convp_d.rearrange("r (c p) -> c r p", p=128), cpo[:], reads=[cpo], semkey="st_misc")
            for g_ in range(2):
                k.dma("sp", deltap_d[2 * g_:2 * g_ + 2].rearrange("h k v -> k h v"), S[g_][:], reads=[S[g_]], semkey="st_misc")
            k.dma("sp", swak_d[:, :], kr[:], reads=[kr], semkey="st_misc")
            k.dma("sp", swav_d[:, :], vB32[:], reads=[vB32], semkey="st_misc")

        yield

    def merge(*gens):
        gens = [[g_, "s%d" % i] for i, g_ in enumerate(gens) if g_ is not None]
        base = max(k.t_eng.values()) if False else min(k.t_eng.values())
        for _, lab in gens:
            k.t_stream[lab] = base
        while gens:
            gens.sort(key=lambda x: k.t_stream[x[1]])
            g_, lab = gens[0]
            k.stream = lab
            try:
                next(g_)
            except StopIteration:
                gens.pop(0)
        k.stream = None

    merge(P1a(0))
    merge(P1b(0))
    for n in range(NT):
        merge(G(n, 0), G(n, 1), W(n), P1a(n + 1) if n + 1 < NT else None)
        merge(E(n), P1b(n + 1) if n + 1 < NT else None)

    k.finish("sp")
    if k.dry:
        return k.need_out
    nc._kb_nops = k.nops
    nc._kb_sig = dict(k.sig)
    nc._kb_cnt = {e: k.cnt[e] for e in k.eng}
    return nc


def rope_tables(pos):
    half = 32
    inv = (1.0 / (10000.0 ** (np.arange(half, dtype=np.float32) / np.float32(half)))).astype(np.float32)
    ang = pos.astype(np.float32)[:, None] * inv[None, :]
    return np.cos(ang).astype(np.float32), np.sin(ang).astype(np.float32)


def prep_shared(inputs):
    w_in = np.asarray(inputs["w_in"][0], np.float32)
    perm_cols = np.concatenate([np.arange(h * 64, (h + 1) * 64) for h in PERM])
    w_f = np.ascontiguousarray(w_in[:, 0:1536])
    zA = w_in[:, 1536:2048]
    beta = w_in[:, 2048:2052]
    dec = w_in[:, 2052:2056]
    qB = w_in[:, 2056:2568][:, perm_cols]
    kB = w_in[:, 2568:2696]
    vB = w_in[:, 2696:2824]
    zB = w_in[:, 2824:3336][:, perm_cols]
    w_t = np.ascontiguousarray(np.concatenate([zA, qB, zB, kB, vB, beta, dec], axis=1))
    w_out = np.asarray(inputs["w_out"][0], np.float32)
    w_o = np.ascontiguousarray(np.concatenate([w_out[0:512], w_out[512:1024][perm_cols]], axis=0))
    sh = {
        "w_ada": np.ascontiguousarray(inputs["w_ada"][0], np.float32),
        "b_ada": np.ascontiguousarray(inputs["b_ada"], np.float32).reshape(1, -1),
        "w_f": w_f, "w_t": w_t, "w_o": w_o,
        "conv_w": np.ascontiguousarray(inputs["conv_w"][0], np.float32),
        "a_log": np.ascontiguousarray(inputs["a_log"], np.float32).reshape(1, 4),
        "dt_bias": np.ascontiguousarray(inputs["dt_bias"], np.float32).reshape(1, 4),
        "norm_a": np.ascontiguousarray(inputs["norm_a"], np.float32).reshape(1, 128),
        "sinks_p": np.ascontiguousarray(np.asarray(inputs["sinks"], np.float32).reshape(8)).reshape(1, 8),
        "ln_g": np.ascontiguousarray(inputs["ln_g"], np.float32).reshape(1, D),
        "ln_b": np.ascontiguousarray(inputs["ln_b"], np.float32).reshape(1, D),
    }
    return sh


def prep_core(inputs, sh, core, ntiles=NTILES, do_sample=True, pad=None):
    b = core // 4
    q = core % 4
    if pad is None:
        pad = (3 - q) * NLOCAL
    real = ntiles - pad
    T = ntiles * 128
    m = dict(sh)
    x = np.zeros((T, D), np.float32)
    x[pad * 128:] = inputs["x_prompt"][b, :real * 128]
    m["x"] = x
    m["c"] = np.ascontiguousarray(inputs["c_prompt"][b], np.float32).reshape(1, D)
    valid = np.zeros((1, ntiles), np.float32)
    valid[0, pad:] = 1.0
    m["valid"] = valid
    pos = np.maximum(np.arange(T) - pad * 128, 0)
    cos, sin = rope_tables(pos)
    m["cosk"] = cos
    m["sink"] = sin
    if do_sample:
        sl = slice(core * NS, (core + 1) * NS)
        m["xs"] = np.ascontiguousarray(inputs["x_sample"][sl, 0], np.float32)
        m["cs"] = np.ascontiguousarray(inputs["c_sample"][sl], np.float32)
        m["s_conv"] = np.ascontiguousarray(inputs["state_conv"][0, sl], np.float32)
        m["s_delta"] = np.ascontiguousarray(inputs["state_delta"][0, sl], np.float32)
        m["s_k"] = np.ascontiguousarray(inputs["cache_swa_k"][0, sl], np.float32).reshape(NS, 128, 128)
        m["s_v"] = np.ascontiguousarray(inputs["cache_swa_v"][0, sl], np.float32).reshape(NS, 128, 128)
        cs_, ss_ = rope_tables(np.array([8192]))
        m["sinks_col"] = np.ascontiguousarray(np.tile(np.asarray(inputs["sinks"], np.float32).reshape(8), NS).reshape(128, 1))
        m["cos_s"] = cs_.reshape(1, 32)
        m["sin_s"] = ss_.reshape(1, 32)
    return m


_NC_CACHE = {}


DO_SAMPLE = True


def kernel(**inputs):
    if "nc" not in _NC_CACHE:
        _NC_CACHE["nc"] = build_program(NTILES, DO_SAMPLE)
    nc = _NC_CACHE["nc"]
    sh = prep_shared(inputs)
    in_maps = [prep_core(inputs, sh, c, NTILES, DO_SAMPLE) for c in range(8)]
    res = run_bass_kernel_spmd(nc, in_maps, core_ids=list(range(8))).results
    yp = np.stack([np.concatenate([res[b * 4 + q]["y"] for q in range(4)], 0) for b in range(2)], 0).astype(np.float32)
    conv_p = np.stack([res[3]["conv_p"], res[7]["conv_p"]], 0)[None].astype(np.float32)
    delta_p = np.stack([res[3]["delta_p"], res[7]["delta_p"]], 0)[None].astype(np.float32)
    swa_k_p = np.stack([res[3]["swa_k_p"], res[7]["swa_k_p"]], 0).reshape(1, 2, 128, 2, 64).astype(np.float32)
    swa_v_p = np.stack([res[3]["swa_v_p"], res[7]["swa_v_p"]], 0).reshape(1, 2, 128, 2, 64).astype(np.float32)
    if DO_SAMPLE:
        ys = np.concatenate([r["ys"] for r in res], 0).reshape(128, 1, D).astype(np.float32)
        conv_s = np.concatenate([r["conv_s"] for r in res], 0)[None].astype(np.float32)
        delta_s = np.concatenate([r["delta_s"] for r in res], 0)[None].astype(np.float32)
        swa_k_s = np.concatenate([r["swa_k_s"] for r in res], 0).reshape(1, 128, 128, 2, 64).astype(np.float32)
        swa_v_s = np.concatenate([r["swa_v_s"] for r in res], 0).reshape(1, 128, 128, 2, 64).astype(np.float32)
    else:
        ys = np.zeros((128, 1, D), np.float32)
        conv_s = np.zeros((1, 128, 3, 1536), np.float32)
        delta_s = np.zeros((1, 128, 4, 128, 128), np.float32)
        swa_k_s = np.zeros((1, 128, 128, 2, 64), np.float32)
        swa_v_s = np.zeros((1, 128, 128, 2, 64), np.float32)
    return (yp, ys, conv_p, delta_p, swa_k_p, swa_v_p, conv_s, delta_s, swa_k_s, swa_v_s)
```

```python
import contextlib
import numpy as np
import concourse.bass as bass
import concourse.mybir as mybir
from concourse.bass_utils import run_bass_kernel_spmd

F32 = mybir.dt.float32
BF16 = mybir.dt.bfloat16
ACT = mybir.ActivationFunctionType
ALU = mybir.AluOpType
AX = mybir.AxisListType

D = 1024
NTILES = 64
NS = 16
ALPHA = 2.0 ** 0.25
NEG = -1.0e30
LN_EPS = 1e-5
RMS_EPS = 1e-6
L2_EPS = 1e-6
WT_COLS = 1800
PERM = [0, 4, 1, 5, 2, 6, 3, 7]


class KB:
    def __init__(self, nc):
        self.nc = nc
        self.es = contextlib.ExitStack()
        self.eng = {"pe": nc.tensor, "dve": nc.vector, "act": nc.scalar, "pool": nc.gpsimd, "sp": nc.sync}
        self.sem = {}
        self.cnt = {}
        for e in self.eng:
            self.sem[e] = self.es.enter_context(nc.semaphore("sem_" + e))
            self.cnt[e] = 0
        self.waited = {}
        self.last_write = {}
        self.readers = {}
        self.ntensors = 0
        self.limit = None
        self.nops = 0
        self.pe_inorder = False
        self.t_eng = {e: 0.0 for e in self.eng}
        self.t_fin = {}
        self.stream = None
        self.t_stream = {}
        self.dry = False
        self.needed = None
        self.need_out = set()
        self.sig = {e: 0 for e in self.eng}
        self.sigval = {}

    def sb(self, name, shape, dt=F32):
        return self.es.enter_context(self.nc.sbuf_tensor(name, list(shape), dt))

    def ps(self, name, shape=(128, 512), dt=F32):
        return self.es.enter_context(self.nc.psum_tensor(name, list(shape), dt))

    def _deps(self, reads, writes, nowaw=False):
        deps = set()
        for t in reads:
            if t in self.last_write:
                deps.add(self.last_write[t])
        for t in writes:
            if t in self.last_write and not nowaw:
                deps.add(self.last_write[t])
            for r in self.readers.get(t, ()):
                deps.add(r)
        return deps

    def _semval(self, src, val):
        if src in self.eng and self.needed is not None:
            return self.sigval[(src, val)]
        return val

    def _wait(self, e, deps):
        for (src, val) in sorted(deps, key=lambda x: str(x)):
            if e == "pe" and src == "pe" and self.pe_inorder:
                continue
            if self.waited.get((e, src), 0) < val:
                if self.dry:
                    self.need_out.add((src, val))
                else:
                    self.eng[e].wait_ge(self.sem[src], self._semval(src, val))
                self.waited[(e, src)] = val

    def _record(self, key, reads, writes):
        for t in writes:
            self.last_write[t] = key
            self.readers[t] = set()
        for t in reads:
            if t not in writes:
                self.readers.setdefault(t, set()).add(key)

    COST = {"pe": 0.2, "dve": 0.45, "act": 0.5, "pool": 1.0, "sp": 0.1}

    def _model(self, e, key, deps, cost):
        t0 = self.t_eng.get(e, 0.0)
        for d in deps:
            t0 = max(t0, self.t_fin.get(d, 0.0) + (0.0 if d[0] == e else 0.15))
        t1 = t0 + cost
        self.t_eng[e] = t1
        self.t_fin[key] = t1
        if self.stream is not None:
            self.t_stream[self.stream] = max(self.t_stream.get(self.stream, 0.0), t1)

    def op(self, e, fn, reads=(), writes=(), cost=None):
        self.nops += 1
        if self.limit is not None and self.nops > self.limit:
            return
        reads = [r.name if hasattr(r, "name") else r for r in reads]
        writes = [w.name if hasattr(w, "name") else w for w in writes]
        writes = list(writes) + [r for r in reads if r.startswith("pb") and r not in writes]
        deps = self._deps(reads, writes)
        self._wait(e, deps)
        self.cnt[e] += 1
        self._model(e, (e, self.cnt[e]), deps, self.COST[e] if cost is None else cost)
        if not self.dry:
            inst = fn(self.eng[e])
            if self.needed is None:
                inst.then_inc(self.sem[e], 1)
            elif (e, self.cnt[e]) in self.needed:
                self.sig[e] += 1
                self.sigval[(e, self.cnt[e])] = self.sig[e]
                inst.then_inc(self.sem[e], 1)
        self._record((e, self.cnt[e]), reads, writes)

    def dma(self, e, out, in_, reads=(), writes=(), semkey=None, nowaw=False, **kw):
        reads = [r.name if hasattr(r, "name") else r for r in reads]
        writes = [w.name if hasattr(w, "name") else w for w in writes]
        self.nops += 1
        if self.limit is not None and self.nops > self.limit:
            return
        if semkey not in self.sem:
            self.sem[semkey] = self.es.enter_context(self.nc.semaphore("semd_" + str(semkey)))
            self.cnt[semkey] = 0
        deps = self._deps(reads, writes, nowaw)
        self._wait(e, deps)
        self.cnt[semkey] += 16
        self._model(e, (semkey, self.cnt[semkey]), deps, 2.0)
        if not self.dry:
            inst = self.eng[e].dma_start(out=out, in_=in_, **kw)
            inst.then_inc(self.sem[semkey], 16)
        self._record((semkey, self.cnt[semkey]), reads, writes)

    def barrier(self):
        for e in self.eng:
            for src, val in self.cnt.items():
                if val > 0 and self.waited.get((e, src), 0) < val:
                    if self.dry:
                        self.need_out.add((src, val))
                    else:
                        self.eng[e].wait_ge(self.sem[src], self._semval(src, val))
                    self.waited[(e, src)] = val
        self.last_write = {}
        self.readers = {}

    def finish(self, e="sp"):
        for src, val in self.cnt.items():
            if val > 0 and self.waited.get((e, src), 0) < val:
                if self.dry:
                    self.need_out.add((src, val))
                else:
                    self.eng[e].wait_ge(self.sem[src], self._semval(src, val))
                self.waited[(e, src)] = val
        self.es.close()


def bc(ap, shape):
    return ap.to_broadcast(list(shape))


NLOCAL = 16


def build_program(ntiles=NTILES, do_sample=True, limit=None, nl=None):
    if nl is None:
        nl = NLOCAL if ntiles >= NLOCAL else ntiles
    plan = _build(ntiles, do_sample, limit, None, nl)
    return _build(ntiles, do_sample, limit, plan, nl)


def _build(ntiles, do_sample, limit, plan, NL):
    nc = bass.Bass("TRN2", target_bir_lowering=False)
    k = KB(nc)
    k.limit = limit
    if plan is None:
        k.dry = True
    else:
        k.needed = plan
    NT = ntiles
    T = NT * 128

    def din(name, shape):
        return nc.dram_tensor(name, list(shape), F32, kind="ExternalInput").ap()

    def dout(name, shape):
        return nc.dram_tensor(name, list(shape), F32, kind="ExternalOutput").ap()

    x_d = din("x", [T, D])
    c_d = din("c", [1, D])
    wada_d = din("w_ada", [D, 3 * D])
    bada_d = din("b_ada", [1, 3 * D])
    wf_d = din("w_f", [D, 1536])
    wt_d = din("w_t", [D, WT_COLS])
    wo_d = din("w_o", [D, D])
    convw_d = din("conv_w", [4, 1536])
    alog_d = din("a_log", [1, 4])
    dtb_d = din("dt_bias", [1, 4])
    norma_d = din("norm_a", [1, 128])
    sinks_d = din("sinks_p", [1, 8])
    lng_d = din("ln_g", [1, D])
    lnb_d = din("ln_b", [1, D])
    cosk_d = din("cosk", [T, 32])
    sink_d = din("sink", [T, 32])

    y_d = dout("y", [NL * 128, D])
    valid_d = din("valid", [1, NT])
    F0 = NT - NL

    def full(n):
        return n >= F0

    def swaprep(n):
        return n == F0 - 1
    convp_d = dout("conv_p", [3, 1536])
    deltap_d = dout("delta_p", [4, 128, 128])
    swak_d = dout("swa_k_p", [128, 128])
    swav_d = dout("swa_v_p", [128, 128])

    if do_sample:
        xs_d = din("xs", [NS, D])
        cs_d = din("cs", [NS, D])
        sconv_d = din("s_conv", [NS, 3, 1536])
        sdelta_d = din("s_delta", [NS, 4, 128, 128])
        sk_d = din("s_k", [NS, 128, 128])
        sv_d = din("s_v", [NS, 128, 128])
        coss_d = din("cos_s", [1, 32])
        sins_d = din("sin_s", [1, 32])
        ys_d = dout("ys", [NS, D])
        convs_d = dout("conv_s", [NS, 3, 1536])
        deltas_d = dout("delta_s", [NS, 4, 128, 128])
        swaks_d = dout("swa_k_s", [NS, 128, 128])
        swavs_d = dout("swa_v_s", [NS, 128, 128])

    ident = k.sb("ident", [128, 128])
    U = k.sb("U", [128, 128])
    ones = k.sb("ones", [128, 128])
    onesb = k.sb("onesb", [128, 128], BF16)
    negA = k.sb("negA", [128, 128])
    negB = k.sb("negB", [128, 128])
    swam = k.sb("swam", [128, 256])
    swam0 = k.sb("swam0", [128, 256])

    k.op("pool", lambda g: g.memset(ones[:], 1.0), writes=[ones])
    k.op("pool", lambda g: g.memset(onesb[:], 1.0), writes=[onesb])
    k.op("pool", lambda g: g.affine_select(out=ident[:], in_=ones[:], pattern=[[-1, 128]], compare_op=ALU.is_equal,
                                            fill=0.0, base=0, channel_multiplier=1), reads=[ones], writes=[ident])
    k.op("pool", lambda g: g.affine_select(out=U[:], in_=ones[:], pattern=[[1, 128]], compare_op=ALU.is_ge,
                                            fill=0.0, base=0, channel_multiplier=-1), reads=[ones], writes=[U])
    zer = k.sb("zer", [128, 256])
    k.op("pool", lambda g: g.memset(zer[:], 0.0), writes=[zer])
    k.op("pool", lambda g: g.affine_select(out=negA[:], in_=zer[:, 0:128], pattern=[[-1, 128]], compare_op=ALU.is_ge,
                                            fill=NEG, base=-1, channel_multiplier=1), reads=[zer], writes=[negA])
    k.op("pool", lambda g: g.affine_select(out=negB[:], in_=zer[:, 0:128], pattern=[[1, 128]], compare_op=ALU.is_ge,
                                            fill=NEG, base=0, channel_multiplier=-1), reads=[zer], writes=[negB])
    swamt = k.sb("swamt", [128, 256])
    k.op("pool", lambda g: g.affine_select(out=swamt[:], in_=zer[:], pattern=[[1, 256]], compare_op=ALU.is_ge,
                                            fill=NEG, base=0, channel_multiplier=-1), reads=[zer], writes=[swamt])
    k.op("pool", lambda g: g.affine_select(out=swam[:], in_=swamt[:], pattern=[[-1, 256]], compare_op=ALU.is_ge,
                                            fill=NEG, base=128, channel_multiplier=1), reads=[swamt], writes=[swam])
    k.op("pool", lambda g: g.memset(swam0[:, 0:128], NEG), writes=[swam0])
    k.op("pool", lambda g: g.tensor_copy(out=swam0[:, 128:256], in_=swam[:, 128:256]), reads=[swam], writes=[swam0])

    PB = [k.ps("pb%d" % i) for i in range(8)]
    def load_bc(name, src, n, parts=128):
        t = k.sb(name, [parts, n])
        k.dma("sp", t[:], src.partition_broadcast(parts), writes=[t], semkey="ld_" + name)
        return t

    lng_bc = load_bc("lng_bc", lng_d[0], D)
    lnb_bc = load_bc("lnb_bc", lnb_d[0], D)
    norma_bc = load_bc("norma_bc", norma_d[0], 128)
    sinks_bc = load_bc("sinks_bc", sinks_d[0], 8)
    valid_bc = load_bc("valid_bc", valid_d[0], NT)
    alog_bc = load_bc("alog_bc", alog_d[0], 4)
    dtb_bc = load_bc("dtb_bc", dtb_d[0], 4)
    cwT = k.sb("cwT", [128, 48])
    bada_fm = k.sb("bada_fm", [128, 24])
    cT = k.sb("cT", [128, 8])
    rowst = k.sb("rowst", [80, 128])
    k.dma("sp", rowst[0:48, :], convw_d.rearrange("j (c p) -> (j c) p", p=128), writes=[rowst], semkey="ld_rowst", nowaw=True)
    k.dma("sp", rowst[48:72, :], bada_d[0].rearrange("(j p) -> j p", p=128), writes=[rowst], semkey="ld_rowst", nowaw=True)
    k.dma("sp", rowst[72:80, :], c_d[0].rearrange("(j p) -> j p", p=128), writes=[rowst], semkey="ld_rowst", nowaw=True)
    cst = [k.sb("cst%d" % i, [128, 2, 32]) for i in range(2)]
    k.op("pe", lambda p: p.transpose(out=PB[0][:, 0:80], in_=rowst[:, :], identity=ident[0:80, 0:80]), reads=[rowst, ident], writes=[PB[0]])
    k.op("dve", lambda v: v.tensor_copy(out=cwT[:], in_=PB[0][:, 0:48]), reads=[PB[0]], writes=[cwT])
    k.op("dve", lambda v: v.tensor_copy(out=bada_fm[:], in_=PB[0][:, 48:72]), reads=[PB[0]], writes=[bada_fm])
    k.op("dve", lambda v: v.tensor_copy(out=cT[:], in_=PB[0][:, 72:80]), reads=[PB[0]], writes=[cT])
    ea = k.sb("ea", [128, 4])
    k.op("act", lambda a: a.activation(out=ea[:], in_=alog_bc[:], func=ACT.Exp), reads=[alog_bc], writes=[ea])
    k.op("dve", lambda v: v.tensor_scalar(out=ea[:], in0=ea[:], scalar1=-1.0, scalar2=None, op0=ALU.mult),
         reads=[ea], writes=[ea])

    Wf = k.sb("Wf", [128, 8, 1536], BF16)
    Wt = k.sb("Wt", [128, 8, WT_COLS], BF16)
    Wo = k.sb("Wo", [128, 8, D], BF16)
    mod_fm = k.sb("mod_fm", [128, 16])
    g1bc = k.sb("g1bc", [128, D])
    k1s = contextlib.ExitStack()
    if do_sample:
        csT = k1s.enter_context(nc.sbuf_tensor("csT", [128, 8, NS], F32))
        mod_s = k1s.enter_context(nc.sbuf_tensor("mod_s", [NS, 3 * D], F32))
    k2 = contextlib.ExitStack()
    def sb2(name, shape, dt=F32):
        return k2.enter_context(nc.sbuf_tensor(name, list(shape), dt))
    stg = [sb2("stg%d" % i, [128, 1800]) for i in range(2)]
    bada_bc = sb2("bada_bc", [128, D])
    k.dma("sp", bada_bc[:], bada_d[0, 2 * D:3 * D].partition_broadcast(128), writes=[bada_bc], semkey="ld_bada_bc")
    si = 0
    cast_engs = ["dve", "pool", "act"]

    def load_cast(dst, src_d, ncols):
        nonlocal si
        per = 2048 // ncols if ncols <= 2048 else 0
        for kc in range(8):
            s = stg[si % 2]
            k.dma(["sp", "act"][si % 2], s[:, 0:ncols], src_d[kc * 128:(kc + 1) * 128, :], writes=[s], semkey="ld_" + s.name)
            e = cast_engs[si % 3]
            if e == "act":
                k.op(e, lambda a, s=s, kc=kc: a.copy(out=dst[:, kc, :], in_=s[:, 0:ncols]), reads=[s], writes=[dst])
            else:
                k.op(e, lambda v, s=s, kc=kc: v.tensor_copy(out=dst[:, kc, :], in_=s[:, 0:ncols]), reads=[s], writes=[dst])
            si += 1

    load_cast(Wf, wf_d, 1536)
    load_cast(Wt, wt_d, WT_COLS)
    load_cast(Wo, wo_d, D)


    wa = [sb2("wa%d" % i, [128, 8, 512]) for i in range(1)]
    c_bcT = sb2("c_bcT", [128, 8, 128])
    k.op("pool", lambda g: g.tensor_copy(out=c_bcT[:], in_=bc(cT[:].rearrange("p (a b) -> p a b", b=1), [128, 8, 128])),
         reads=[cT], writes=[c_bcT])
    if do_sample:
        cs_sb = sb2("cs_sb", [NS, D])
        k.dma("sp", cs_sb[:], cs_d[:, :], writes=[cs_sb], semkey="ld_cs")
        for kc in range(8):
            k.op("pe", lambda p, kc=kc: p.transpose(out=PB[0][:, kc * NS:(kc + 1) * NS], in_=cs_sb[:, kc * 128:(kc + 1) * 128],
                                                    identity=ident[0:NS, 0:NS]), reads=[cs_sb, ident], writes=[PB[0]])
        k.op("dve", lambda v: v.tensor_copy(out=csT[:].rearrange("p a b -> p (a b)"), in_=PB[0][:, 0:8 * NS]),
             reads=[PB[0]], writes=[csT])
        bada_s = sb2("bada_s", [NS, 3 * D])
        k.dma("sp", bada_s[:], bada_d[0].partition_broadcast(NS), writes=[bada_s], semkey="ld_bada_s")
    for j in range(6):
        w = wa[0]
        for kc in range(8):
            k.dma("sp", w[:, kc, :], wada_d[kc * 128:(kc + 1) * 128, j * 512:(j + 1) * 512], writes=[w],
                  semkey="ld_" + w.name, nowaw=True)
        if j < 4:
            for sub in range(4):
                col = j * 4 + sub
                for kc in range(8):
                    k.op("pe", lambda p, kc=kc, sub=sub, col=col, w=w: p.matmul(
                        PB[1][:, col:col + 1], lhsT=w[:, kc, sub * 128:(sub + 1) * 128], rhs=cT[:, kc:kc + 1],
                        start=(kc == 0), stop=(kc == 7)), reads=[w, cT], writes=[PB[1]])
        else:
            for kc in range(8):
                k.op("pe", lambda p, kc=kc, w=w: p.matmul(PB[2 + (j - 4)][:, :], lhsT=c_bcT[:, kc, :], rhs=w[:, kc, :],
                                                          start=(kc == 0), stop=(kc == 7)),
                     reads=[w, c_bcT], writes=[PB[2 + (j - 4)]])
        if do_sample:
            for kc in range(8):
                k.op("pe", lambda p, kc=kc, w=w: p.matmul(PB[4 + j % 2][0:NS, :], lhsT=csT[:, kc, :], rhs=w[:, kc, :],
                                                          start=(kc == 0), stop=(kc == 7)),
                     reads=[w, csT], writes=[PB[4 + j % 2]])
            k.op("dve", lambda v, j=j: v.tensor_tensor(out=mod_s[:, j * 512:(j + 1) * 512], in0=PB[4 + j % 2][0:NS, :],
                                                       in1=bada_s[:, j * 512:(j + 1) * 512], op=ALU.add),
                 reads=[PB[4 + j % 2], bada_s], writes=[mod_s])
    k.op("dve", lambda v: v.tensor_tensor(out=mod_fm[:], in0=PB[1][:, 0:16], in1=bada_fm[:, 0:16], op=ALU.add),
         reads=[PB[1], bada_fm], writes=[mod_fm])
    k.op("dve", lambda v: v.tensor_scalar(out=mod_fm[:, 8:16], in0=mod_fm[:, 8:16], scalar1=1.0, scalar2=None, op0=ALU.add),
         reads=[mod_fm], writes=[mod_fm])
    for j in range(2):
        k.op("dve", lambda v, j=j: v.scalar_tensor_tensor(out=g1bc[:, j * 512:(j + 1) * 512], in0=PB[2 + j][:, :], scalar=1.0,
                                                           in1=bada_bc[:, j * 512:(j + 1) * 512], op0=ALU.add, op1=ALU.add),
             reads=[PB[2 + j], bada_bc], writes=[g1bc])


    if do_sample:
        k.barrier()
        k2.close()
        k2 = contextlib.ExitStack()
        P16 = NS
        sinkcol_d = din("sinks_col", [128, 1])
        xs = sb2("xs_sb", [P16, D]); hs = sb2("hs", [P16, D]); mix_s = sb2("mix_s", [P16, D])
        hsT = sb2("hsT", [128, 8, P16], BF16)
        pqkv = sb2("pqkv", [P16, 1536])
        qkv_s = sb2("qkv_s", [P16, 12, 128])
        zAs_s = sb2("zAs_s", [P16, 512]); zBs_s = sb2("zBs_s", [P16, 512])
        qr_s = sb2("qr_s", [P16, 512]); kr_s = sb2("kr_s", [P16, 128]); v_s = sb2("v_s", [P16, 128])
        vsb = sb2("vsb", [P16, 128], BF16)
        bd_s = sb2("bd_s", [P16, 8]); bdt_s = sb2("bdt_s", [P16, 8])
        beta_s = sb2("beta_s", [P16, 4]); nbeta_s = sb2("nbeta_s", [P16, 4]); g_s = sb2("g_s", [P16, 4]); eg_s = sb2("eg_s", [P16, 4])
        cs16 = sb2("cs16", [P16, 2, 32])
        k.dma("sp", cs16[:, 0, :], coss_d[0].partition_broadcast(P16), writes=[cs16], semkey="ld_cs16", nowaw=True)
        k.dma("sp", cs16[:, 1, :], sins_d[0].partition_broadcast(P16), writes=[cs16], semkey="ld_cs16", nowaw=True)
        k.dma("sp", xs[:], xs_d[:, :], writes=[xs], semkey="ld_xs")
        k.op("dve", lambda v: v.scalar_tensor_tensor(out=hs[:], in0=mod_s[:, D:2 * D], scalar=1.0, in1=xs[:], op0=ALU.add, op1=ALU.mult),
             reads=[mod_s, xs], writes=[hs])
        k.op("dve", lambda v: v.tensor_tensor(out=hs[:], in0=hs[:], in1=mod_s[:, 0:D], op=ALU.add), reads=[hs, mod_s], writes=[hs])
        for kc in range(8):
            k.op("pe", lambda p, kc=kc: p.transpose(out=PB[0][:, kc * P16:(kc + 1) * P16], in_=hs[:, kc * 128:(kc + 1) * 128],
                                                    identity=ident[0:P16, 0:P16]), reads=[hs, ident], writes=[PB[0]])
        k.op("dve", lambda v: v.tensor_copy(out=hsT[:].rearrange("p a b -> p (a b)"), in_=PB[0][:, 0:8 * P16]), reads=[PB[0]], writes=[hsT])
        for j in range(3):
            for kc in range(8):
                k.op("pe", lambda p, j=j, kc=kc: p.matmul(PB[1 + j][0:P16, :], lhsT=hsT[:, kc, :], rhs=Wf[:, kc, j * 512:(j + 1) * 512],
                                                          start=(kc == 0), stop=(kc == 7)), reads=[hsT, Wf], writes=[PB[1 + j]])
        offs = [(0, 512), (512, 512), (1024, 512), (1536, 264)]
        for j, (o, w_) in enumerate(offs):
            for kc in range(8):
                k.op("pe", lambda p, j=j, o=o, w_=w_, kc=kc: p.matmul(PB[4 + j][0:P16, 0:w_], lhsT=hsT[:, kc, :], rhs=Wt[:, kc, o:o + w_],
                                                                      start=(kc == 0), stop=(kc == 7)), reads=[hsT, Wt], writes=[PB[4 + j]])
        for j in range(3):
            k.op("dve", lambda v, j=j: v.tensor_copy(out=pqkv[:, j * 512:(j + 1) * 512], in_=PB[1 + j][0:P16, :]), reads=[PB[1 + j]], writes=[pqkv])
        k.op("act", lambda a: a.activation(out=zAs_s[:], in_=PB[4][0:P16, :], func=ACT.Silu), reads=[PB[4]], writes=[zAs_s])
        k.op("act", lambda a: a.activation(out=zBs_s[:], in_=PB[6][0:P16, :], func=ACT.Silu), reads=[PB[6]], writes=[zBs_s])
        rts = [sb2("rts%d" % i, [P16, 8, 32]) for i in range(4)]
        q3 = PB[5][0:P16, :].rearrange("p (a b) -> p a b", a=8)
        qr3 = qr_s[:].rearrange("p (a b) -> p a b", a=8)
        cq = bc(cs16[:, 0:1, :], [P16, 8, 32]); sq_ = bc(cs16[:, 1:2, :], [P16, 8, 32])
        k.op("dve", lambda v: v.tensor_tensor(out=rts[0][:], in0=q3[:, :, 0:32], in1=cq, op=ALU.mult), reads=[PB[5], cs16], writes=[rts[0]])
        k.op("dve", lambda v: v.tensor_tensor(out=rts[1][:], in0=q3[:, :, 32:64], in1=sq_, op=ALU.mult), reads=[PB[5], cs16], writes=[rts[1]])
        k.op("dve", lambda v: v.tensor_tensor(out=rts[2][:], in0=q3[:, :, 32:64], in1=cq, op=ALU.mult), reads=[PB[5], cs16], writes=[rts[2]])
        k.op("dve", lambda v: v.tensor_tensor(out=rts[3][:], in0=q3[:, :, 0:32], in1=sq_, op=ALU.mult), reads=[PB[5], cs16], writes=[rts[3]])
        k.op("dve", lambda v: v.tensor_tensor(out=qr3[:, :, 0:32], in0=rts[0][:], in1=rts[1][:], op=ALU.subtract), reads=[rts[0], rts[1]], writes=[qr_s])
        k.op("dve", lambda v: v.tensor_tensor(out=qr3[:, :, 32:64], in0=rts[2][:], in1=rts[3][:], op=ALU.add), reads=[rts[2], rts[3]], writes=[qr_s])
        k.op("dve", lambda v: v.tensor_scalar(out=qr_s[:], in0=qr_s[:], scalar1=0.125, scalar2=None, op0=ALU.mult), reads=[qr_s], writes=[qr_s])
        k3 = PB[7][0:P16, 0:128].rearrange("p (a b) -> p a b", a=2)
        kr3 = kr_s[:].rearrange("p (a b) -> p a b", a=2)
        ck = bc(cs16[:, 0:1, :], [P16, 2, 32]); sk_ = bc(cs16[:, 1:2, :], [P16, 2, 32])
        k.op("dve", lambda v: v.tensor_tensor(out=rts[0][:, 0:2, :], in0=k3[:, :, 0:32], in1=ck, op=ALU.mult), reads=[PB[7], cs16], writes=[rts[0]])
        k.op("dve", lambda v: v.tensor_tensor(out=rts[1][:, 0:2, :], in0=k3[:, :, 32:64], in1=sk_, op=ALU.mult), reads=[PB[7], cs16], writes=[rts[1]])
        k.op("dve", lambda v: v.tensor_tensor(out=rts[2][:, 0:2, :], in0=k3[:, :, 32:64], in1=ck, op=ALU.mult), reads=[PB[7], cs16], writes=[rts[2]])
        k.op("dve", lambda v: v.tensor_tensor(out=rts[3][:, 0:2, :], in0=k3[:, :, 0:32], in1=sk_, op=ALU.mult), reads=[PB[7], cs16], writes=[rts[3]])
        k.op("dve", lambda v: v.tensor_tensor(out=kr3[:, :, 0:32], in0=rts[0][:, 0:2, :], in1=rts[1][:, 0:2, :], op=ALU.subtract), reads=[rts[0], rts[1]], writes=[kr_s])
        k.op("dve", lambda v: v.tensor_tensor(out=kr3[:, :, 32:64], in0=rts[2][:, 0:2, :], in1=rts[3][:, 0:2, :], op=ALU.add), reads=[rts[2], rts[3]], writes=[kr_s])
        k.op("dve", lambda v: v.tensor_copy(out=v_s[:], in_=PB[7][0:P16, 128:256]), reads=[PB[7]], writes=[v_s])
        k.op("dve", lambda v: v.tensor_copy(out=vsb[:], in_=PB[7][0:P16, 128:256]), reads=[PB[7]], writes=[vsb])
        k.op("dve", lambda v: v.tensor_copy(out=bd_s[:], in_=PB[7][0:P16, 256:264]), reads=[PB[7]], writes=[bd_s])
        k.op("act", lambda a: a.activation(out=bdt_s[:, 0:4], in_=bd_s[:, 0:4], func=ACT.Exp, scale=-1.0), reads=[bd_s], writes=[bdt_s])
        k.op("dve", lambda v: v.tensor_scalar(out=bdt_s[:, 0:4], in0=bdt_s[:, 0:4], scalar1=1.0, scalar2=None, op0=ALU.add), reads=[bdt_s], writes=[bdt_s])
        k.op("dve", lambda v: v.reciprocal(out=beta_s[:], in_=bdt_s[:, 0:4]), reads=[bdt_s], writes=[beta_s])
        k.op("dve", lambda v: v.tensor_scalar(out=nbeta_s[:], in0=beta_s[:], scalar1=-1.0, scalar2=None, op0=ALU.mult), reads=[beta_s], writes=[nbeta_s])
        k.op("dve", lambda v: v.tensor_tensor(out=bd_s[:, 4:8], in0=bd_s[:, 4:8], in1=dtb_bc[0:P16, :], op=ALU.add), reads=[bd_s, dtb_bc], writes=[bd_s])
        k.op("act", lambda a: a.activation(out=bdt_s[:, 4:8], in_=bd_s[:, 4:8], func=ACT.Exp), reads=[bd_s], writes=[bdt_s])
        k.op("act", lambda a: a.activation(out=bdt_s[:, 4:8], in_=bdt_s[:, 4:8], func=ACT.Ln, bias=1.0, scale=1.0), reads=[bdt_s], writes=[bdt_s])
        k.op("dve", lambda v: v.tensor_tensor(out=g_s[:], in0=bdt_s[:, 4:8], in1=ea[0:P16, :], op=ALU.mult), reads=[bdt_s, ea], writes=[g_s])
        k.op("act", lambda a: a.activation(out=eg_s[:], in_=g_s[:], func=ACT.Exp), reads=[g_s], writes=[eg_s])
        k3s = contextlib.ExitStack()
        def sb3(name, shape, dt=F32):
            return k3s.enter_context(nc.sbuf_tensor(name, list(shape), dt))
        xp4 = sb3("xp4", [P16, 4, 1536]); cwb = sb3("cwb", [P16, 4, 1536]); tmpc = xp4
        acc_s = sb3("acc_s", [P16, 1536])
        k.dma("sp", xp4[:, 0:3, :], sconv_d[:, :, :], writes=[xp4], semkey="ld_xp4", nowaw=True)
        k.dma("sp", cwb[:].rearrange("p a b -> p (a b)"), convw_d.rearrange("a b -> (a b)").partition_broadcast(P16), writes=[cwb], semkey="ld_cwb")
        k.op("act", lambda a: a.copy(out=xp4[:, 3, :], in_=pqkv[:]), reads=[pqkv], writes=[xp4])
        k.dma("sp", convs_d[:, :, :], xp4[:, 1:4, :], reads=[xp4], semkey="st_smisc")
        k.op("dve", lambda v: v.tensor_tensor(out=tmpc[:], in0=xp4[:], in1=cwb[:], op=ALU.mult), reads=[xp4, cwb], writes=[tmpc])
        k.op("dve", lambda v: v.tensor_reduce(out=acc_s[:], in_=tmpc[:].rearrange("p j c -> p c j"), axis=AX.X, op=ALU.add), reads=[tmpc], writes=[acc_s])
        k.op("act", lambda a: a.activation(out=qkv_s[:].rearrange("p a b -> p (a b)"), in_=acc_s[:], func=ACT.Silu), reads=[acc_s], writes=[qkv_s])
        sqs = sb3("sqs", [P16, 8, 128]); sss = sb3("sss", [P16, 8])
        k.op("dve", lambda v: v.tensor_tensor(out=sqs[:], in0=qkv_s[:, 0:8, :], in1=qkv_s[:, 0:8, :], op=ALU.mult), reads=[qkv_s], writes=[sqs])
        k.op("dve", lambda v: v.tensor_reduce(out=sss[:], in_=sqs[:], axis=AX.X, op=ALU.add), reads=[sqs], writes=[sss])
        k.op("act", lambda a: a.activation(out=sss[:], in_=sss[:], func=ACT.Ln, bias=L2_EPS, scale=1.0), reads=[sss], writes=[sss])
        k.op("act", lambda a: a.activation(out=sss[:, 0:4], in_=sss[:, 0:4], func=ACT.Exp, bias=float(-0.5 * np.log(128.0)), scale=-0.5), reads=[sss], writes=[sss])
        k.op("act", lambda a: a.activation(out=sss[:, 4:8], in_=sss[:, 4:8], func=ACT.Exp, scale=-0.5), reads=[sss], writes=[sss])
        k.op("dve", lambda v: v.tensor_tensor(out=qkv_s[:, 0:8, :], in0=qkv_s[:, 0:8, :], in1=bc(sss[:].rearrange("p (a b) -> p a b", b=1), [P16, 8, 128]), op=ALU.mult),
             reads=[qkv_s, sss], writes=[qkv_s])
        k.barrier()
        k3s.close()
        k3s = contextlib.ExitStack()
        Ssb = sb3("Ssb", [128, P16 * 4, 128])
        sdv = sdelta_d.rearrange("b h k v -> k (b h) v")
        for i4 in range(4):
            k.dma("sp", Ssb[:, i4 * 16:(i4 + 1) * 16, :], sdv[:, i4 * 16:(i4 + 1) * 16, :], writes=[Ssb], semkey="ld_Ssb", nowaw=True)
        qkT_s = sb3("qkT_s", [128, 8, P16])
        for c in range(8):
            k.op("pe", lambda p, c=c: p.transpose(out=PB[0][:, c * P16:(c + 1) * P16], in_=qkv_s[:, c, :], identity=ident[0:P16, 0:P16]),
                 reads=[qkv_s, ident], writes=[PB[0]])
        k.op("dve", lambda v: v.tensor_copy(out=qkT_s[:].rearrange("p a b -> p (a b)"), in_=PB[0][:, 0:8 * P16]), reads=[PB[0]], writes=[qkT_s])
        dmask = sb3("dmask", [P16, P16, 128])
        k.op("pool", lambda g: g.tensor_copy(out=dmask[:], in_=bc(ident[0:P16, 0:P16].rearrange("p (a b) -> p a b", b=1), [P16, P16, 128])),
             reads=[ident], writes=[dmask])
        egm = sb3("egm", [P16, P16, 4]); egbc = sb3("egbc", [128, P16 * 4])
        k.op("dve", lambda v: v.tensor_tensor(out=egm[:], in0=bc(eg_s[:].rearrange("p (a b) -> p a b", a=1), [P16, P16, 4]),
                                              in1=bc(ident[0:P16, 0:P16].rearrange("p (a b) -> p a b", b=1), [P16, P16, 4]), op=ALU.mult),
             reads=[eg_s, ident], writes=[egm])
        k.op("pe", lambda p: p.matmul(PB[1][:, 0:P16 * 4], lhsT=ones[0:P16, :], rhs=egm[:].rearrange("p a b -> p (a b)"), start=True, stop=True),
             reads=[ones, egm], writes=[PB[1]])
        k.op("dve", lambda v: v.tensor_copy(out=egbc[:], in_=PB[1][:, 0:P16 * 4]), reads=[PB[1]], writes=[egbc])
        pred = sb3("pred", [P16, 4, 128]); qS = sb3("qS", [P16, 4, 128]); tmpd = sb3("tmpd", [P16, P16, 128])
        dd = sb3("dd", [P16, 4, 128]); Dm = sb3("Dm", [P16, P16, 128]); o_s = sb3("o_s", [P16, 4, 128])
        qk_s = sb3("qk_s", [P16, 4]); qkt = sb3("qkt", [P16, 4, 128])
        k.op("dve", lambda v: v.tensor_tensor(out=qkt[:], in0=qkv_s[:, 0:4, :], in1=qkv_s[:, 4:8, :], op=ALU.mult), reads=[qkv_s], writes=[qkt])
        k.op("dve", lambda v: v.tensor_reduce(out=qk_s[:], in_=qkt[:], axis=AX.X, op=ALU.add), reads=[qkt], writes=[qk_s])
        for h in range(4):
            for which, dst in ((4, pred), (0, qS)):
                banks = [PB[2], PB[3], PB[4], PB[5]] if which == 4 else [PB[6], PB[7], PB[0], PB[1]]
                for b in range(P16):
                    pb = banks[b // 4]
                    k.op("pe", lambda p, b=b, pb=pb, which=which, h=h: p.matmul(pb[0:P16, (b % 4) * 128:(b % 4 + 1) * 128], lhsT=qkT_s[:, which + h, :],
                                                                               rhs=Ssb[:, b * 4 + h, :], start=True, stop=True),
                         reads=[qkT_s, Ssb], writes=[pb])
                for j in range(4):
                    k.op("dve", lambda v, j=j, banks=banks: v.tensor_tensor(out=tmpd[:, 4 * j:4 * j + 4, :], in0=banks[j][0:P16, :].rearrange("p (a b) -> p a b", a=4),
                                                                            in1=dmask[:, 4 * j:4 * j + 4, :], op=ALU.mult), reads=[banks[j], dmask], writes=[tmpd])
                k.op("dve", lambda v, dst=dst, h=h: v.tensor_reduce(out=dst[:, h, :], in_=tmpd[:].rearrange("p b v -> p v b"), axis=AX.X, op=ALU.add),
                     reads=[tmpd], writes=[dst])
            k.op("dve", lambda v, h=h: v.scalar_tensor_tensor(out=dd[:, h, :], in0=pred[:, h, :], scalar=eg_s[:, h:h + 1], in1=qkv_s[:, 8 + h, :],
                                                               op0=ALU.mult, op1=ALU.subtract), reads=[pred, eg_s, qkv_s], writes=[dd])
            k.op("dve", lambda v, h=h: v.tensor_scalar(out=dd[:, h, :], in0=dd[:, h, :], scalar1=nbeta_s[:, h:h + 1], scalar2=None, op0=ALU.mult),
                 reads=[dd, nbeta_s], writes=[dd])
            k.op("dve", lambda v, h=h: v.tensor_scalar(out=o_s[:, h, :], in0=dd[:, h, :], scalar1=qk_s[:, h:h + 1], scalar2=None, op0=ALU.mult),
                 reads=[dd, qk_s], writes=[o_s])
            k.op("dve", lambda v, h=h: v.scalar_tensor_tensor(out=o_s[:, h, :], in0=qS[:, h, :], scalar=eg_s[:, h:h + 1], in1=o_s[:, h, :],
                                                               op0=ALU.mult, op1=ALU.add), reads=[qS, eg_s, o_s], writes=[o_s])
            k.op("dve", lambda v, h=h: v.tensor_tensor(out=Dm[:], in0=bc(dd[:, h:h + 1, :], [P16, P16, 128]), in1=dmask[:], op=ALU.mult),
                 reads=[dd, dmask], writes=[Dm])
            banks = [PB[2], PB[3], PB[4], PB[5]]
            for b in range(P16):
                pb = banks[b // 4]
                k.op("pe", lambda p, b=b, pb=pb, h=h: p.matmul(pb[:, (b % 4) * 128:(b % 4 + 1) * 128], lhsT=qkv_s[:, 4 + h, :], rhs=Dm[:, b, :],
                                                               start=True, stop=True), reads=[qkv_s, Dm], writes=[pb])
            for b in range(P16):
                pb = banks[b // 4]
                k.op("dve", lambda v, b=b, pb=pb, h=h: v.scalar_tensor_tensor(out=Ssb[:, b * 4 + h, :], in0=Ssb[:, b * 4 + h, :], scalar=egbc[:, b * 4 + h:b * 4 + h + 1],
                                                                              in1=pb[:, (b % 4) * 128:(b % 4 + 1) * 128], op0=ALU.mult, op1=ALU.add),
                     reads=[Ssb, egbc, pb], writes=[Ssb])
        ddv = deltas_d.rearrange("b h k v -> k (b h) v")
        for i4 in range(4):
            k.dma(["sp", "act", "sp", "act"][i4], ddv[:, i4 * 16:(i4 + 1) * 16, :], Ssb[:, i4 * 16:(i4 + 1) * 16, :], reads=[Ssb], semkey="st_smisc")
        oss_s = sb3("oss_s", [P16, 4])
        k.op("dve", lambda v: v.tensor_tensor(out=qkt[:], in0=o_s[:], in1=o_s[:], op=ALU.mult), reads=[o_s], writes=[qkt])
        k.op("dve", lambda v: v.tensor_reduce(out=oss_s[:], in_=qkt[:], axis=AX.X, op=ALU.add), reads=[qkt], writes=[oss_s])
        k.op("act", lambda a: a.activation(out=oss_s[:], in_=oss_s[:], func=ACT.Ln, bias=RMS_EPS, scale=1.0 / 128.0), reads=[oss_s], writes=[oss_s])
        k.op("act", lambda a: a.activation(out=oss_s[:], in_=oss_s[:], func=ACT.Exp, scale=-0.5), reads=[oss_s], writes=[oss_s])
        k.op("dve", lambda v: v.tensor_tensor(out=o_s[:], in0=o_s[:], in1=bc(oss_s[:].rearrange("p (a b) -> p a b", b=1), [P16, 4, 128]), op=ALU.mult),
             reads=[o_s, oss_s], writes=[o_s])
        k.op("dve", lambda v: v.tensor_tensor(out=o_s[:], in0=o_s[:], in1=bc(norma_bc[0:P16, :].rearrange("p (a b) -> p a b", a=1), [P16, 4, 128]), op=ALU.mult),
             reads=[o_s, norma_bc], writes=[o_s])
        k.op("dve", lambda v: v.tensor_tensor(out=mix_s[:, 0:512], in0=o_s[:].rearrange("p a b -> p (a b)"), in1=zAs_s[:], op=ALU.mult),
             reads=[o_s, zAs_s], writes=[mix_s])
        k.barrier()
        k3s.close()
        k3s = contextlib.ExitStack()
        Kc = sb3("Kc", [128, P16, 128]); Vc = sb3("Vc", [128, P16, 128])
        KcT = sb3("KcT", [128, P16, 128], BF16); VcB = sb3("VcB", [128, P16, 128], BF16)
        k.dma("sp", Kc[:], sk_d.rearrange("b s c -> s b c"), writes=[Kc], semkey="ld_Kc")
        k.dma("act", Vc[:], sv_d.rearrange("b s c -> s b c"), writes=[Vc], semkey="ld_Vc")
        k.dma("sp", swaks_d[:, 0:127, :], sk_d[:, 1:128, :], semkey="st_smisc")
        k.dma("act", swavs_d[:, 0:127, :], sv_d[:, 1:128, :], semkey="st_smisc")
        k.dma("sp", swaks_d[:, 127, :], kr_s[:], reads=[kr_s], semkey="st_smisc")
        k.dma("sp", swavs_d[:, 127, :], v_s[:], reads=[v_s], semkey="st_smisc")
        k.op("pool", lambda g: g.tensor_copy(out=VcB[:], in_=Vc[:]), reads=[Vc], writes=[VcB])
        for b in range(P16):
            pb = [PB[2], PB[3], PB[4], PB[5]][b // 4]
            k.op("pe", lambda p, b=b, pb=pb: p.transpose(out=pb[:, (b % 4) * 128:(b % 4 + 1) * 128], in_=Kc[:, b, :], identity=ident[:]),
                 reads=[Kc, ident], writes=[pb])
        for j in range(4):
            pb = [PB[2], PB[3], PB[4], PB[5]][j]
            k.op("act", lambda a, j=j, pb=pb: a.copy(out=KcT[:, 4 * j:4 * j + 4, :], in_=pb[:, :].rearrange("p (a b) -> p a b", a=4)), reads=[pb], writes=[KcT])
        qT_s = sb3("qT_s", [128, 4, P16]); Aq = sb3("Aq", [128, P16, 2, 4], BF16)
        knT = sb3("knT", [128, P16], BF16); zBT = sb3("zBT", [128, 4, P16])
        for c in range(4):
            k.op("pe", lambda p, c=c: p.transpose(out=PB[6][:, c * P16:(c + 1) * P16], in_=qr_s[:, c * 128:(c + 1) * 128], identity=ident[0:P16, 0:P16]),
                 reads=[qr_s, ident], writes=[PB[6]])
        k.op("pe", lambda p: p.transpose(out=PB[6][:, 4 * P16:5 * P16], in_=kr_s[:], identity=ident[0:P16, 0:P16]), reads=[kr_s, ident], writes=[PB[6]])
        for c in range(4):
            k.op("pe", lambda p, c=c: p.transpose(out=PB[6][:, (5 + c) * P16:(6 + c) * P16], in_=zBs_s[:, c * 128:(c + 1) * 128], identity=ident[0:P16, 0:P16]),
                 reads=[zBs_s, ident], writes=[PB[6]])
        k.op("dve", lambda v: v.tensor_copy(out=qT_s[:].rearrange("p a b -> p (a b)"), in_=PB[6][:, 0:4 * P16]), reads=[PB[6]], writes=[qT_s])
        k.op("dve", lambda v: v.tensor_copy(out=knT[:], in_=PB[6][:, 4 * P16:5 * P16]), reads=[PB[6]], writes=[knT])
        k.op("dve", lambda v: v.tensor_copy(out=zBT[:].rearrange("p a b -> p (a b)"), in_=PB[6][:, 5 * P16:9 * P16]), reads=[PB[6]], writes=[zBT])
        k.op("pool", lambda g: g.memset(Aq[:], 0.0), writes=[Aq])
        k.op("dve", lambda v: v.tensor_copy(out=Aq[0:64, :, 0, :], in_=qT_s[0:64, :, :].rearrange("p c b -> p b c")), reads=[qT_s], writes=[Aq])
        k.op("dve", lambda v: v.tensor_copy(out=Aq[64:128, :, 1, :], in_=qT_s[64:128, :, :].rearrange("p c b -> p b c")), reads=[qT_s], writes=[Aq])
        for b in range(P16):
            k.op("pe", lambda p, b=b: p.matmul(PB[7][:, b * 8:(b + 1) * 8], lhsT=KcT[:, b, :], rhs=Aq[:, b, :, :].rearrange("p a b -> p (a b)"),
                                               start=True, stop=True), reads=[KcT, Aq], writes=[PB[7]])
        STs = sb3("STs", [128, 128])
        k.op("dve", lambda v: v.tensor_copy(out=STs[:], in_=PB[7][:, 0:128]), reads=[PB[7]], writes=[STs])
        k.op("pe", lambda p: p.transpose(out=PB[0][:, 0:128], in_=STs[:], identity=ident[:]), reads=[STs, ident], writes=[PB[0]])
        k.op("pe", lambda p: p.matmul(PB[0][:, 128:128 + P16], lhsT=Aq[:].rearrange("p a b c -> p (a b c)"), rhs=knT[:], start=True, stop=True),
             reads=[Aq, knT], writes=[PB[0]])
        M2 = sb3("M2", [128, P16]); M2t = sb3("M2t", [128, P16])
        k.op("pool", lambda g: g.affine_select(out=M2t[:], in_=ones[:, 0:P16], pattern=[[-8, P16]], compare_op=ALU.is_ge, fill=0.0, base=0, channel_multiplier=1),
             reads=[ones], writes=[M2t])
        k.op("pool", lambda g: g.affine_select(out=M2[:], in_=M2t[:], pattern=[[8, P16]], compare_op=ALU.is_ge, fill=0.0, base=7, channel_multiplier=-1),
             reads=[M2t], writes=[M2])
        sm = sb3("sm", [128, 16]); tmps = sb3("tmps", [128, P16])
        sinkcol = sb3("sinkcol", [128, 1])
        k.dma("sp", sinkcol[:], sinkcol_d[:, :], writes=[sinkcol], semkey="ld_sinkcol")
        k.op("dve", lambda v: v.tensor_tensor(out=tmps[:], in0=PB[0][:, 128:128 + P16], in1=M2[:], op=ALU.mult), reads=[PB[0], M2], writes=[tmps])
        k.op("dve", lambda v: v.tensor_reduce(out=sm[:, 0:1], in_=tmps[:], axis=AX.X, op=ALU.add), reads=[tmps], writes=[sm])
        k.op("dve", lambda v: v.tensor_reduce(out=sm[:, 1:2], in_=PB[0][:, 0:128], axis=AX.X, op=ALU.max), reads=[PB[0]], writes=[sm])
        k.op("dve", lambda v: v.tensor_tensor(out=sm[:, 1:2], in0=sm[:, 1:2], in1=sm[:, 0:1], op=ALU.max), reads=[sm], writes=[sm])
        k.op("dve", lambda v: v.tensor_tensor(out=sm[:, 1:2], in0=sm[:, 1:2], in1=sinkcol[:], op=ALU.max), reads=[sm, sinkcol], writes=[sm])
        k.op("dve", lambda v: v.tensor_scalar(out=sm[:, 2:3], in0=sm[:, 1:2], scalar1=-1.0, scalar2=None, op0=ALU.mult), reads=[sm], writes=[sm])
        Ps = sb3("Ps", [128, 128])
        k.op("act", lambda a: a.activation(out=Ps[:], in_=PB[0][:, 0:128], func=ACT.Exp, bias=sm[:, 2:3], scale=1.0, accum_out=sm[:, 3:4]),
             reads=[PB[0], sm], writes=[Ps, sm])
        k.op("act", lambda a: a.activation(out=sm[:, 4:5], in_=sm[:, 0:1], func=ACT.Exp, bias=sm[:, 2:3], scale=1.0), reads=[sm], writes=[sm])
        k.op("act", lambda a: a.activation(out=sm[:, 5:6], in_=sinkcol[:], func=ACT.Exp, bias=sm[:, 2:3], scale=1.0), reads=[sm, sinkcol], writes=[sm])
        k.op("dve", lambda v: v.tensor_tensor(out=sm[:, 6:7], in0=sm[:, 3:4], in1=sm[:, 4:5], op=ALU.add), reads=[sm], writes=[sm])
        k.op("dve", lambda v: v.tensor_tensor(out=sm[:, 6:7], in0=sm[:, 6:7], in1=sm[:, 5:6], op=ALU.add), reads=[sm], writes=[sm])
        k.op("dve", lambda v: v.reciprocal(out=sm[:, 7:8], in_=sm[:, 6:7]), reads=[sm], writes=[sm])
        k.op("dve", lambda v: v.tensor_scalar(out=Ps[:], in0=Ps[:], scalar1=sm[:, 7:8], scalar2=None, op0=ALU.mult), reads=[Ps, sm], writes=[Ps])
        k.op("dve", lambda v: v.tensor_tensor(out=sm[:, 8:9], in0=sm[:, 4:5], in1=sm[:, 7:8], op=ALU.mult), reads=[sm], writes=[sm])
        PsT = sb3("PsT", [128, 128], BF16); Wd = sb3("Wd", [128, P16]); Wn = sb3("Wn", [P16, 128], BF16)
        k.op("pe", lambda p: p.transpose(out=PB[1][:, 0:128], in_=Ps[:], identity=ident[:]), reads=[Ps, ident], writes=[PB[1]])
        k.op("act", lambda a: a.copy(out=PsT[:], in_=PB[1][:, 0:128]), reads=[PB[1]], writes=[PsT])
        k.op("dve", lambda v: v.tensor_scalar(out=Wd[:], in0=M2[:], scalar1=sm[:, 8:9], scalar2=None, op0=ALU.mult), reads=[M2, sm], writes=[Wd])
        k.op("pe", lambda p: p.transpose(out=PB[1][0:P16, 128:256], in_=Wd[:], identity=ident[:]), reads=[Wd, ident], writes=[PB[1]])
        k.op("act", lambda a: a.copy(out=Wn[:], in_=PB[1][0:P16, 128:256]), reads=[PB[1]], writes=[Wn])
        for b in range(P16):
            k.op("pe", lambda p, b=b: p.matmul(PB[2][:, b * 8:(b + 1) * 8], lhsT=VcB[:, b, :], rhs=PsT[:, b * 8:(b + 1) * 8], start=True, stop=False),
                 reads=[VcB, PsT], writes=[PB[2]])
            k.op("pe", lambda p, b=b: p.matmul(PB[2][:, b * 8:(b + 1) * 8], lhsT=vsb[:], rhs=Wn[:, b * 8:(b + 1) * 8], start=False, stop=True),
                 reads=[vsb, Wn], writes=[PB[2]])
        obT = sb3("obT", [128, 4, P16])
        OT4 = PB[2][:, 0:128].rearrange("p (b k c) -> p b k c", b=P16, k=2)
        k.op("dve", lambda v: v.tensor_copy(out=obT[0:64, :, :].rearrange("p c b -> p b c"), in_=OT4[0:64, :, 0, :]), reads=[PB[2]], writes=[obT])
        k.op("dve", lambda v: v.tensor_copy(out=obT[64:128, :, :].rearrange("p c b -> p b c"), in_=OT4[64:128, :, 1, :]), reads=[PB[2]], writes=[obT])
        mixT_s = sb3("mixT_s", [128, 8, P16], BF16)
        k.op("dve", lambda v: v.tensor_tensor(out=mixT_s[:, 4:8, :], in0=obT[:], in1=zBT[:], op=ALU.mult), reads=[obT, zBT], writes=[mixT_s])
        for c in range(4):
            k.op("pe", lambda p, c=c: p.transpose(out=PB[3][:, c * P16:(c + 1) * P16], in_=mix_s[:, c * 128:(c + 1) * 128], identity=ident[0:P16, 0:P16]),
                 reads=[mix_s, ident], writes=[PB[3]])
        k.op("dve", lambda v: v.tensor_copy(out=mixT_s[:, 0:4, :].rearrange("p a b -> p (a b)"), in_=PB[3][:, 0:4 * P16]), reads=[PB[3]], writes=[mixT_s])
        for j in range(2):
            for kc in range(8):
                k.op("pe", lambda p, j=j, kc=kc: p.matmul(PB[4 + j][0:P16, :], lhsT=mixT_s[:, kc, :], rhs=Wo[:, kc, j * 512:(j + 1) * 512],
                                                          start=(kc == 0), stop=(kc == 7)), reads=[mixT_s, Wo], writes=[PB[4 + j]])
        ypre_s = sb3("ypre_s", [P16, D]); yo_s = sb3("yo_s", [P16, D]); st_s = sb3("st_s", [P16, 8])
        for j in range(2):
            k.op("dve", lambda v, j=j: v.scalar_tensor_tensor(out=ypre_s[:, j * 512:(j + 1) * 512], in0=mod_s[:, 2 * D + j * 512:2 * D + (j + 1) * 512], scalar=1.0,
                                                               in1=PB[4 + j][0:P16, :], op0=ALU.add, op1=ALU.mult), reads=[mod_s, PB[4 + j]], writes=[ypre_s])
        k.op("dve", lambda v: v.scalar_tensor_tensor(out=ypre_s[:], in0=xs[:], scalar=ALPHA, in1=ypre_s[:], op0=ALU.mult, op1=ALU.add),
             reads=[xs, ypre_s], writes=[ypre_s])
        k.op("act", lambda a: a.activation(out=yo_s[:], in_=ypre_s[:], func=ACT.Identity, accum_out=st_s[:, 0:1]), reads=[ypre_s], writes=[yo_s, st_s])
        k.op("act", lambda a: a.activation(out=yo_s[:], in_=ypre_s[:], func=ACT.Square, accum_out=st_s[:, 1:2]), reads=[ypre_s], writes=[yo_s, st_s])
        k.op("dve", lambda v: v.tensor_scalar(out=st_s[:, 2:3], in0=st_s[:, 0:1], scalar1=1.0 / D, scalar2=None, op0=ALU.mult), reads=[st_s], writes=[st_s])
        k.op("dve", lambda v: v.tensor_tensor(out=st_s[:, 3:4], in0=st_s[:, 2:3], in1=st_s[:, 2:3], op=ALU.mult), reads=[st_s], writes=[st_s])
        k.op("dve", lambda v: v.scalar_tensor_tensor(out=st_s[:, 4:5], in0=st_s[:, 1:2], scalar=1.0 / D, in1=st_s[:, 3:4], op0=ALU.mult, op1=ALU.subtract),
             reads=[st_s], writes=[st_s])
        k.op("act", lambda a: a.activation(out=st_s[:, 5:6], in_=st_s[:, 4:5], func=ACT.Ln, bias=LN_EPS, scale=1.0), reads=[st_s], writes=[st_s])
        k.op("act", lambda a: a.activation(out=st_s[:, 5:6], in_=st_s[:, 5:6], func=ACT.Exp, scale=-0.5), reads=[st_s], writes=[st_s])
        k.op("dve", lambda v: v.scalar_tensor_tensor(out=st_s[:, 6:7], in0=st_s[:, 2:3], scalar=-1.0, in1=st_s[:, 5:6], op0=ALU.mult, op1=ALU.mult),
             reads=[st_s], writes=[st_s])
        k.op("act", lambda a: a.activation(out=yo_s[:], in_=ypre_s[:], func=ACT.Identity, bias=st_s[:, 6:7], scale=st_s[:, 5:6]), reads=[ypre_s, st_s], writes=[yo_s])
        k.op("dve", lambda v: v.tensor_tensor(out=yo_s[:], in0=yo_s[:], in1=lng_bc[0:P16, :], op=ALU.mult), reads=[yo_s, lng_bc], writes=[yo_s])
        k.op("dve", lambda v: v.tensor_tensor(out=yo_s[:], in0=yo_s[:], in1=lnb_bc[0:P16, :], op=ALU.add), reads=[yo_s, lnb_bc], writes=[yo_s])
        k.dma("sp", ys_d[:, :], yo_s[:], reads=[yo_s], semkey="st_smisc")
        k.barrier()
        k3s.close()

    k.barrier()
    k2.close()
    k1s.close()
    identb = k.sb("identb", [128, 128], BF16); Ub = k.sb("Ub", [128, 128], BF16)
    k.op("pool", lambda g: g.tensor_copy(out=identb[:], in_=ident[:]), reads=[ident], writes=[identb])
    k.op("pool", lambda g: g.tensor_copy(out=Ub[:], in_=U[:]), reads=[U], writes=[Ub])
    Xt = [k.sb("Xt%d" % i, [128, D]) for i in range(2)]
    Xb = k.sb("Xb", [128, D], BF16)
    hT = k.sb("hT", [128, 8, 128], BF16)
    pre = k.sb("pre", [128, 12, 131])
    k.op("pool", lambda g: g.memset(pre[:], 0.0), writes=[pre])
    cm = [k.sb("cm%d" % i, [128, 12, 128]) for i in range(2)]
    qkvs = cm[0]
    qkb = k.sb("qkb", [128, 12, 128], BF16)
    sqb = k.sb("sqb", [128, 8, 128], BF16)
    lnss = k.sb("lnss", [128, 8, 128])
    rn = lnss
    zAs = k.sb("zAs", [128, 512]); zBs2 = [k.sb("zBs%d" % i, [128, 512]) for i in range(2)]
    qrb2 = [k.sb("qrb%d" % i, [128, 512], BF16) for i in range(2)]; kr2 = [k.sb("kr%d" % i, [128, 128]) for i in range(2)]
    rt = [k.sb("rt%d" % i, [128, 8, 32]) for i in range(4)]
    vB = [k.sb("vB%d" % i, [128, 128], BF16) for i in range(3)]
    vB32 = k.sb("vB32", [128, 128])
    kTb = [k.sb("kTb%d" % i, [128, 128], BF16) for i in range(2)]
    k.op("pool", lambda g: g.memset(kTb[1][:], 0.0), writes=[kTb[1]])
    k.op("pool", lambda g: g.memset(vB[2][:], 0.0), writes=[vB[2]])
    qTb = k.sb("qTb", [128, 4, 128], BF16)
    bd = k.sb("bd", [128, 8]); bdt = k.sb("bdt", [128, 8])
    beta = k.sb("beta", [128, 4]); negbeta = k.sb("negbeta", [128, 4]); gg = k.sb("gg", [128, 4])
    gsp = k.sb("gsp", [128, 8], BF16); gtmp = k.sb("gtmp", [128, 4])
    gUh = k.sb("gUh", [128, 4, 128], BF16); gUl = k.sb("gUl", [128, 4, 128], BF16)
    Gc = k.sb("Gc", [128, 4]); negGc = k.sb("negGc", [128, 4]); eG = k.sb("eG", [128, 4]); beG = k.sb("beG", [128, 4])
    edG = k.sb("edG", [128, 4]); egl = k.sb("egl", [128, 4]); dG = k.sb("dG", [128, 4]); Glb = k.sb("Glb", [128, 4])
    tmp1 = k.sb("tmp1", [128, 4, 128]); tmp2 = k.sb("tmp2", [128, 4, 128])
    E1 = tmp1; E2 = tmp2
    eGbc = k.sb("eGbc", [128, 4, 128]); qdT = k.sb("qdT", [128, 4, 128], BF16)
    AqkT = k.sb("AqkT", [128, 4, 128], BF16)
    Y0f = k.sb("Y0f", [128, 4, 128])
    XRh = [[k.sb("XRh%d_%d" % (g_, i), [128, 2, 256], BF16) for i in range(2)] for g_ in range(2)]
    XRl = [[k.sb("XRl%d_%d" % (g_, i), [128, 2, 256], BF16) for i in range(2)] for g_ in range(2)]
    Yh = [[k.sb("Yh%d_%d" % (g_, i), [128, 2, 128], BF16) for i in range(2)] for g_ in range(2)]
    Yl = [[k.sb("Yl%d_%d" % (g_, i), [128, 2, 128], BF16) for i in range(2)] for g_ in range(2)]
    Rf = [k.sb("Rf%d" % g_, [128, 2, 128]) for g_ in range(2)]
    kbt = k.sb("kbt", [128, 4, 128], BF16); kdt = k.sb("kdt", [128, 4, 128], BF16); bvt = k.sb("bvt", [128, 4, 128], BF16)
    wT = [k.sb("wT%d" % g_, [128, 2, 128], BF16) for g_ in range(2)]
    ut = [k.sb("ut%d" % g_, [128, 2, 128]) for g_ in range(2)]
    uub = [k.sb("uub%d" % g_, [128, 2, 128], BF16) for g_ in range(2)]
    S = [k.sb("S%d" % g_, [128, 2, 128]) for g_ in range(2)]
    Sb = [k.sb("Sb%d" % g_, [128, 2, 128], BF16) for g_ in range(2)]
    for g_ in range(2):
        k.op("pool", lambda g, g_=g_: g.memset(S[g_][:], 0.0), writes=[S[g_]])
        k.op("pool", lambda g, g_=g_: g.memset(Sb[g_][:], 0.0), writes=[Sb[g_]])
    oss = [k.sb("oss%d" % g_, [128, 2]) for g_ in range(2)]; orr = [k.sb("orr%d" % g_, [128, 2]) for g_ in range(2)]
    nz = k.sb("nz", [128, 4, 128])
    mixb = k.sb("mixb", [128, D], BF16); obt = k.sb("obt", [128, 512])
    ojunk = [tmp1[:, 0, :], tmp2[:, 0, :]]
    mixT = k.sb("mixT", [128, 8, 128], BF16)
    SC = k.sb("SC", [128, 8, 256]); Pb = k.sb("Pb", [128, 8, 256], BF16)
    PTb = k.sb("PTb", [128, 16, 128], BF16)
    mx = k.sb("mx", [128, 8]); negm = k.sb("negm", [128, 8]); rs = k.sb("rs", [128, 8]); es_ = k.sb("es_", [128, 8])
    rden = k.sb("rden", [128, 8])
    SCf = SC[:].rearrange("p a b -> p (a b)")
    ypre = SCf[:, 0:D]
    st = k.sb("st", [128, 8]); negpad = k.sb("negpad", [128, 1])
    Yt = [SCf[:, D:2 * D]]

    def v3(ap, a):
        return ap.rearrange("p (a b) -> p a b", a=a)

    def pbf(pb):
        return pb[:, :].bitcast(BF16)

    def mm(out, lhsT, rhs, start, stop, reads, writes):
        ncol = int(np.prod(out.shape[1:]))
        k.op("pe", lambda p: p.matmul(out, lhsT=lhsT, rhs=rhs, start=start, stop=stop), reads=reads, writes=writes,
             cost=0.14 + ncol / 1200.0)

    def tr(out, in_, idn, reads, writes):
        k.op("pe", lambda p: p.transpose(out=out, in_=in_, identity=idn), reads=reads + [idn], writes=writes, cost=0.25)

    k.barrier()
    k.pe_inorder = True
    krb = k.sb("krb", [128, 128], BF16)
    gdone = {}

    def P1a(n):
        f_ = full(n); sp_ = swaprep(n)
        c0 = 0 if f_ else 4
        X = Xt[n % 2]
        cs_t = cst[n % 2]
        zBs = zBs2[n % 2]; qrb = qrb2[n % 2]; kr = kr2[n % 2]
        vcur = vB[n % 3]
        k.dma("sp", X[:], x_d[n * 128:(n + 1) * 128, :], writes=[X], semkey="ld_" + X.name)
        if f_ or sp_:
            k.dma("sp", cs_t[:, 0, :], cosk_d[n * 128:(n + 1) * 128, :], writes=[cs_t], semkey="ld_" + cs_t.name)
            k.dma("sp", cs_t[:, 1, :], sink_d[n * 128:(n + 1) * 128, :], writes=[cs_t], semkey="ld_" + cs_t.name, nowaw=True)
        yield
        k.op("pool", lambda g: g.tensor_copy(out=Xb[:], in_=X[:]), reads=[X], writes=[Xb])
        P3b = pbf(PB[2])
        yield
        for kc in range(8):
            tr(P3b[:, kc * 128:(kc + 1) * 128], Xb[:, kc * 128:(kc + 1) * 128], identb[:], [Xb], [PB[2]])
        yield
        for kc in range(8):
            k.op("act" if kc % 2 == 0 else "dve",
                 (lambda a, kc=kc: a.activation(out=hT[:, kc, :], in_=P3b[:, kc * 128:(kc + 1) * 128], func=ACT.Identity,
                                                bias=mod_fm[:, kc:kc + 1], scale=mod_fm[:, 8 + kc:9 + kc])) if kc % 2 == 0 else
                 (lambda v, kc=kc: v.tensor_scalar(out=hT[:, kc, :], in0=P3b[:, kc * 128:(kc + 1) * 128], scalar1=mod_fm[:, 8 + kc:9 + kc],
                                                   scalar2=mod_fm[:, kc:kc + 1], op0=ALU.mult, op1=ALU.add)),
                 reads=[PB[2], mod_fm], writes=[hT])
            if kc % 2 == 1:
                yield
        pc = pre
        k.op("pool", lambda g: g.tensor_copy(out=pc[:, :, 0:3], in_=pc[:, :, 128:131]), reads=[pc], writes=[pc])
        yield
        for j in range(3):
            if j == 0 and not (f_ or sp_):
                continue
            pb = [PB[3], PB[2], PB[3]][j]
            for c4 in range(4):
                c = j * 4 + c4
                for kc in range(8):
                    mm(pb[:, c4 * 128:(c4 + 1) * 128], Wf[:, kc, c * 128:(c + 1) * 128], hT[:, kc, :], kc == 0, kc == 7, [Wf, hT], [pb])
                yield
            if j != 1:
                k.op("act", lambda a, j=j, pb=pb: a.activation(out=pc[:, j * 4:(j + 1) * 4, 3:131], in_=v3(pb[:, :], 4), func=ACT.Identity,
                                                               scale=valid_bc[:, n:n + 1]), reads=[pb, valid_bc], writes=[pc])
            else:
                k.op("dve", lambda v, j=j, pb=pb: v.tensor_scalar(out=pc[:, j * 4:(j + 1) * 4, 3:131], in0=v3(pb[:, :], 4), scalar1=valid_bc[:, n:n + 1],
                                                                  scalar2=None, op0=ALU.mult), reads=[pb, valid_bc], writes=[pc])
            yield
        nch = 12 - c0
        k.op("dve", lambda v: v.tensor_tensor(out=cm[0][:, c0:12, :], in0=pc[:, c0:12, 0:128], in1=bc(cwT[:, c0:12].rearrange("p (a b) -> p a b", b=1), [128, nch, 128]), op=ALU.mult),
             reads=[pc, cwT], writes=[cm[0]], cost=0.13 * nch)
        yield
        for j in range(1, 4):
            k.op("pool", lambda g, j=j: g.tensor_tensor(out=cm[1][:, c0:12, :], in0=pc[:, c0:12, j:j + 128], in1=bc(cwT[:, j * 12 + c0:(j + 1) * 12].rearrange("p (a b) -> p a b", b=1), [128, nch, 128]), op=ALU.mult),
                 reads=[pc, cwT], writes=[cm[1]], cost=0.25 * nch)
            yield
            k.op("dve", lambda v: v.tensor_tensor(out=cm[0][:, c0:12, :], in0=cm[0][:, c0:12, :], in1=cm[1][:, c0:12, :], op=ALU.add), reads=[cm[0], cm[1]], writes=[cm[0]],
                 cost=0.13 * nch)
            yield
        k.op("act", lambda a: a.activation(out=qkvs[:, c0:12, :], in_=cm[0][:, c0:12, :], func=ACT.Silu), reads=[cm[0]], writes=[qkvs], cost=0.12 * nch)
        yield
        offs = [(0, 512), (512, 512), (1024, 512), (1536, 264)]

        def tproj(j, pb):
            o, w_ = offs[j]
            for kc in range(8):
                mm(pb[:, 0:w_], hT[:, kc, :], Wt[:, kc, o:o + w_], kc == 0, kc == 7, [hT, Wt], [pb])
        PzA, PqB = PB[2], PB[3]
        if f_:
            tproj(0, PzA)
            yield
            k.op("act", lambda a: a.activation(out=zAs[:], in_=PzA[:, :], func=ACT.Silu), reads=[PzA], writes=[zAs])
            yield
            tproj(1, PqB)
            yield
            q3 = v3(PqB[:, :], 8)
            qr3 = v3(qrb[:], 8)
            cq = bc(cs_t[:, 0:1, :], [128, 8, 32]); sq_ = bc(cs_t[:, 1:2, :], [128, 8, 32])
            k.op("dve", lambda v: v.tensor_tensor(out=rt[0][:], in0=q3[:, :, 0:32], in1=cq, op=ALU.mult), reads=[PqB, cs_t], writes=[rt[0]])
            k.op("dve", lambda v: v.tensor_tensor(out=rt[1][:], in0=q3[:, :, 32:64], in1=sq_, op=ALU.mult), reads=[PqB, cs_t], writes=[rt[1]])
            yield
            k.op("dve", lambda v: v.tensor_tensor(out=rt[2][:], in0=q3[:, :, 32:64], in1=cq, op=ALU.mult), reads=[PqB, cs_t], writes=[rt[2]])
            k.op("dve", lambda v: v.tensor_tensor(out=rt[3][:], in0=q3[:, :, 0:32], in1=sq_, op=ALU.mult), reads=[PqB, cs_t], writes=[rt[3]])
            yield
            k.op("pool", lambda g: g.tensor_tensor(out=qr3[:, :, 0:32], in0=rt[0][:], in1=rt[1][:], op=ALU.subtract), reads=[rt[0], rt[1]], writes=[qrb])
            k.op("pool", lambda g: g.tensor_tensor(out=qr3[:, :, 32:64], in0=rt[2][:], in1=rt[3][:], op=ALU.add), reads=[rt[2], rt[3]], writes=[qrb])
            yield
        PzB, Pk = PB[2], PB[3]
        if f_:
            tproj(2, PzB)
            yield
            k.op("act", lambda a: a.activation(out=zBs[:], in_=PzB[:, :], func=ACT.Silu), reads=[PzB], writes=[zBs])
            yield
        tproj(3, Pk)
        yield
        if f_ or sp_:
            k3 = v3(Pk[:, 0:128], 2)
            kr3 = v3(kr[:], 2)
            ck = bc(cs_t[:, 0:1, :], [128, 2, 32]); sk_ = bc(cs_t[:, 1:2, :], [128, 2, 32])
            k.op("dve", lambda v: v.tensor_tensor(out=rt[0][:, 0:2, :], in0=k3[:, :, 0:32], in1=ck, op=ALU.mult), reads=[Pk, cs_t], writes=[rt[0]])
            k.op("dve", lambda v: v.tensor_tensor(out=rt[1][:, 0:2, :], in0=k3[:, :, 32:64], in1=sk_, op=ALU.mult), reads=[Pk, cs_t], writes=[rt[1]])
            yield
            k.op("dve", lambda v: v.tensor_tensor(out=rt[2][:, 0:2, :], in0=k3[:, :, 32:64], in1=ck, op=ALU.mult), reads=[Pk, cs_t], writes=[rt[2]])
            k.op("dve", lambda v: v.tensor_tensor(out=rt[3][:, 0:2, :], in0=k3[:, :, 0:32], in1=sk_, op=ALU.mult), reads=[Pk, cs_t], writes=[rt[3]])
            yield
            k.op("pool", lambda g: g.tensor_tensor(out=kr3[:, :, 0:32], in0=rt[0][:, 0:2, :], in1=rt[1][:, 0:2, :], op=ALU.subtract), reads=[rt[0], rt[1]], writes=[kr])
            k.op("pool", lambda g: g.tensor_tensor(out=kr3[:, :, 32:64], in0=rt[2][:, 0:2, :], in1=rt[3][:, 0:2, :], op=ALU.add), reads=[rt[2], rt[3]], writes=[kr])
            yield
            k.op("dve", lambda v: v.tensor_copy(out=vcur[:], in_=Pk[:, 128:256]), reads=[Pk], writes=[vcur])
            if n == NT - 1:
                k.op("dve", lambda v: v.tensor_copy(out=vB32[:], in_=Pk[:, 128:256]), reads=[Pk], writes=[vB32])
        k.op("dve", lambda v: v.tensor_copy(out=bd[:], in_=Pk[:, 256:264]), reads=[Pk], writes=[bd])
        yield
        j0 = 0 if f_ else 1
        k.op("pool", lambda g: g.tensor_tensor(out=sqb[:, c0:8, :], in0=qkvs[:, c0:8, :], in1=qkvs[:, c0:8, :], op=ALU.mult), reads=[qkvs], writes=[sqb],
             cost=0.25 * (8 - c0))
        yield
        for j in range(j0, 2):
            mm(PB[2 + j][:, :], onesb[:], sqb[:, j * 4:(j + 1) * 4, :].rearrange("p a b -> p (a b)"), True, True, [onesb, sqb], [PB[2 + j]])
        yield
        for j in range(j0, 2):
            k.op("act", lambda a, j=j: a.activation(out=lnss[:, j * 4:(j + 1) * 4, :].rearrange("p a b -> p (a b)"), in_=PB[2 + j][:, :],
                                                    func=ACT.Ln, bias=L2_EPS, scale=1.0), reads=[PB[2 + j]], writes=[lnss])
        yield
        if f_:
            k.op("act", lambda a: a.activation(out=rn[:, 0:4, :], in_=lnss[:, 0:4, :], func=ACT.Exp, bias=float(-0.5 * np.log(128.0)), scale=-0.5), reads=[lnss], writes=[rn])
        k.op("act", lambda a: a.activation(out=rn[:, 4:8, :], in_=lnss[:, 4:8, :], func=ACT.Exp, scale=-0.5), reads=[lnss], writes=[rn])
        yield
        k.op("dve", lambda v: v.tensor_tensor(out=qkvs[:, c0:8, :], in0=qkvs[:, c0:8, :], in1=rn[:, c0:8, :], op=ALU.mult), reads=[qkvs, rn], writes=[qkvs],
             cost=0.13 * (8 - c0))
        yield
        k.op("pool", lambda g: g.tensor_copy(out=qkb[:, c0:12, :], in_=qkvs[:, c0:12, :]), reads=[qkvs], writes=[qkb], cost=0.25 * nch)
        yield
        k.op("act", lambda a: a.activation(out=bdt[:, 0:4], in_=bd[:, 0:4], func=ACT.Exp, scale=-1.0), reads=[bd], writes=[bdt])
        k.op("dve", lambda v: v.tensor_scalar(out=bdt[:, 0:4], in0=bdt[:, 0:4], scalar1=1.0, scalar2=None, op0=ALU.add), reads=[bdt], writes=[bdt])
        k.op("dve", lambda v: v.reciprocal(out=beta[:], in_=bdt[:, 0:4]), reads=[bdt], writes=[beta])
        k.op("dve", lambda v: v.tensor_scalar(out=beta[:], in0=beta[:], scalar1=valid_bc[:, n:n + 1], scalar2=None, op0=ALU.mult), reads=[beta, valid_bc], writes=[beta])
        k.op("dve", lambda v: v.tensor_scalar(out=negbeta[:], in0=beta[:], scalar1=-1.0, scalar2=None, op0=ALU.mult), reads=[beta], writes=[negbeta])
        yield
        k.op("dve", lambda v: v.tensor_tensor(out=bd[:, 4:8], in0=bd[:, 4:8], in1=dtb_bc[:], op=ALU.add), reads=[bd, dtb_bc], writes=[bd])
        k.op("act", lambda a: a.activation(out=bdt[:, 4:8], in_=bd[:, 4:8], func=ACT.Exp), reads=[bd], writes=[bdt])
        k.op("act", lambda a: a.activation(out=bdt[:, 4:8], in_=bdt[:, 4:8], func=ACT.Ln, bias=1.0, scale=1.0), reads=[bdt], writes=[bdt])
        k.op("dve", lambda v: v.tensor_tensor(out=gg[:], in0=bdt[:, 4:8], in1=ea[:], op=ALU.mult), reads=[bdt, ea], writes=[gg])
        yield

    def P1b(n, pl=False):
        f_ = full(n)
        Bsm, Bx0, Bkv = (PB[0], PB[0], PB[1]) if pl else (PB[1], PB[0], PB[1])
        yield
        if f_:
            k.op("pool", lambda g: g.tensor_tensor(out=nz[:], in0=v3(zAs[:], 4), in1=bc(norma_bc[:].rearrange("p (a b) -> p a b", a=1), [128, 4, 128]), op=ALU.mult),
                 reads=[zAs, norma_bc], writes=[nz])
        yield
        k.op("dve", lambda v: v.tensor_copy(out=gsp[:, 0:4], in_=gg[:]), reads=[gg], writes=[gsp])
        yield
        k.op("dve", lambda v: v.tensor_tensor(out=gtmp[:], in0=gg[:], in1=gsp[:, 0:4], op=ALU.subtract), reads=[gg, gsp], writes=[gtmp])
        yield
        k.op("dve", lambda v: v.tensor_copy(out=gsp[:, 4:8], in_=gtmp[:]), reads=[gtmp], writes=[gsp])
        Ub4 = bc(Ub[:].rearrange("p (a b) -> p a b", a=1), [128, 4, 128])
        yield
        k.op("pool", lambda g: g.tensor_tensor(out=gUh[:], in0=Ub4, in1=bc(gsp[:, 0:4].rearrange("p (a b) -> p a b", b=1), [128, 4, 128]), op=ALU.mult),
             reads=[Ub, gsp], writes=[gUh])
        yield
        k.op("pool", lambda g: g.tensor_tensor(out=gUl[:], in0=Ub4, in1=bc(gsp[:, 4:8].rearrange("p (a b) -> p a b", b=1), [128, 4, 128]), op=ALU.mult),
             reads=[Ub, gsp], writes=[gUl])
        PG, PK_, PQ = (PB[1], PB[0], PB[5]) if pl else (PB[3], PB[4], PB[5])
        yield
        mm(Bsm[:, 0:8], Ub[:], gsp[:], True, True, [Ub, gsp], [Bsm])
        yield
        mm(Bsm[:, 8:16], onesb[:], gsp[:], True, True, [onesb, gsp], [Bsm])
        yield
        mm(PG[:, :], onesb[:], gUh[:].rearrange("p a b -> p (a b)"), True, False, [onesb, gUh], [PG])
        yield
        mm(PG[:, :], onesb[:], gUl[:].rearrange("p a b -> p (a b)"), False, True, [onesb, gUl], [PG])
        yield
        k.op("dve", lambda v: v.tensor_copy(out=gtmp[:], in_=Bsm[:, 0:4]), reads=[Bsm], writes=[gtmp])
        yield
        k.op("dve", lambda v: v.tensor_tensor(out=Gc[:], in0=gtmp[:], in1=Bsm[:, 4:8], op=ALU.add), reads=[gtmp, Bsm], writes=[Gc])
        yield
        k.op("dve", lambda v: v.tensor_copy(out=gtmp[:], in_=Bsm[:, 8:12]), reads=[Bsm], writes=[gtmp])
        yield
        k.op("dve", lambda v: v.tensor_tensor(out=Glb[:], in0=gtmp[:], in1=Bsm[:, 12:16], op=ALU.add), reads=[gtmp, Bsm], writes=[Glb])
        yield
        k.op("dve", lambda v: v.tensor_scalar(out=negGc[:], in0=Gc[:], scalar1=-1.0, scalar2=None, op0=ALU.mult), reads=[Gc], writes=[negGc])
        yield
        k.op("dve", lambda v: v.tensor_tensor(out=dG[:], in0=Glb[:], in1=Gc[:], op=ALU.subtract), reads=[Glb, Gc], writes=[dG])
        yield
        k.op("act", lambda a: a.activation(out=eG[:], in_=Gc[:], func=ACT.Exp), reads=[Gc], writes=[eG])
        yield
        k.op("act", lambda a: a.activation(out=edG[:], in_=dG[:], func=ACT.Exp), reads=[dG], writes=[edG])
        yield
        yield
        k.op("dve", lambda v: v.tensor_tensor(out=beG[:], in0=eG[:], in1=beta[:], op=ALU.mult), reads=[eG, beta], writes=[beG])
        PG3 = v3(PG[:, :], 4)
        yield
        k.op("dve", lambda v: v.scalar_tensor_tensor(out=tmp1[:], in0=PG3, scalar=-1.0, in1=bc(negA[:].rearrange("p (a b) -> p a b", a=1), [128, 4, 128]),
                                                     op0=ALU.mult, op1=ALU.add), reads=[PG, negA], writes=[tmp1])
        yield
        if f_:
            k.op("dve", lambda v: v.tensor_tensor(out=tmp2[:], in0=PG3, in1=bc(negB[:].rearrange("p (a b) -> p a b", a=1), [128, 4, 128]), op=ALU.add),
                 reads=[PG, negB], writes=[tmp2])
            yield
            k.op("act", lambda a: a.activation(out=eGbc[:], in_=PG3, func=ACT.Exp), reads=[PG], writes=[eGbc])
        yield
        for h in range(4):
            k.op("act", lambda a, h=h: a.activation(out=E1[:, h, :], in_=tmp1[:, h, :], func=ACT.Exp, bias=Gc[:, h:h + 1], scale=1.0), reads=[tmp1, Gc], writes=[E1])
            if f_:
                k.op("act", lambda a, h=h: a.activation(out=E2[:, h, :], in_=tmp2[:, h, :], func=ACT.Exp, bias=negGc[:, h:h + 1], scale=1.0), reads=[tmp2, negGc], writes=[E2])
        yield
        for h in range(4):
            mm(PK_[:, h * 128:(h + 1) * 128], qkb[:, 4 + h, :], qkb[:, 4 + h, :], True, True, [qkb], [PK_])
        yield
        for h in range(4):
            if f_:
                mm(PQ[:, h * 128:(h + 1) * 128], qkb[:, 4 + h, :], qkb[:, h, :], True, True, [qkb], [PQ])
        yield
        for h in range(4):
            k.op("dve", lambda v, h=h: v.scalar_tensor_tensor(out=Y0f[:, h, :], in0=PK_[:, h * 128:(h + 1) * 128], scalar=negbeta[:, h:h + 1],
                                                               in1=E1[:, h, :], op0=ALU.mult, op1=ALU.mult), reads=[PK_, negbeta, E1], writes=[Y0f])
        yield
        if f_:
            k.op("dve", lambda v: v.tensor_tensor(out=AqkT[:], in0=v3(PQ[:, :], 4), in1=E2[:], op=ALU.mult), reads=[PQ, E2], writes=[AqkT])
            yield
            k.op("pool", lambda g: g.tensor_tensor(out=qdT[:], in0=qkvs[:, 0:4, :], in1=eGbc[:], op=ALU.mult), reads=[qkvs, eGbc], writes=[qdT])
        if pl:
            while not (gdone.get((n - 1, 0)) and gdone.get((n - 1, 1))):
                yield "wait"
        k.op("act", lambda a: a.activation(out=egl[:], in_=Glb[:], func=ACT.Exp), reads=[Glb], writes=[egl])
        P5b = pbf(Bx0)
        id2 = bc(ident[:].rearrange("p (a b) -> p a b", a=1), [128, 2, 128])
        for g_ in range(2):
            yield
            k.op("act", lambda a, g_=g_: a.copy(out=Yh[g_][0][:], in_=Y0f[:, 2 * g_:2 * g_ + 2, :]), reads=[Y0f], writes=[Yh[g_][0]])
            yield
            k.op("pool", lambda g, g_=g_: g.tensor_tensor(out=Yl[g_][0][:], in0=Y0f[:, 2 * g_:2 * g_ + 2, :], in1=Yh[g_][0][:], op=ALU.subtract),
                 reads=[Y0f, Yh[g_][0]], writes=[Yl[g_][0]])
            yield
            for hh in range(2):
                h = 2 * g_ + hh
                tr(P5b[:, h * 128:(h + 1) * 128], Yh[g_][0][:, hh, :], identb[:], [Yh[g_][0]], [Bx0])
                tr(P5b[:, (4 + h) * 128:(5 + h) * 128], Yl[g_][0][:, hh, :], identb[:], [Yl[g_][0]], [Bx0])
        for g_ in range(2):
            yield
            k.op("act", lambda a, g_=g_: a.copy(out=XRh[g_][0][:, :, 0:128], in_=v3(P5b[:, g_ * 256:(g_ + 1) * 256], 2)), reads=[Bx0], writes=[XRh[g_][0]])
            yield
            k.op("dve", lambda v, g_=g_: v.tensor_copy(out=XRl[g_][0][:, :, 0:128], in_=v3(P5b[:, 512 + g_ * 256:512 + (g_ + 1) * 256], 2)), reads=[Bx0], writes=[XRl[g_][0]])
            yield
            k.op("pool", lambda g, g_=g_: g.tensor_copy(out=Rf[g_][:], in_=id2), reads=[ident], writes=[Rf[g_]])
            k.op("pool", lambda g, g_=g_: g.tensor_copy(out=XRh[g_][0][:, :, 128:256], in_=id2), reads=[ident], writes=[XRh[g_][0]])
            k.op("pool", lambda g, g_=g_: g.memset(XRl[g_][0][:, :, 128:256], 0.0), writes=[XRl[g_][0]])
        P1b_ = pbf(Bkv)
        yield
        for h in range(4):
            tr(P1b_[:, h * 128:(h + 1) * 128], qkb[:, 4 + h, :], identb[:], [qkb], [Bkv])
        yield
        for h in range(4):
            tr(P1b_[:, (4 + h) * 128:(5 + h) * 128], qkb[:, 8 + h, :], identb[:], [qkb], [Bkv])
        yield
        for h in range(4):
            k.op("dve", lambda v, h=h: v.tensor_scalar(out=kbt[:, h, :], in0=P1b_[:, h * 128:(h + 1) * 128], scalar1=beG[:, h:h + 1], scalar2=None, op0=ALU.mult),
                 reads=[Bkv, beG], writes=[kbt])
            k.op("act", lambda a, h=h: a.activation(out=kdt[:, h, :], in_=P1b_[:, h * 128:(h + 1) * 128], func=ACT.Identity, scale=edG[:, h:h + 1]),
                 reads=[Bkv, edG], writes=[kdt])
            k.op("act", lambda a, h=h: a.activation(out=bvt[:, h, :], in_=P1b_[:, (4 + h) * 128:(5 + h) * 128], func=ACT.Identity, scale=beta[:, h:h + 1]),
                 reads=[Bkv, beta], writes=[bvt])
        yield

    def G(n, g_):
        BA, BB = PB[4 + g_], PB[6 + g_]
        for lev in range(7):
            cur = lev % 2; nxt = (lev + 1) % 2
            xh, xl, yh, yl = XRh[g_][cur], XRl[g_][cur], Yh[g_][cur], Yl[g_][cur]
            xhn, xln, yhn, yln = XRh[g_][nxt], XRl[g_][nxt], Yh[g_][nxt], Yl[g_][nxt]
            last = (lev == 6)
            yield
            for hh in range(2):
                if not last:
                    o_ = BA[:, hh * 256:(hh + 1) * 256]
                    r_h = xh[:, hh, :]; r_l = xl[:, hh, :]
                else:
                    o_ = BA[:, hh * 256 + 128:(hh + 1) * 256]
                    r_h = xh[:, hh, 128:256]; r_l = xl[:, hh, 128:256]
                mm(o_, yh[:, hh, :], r_h, True, False, [yh, xh], [BA])
                mm(o_, yh[:, hh, :], r_l, False, False, [yh, xl], [BA])
                mm(o_, yl[:, hh, :], r_h, False, True, [yl, xh], [BA])
            if not last:
                yield
                for hh in range(2):
                    o_ = BB[:, hh * 128:(hh + 1) * 128]
                    mm(o_, xh[:, hh, 0:128], yh[:, hh, :], True, False, [xh, yh], [BB])
                    mm(o_, xh[:, hh, 0:128], yl[:, hh, :], False, False, [xh, yl], [BB])
                    mm(o_, xl[:, hh, 0:128], yh[:, hh, :], False, True, [xl, yh], [BB])
            yield
            k.op("dve", lambda v: v.tensor_tensor(out=Rf[g_][:], in0=Rf[g_][:], in1=v3(BA[:, :], 2)[:, :, 128:256], op=ALU.add), reads=[Rf[g_], BA], writes=[Rf[g_]],
                 cost=0.3)
            yield
            k.op("act", lambda a: a.copy(out=xhn[:, :, 128:256], in_=Rf[g_][:]), reads=[Rf[g_]], writes=[xhn], cost=0.3)
            if not last:
                yield
                k.op("pool", lambda g: g.tensor_tensor(out=xln[:, :, 128:256], in0=Rf[g_][:], in1=xhn[:, :, 128:256], op=ALU.subtract), reads=[Rf[g_], xhn], writes=[xln],
                     cost=0.6)
                yield
                k.op("act", lambda a: a.copy(out=xhn[:, :, 0:128], in_=v3(BA[:, :], 2)[:, :, 0:128]), reads=[BA], writes=[xhn], cost=0.3)
                yield
                k.op("dve", lambda v: v.tensor_tensor(out=xln[:, :, 0:128], in0=v3(BA[:, :], 2)[:, :, 0:128], in1=xhn[:, :, 0:128], op=ALU.subtract),
                     reads=[BA, xhn], writes=[xln], cost=0.3)
                yield
                k.op("act", lambda a: a.copy(out=yhn[:], in_=v3(BB[:, 0:256], 2)), reads=[BB], writes=[yhn], cost=0.3)
                yield
                k.op("dve", lambda v: v.tensor_tensor(out=yln[:], in0=v3(BB[:, 0:256], 2), in1=yhn[:], op=ALU.subtract), reads=[BB, yhn], writes=[yln], cost=0.3)
        Rh = XRh[g_][1]
        yield
        for hh in range(2):
            h = 2 * g_ + hh
            mm(BA[:, hh * 128:(hh + 1) * 128], kbt[:, h, :], Rh[:, hh, 128:256], True, True, [kbt, Rh], [BA])
        for hh in range(2):
            h = 2 * g_ + hh
            mm(BB[:, hh * 128:(hh + 1) * 128], Rh[:, hh, 128:256], bvt[:, h, :], True, True, [Rh, bvt], [BB])
        yield
        k.op("act", lambda a: a.copy(out=wT[g_][:], in_=v3(BA[:, 0:256], 2)), reads=[BA], writes=[wT[g_]], cost=0.3)
        yield
        k.op("dve", lambda v: v.tensor_copy(out=ut[g_][:], in_=v3(BB[:, 0:256], 2)), reads=[BB], writes=[ut[g_]], cost=0.3)
        yield
        for hh in range(2):
            mm(BA[:, 256 + hh * 128:256 + (hh + 1) * 128], wT[g_][:, hh, :], Sb[g_][:, hh, :], True, True, [wT[g_], Sb[g_]], [BA])
        yield
        k.op("dve", lambda v: v.tensor_tensor(out=uub[g_][:], in0=ut[g_][:], in1=v3(BA[:, 256:512], 2), op=ALU.subtract), reads=[ut[g_], BA], writes=[uub[g_]], cost=0.3)
        yield
        for hh in range(2):
            h = 2 * g_ + hh
            if full(n):
                mm(BB[:, 256 + hh * 128:256 + (hh + 1) * 128], qdT[:, h, :], Sb[g_][:, hh, :], True, False, [qdT, Sb[g_]], [BB])
                mm(BB[:, 256 + hh * 128:256 + (hh + 1) * 128], AqkT[:, h, :], uub[g_][:, hh, :], False, True, [AqkT, uub[g_]], [BB])
        for hh in range(2):
            h = 2 * g_ + hh
            mm(BA[:, hh * 128:(hh + 1) * 128], kdt[:, h, :], uub[g_][:, hh, :], True, True, [kdt, uub[g_]], [BA])
        yield
        for hh in range(2):
            h = 2 * g_ + hh
            k.op("dve", lambda v, hh=hh, h=h: v.scalar_tensor_tensor(out=S[g_][:, hh, :], in0=S[g_][:, hh, :], scalar=egl[:, h:h + 1], in1=BA[:, hh * 128:(hh + 1) * 128],
                                                                   op0=ALU.mult, op1=ALU.add), reads=[S[g_], egl, BA], writes=[S[g_]], cost=0.2)
        yield
        k.op("pool", lambda g: g.tensor_copy(out=Sb[g_][:], in_=S[g_][:]), reads=[S[g_]], writes=[Sb[g_]], cost=0.5)
        yield
        gdone[(n, g_)] = True
        if not full(n):
            return
        for hh in range(2):
            k.op("act", lambda a, hh=hh: a.activation(out=ojunk[g_][:], in_=BB[:, 256 + hh * 128:256 + (hh + 1) * 128], func=ACT.Square, accum_out=oss[g_][:, hh:hh + 1]),
                 reads=[BB], writes=[ojunk[g_], oss[g_]], cost=0.25)
        yield
        k.op("act", lambda a: a.activation(out=orr[g_][:], in_=oss[g_][:], func=ACT.Ln, bias=RMS_EPS, scale=1.0 / 128.0), reads=[oss[g_]], writes=[orr[g_]], cost=0.2)
        k.op("act", lambda a: a.activation(out=orr[g_][:], in_=orr[g_][:], func=ACT.Exp, scale=-0.5), reads=[orr[g_]], writes=[orr[g_]], cost=0.2)
        yield
        for hh in range(2):
            h = 2 * g_ + hh
            k.op("dve", lambda v, hh=hh, h=h: v.scalar_tensor_tensor(out=mixb[:, h * 128:(h + 1) * 128], in0=BB[:, 256 + hh * 128:256 + (hh + 1) * 128], scalar=orr[g_][:, hh:hh + 1],
                                                                   in1=nz[:, h, :], op0=ALU.mult, op1=ALU.mult), reads=[BB, orr[g_], nz], writes=[mixb], cost=0.2)
        yield

    def W(n):
        vcur = vB[n % 3]; vprev = vB[(n + 2) % 3]
        zBs = zBs2[n % 2]; qrb = qrb2[n % 2]; kr = kr2[n % 2]
        kcur = kTb[n % 2]; kprev = kTb[(n + 1) % 2]
        if not (full(n) or swaprep(n)):
            return
        if swaprep(n):
            yield
            k.op("pool", lambda g: g.tensor_copy(out=krb[:], in_=kr[:]), reads=[kr], writes=[krb])
            yield
            tr(pbf(PB[0])[:, 512:640], krb[:], identb[:], [krb], [PB[0]])
            yield
            k.op("dve", lambda v: v.tensor_copy(out=kcur[:], in_=pbf(PB[0])[:, 512:640]), reads=[PB[0]], writes=[kcur])
            yield
            return
        P0b = pbf(PB[0])
        yield
        for c in range(4):
            tr(P0b[:, c * 128:(c + 1) * 128], qrb[:, c * 128:(c + 1) * 128], identb[:], [qrb], [PB[0]])
        yield
        k.op("act", lambda a: a.activation(out=qTb[:], in_=v3(P0b[:, 0:512], 4), func=ACT.Identity, scale=0.125), reads=[PB[0]], writes=[qTb])
        yield
        k.op("pool", lambda g: g.tensor_copy(out=krb[:], in_=kr[:]), reads=[kr], writes=[krb])
        yield
        tr(pbf(PB[0])[:, 512:640], krb[:], identb[:], [krb], [PB[0]])
        yield
        k.op("dve", lambda v: v.tensor_copy(out=kcur[:], in_=pbf(PB[0])[:, 512:640]), reads=[PB[0]], writes=[kcur])
        msk = swam
        rnd = 0
        for kv in range(2):
            for cp in range(2):
                pb = PB[1] if rnd % 2 == 0 else PB[0]
                rnd += 1
                yield
                for cc in range(2):
                    c = cp * 2 + cc
                    o = cc * 256
                    mm(pb[:, o:o + 128], qTb[64 * kv:64 * kv + 64, c, :], kprev[64 * kv:64 * kv + 64, :], True, True, [qTb, kprev], [pb])
                    mm(pb[:, o + 128:o + 256], qTb[64 * kv:64 * kv + 64, c, :], kcur[64 * kv:64 * kv + 64, :], True, True, [qTb, kcur], [pb])
                yield
                s0 = kv * 4 + cp * 2
                k.op("dve", lambda v, pb=pb, s0=s0: v.tensor_tensor(out=SC[:, s0:s0 + 2, :], in0=v3(pb[:, :], 2),
                                                                     in1=bc(msk[:].rearrange("p (a b) -> p a b", a=1), [128, 2, 256]), op=ALU.add),
                     reads=[pb, msk], writes=[SC])
        yield
        if n == F0:
            if F0 == 0:
                k.op("dve", lambda v: v.tensor_scalar(out=SC[:, :, 0:128], in0=SC[:, :, 0:128], scalar1=NEG, scalar2=None, op0=ALU.add), reads=[SC], writes=[SC])
            else:
                k.op("dve", lambda v: v.tensor_scalar(out=negpad[:], in0=valid_bc[:, F0 - 1:F0], scalar1=-1.0, scalar2=-NEG, op0=ALU.add, op1=ALU.mult),
                     reads=[valid_bc], writes=[negpad])
                k.op("dve", lambda v: v.tensor_scalar(out=SC[:, :, 0:128], in0=SC[:, :, 0:128], scalar1=negpad[:, 0:1], scalar2=None, op0=ALU.add),
                     reads=[SC, negpad], writes=[SC])
            yield
        k.op("dve", lambda v: v.tensor_reduce(out=mx[:], in_=SC[:], axis=AX.X, op=ALU.max), reads=[SC], writes=[mx])
        yield
        k.op("dve", lambda v: v.tensor_tensor(out=mx[:], in0=mx[:], in1=sinks_bc[:], op=ALU.max), reads=[mx, sinks_bc], writes=[mx])
        yield
        k.op("dve", lambda v: v.tensor_scalar(out=negm[:], in0=mx[:], scalar1=-1.0, scalar2=None, op0=ALU.mult), reads=[mx], writes=[negm])
        yield
        for s_ in range(8):
            k.op("act", lambda a, s_=s_: a.activation(out=Pb[:, s_, :], in_=SC[:, s_, :], func=ACT.Exp, bias=negm[:, s_:s_ + 1], scale=1.0,
                                                      accum_out=rs[:, s_:s_ + 1]), reads=[SC, negm], writes=[Pb, rs])
        yield
        k.op("dve", lambda v: v.tensor_tensor(out=es_[:], in0=sinks_bc[:], in1=mx[:], op=ALU.subtract), reads=[sinks_bc, mx], writes=[es_])
        yield
        k.op("act", lambda a: a.activation(out=es_[:], in_=es_[:], func=ACT.Exp), reads=[es_], writes=[es_])
        yield
        k.op("dve", lambda v: v.tensor_tensor(out=es_[:], in0=es_[:], in1=rs[:], op=ALU.add), reads=[es_, rs], writes=[es_])
        yield
        k.op("dve", lambda v: v.reciprocal(out=es_[:], in_=es_[:]), reads=[es_], writes=[es_])
        yield
        k.op("dve", lambda v: v.tensor_copy(out=rden[:].rearrange("p (c k) -> p k c", k=2), in_=es_[:].rearrange("p (k c) -> p k c", k=2)),
             reads=[es_], writes=[rden])
        PPT = [PB[1], PB[0]]
        yield
        for s_ in range(8):
            for half in range(2):
                i16 = s_ * 2 + half
                pb = PPT[i16 // 8]
                o = (i16 % 8) * 128
                tr(pbf(pb)[:, o:o + 128], Pb[:, s_, half * 128:(half + 1) * 128], identb[:], [Pb], [pb])
        yield
        k.op("act", lambda a: a.copy(out=PTb[:, 0:8, :], in_=v3(pbf(PPT[0]), 8)), reads=[PPT[0]], writes=[PTb])
        yield
        k.op("dve", lambda v: v.tensor_copy(out=PTb[:, 8:16, :], in_=v3(pbf(PPT[1]), 8)), reads=[PPT[1]], writes=[PTb])
        PV = PB[1]
        yield
        for c in range(4):
            for kv in range(2):
                slot = c * 2 + kv
                sp_ = kv * 4 + c
                mm(PV[:, slot * 64:(slot + 1) * 64], PTb[:, 2 * sp_, :], vprev[:, 64 * kv:64 * kv + 64], True, False, [PTb, vprev], [PV])
                mm(PV[:, slot * 64:(slot + 1) * 64], PTb[:, 2 * sp_ + 1, :], vcur[:, 64 * kv:64 * kv + 64], False, True, [PTb, vcur], [PV])
        yield
        k.op("dve", lambda v: v.tensor_tensor(out=v3(obt[:], 8), in0=v3(PV[:, :], 8), in1=bc(rden[:].rearrange("p (a b) -> p a b", b=1), [128, 8, 64]), op=ALU.mult),
             reads=[PV, rden], writes=[obt])
        yield
        k.op("pool", lambda g: g.tensor_tensor(out=mixb[:, 512:1024], in0=obt[:], in1=zBs[:], op=ALU.mult), reads=[obt, zBs], writes=[mixb])

        yield

    def E(n):
        X = Xt[n % 2]; pc = pre; kr = kr2[n % 2]
        if not full(n):
            return
        P3b = pbf(PB[6])
        yield
        for kc in range(8):
            tr(P3b[:, kc * 128:(kc + 1) * 128], mixb[:, kc * 128:(kc + 1) * 128], identb[:], [mixb], [PB[6]])
        yield
        k.op("act", lambda a: a.copy(out=mixT[:, 0:4, :], in_=v3(P3b[:, 0:512], 4)), reads=[PB[6]], writes=[mixT])
        yield
        k.op("dve", lambda v: v.tensor_copy(out=mixT[:, 4:8, :], in_=v3(P3b[:, 512:1024], 4)), reads=[PB[6]], writes=[mixT])
        yield
        for j in range(2):
            for kc in range(8):
                mm(PB[6 + j][:, :], mixT[:, kc, :], Wo[:, kc, j * 512:(j + 1) * 512], kc == 0, kc == 7, [mixT, Wo], [PB[6 + j]])
        yield
        for j in range(2):
            k.op("dve", lambda v, j=j: v.tensor_tensor(out=ypre[:, j * 512:(j + 1) * 512], in0=PB[6 + j][:, :], in1=g1bc[:, j * 512:(j + 1) * 512], op=ALU.mult),
                 reads=[PB[6 + j], g1bc], writes=[ypre])
        yield
        k.op("dve", lambda g: g.scalar_tensor_tensor(out=ypre[:], in0=X[:], scalar=ALPHA, in1=ypre[:], op0=ALU.mult, op1=ALU.add), reads=[X, ypre], writes=[ypre])
        Yo = Yt[0]
        yjunk = Yo
        yield
        k.op("act", lambda a: a.activation(out=yjunk[:], in_=ypre[:], func=ACT.Identity, accum_out=st[:, 0:1]), reads=[ypre], writes=[yjunk, st])
        yield
        k.op("act", lambda a: a.activation(out=yjunk[:], in_=ypre[:], func=ACT.Square, accum_out=st[:, 1:2]), reads=[ypre], writes=[yjunk, st])
        yield
        k.op("dve", lambda v: v.tensor_scalar(out=st[:, 2:3], in0=st[:, 0:1], scalar1=1.0 / D, scalar2=None, op0=ALU.mult), reads=[st], writes=[st])
        yield
        k.op("dve", lambda v: v.tensor_tensor(out=st[:, 3:4], in0=st[:, 2:3], in1=st[:, 2:3], op=ALU.mult), reads=[st], writes=[st])
        yield
        k.op("dve", lambda v: v.scalar_tensor_tensor(out=st[:, 4:5], in0=st[:, 1:2], scalar=1.0 / D, in1=st[:, 3:4], op0=ALU.mult, op1=ALU.subtract), reads=[st], writes=[st])
        yield
        k.op("act", lambda a: a.activation(out=st[:, 5:6], in_=st[:, 4:5], func=ACT.Ln, bias=LN_EPS, scale=1.0), reads=[st], writes=[st])
        yield
        k.op("act", lambda a: a.activation(out=st[:, 5:6], in_=st[:, 5:6], func=ACT.Exp, scale=-0.5), reads=[st], writes=[st])
        yield
        k.op("dve", lambda v: v.scalar_tensor_tensor(out=st[:, 6:7], in0=st[:, 2:3], scalar=-1.0, in1=st[:, 5:6], op0=ALU.mult, op1=ALU.mult), reads=[st], writes=[st])
        yield
        k.op("act", lambda a: a.activation(out=yjunk[:], in_=ypre[:], func=ACT.Identity, bias=st[:, 6:7], scale=st[:, 5:6]), reads=[ypre, st], writes=[yjunk])
        yield
        k.op("pool", lambda g: g.tensor_tensor(out=yjunk[:], in0=yjunk[:], in1=lng_bc[:], op=ALU.mult), reads=[yjunk, lng_bc], writes=[yjunk])
        yield
        k.op("dve", lambda v: v.tensor_tensor(out=Yo[:], in0=yjunk[:], in1=lnb_bc[:], op=ALU.add), reads=[yjunk, lnb_bc], writes=[Yo])
        yield
        k.dma("sp", y_d[(n - F0) * 128:(n - F0 + 1) * 128, :], Yo[:], reads=[Yo], semkey="st_" + Yo.name)

        if n == NT - 1:
            k.barrier()
            k.pe_inorder = False
            cpo_in = rt[0][:].rearrange("p a b -> p (a b)")[:, 0:36].rearrange("p (a b) -> p a b", a=3)
            cpo = obt[0:12, 0:384].rearrange("p (a b) -> p a b", a=3)
            k.op("dve", lambda v: v.tensor_copy(out=cpo_in[:], in_=pc[:, :, 128:131].rearrange("p c r -> p r c")), reads=[pc], writes=[cpo_in])
            for r_ in range(3):
                tr(PB[6][0:12, r_ * 128:(r_ + 1) * 128], cpo_in[:, r_, :], ident[:], [cpo_in], [PB[6]])
            k.op("dve", lambda v: v.tensor_copy(out=cpo[:].rearrange("p a b -> p (a b)"), in_=PB[6][0:12, 0:384]), reads=[PB[6]], writes=[cpo])
            k.dma("sp", convp_d.rearrange("r (c p) -> c r p", p=128), cpo[:], reads=[cpo], semkey="st_misc")
            for g_ in range(2):
                k.dma("sp", deltap_d[2 * g_:2 * g_ + 2].rearrange("h k v -> k h v"), S[g_][:], reads=[S[g_]], semkey="st_misc")
            k.dma("sp", swak_d[:, :], kr[:], reads=[kr], semkey="st_misc")
            k.dma("sp", swav_d[:, :], vB32[:], reads=[vB32], semkey="st_misc")

        yield

    def chain(*gens):
        for g_ in gens:
            for v_ in g_:
                yield v_

    def merge(*gens):
        gens = [[g_, "s%d" % i] for i, g_ in enumerate(gens) if g_ is not None]
        base = min(k.t_eng.values())
        for _, lab in gens:
            k.t_stream[lab] = base
        while gens:
            gens.sort(key=lambda x: k.t_stream[x[1]])
            g_, lab = gens[0]
            k.stream = lab
            try:
                r_ = next(g_)
            except StopIteration:
                gens.pop(0)
                continue
            if r_ == "wait":
                assert len(gens) > 1, "gated stream waits on nothing"
                k.t_stream[lab] = max(k.t_stream[x[1]] for x in gens[1:]) + 1e-3
        k.stream = None

    merge(P1a(0))
    merge(P1b(0))
    for n in range(NT):
        nxt = n + 1 < NT
        pl = nxt and (n + 1 < F0)
        if pl:
            merge(G(n, 0), G(n, 1), W(n), chain(P1a(n + 1), P1b(n + 1, True)))
        else:
            merge(G(n, 0), G(n, 1), W(n), P1a(n + 1) if nxt else None)
        merge(E(n), P1b(n + 1) if (nxt and not pl) else None)

    k.finish("sp")
    if k.dry:
        return k.need_out
    nc._kb_nops = k.nops
    nc._kb_sig = dict(k.sig)
    nc._kb_cnt = {e: k.cnt[e] for e in k.eng}
    return nc


def rope_tables(pos):
    half = 32
    inv = (1.0 / (10000.0 ** (np.arange(half, dtype=np.float32) / np.float32(half)))).astype(np.float32)
    ang = pos.astype(np.float32)[:, None] * inv[None, :]
    return np.cos(ang).astype(np.float32), np.sin(ang).astype(np.float32)


def prep_shared(inputs):
    w_in = np.asarray(inputs["w_in"][0], np.float32)
    perm_cols = np.concatenate([np.arange(h * 64, (h + 1) * 64) for h in PERM])
    w_f = np.ascontiguousarray(w_in[:, 0:1536])
    zA = w_in[:, 1536:2048]
    beta = w_in[:, 2048:2052]
    dec = w_in[:, 2052:2056]
    qB = w_in[:, 2056:2568][:, perm_cols]
    kB = w_in[:, 2568:2696]
    vB = w_in[:, 2696:2824]
    zB = w_in[:, 2824:3336][:, perm_cols]
    w_t = np.ascontiguousarray(np.concatenate([zA, qB, zB, kB, vB, beta, dec], axis=1))
    w_out = np.asarray(inputs["w_out"][0], np.float32)
    w_o = np.ascontiguousarray(np.concatenate([w_out[0:512], w_out[512:1024][perm_cols]], axis=0))
    sh = {
        "w_ada": np.ascontiguousarray(inputs["w_ada"][0], np.float32),
        "b_ada": np.ascontiguousarray(inputs["b_ada"], np.float32).reshape(1, -1),
        "w_f": w_f, "w_t": w_t, "w_o": w_o,
        "conv_w": np.ascontiguousarray(inputs["conv_w"][0], np.float32),
        "a_log": np.ascontiguousarray(inputs["a_log"], np.float32).reshape(1, 4),
        "dt_bias": np.ascontiguousarray(inputs["dt_bias"], np.float32).reshape(1, 4),
        "norm_a": np.ascontiguousarray(inputs["norm_a"], np.float32).reshape(1, 128),
        "sinks_p": np.ascontiguousarray(np.asarray(inputs["sinks"], np.float32).reshape(8)).reshape(1, 8),
        "ln_g": np.ascontiguousarray(inputs["ln_g"], np.float32).reshape(1, D),
        "ln_b": np.ascontiguousarray(inputs["ln_b"], np.float32).reshape(1, D),
    }
    return sh


def prep_core(inputs, sh, core, ntiles=NTILES, do_sample=True, pad=None):
    b = core // 4
    q = core % 4
    if pad is None:
        pad = (3 - q) * NLOCAL
    real = ntiles - pad
    T = ntiles * 128
    m = dict(sh)
    x = np.zeros((T, D), np.float32)
    x[pad * 128:] = inputs["x_prompt"][b, :real * 128]
    m["x"] = x
    m["c"] = np.ascontiguousarray(inputs["c_prompt"][b], np.float32).reshape(1, D)
    valid = np.zeros((1, ntiles), np.float32)
    valid[0, pad:] = 1.0
    m["valid"] = valid
    pos = np.maximum(np.arange(T) - pad * 128, 0)
    cos, sin = rope_tables(pos)
    m["cosk"] = cos
    m["sink"] = sin
    if do_sample:
        sl = slice(core * NS, (core + 1) * NS)
        m["xs"] = np.ascontiguousarray(inputs["x_sample"][sl, 0], np.float32)
        m["cs"] = np.ascontiguousarray(inputs["c_sample"][sl], np.float32)
        m["s_conv"] = np.ascontiguousarray(inputs["state_conv"][0, sl], np.float32)
        m["s_delta"] = np.ascontiguousarray(inputs["state_delta"][0, sl], np.float32)
        m["s_k"] = np.ascontiguousarray(inputs["cache_swa_k"][0, sl], np.float32).reshape(NS, 128, 128)
        m["s_v"] = np.ascontiguousarray(inputs["cache_swa_v"][0, sl], np.float32).reshape(NS, 128, 128)
        cs_, ss_ = rope_tables(np.array([8192]))
        m["sinks_col"] = np.ascontiguousarray(np.tile(np.asarray(inputs["sinks"], np.float32).reshape(8), NS).reshape(128, 1))
        m["cos_s"] = cs_.reshape(1, 32)
        m["sin_s"] = ss_.reshape(1, 32)
    return m


_NC_CACHE = {}


DO_SAMPLE = True


def kernel(**inputs):
    if "nc" not in _NC_CACHE:
        _NC_CACHE["nc"] = build_program(NTILES, DO_SAMPLE)
    nc = _NC_CACHE["nc"]
    sh = prep_shared(inputs)
    in_maps = [prep_core(inputs, sh, c, NTILES, DO_SAMPLE) for c in range(8)]
    res = run_bass_kernel_spmd(nc, in_maps, core_ids=list(range(8))).results
    yp = np.stack([np.concatenate([res[b * 4 + q]["y"] for q in range(4)], 0) for b in range(2)], 0).astype(np.float32)
    conv_p = np.stack([res[3]["conv_p"], res[7]["conv_p"]], 0)[None].astype(np.float32)
    delta_p = np.stack([res[3]["delta_p"], res[7]["delta_p"]], 0)[None].astype(np.float32)
    swa_k_p = np.stack([res[3]["swa_k_p"], res[7]["swa_k_p"]], 0).reshape(1, 2, 128, 2, 64).astype(np.float32)
    swa_v_p = np.stack([res[3]["swa_v_p"], res[7]["swa_v_p"]], 0).reshape(1, 2, 128, 2, 64).astype(np.float32)
    if DO_SAMPLE:
        ys = np.concatenate([r["ys"] for r in res], 0).reshape(128, 1, D).astype(np.float32)
        conv_s = np.concatenate([r["conv_s"] for r in res], 0)[None].astype(np.float32)
        delta_s = np.concatenate([r["delta_s"] for r in res], 0)[None].astype(np.float32)
        swa_k_s = np.concatenate([r["swa_k_s"] for r in res], 0).reshape(1, 128, 128, 2, 64).astype(np.float32)
        swa_v_s = np.concatenate([r["swa_v_s"] for r in res], 0).reshape(1, 128, 128, 2, 64).astype(np.float32)
    else:
        ys = np.zeros((128, 1, D), np.float32)
        conv_s = np.zeros((1, 128, 3, 1536), np.float32)
        delta_s = np.zeros((1, 128, 4, 128, 128), np.float32)
        swa_k_s = np.zeros((1, 128, 128, 2, 64), np.float32)
        swa_v_s = np.zeros((1, 128, 128, 2, 64), np.float32)
    return (yp, ys, conv_p, delta_p, swa_k_p, swa_v_p, conv_s, delta_s, swa_k_s, swa_v_s)
```

```python
import contextlib
import numpy as np
import concourse.bass as bass
import concourse.mybir as mybir
from concourse.bass_utils import run_bass_kernel_spmd

F32 = mybir.dt.float32
BF16 = mybir.dt.bfloat16
ACT = mybir.ActivationFunctionType
ALU = mybir.AluOpType
AX = mybir.AxisListType

D = 1024
NTILES = 64
NS = 16
ALPHA = 2.0 ** 0.25
NEG = -1.0e30
LN_EPS = 1e-5
RMS_EPS = 1e-6
L2_EPS = 1e-6
WT_COLS = 1800
PERM = [0, 4, 1, 5, 2, 6, 3, 7]


class KB:
    def __init__(self, nc):
        self.nc = nc
        self.es = contextlib.ExitStack()
        self.eng = {"pe": nc.tensor, "dve": nc.vector, "act": nc.scalar, "pool": nc.gpsimd, "sp": nc.sync}
        self.sem = {}
        self.cnt = {}
        for e in self.eng:
            self.sem[e] = self.es.enter_context(nc.semaphore("sem_" + e))
            self.cnt[e] = 0
        self.waited = {}
        self.last_write = {}
        self.readers = {}
        self.ntensors = 0
        self.limit = None
        self.nops = 0
        self.pe_inorder = False
        self.t_eng = {e: 0.0 for e in self.eng}
        self.t_fin = {}
        self.stream = None
        self.t_stream = {}
        self.dry = False
        self.needed = None
        self.need_out = set()
        self.sig = {e: 0 for e in self.eng}
        self.sigval = {}

    def sb(self, name, shape, dt=F32):
        return self.es.enter_context(self.nc.sbuf_tensor(name, list(shape), dt))

    def ps(self, name, shape=(128, 512), dt=F32):
        return self.es.enter_context(self.nc.psum_tensor(name, list(shape), dt))

    def _deps(self, reads, writes, nowaw=False):
        deps = set()
        for t in reads:
            if t in self.last_write:
                deps.add(self.last_write[t])
        for t in writes:
            if t in self.last_write and not nowaw:
                deps.add(self.last_write[t])
            for r in self.readers.get(t, ()):
                deps.add(r)
        return deps

    def _semval(self, src, val):
        if src in self.eng and self.needed is not None:
            return self.sigval[(src, val)]
        return val

    def _wait(self, e, deps):
        for (src, val) in sorted(deps, key=lambda x: str(x)):
            if e == "pe" and src == "pe" and self.pe_inorder:
                continue
            if self.waited.get((e, src), 0) < val:
                if self.dry:
                    self.need_out.add((src, val))
                else:
                    self.eng[e].wait_ge(self.sem[src], self._semval(src, val))
                self.waited[(e, src)] = val

    def _record(self, key, reads, writes):
        for t in writes:
            self.last_write[t] = key
            self.readers[t] = set()
        for t in reads:
            if t not in writes:
                self.readers.setdefault(t, set()).add(key)

    COST = {"pe": 0.2, "dve": 0.45, "act": 0.5, "pool": 1.0, "sp": 0.1}

    def _model(self, e, key, deps, cost):
        t0 = self.t_eng.get(e, 0.0)
        for d in deps:
            t0 = max(t0, self.t_fin.get(d, 0.0) + (0.0 if d[0] == e else 0.15))
        t1 = t0 + cost
        self.t_eng[e] = t1
        self.t_fin[key] = t1
        if self.stream is not None:
            self.t_stream[self.stream] = max(self.t_stream.get(self.stream, 0.0), t1)

    def op(self, e, fn, reads=(), writes=(), cost=None):
        self.nops += 1
        if self.limit is not None and self.nops > self.limit:
            return
        reads = [r.name if hasattr(r, "name") else r for r in reads]
        writes = [w.name if hasattr(w, "name") else w for w in writes]
        writes = list(writes) + [r for r in reads if r.startswith("pb") and r not in writes]
        deps = self._deps(reads, writes)
        self._wait(e, deps)
        self.cnt[e] += 1
        self._model(e, (e, self.cnt[e]), deps, self.COST[e] if cost is None else cost)
        if not self.dry:
            inst = fn(self.eng[e])
            if self.needed is None:
                inst.then_inc(self.sem[e], 1)
            elif (e, self.cnt[e]) in self.needed:
                self.sig[e] += 1
                self.sigval[(e, self.cnt[e])] = self.sig[e]
                inst.then_inc(self.sem[e], 1)
        self._record((e, self.cnt[e]), reads, writes)

    def dma(self, e, out, in_, reads=(), writes=(), semkey=None, nowaw=False, **kw):
        reads = [r.name if hasattr(r, "name") else r for r in reads]
        writes = [w.name if hasattr(w, "name") else w for w in writes]
        self.nops += 1
        if self.limit is not None and self.nops > self.limit:
            return
        if semkey not in self.sem:
            self.sem[semkey] = self.es.enter_context(self.nc.semaphore("semd_" + str(semkey)))
            self.cnt[semkey] = 0
        deps = self._deps(reads, writes, nowaw)
        self._wait(e, deps)
        self.cnt[semkey] += 16
        self._model(e, (semkey, self.cnt[semkey]), deps, 2.0)
        if not self.dry:
            inst = self.eng[e].dma_start(out=out, in_=in_, **kw)
            inst.then_inc(self.sem[semkey], 16)
        self._record((semkey, self.cnt[semkey]), reads, writes)

    def barrier(self):
        for e in self.eng:
            for src, val in self.cnt.items():
                if val > 0 and self.waited.get((e, src), 0) < val:
                    if self.dry:
                        self.need_out.add((src, val))
                    else:
                        self.eng[e].wait_ge(self.sem[src], self._semval(src, val))
                    self.waited[(e, src)] = val
        self.last_write = {}
        self.readers = {}

    def finish(self, e="sp"):
        for src, val in self.cnt.items():
            if val > 0 and self.waited.get((e, src), 0) < val:
                if self.dry:
                    self.need_out.add((src, val))
                else:
                    self.eng[e].wait_ge(self.sem[src], self._semval(src, val))
                self.waited[(e, src)] = val
        self.es.close()


def bc(ap, shape):
    return ap.to_broadcast(list(shape))


NLOCAL = 16


def build_program(ntiles=NTILES, do_sample=True, limit=None, nl=None):
    if nl is None:
        nl = NLOCAL if ntiles >= NLOCAL else ntiles
    plan = _build(ntiles, do_sample, limit, None, nl)
    return _build(ntiles, do_sample, limit, plan, nl)


def _build(ntiles, do_sample, limit, plan, NL):
    nc = bass.Bass("TRN2", target_bir_lowering=False)
    k = KB(nc)
    k.limit = limit
    if plan is None:
        k.dry = True
    else:
        k.needed = plan
    NT = ntiles
    T = NT * 128

    def din(name, shape):
        return nc.dram_tensor(name, list(shape), F32, kind="ExternalInput").ap()

    def dout(name, shape):
        return nc.dram_tensor(name, list(shape), F32, kind="ExternalOutput").ap()

    x_d = din("x", [T, D])
    c_d = din("c", [1, D])
    wada_d = din("w_ada", [D, 3 * D])
    bada_d = din("b_ada", [1, 3 * D])
    wf_d = din("w_f", [D, 1536])
    wt_d = din("w_t", [D, WT_COLS])
    wo_d = din("w_o", [D, D])
    convw_d = din("conv_w", [4, 1536])
    alog_d = din("a_log", [1, 4])
    dtb_d = din("dt_bias", [1, 4])
    norma_d = din("norm_a", [1, 128])
    sinks_d = din("sinks_p", [1, 8])
    lng_d = din("ln_g", [1, D])
    lnb_d = din("ln_b", [1, D])
    cosk_d = din("cosk", [T, 32])
    sink_d = din("sink", [T, 32])

    y_d = dout("y", [NL * 128, D])
    valid_d = din("valid", [1, NT])
    F0 = NT - NL

    def full(n):
        return n >= F0

    def swaprep(n):
        return n == F0 - 1
    convp_d = dout("conv_p", [3, 1536])
    deltap_d = dout("delta_p", [4, 128, 128])
    swak_d = dout("swa_k_p", [128, 128])
    swav_d = dout("swa_v_p", [128, 128])

    if do_sample:
        xs_d = din("xs", [NS, D])
        cs_d = din("cs", [NS, D])
        sconv_d = din("s_conv", [NS, 3, 1536])
        sdelta_d = din("s_delta", [NS, 4, 128, 128])
        sk_d = din("s_k", [NS, 128, 128])
        sv_d = din("s_v", [NS, 128, 128])
        coss_d = din("cos_s", [1, 32])
        sins_d = din("sin_s", [1, 32])
        ys_d = dout("ys", [NS, D])
        convs_d = dout("conv_s", [NS, 3, 1536])
        deltas_d = dout("delta_s", [NS, 4, 128, 128])
        swaks_d = dout("swa_k_s", [NS, 128, 128])
        swavs_d = dout("swa_v_s", [NS, 128, 128])

    ident = k.sb("ident", [128, 128])
    U = k.sb("U", [128, 128])
    ones = k.sb("ones", [128, 128])
    onesb = k.sb("onesb", [128, 128], BF16)
    negA = k.sb("negA", [128, 128])
    negB = k.sb("negB", [128, 128])
    swam = k.sb("swam", [128, 256])

    k.op("pool", lambda g: g.memset(ones[:], 1.0), writes=[ones])
    k.op("pool", lambda g: g.memset(onesb[:], 1.0), writes=[onesb])
    k.op("pool", lambda g: g.affine_select(out=ident[:], in_=ones[:], pattern=[[-1, 128]], compare_op=ALU.is_equal,
                                            fill=0.0, base=0, channel_multiplier=1), reads=[ones], writes=[ident])
    k.op("pool", lambda g: g.affine_select(out=U[:], in_=ones[:], pattern=[[1, 128]], compare_op=ALU.is_ge,
                                            fill=0.0, base=0, channel_multiplier=-1), reads=[ones], writes=[U])
    zer = k.sb("zer", [128, 256])
    k.op("pool", lambda g: g.memset(zer[:], 0.0), writes=[zer])
    k.op("pool", lambda g: g.affine_select(out=negA[:], in_=zer[:, 0:128], pattern=[[-1, 128]], compare_op=ALU.is_ge,
                                            fill=NEG, base=-1, channel_multiplier=1), reads=[zer], writes=[negA])
    k.op("pool", lambda g: g.affine_select(out=negB[:], in_=zer[:, 0:128], pattern=[[1, 128]], compare_op=ALU.is_ge,
                                            fill=NEG, base=0, channel_multiplier=-1), reads=[zer], writes=[negB])
    swamt = k.sb("swamt", [128, 256])
    k.op("pool", lambda g: g.affine_select(out=swamt[:], in_=zer[:], pattern=[[1, 256]], compare_op=ALU.is_ge,
                                            fill=NEG, base=0, channel_multiplier=-1), reads=[zer], writes=[swamt])
    k.op("pool", lambda g: g.affine_select(out=swam[:], in_=swamt[:], pattern=[[-1, 256]], compare_op=ALU.is_ge,
                                            fill=NEG, base=128, channel_multiplier=1), reads=[swamt], writes=[swam])

    PB = [k.ps("pb%d" % i) for i in range(8)]
    def load_bc(name, src, n, parts=128):
        t = k.sb(name, [parts, n])
        k.dma("sp", t[:], src.partition_broadcast(parts), writes=[t], semkey="ld_" + name)
        return t

    lng_bc = load_bc("lng_bc", lng_d[0], D)
    lnb_bc = load_bc("lnb_bc", lnb_d[0], D)
    norma_bc = load_bc("norma_bc", norma_d[0], 128)
    sinks_bc = load_bc("sinks_bc", sinks_d[0], 8)
    valid_bc = load_bc("valid_bc", valid_d[0], NT)
    alog_bc = load_bc("alog_bc", alog_d[0], 4)
    dtb_bc = load_bc("dtb_bc", dtb_d[0], 4)
    cwT = k.sb("cwT", [128, 48])
    bada_fm = k.sb("bada_fm", [128, 24])
    cT = k.sb("cT", [128, 8])
    rowst = k.sb("rowst", [80, 128])
    k.dma("sp", rowst[0:48, :], convw_d.rearrange("j (c p) -> (j c) p", p=128), writes=[rowst], semkey="ld_rowst", nowaw=True)
    k.dma("sp", rowst[48:72, :], bada_d[0].rearrange("(j p) -> j p", p=128), writes=[rowst], semkey="ld_rowst", nowaw=True)
    k.dma("sp", rowst[72:80, :], c_d[0].rearrange("(j p) -> j p", p=128), writes=[rowst], semkey="ld_rowst", nowaw=True)
    cst = [k.sb("cst%d" % i, [128, 2, 32]) for i in range(2)]
    k.op("pe", lambda p: p.transpose(out=PB[0][:, 0:80], in_=rowst[:, :], identity=ident[0:80, 0:80]), reads=[rowst, ident], writes=[PB[0]])
    k.op("dve", lambda v: v.tensor_copy(out=cwT[:], in_=PB[0][:, 0:48]), reads=[PB[0]], writes=[cwT])
    k.op("dve", lambda v: v.tensor_copy(out=bada_fm[:], in_=PB[0][:, 48:72]), reads=[PB[0]], writes=[bada_fm])
    k.op("dve", lambda v: v.tensor_copy(out=cT[:], in_=PB[0][:, 72:80]), reads=[PB[0]], writes=[cT])
    ea = k.sb("ea", [128, 4])
    k.op("act", lambda a: a.activation(out=ea[:], in_=alog_bc[:], func=ACT.Exp), reads=[alog_bc], writes=[ea])
    k.op("dve", lambda v: v.tensor_scalar(out=ea[:], in0=ea[:], scalar1=-1.0, scalar2=None, op0=ALU.mult),
         reads=[ea], writes=[ea])

    Wf = k.sb("Wf", [128, 8, 1536], BF16)
    Wt = k.sb("Wt", [128, 8, WT_COLS], BF16)
    Wo = k.sb("Wo", [128, 8, D], BF16)
    mod_fm = k.sb("mod_fm", [128, 16])
    g1bc = k.sb("g1bc", [128, D])
    k1s = contextlib.ExitStack()
    if do_sample:
        csT = k1s.enter_context(nc.sbuf_tensor("csT", [128, 8, NS], F32))
        mod_s = k1s.enter_context(nc.sbuf_tensor("mod_s", [NS, 3 * D], F32))
    k2 = contextlib.ExitStack()
    def sb2(name, shape, dt=F32):
        return k2.enter_context(nc.sbuf_tensor(name, list(shape), dt))
    stg = [sb2("stg%d" % i, [128, 1800]) for i in range(2)]
    bada_bc = sb2("bada_bc", [128, D])
    k.dma("sp", bada_bc[:], bada_d[0, 2 * D:3 * D].partition_broadcast(128), writes=[bada_bc], semkey="ld_bada_bc")
    si = 0
    cast_engs = ["dve", "pool", "act"]

    def load_cast(dst, src_d, ncols):
        nonlocal si
        per = 2048 // ncols if ncols <= 2048 else 0
        for kc in range(8):
            s = stg[si % 2]
            k.dma(["sp", "act"][si % 2], s[:, 0:ncols], src_d[kc * 128:(kc + 1) * 128, :], writes=[s], semkey="ld_" + s.name)
            e = cast_engs[si % 3]
            if e == "act":
                k.op(e, lambda a, s=s, kc=kc: a.copy(out=dst[:, kc, :], in_=s[:, 0:ncols]), reads=[s], writes=[dst])
            else:
                k.op(e, lambda v, s=s, kc=kc: v.tensor_copy(out=dst[:, kc, :], in_=s[:, 0:ncols]), reads=[s], writes=[dst])
            si += 1

    load_cast(Wf, wf_d, 1536)
    load_cast(Wt, wt_d, WT_COLS)
    load_cast(Wo, wo_d, D)


    wa = [sb2("wa%d" % i, [128, 8, 512]) for i in range(1)]
    c_bcT = sb2("c_bcT", [128, 8, 128])
    k.op("pool", lambda g: g.tensor_copy(out=c_bcT[:], in_=bc(cT[:].rearrange("p (a b) -> p a b", b=1), [128, 8, 128])),
         reads=[cT], writes=[c_bcT])
    if do_sample:
        cs_sb = sb2("cs_sb", [NS, D])
        k.dma("sp", cs_sb[:], cs_d[:, :], writes=[cs_sb], semkey="ld_cs")
        for kc in range(8):
            k.op("pe", lambda p, kc=kc: p.transpose(out=PB[0][:, kc * NS:(kc + 1) * NS], in_=cs_sb[:, kc * 128:(kc + 1) * 128],
                                                    identity=ident[0:NS, 0:NS]), reads=[cs_sb, ident], writes=[PB[0]])
        k.op("dve", lambda v: v.tensor_copy(out=csT[:].rearrange("p a b -> p (a b)"), in_=PB[0][:, 0:8 * NS]),
             reads=[PB[0]], writes=[csT])
        bada_s = sb2("bada_s", [NS, 3 * D])
        k.dma("sp", bada_s[:], bada_d[0].partition_broadcast(NS), writes=[bada_s], semkey="ld_bada_s")
    for j in range(6):
        w = wa[0]
        for kc in range(8):
            k.dma("sp", w[:, kc, :], wada_d[kc * 128:(kc + 1) * 128, j * 512:(j + 1) * 512], writes=[w],
                  semkey="ld_" + w.name, nowaw=True)
        if j < 4:
            for sub in range(4):
                col = j * 4 + sub
                for kc in range(8):
                    k.op("pe", lambda p, kc=kc, sub=sub, col=col, w=w: p.matmul(
                        PB[1][:, col:col + 1], lhsT=w[:, kc, sub * 128:(sub + 1) * 128], rhs=cT[:, kc:kc + 1],
                        start=(kc == 0), stop=(kc == 7)), reads=[w, cT], writes=[PB[1]])
        else:
            for kc in range(8):
                k.op("pe", lambda p, kc=kc, w=w: p.matmul(PB[2 + (j - 4)][:, :], lhsT=c_bcT[:, kc, :], rhs=w[:, kc, :],
                                                          start=(kc == 0), stop=(kc == 7)),
                     reads=[w, c_bcT], writes=[PB[2 + (j - 4)]])
        if do_sample:
            for kc in range(8):
                k.op("pe", lambda p, kc=kc, w=w: p.matmul(PB[4 + j % 2][0:NS, :], lhsT=csT[:, kc, :], rhs=w[:, kc, :],
                                                          start=(kc == 0), stop=(kc == 7)),
                     reads=[w, csT], writes=[PB[4 + j % 2]])
            k.op("dve", lambda v, j=j: v.tensor_tensor(out=mod_s[:, j * 512:(j + 1) * 512], in0=PB[4 + j % 2][0:NS, :],
                                                       in1=bada_s[:, j * 512:(j + 1) * 512], op=ALU.add),
                 reads=[PB[4 + j % 2], bada_s], writes=[mod_s])
    k.op("dve", lambda v: v.tensor_tensor(out=mod_fm[:], in0=PB[1][:, 0:16], in1=bada_fm[:, 0:16], op=ALU.add),
         reads=[PB[1], bada_fm], writes=[mod_fm])
    k.op("dve", lambda v: v.tensor_scalar(out=mod_fm[:, 8:16], in0=mod_fm[:, 8:16], scalar1=1.0, scalar2=None, op0=ALU.add),
         reads=[mod_fm], writes=[mod_fm])
    for j in range(2):
        k.op("dve", lambda v, j=j: v.scalar_tensor_tensor(out=g1bc[:, j * 512:(j + 1) * 512], in0=PB[2 + j][:, :], scalar=1.0,
                                                           in1=bada_bc[:, j * 512:(j + 1) * 512], op0=ALU.add, op1=ALU.add),
             reads=[PB[2 + j], bada_bc], writes=[g1bc])


    if do_sample:
        k.barrier()
        k2.close()
        k2 = contextlib.ExitStack()
        P16 = NS
        sinkcol_d = din("sinks_col", [128, 1])
        xs = sb2("xs_sb", [P16, D]); hs = sb2("hs", [P16, D]); mix_s = sb2("mix_s", [P16, D])
        hsT = sb2("hsT", [128, 8, P16], BF16)
        pqkv = sb2("pqkv", [P16, 1536])
        qkv_s = sb2("qkv_s", [P16, 12, 128])
        zAs_s = sb2("zAs_s", [P16, 512]); zBs_s = sb2("zBs_s", [P16, 512])
        qr_s = sb2("qr_s", [P16, 512]); kr_s = sb2("kr_s", [P16, 128]); v_s = sb2("v_s", [P16, 128])
        vsb = sb2("vsb", [P16, 128], BF16)
        bd_s = sb2("bd_s", [P16, 8]); bdt_s = sb2("bdt_s", [P16, 8])
        beta_s = sb2("beta_s", [P16, 4]); nbeta_s = sb2("nbeta_s", [P16, 4]); g_s = sb2("g_s", [P16, 4]); eg_s = sb2("eg_s", [P16, 4])
        cs16 = sb2("cs16", [P16, 2, 32])
        k.dma("sp", cs16[:, 0, :], coss_d[0].partition_broadcast(P16), writes=[cs16], semkey="ld_cs16", nowaw=True)
        k.dma("sp", cs16[:, 1, :], sins_d[0].partition_broadcast(P16), writes=[cs16], semkey="ld_cs16", nowaw=True)
        k.dma("sp", xs[:], xs_d[:, :], writes=[xs], semkey="ld_xs")
        k.op("dve", lambda v: v.scalar_tensor_tensor(out=hs[:], in0=mod_s[:, D:2 * D], scalar=1.0, in1=xs[:], op0=ALU.add, op1=ALU.mult),
             reads=[mod_s, xs], writes=[hs])
        k.op("dve", lambda v: v.tensor_tensor(out=hs[:], in0=hs[:], in1=mod_s[:, 0:D], op=ALU.add), reads=[hs, mod_s], writes=[hs])
        for kc in range(8):
            k.op("pe", lambda p, kc=kc: p.transpose(out=PB[0][:, kc * P16:(kc + 1) * P16], in_=hs[:, kc * 128:(kc + 1) * 128],
                                                    identity=ident[0:P16, 0:P16]), reads=[hs, ident], writes=[PB[0]])
        k.op("dve", lambda v: v.tensor_copy(out=hsT[:].rearrange("p a b -> p (a b)"), in_=PB[0][:, 0:8 * P16]), reads=[PB[0]], writes=[hsT])
        for j in range(3):
            for kc in range(8):
                k.op("pe", lambda p, j=j, kc=kc: p.matmul(PB[1 + j][0:P16, :], lhsT=hsT[:, kc, :], rhs=Wf[:, kc, j * 512:(j + 1) * 512],
                                                          start=(kc == 0), stop=(kc == 7)), reads=[hsT, Wf], writes=[PB[1 + j]])
        offs = [(0, 512), (512, 512), (1024, 512), (1536, 264)]
        for j, (o, w_) in enumerate(offs):
            for kc in range(8):
                k.op("pe", lambda p, j=j, o=o, w_=w_, kc=kc: p.matmul(PB[4 + j][0:P16, 0:w_], lhsT=hsT[:, kc, :], rhs=Wt[:, kc, o:o + w_],
                                                                      start=(kc == 0), stop=(kc == 7)), reads=[hsT, Wt], writes=[PB[4 + j]])
        for j in range(3):
            k.op("dve", lambda v, j=j: v.tensor_copy(out=pqkv[:, j * 512:(j + 1) * 512], in_=PB[1 + j][0:P16, :]), reads=[PB[1 + j]], writes=[pqkv])
        k.op("act", lambda a: a.activation(out=zAs_s[:], in_=PB[4][0:P16, :], func=ACT.Silu), reads=[PB[4]], writes=[zAs_s])
        k.op("act", lambda a: a.activation(out=zBs_s[:], in_=PB[6][0:P16, :], func=ACT.Silu), reads=[PB[6]], writes=[zBs_s])
        rts = [sb2("rts%d" % i, [P16, 8, 32]) for i in range(4)]
        q3 = PB[5][0:P16, :].rearrange("p (a b) -> p a b", a=8)
        qr3 = qr_s[:].rearrange("p (a b) -> p a b", a=8)
        cq = bc(cs16[:, 0:1, :], [P16, 8, 32]); sq_ = bc(cs16[:, 1:2, :], [P16, 8, 32])
        k.op("dve", lambda v: v.tensor_tensor(out=rts[0][:], in0=q3[:, :, 0:32], in1=cq, op=ALU.mult), reads=[PB[5], cs16], writes=[rts[0]])
        k.op("dve", lambda v: v.tensor_tensor(out=rts[1][:], in0=q3[:, :, 32:64], in1=sq_, op=ALU.mult), reads=[PB[5], cs16], writes=[rts[1]])
        k.op("dve", lambda v: v.tensor_tensor(out=rts[2][:], in0=q3[:, :, 32:64], in1=cq, op=ALU.mult), reads=[PB[5], cs16], writes=[rts[2]])
        k.op("dve", lambda v: v.tensor_tensor(out=rts[3][:], in0=q3[:, :, 0:32], in1=sq_, op=ALU.mult), reads=[PB[5], cs16], writes=[rts[3]])
        k.op("dve", lambda v: v.tensor_tensor(out=qr3[:, :, 0:32], in0=rts[0][:], in1=rts[1][:], op=ALU.subtract), reads=[rts[0], rts[1]], writes=[qr_s])
        k.op("dve", lambda v: v.tensor_tensor(out=qr3[:, :, 32:64], in0=rts[2][:], in1=rts[3][:], op=ALU.add), reads=[rts[2], rts[3]], writes=[qr_s])
        k.op("dve", lambda v: v.tensor_scalar(out=qr_s[:], in0=qr_s[:], scalar1=0.125, scalar2=None, op0=ALU.mult), reads=[qr_s], writes=[qr_s])
        k3 = PB[7][0:P16, 0:128].rearrange("p (a b) -> p a b", a=2)
        kr3 = kr_s[:].rearrange("p (a b) -> p a b", a=2)
        ck = bc(cs16[:, 0:1, :], [P16, 2, 32]); sk_ = bc(cs16[:, 1:2, :], [P16, 2, 32])
        k.op("dve", lambda v: v.tensor_tensor(out=rts[0][:, 0:2, :], in0=k3[:, :, 0:32], in1=ck, op=ALU.mult), reads=[PB[7], cs16], writes=[rts[0]])
        k.op("dve", lambda v: v.tensor_tensor(out=rts[1][:, 0:2, :], in0=k3[:, :, 32:64], in1=sk_, op=ALU.mult), reads=[PB[7], cs16], writes=[rts[1]])
        k.op("dve", lambda v: v.tensor_tensor(out=rts[2][:, 0:2, :], in0=k3[:, :, 32:64], in1=ck, op=ALU.mult), reads=[PB[7], cs16], writes=[rts[2]])
        k.op("dve", lambda v: v.tensor_tensor(out=rts[3][:, 0:2, :], in0=k3[:, :, 0:32], in1=sk_, op=ALU.mult), reads=[PB[7], cs16], writes=[rts[3]])
        k.op("dve", lambda v: v.tensor_tensor(out=kr3[:, :, 0:32], in0=rts[0][:, 0:2, :], in1=rts[1][:, 0:2, :], op=ALU.subtract), reads=[rts[0], rts[1]], writes=[kr_s])
        k.op("dve", lambda v: v.tensor_tensor(out=kr3[:, :, 32:64], in0=rts[2][:, 0:2, :], in1=rts[3][:, 0:2, :], op=ALU.add), reads=[rts[2], rts[3]], writes=[kr_s])
        k.op("dve", lambda v: v.tensor_copy(out=v_s[:], in_=PB[7][0:P16, 128:256]), reads=[PB[7]], writes=[v_s])
        k.op("dve", lambda v: v.tensor_copy(out=vsb[:], in_=PB[7][0:P16, 128:256]), reads=[PB[7]], writes=[vsb])
        k.op("dve", lambda v: v.tensor_copy(out=bd_s[:], in_=PB[7][0:P16, 256:264]), reads=[PB[7]], writes=[bd_s])
        k.op("act", lambda a: a.activation(out=bdt_s[:, 0:4], in_=bd_s[:, 0:4], func=ACT.Exp, scale=-1.0), reads=[bd_s], writes=[bdt_s])
        k.op("dve", lambda v: v.tensor_scalar(out=bdt_s[:, 0:4], in0=bdt_s[:, 0:4], scalar1=1.0, scalar2=None, op0=ALU.add), reads=[bdt_s], writes=[bdt_s])
        k.op("dve", lambda v: v.reciprocal(out=beta_s[:], in_=bdt_s[:, 0:4]), reads=[bdt_s], writes=[beta_s])
        k.op("dve", lambda v: v.tensor_scalar(out=nbeta_s[:], in0=beta_s[:], scalar1=-1.0, scalar2=None, op0=ALU.mult), reads=[beta_s], writes=[nbeta_s])
        k.op("dve", lambda v: v.tensor_tensor(out=bd_s[:, 4:8], in0=bd_s[:, 4:8], in1=dtb_bc[0:P16, :], op=ALU.add), reads=[bd_s, dtb_bc], writes=[bd_s])
        k.op("act", lambda a: a.activation(out=bdt_s[:, 4:8], in_=bd_s[:, 4:8], func=ACT.Exp), reads=[bd_s], writes=[bdt_s])
        k.op("act", lambda a: a.activation(out=bdt_s[:, 4:8], in_=bdt_s[:, 4:8], func=ACT.Ln, bias=1.0, scale=1.0), reads=[bdt_s], writes=[bdt_s])
        k.op("dve", lambda v: v.tensor_tensor(out=g_s[:], in0=bdt_s[:, 4:8], in1=ea[0:P16, :], op=ALU.mult), reads=[bdt_s, ea], writes=[g_s])
        k.op("act", lambda a: a.activation(out=eg_s[:], in_=g_s[:], func=ACT.Exp), reads=[g_s], writes=[eg_s])
        k3s = contextlib.ExitStack()
        def sb3(name, shape, dt=F32):
            return k3s.enter_context(nc.sbuf_tensor(name, list(shape), dt))
        xp4 = sb3("xp4", [P16, 4, 1536]); cwb = sb3("cwb", [P16, 4, 1536]); tmpc = xp4
        acc_s = sb3("acc_s", [P16, 1536])
        k.dma("sp", xp4[:, 0:3, :], sconv_d[:, :, :], writes=[xp4], semkey="ld_xp4", nowaw=True)
        k.dma("sp", cwb[:].rearrange("p a b -> p (a b)"), convw_d.rearrange("a b -> (a b)").partition_broadcast(P16), writes=[cwb], semkey="ld_cwb")
        k.op("act", lambda a: a.copy(out=xp4[:, 3, :], in_=pqkv[:]), reads=[pqkv], writes=[xp4])
        k.dma("sp", convs_d[:, :, :], xp4[:, 1:4, :], reads=[xp4], semkey="st_smisc")
        k.op("dve", lambda v: v.tensor_tensor(out=tmpc[:], in0=xp4[:], in1=cwb[:], op=ALU.mult), reads=[xp4, cwb], writes=[tmpc])
        k.op("dve", lambda v: v.tensor_reduce(out=acc_s[:], in_=tmpc[:].rearrange("p j c -> p c j"), axis=AX.X, op=ALU.add), reads=[tmpc], writes=[acc_s])
        k.op("act", lambda a: a.activation(out=qkv_s[:].rearrange("p a b -> p (a b)"), in_=acc_s[:], func=ACT.Silu), reads=[acc_s], writes=[qkv_s])
        sqs = sb3("sqs", [P16, 8, 128]); sss = sb3("sss", [P16, 8])
        k.op("dve", lambda v: v.tensor_tensor(out=sqs[:], in0=qkv_s[:, 0:8, :], in1=qkv_s[:, 0:8, :], op=ALU.mult), reads=[qkv_s], writes=[sqs])
        k.op("dve", lambda v: v.tensor_reduce(out=sss[:], in_=sqs[:], axis=AX.X, op=ALU.add), reads=[sqs], writes=[sss])
        k.op("act", lambda a: a.activation(out=sss[:], in_=sss[:], func=ACT.Ln, bias=L2_EPS, scale=1.0), reads=[sss], writes=[sss])
        k.op("act", lambda a: a.activation(out=sss[:, 0:4], in_=sss[:, 0:4], func=ACT.Exp, bias=float(-0.5 * np.log(128.0)), scale=-0.5), reads=[sss], writes=[sss])
        k.op("act", lambda a: a.activation(out=sss[:, 4:8], in_=sss[:, 4:8], func=ACT.Exp, scale=-0.5), reads=[sss], writes=[sss])
        k.op("dve", lambda v: v.tensor_tensor(out=qkv_s[:, 0:8, :], in0=qkv_s[:, 0:8, :], in1=bc(sss[:].rearrange("p (a b) -> p a b", b=1), [P16, 8, 128]), op=ALU.mult),
             reads=[qkv_s, sss], writes=[qkv_s])
        k.barrier()
        k3s.close()
        k3s = contextlib.ExitStack()
        Ssb = sb3("Ssb", [128, P16 * 4, 128])
        sdv = sdelta_d.rearrange("b h k v -> k (b h) v")
        for i4 in range(4):
            k.dma("sp", Ssb[:, i4 * 16:(i4 + 1) * 16, :], sdv[:, i4 * 16:(i4 + 1) * 16, :], writes=[Ssb], semkey="ld_Ssb", nowaw=True)
        qkT_s = sb3("qkT_s", [128, 8, P16])
        for c in range(8):
            k.op("pe", lambda p, c=c: p.transpose(out=PB[0][:, c * P16:(c + 1) * P16], in_=qkv_s[:, c, :], identity=ident[0:P16, 0:P16]),
                 reads=[qkv_s, ident], writes=[PB[0]])
        k.op("dve", lambda v: v.tensor_copy(out=qkT_s[:].rearrange("p a b -> p (a b)"), in_=PB[0][:, 0:8 * P16]), reads=[PB[0]], writes=[qkT_s])
        dmask = sb3("dmask", [P16, P16, 128])
        k.op("pool", lambda g: g.tensor_copy(out=dmask[:], in_=bc(ident[0:P16, 0:P16].rearrange("p (a b) -> p a b", b=1), [P16, P16, 128])),
             reads=[ident], writes=[dmask])
        egm = sb3("egm", [P16, P16, 4]); egbc = sb3("egbc", [128, P16 * 4])
        k.op("dve", lambda v: v.tensor_tensor(out=egm[:], in0=bc(eg_s[:].rearrange("p (a b) -> p a b", a=1), [P16, P16, 4]),
                                              in1=bc(ident[0:P16, 0:P16].rearrange("p (a b) -> p a b", b=1), [P16, P16, 4]), op=ALU.mult),
             reads=[eg_s, ident], writes=[egm])
        k.op("pe", lambda p: p.matmul(PB[1][:, 0:P16 * 4], lhsT=ones[0:P16, :], rhs=egm[:].rearrange("p a b -> p (a b)"), start=True, stop=True),
             reads=[ones, egm], writes=[PB[1]])
        k.op("dve", lambda v: v.tensor_copy(out=egbc[:], in_=PB[1][:, 0:P16 * 4]), reads=[PB[1]], writes=[egbc])
        pred = sb3("pred", [P16, 4, 128]); qS = sb3("qS", [P16, 4, 128]); tmpd = sb3("tmpd", [P16, P16, 128])
        dd = sb3("dd", [P16, 4, 128]); Dm = sb3("Dm", [P16, P16, 128]); o_s = sb3("o_s", [P16, 4, 128])
        qk_s = sb3("qk_s", [P16, 4]); qkt = sb3("qkt", [P16, 4, 128])
        k.op("dve", lambda v: v.tensor_tensor(out=qkt[:], in0=qkv_s[:, 0:4, :], in1=qkv_s[:, 4:8, :], op=ALU.mult), reads=[qkv_s], writes=[qkt])
        k.op("dve", lambda v: v.tensor_reduce(out=qk_s[:], in_=qkt[:], axis=AX.X, op=ALU.add), reads=[qkt], writes=[qk_s])
        for h in range(4):
            for which, dst in ((4, pred), (0, qS)):
                banks = [PB[2], PB[3], PB[4], PB[5]] if which == 4 else [PB[6], PB[7], PB[0], PB[1]]
                for b in range(P16):
                    pb = banks[b // 4]
                    k.op("pe", lambda p, b=b, pb=pb, which=which, h=h: p.matmul(pb[0:P16, (b % 4) * 128:(b % 4 + 1) * 128], lhsT=qkT_s[:, which + h, :],
                                                                               rhs=Ssb[:, b * 4 + h, :], start=True, stop=True),
                         reads=[qkT_s, Ssb], writes=[pb])
                for j in range(4):
                    k.op("dve", lambda v, j=j, banks=banks: v.tensor_tensor(out=tmpd[:, 4 * j:4 * j + 4, :], in0=banks[j][0:P16, :].rearrange("p (a b) -> p a b", a=4),
                                                                            in1=dmask[:, 4 * j:4 * j + 4, :], op=ALU.mult), reads=[banks[j], dmask], writes=[tmpd])
                k.op("dve", lambda v, dst=dst, h=h: v.tensor_reduce(out=dst[:, h, :], in_=tmpd[:].rearrange("p b v -> p v b"), axis=AX.X, op=ALU.add),
                     reads=[tmpd], writes=[dst])
            k.op("dve", lambda v, h=h: v.scalar_tensor_tensor(out=dd[:, h, :], in0=pred[:, h, :], scalar=eg_s[:, h:h + 1], in1=qkv_s[:, 8 + h, :],
                                                               op0=ALU.mult, op1=ALU.subtract), reads=[pred, eg_s, qkv_s], writes=[dd])
            k.op("dve", lambda v, h=h: v.tensor_scalar(out=dd[:, h, :], in0=dd[:, h, :], scalar1=nbeta_s[:, h:h + 1], scalar2=None, op0=ALU.mult),
                 reads=[dd, nbeta_s], writes=[dd])
            k.op("dve", lambda v, h=h: v.tensor_scalar(out=o_s[:, h, :], in0=dd[:, h, :], scalar1=qk_s[:, h:h + 1], scalar2=None, op0=ALU.mult),
                 reads=[dd, qk_s], writes=[o_s])
            k.op("dve", lambda v, h=h: v.scalar_tensor_tensor(out=o_s[:, h, :], in0=qS[:, h, :], scalar=eg_s[:, h:h + 1], in1=o_s[:, h, :],
                                                               op0=ALU.mult, op1=ALU.add), reads=[qS, eg_s, o_s], writes=[o_s])
            k.op("dve", lambda v, h=h: v.tensor_tensor(out=Dm[:], in0=bc(dd[:, h:h + 1, :], [P16, P16, 128]), in1=dmask[:], op=ALU.mult),
                 reads=[dd, dmask], writes=[Dm])
            banks = [PB[2], PB[3], PB[4], PB[5]]
            for b in range(P16):
                pb = banks[b // 4]
                k.op("pe", lambda p, b=b, pb=pb, h=h: p.matmul(pb[:, (b % 4) * 128:(b % 4 + 1) * 128], lhsT=qkv_s[:, 4 + h, :], rhs=Dm[:, b, :],
                                                               start=True, stop=True), reads=[qkv_s, Dm], writes=[pb])
            for b in range(P16):
                pb = banks[b // 4]
                k.op("dve", lambda v, b=b, pb=pb, h=h: v.scalar_tensor_tensor(out=Ssb[:, b * 4 + h, :], in0=Ssb[:, b * 4 + h, :], scalar=egbc[:, b * 4 + h:b * 4 + h + 1],
                                                                              in1=pb[:, (b % 4) * 128:(b % 4 + 1) * 128], op0=ALU.mult, op1=ALU.add),
                     reads=[Ssb, egbc, pb], writes=[Ssb])
        ddv = deltas_d.rearrange("b h k v -> k (b h) v")
        for i4 in range(4):
            k.dma(["sp", "act", "sp", "act"][i4], ddv[:, i4 * 16:(i4 + 1) * 16, :], Ssb[:, i4 * 16:(i4 + 1) * 16, :], reads=[Ssb], semkey="st_smisc")
        oss_s = sb3("oss_s", [P16, 4])
        k.op("dve", lambda v: v.tensor_tensor(out=qkt[:], in0=o_s[:], in1=o_s[:], op=ALU.mult), reads=[o_s], writes=[qkt])
        k.op("dve", lambda v: v.tensor_reduce(out=oss_s[:], in_=qkt[:], axis=AX.X, op=ALU.add), reads=[qkt], writes=[oss_s])
        k.op("act", lambda a: a.activation(out=oss_s[:], in_=oss_s[:], func=ACT.Ln, bias=RMS_EPS, scale=1.0 / 128.0), reads=[oss_s], writes=[oss_s])
        k.op("act", lambda a: a.activation(out=oss_s[:], in_=oss_s[:], func=ACT.Exp, scale=-0.5), reads=[oss_s], writes=[oss_s])
        k.op("dve", lambda v: v.tensor_tensor(out=o_s[:], in0=o_s[:], in1=bc(oss_s[:].rearrange("p (a b) -> p a b", b=1), [P16, 4, 128]), op=ALU.mult),
             reads=[o_s, oss_s], writes=[o_s])
        k.op("dve", lambda v: v.tensor_tensor(out=o_s[:], in0=o_s[:], in1=bc(norma_bc[0:P16, :].rearrange("p (a b) -> p a b", a=1), [P16, 4, 128]), op=ALU.mult),
             reads=[o_s, norma_bc], writes=[o_s])
        k.op("dve", lambda v: v.tensor_tensor(out=mix_s[:, 0:512], in0=o_s[:].rearrange("p a b -> p (a b)"), in1=zAs_s[:], op=ALU.mult),
             reads=[o_s, zAs_s], writes=[mix_s])
        k.barrier()
        k3s.close()
        k3s = contextlib.ExitStack()
        Kc = sb3("Kc", [128, P16, 128]); Vc = sb3("Vc", [128, P16, 128])
        KcT = sb3("KcT", [128, P16, 128], BF16); VcB = sb3("VcB", [128, P16, 128], BF16)
        k.dma("sp", Kc[:], sk_d.rearrange("b s c -> s b c"), writes=[Kc], semkey="ld_Kc")
        k.dma("act", Vc[:], sv_d.rearrange("b s c -> s b c"), writes=[Vc], semkey="ld_Vc")
        k.dma("sp", swaks_d[:, 0:127, :], sk_d[:, 1:128, :], semkey="st_smisc")
        k.dma("act", swavs_d[:, 0:127, :], sv_d[:, 1:128, :], semkey="st_smisc")
        k.dma("sp", swaks_d[:, 127, :], kr_s[:], reads=[kr_s], semkey="st_smisc")
        k.dma("sp", swavs_d[:, 127, :], v_s[:], reads=[v_s], semkey="st_smisc")
        k.op("pool", lambda g: g.tensor_copy(out=VcB[:], in_=Vc[:]), reads=[Vc], writes=[VcB])
        for b in range(P16):
            pb = [PB[2], PB[3], PB[4], PB[5]][b // 4]
            k.op("pe", lambda p, b=b, pb=pb: p.transpose(out=pb[:, (b % 4) * 128:(b % 4 + 1) * 128], in_=Kc[:, b, :], identity=ident[:]),
                 reads=[Kc, ident], writes=[pb])
        for j in range(4):
            pb = [PB[2], PB[3], PB[4], PB[5]][j]
            k.op("act", lambda a, j=j, pb=pb: a.copy(out=KcT[:, 4 * j:4 * j + 4, :], in_=pb[:, :].rearrange("p (a b) -> p a b", a=4)), reads=[pb], writes=[KcT])
        qT_s = sb3("qT_s", [128, 4, P16]); Aq = sb3("Aq", [128, P16, 2, 4], BF16)
        knT = sb3("knT", [128, P16], BF16); zBT = sb3("zBT", [128, 4, P16])
        for c in range(4):
            k.op("pe", lambda p, c=c: p.transpose(out=PB[6][:, c * P16:(c + 1) * P16], in_=qr_s[:, c * 128:(c + 1) * 128], identity=ident[0:P16, 0:P16]),
                 reads=[qr_s, ident], writes=[PB[6]])
        k.op("pe", lambda p: p.transpose(out=PB[6][:, 4 * P16:5 * P16], in_=kr_s[:], identity=ident[0:P16, 0:P16]), reads=[kr_s, ident], writes=[PB[6]])
        for c in range(4):
            k.op("pe", lambda p, c=c: p.transpose(out=PB[6][:, (5 + c) * P16:(6 + c) * P16], in_=zBs_s[:, c * 128:(c + 1) * 128], identity=ident[0:P16, 0:P16]),
                 reads=[zBs_s, ident], writes=[PB[6]])
        k.op("dve", lambda v: v.tensor_copy(out=qT_s[:].rearrange("p a b -> p (a b)"), in_=PB[6][:, 0:4 * P16]), reads=[PB[6]], writes=[qT_s])
        k.op("dve", lambda v: v.tensor_copy(out=knT[:], in_=PB[6][:, 4 * P16:5 * P16]), reads=[PB[6]], writes=[knT])
        k.op("dve", lambda v: v.tensor_copy(out=zBT[:].rearrange("p a b -> p (a b)"), in_=PB[6][:, 5 * P16:9 * P16]), reads=[PB[6]], writes=[zBT])
        k.op("pool", lambda g: g.memset(Aq[:], 0.0), writes=[Aq])
        k.op("dve", lambda v: v.tensor_copy(out=Aq[0:64, :, 0, :], in_=qT_s[0:64, :, :].rearrange("p c b -> p b c")), reads=[qT_s], writes=[Aq])
        k.op("dve", lambda v: v.tensor_copy(out=Aq[64:128, :, 1, :], in_=qT_s[64:128, :, :].rearrange("p c b -> p b c")), reads=[qT_s], writes=[Aq])
        for b in range(P16):
            k.op("pe", lambda p, b=b: p.matmul(PB[7][:, b * 8:(b + 1) * 8], lhsT=KcT[:, b, :], rhs=Aq[:, b, :, :].rearrange("p a b -> p (a b)"),
                                               start=True, stop=True), reads=[KcT, Aq], writes=[PB[7]])
        STs = sb3("STs", [128, 128])
        k.op("dve", lambda v: v.tensor_copy(out=STs[:], in_=PB[7][:, 0:128]), reads=[PB[7]], writes=[STs])
        k.op("pe", lambda p: p.transpose(out=PB[0][:, 0:128], in_=STs[:], identity=ident[:]), reads=[STs, ident], writes=[PB[0]])
        k.op("pe", lambda p: p.matmul(PB[0][:, 128:128 + P16], lhsT=Aq[:].rearrange("p a b c -> p (a b c)"), rhs=knT[:], start=True, stop=True),
             reads=[Aq, knT], writes=[PB[0]])
        M2 = sb3("M2", [128, P16]); M2t = sb3("M2t", [128, P16])
        k.op("pool", lambda g: g.affine_select(out=M2t[:], in_=ones[:, 0:P16], pattern=[[-8, P16]], compare_op=ALU.is_ge, fill=0.0, base=0, channel_multiplier=1),
             reads=[ones], writes=[M2t])
        k.op("pool", lambda g: g.affine_select(out=M2[:], in_=M2t[:], pattern=[[8, P16]], compare_op=ALU.is_ge, fill=0.0, base=7, channel_multiplier=-1),
             reads=[M2t], writes=[M2])
        sm = sb3("sm", [128, 16]); tmps = sb3("tmps", [128, P16])
        sinkcol = sb3("sinkcol", [128, 1])
        k.dma("sp", sinkcol[:], sinkcol_d[:, :], writes=[sinkcol], semkey="ld_sinkcol")
        k.op("dve", lambda v: v.tensor_tensor(out=tmps[:], in0=PB[0][:, 128:128 + P16], in1=M2[:], op=ALU.mult), reads=[PB[0], M2], writes=[tmps])
        k.op("dve", lambda v: v.tensor_reduce(out=sm[:, 0:1], in_=tmps[:], axis=AX.X, op=ALU.add), reads=[tmps], writes=[sm])
        k.op("dve", lambda v: v.tensor_reduce(out=sm[:, 1:2], in_=PB[0][:, 0:128], axis=AX.X, op=ALU.max), reads=[PB[0]], writes=[sm])
        k.op("dve", lambda v: v.tensor_tensor(out=sm[:, 1:2], in0=sm[:, 1:2], in1=sm[:, 0:1], op=ALU.max), reads=[sm], writes=[sm])
        k.op("dve", lambda v: v.tensor_tensor(out=sm[:, 1:2], in0=sm[:, 1:2], in1=sinkcol[:], op=ALU.max), reads=[sm, sinkcol], writes=[sm])
        k.op("dve", lambda v: v.tensor_scalar(out=sm[:, 2:3], in0=sm[:, 1:2], scalar1=-1.0, scalar2=None, op0=ALU.mult), reads=[sm], writes=[sm])
        Ps = sb3("Ps", [128, 128])
        k.op("act", lambda a: a.activation(out=Ps[:], in_=PB[0][:, 0:128], func=ACT.Exp, bias=sm[:, 2:3], scale=1.0, accum_out=sm[:, 3:4]),
             reads=[PB[0], sm], writes=[Ps, sm])
        k.op("act", lambda a: a.activation(out=sm[:, 4:5], in_=sm[:, 0:1], func=ACT.Exp, bias=sm[:, 2:3], scale=1.0), reads=[sm], writes=[sm])
        k.op("act", lambda a: a.activation(out=sm[:, 5:6], in_=sinkcol[:], func=ACT.Exp, bias=sm[:, 2:3], scale=1.0), reads=[sm, sinkcol], writes=[sm])
        k.op("dve", lambda v: v.tensor_tensor(out=sm[:, 6:7], in0=sm[:, 3:4], in1=sm[:, 4:5], op=ALU.add), reads=[sm], writes=[sm])
        k.op("dve", lambda v: v.tensor_tensor(out=sm[:, 6:7], in0=sm[:, 6:7], in1=sm[:, 5:6], op=ALU.add), reads=[sm], writes=[sm])
        k.op("dve", lambda v: v.reciprocal(out=sm[:, 7:8], in_=sm[:, 6:7]), reads=[sm], writes=[sm])
        k.op("dve", lambda v: v.tensor_scalar(out=Ps[:], in0=Ps[:], scalar1=sm[:, 7:8], scalar2=None, op0=ALU.mult), reads=[Ps, sm], writes=[Ps])
        k.op("dve", lambda v: v.tensor_tensor(out=sm[:, 8:9], in0=sm[:, 4:5], in1=sm[:, 7:8], op=ALU.mult), reads=[sm], writes=[sm])
        PsT = sb3("PsT", [128, 128], BF16); Wd = sb3("Wd", [128, P16]); Wn = sb3("Wn", [P16, 128], BF16)
        k.op("pe", lambda p: p.transpose(out=PB[1][:, 0:128], in_=Ps[:], identity=ident[:]), reads=[Ps, ident], writes=[PB[1]])
        k.op("act", lambda a: a.copy(out=PsT[:], in_=PB[1][:, 0:128]), reads=[PB[1]], writes=[PsT])
        k.op("dve", lambda v: v.tensor_scalar(out=Wd[:], in0=M2[:], scalar1=sm[:, 8:9], scalar2=None, op0=ALU.mult), reads=[M2, sm], writes=[Wd])
        k.op("pe", lambda p: p.transpose(out=PB[1][0:P16, 128:256], in_=Wd[:], identity=ident[:]), reads=[Wd, ident], writes=[PB[1]])
        k.op("act", lambda a: a.copy(out=Wn[:], in_=PB[1][0:P16, 128:256]), reads=[PB[1]], writes=[Wn])
        for b in range(P16):
            k.op("pe", lambda p, b=b: p.matmul(PB[2][:, b * 8:(b + 1) * 8], lhsT=VcB[:, b, :], rhs=PsT[:, b * 8:(b + 1) * 8], start=True, stop=False),
                 reads=[VcB, PsT], writes=[PB[2]])
            k.op("pe", lambda p, b=b: p.matmul(PB[2][:, b * 8:(b + 1) * 8], lhsT=vsb[:], rhs=Wn[:, b * 8:(b + 1) * 8], start=False, stop=True),
                 reads=[vsb, Wn], writes=[PB[2]])
        obT = sb3("obT", [128, 4, P16])
        OT4 = PB[2][:, 0:128].rearrange("p (b k c) -> p b k c", b=P16, k=2)
        k.op("dve", lambda v: v.tensor_copy(out=obT[0:64, :, :].rearrange("p c b -> p b c"), in_=OT4[0:64, :, 0, :]), reads=[PB[2]], writes=[obT])
        k.op("dve", lambda v: v.tensor_copy(out=obT[64:128, :, :].rearrange("p c b -> p b c"), in_=OT4[64:128, :, 1, :]), reads=[PB[2]], writes=[obT])
        mixT_s = sb3("mixT_s", [128, 8, P16], BF16)
        k.op("dve", lambda v: v.tensor_tensor(out=mixT_s[:, 4:8, :], in0=obT[:], in1=zBT[:], op=ALU.mult), reads=[obT, zBT], writes=[mixT_s])
        for c in range(4):
            k.op("pe", lambda p, c=c: p.transpose(out=PB[3][:, c * P16:(c + 1) * P16], in_=mix_s[:, c * 128:(c + 1) * 128], identity=ident[0:P16, 0:P16]),
                 reads=[mix_s, ident], writes=[PB[3]])
        k.op("dve", lambda v: v.tensor_copy(out=mixT_s[:, 0:4, :].rearrange("p a b -> p (a b)"), in_=PB[3][:, 0:4 * P16]), reads=[PB[3]], writes=[mixT_s])
        for j in range(2):
            for kc in range(8):
                k.op("pe", lambda p, j=j, kc=kc: p.matmul(PB[4 + j][0:P16, :], lhsT=mixT_s[:, kc, :], rhs=Wo[:, kc, j * 512:(j + 1) * 512],
                                                          start=(kc == 0), stop=(kc == 7)), reads=[mixT_s, Wo], writes=[PB[4 + j]])
        ypre_s = sb3("ypre_s", [P16, D]); yo_s = sb3("yo_s", [P16, D]); st_s = sb3("st_s", [P16, 8])
        for j in range(2):
            k.op("dve", lambda v, j=j: v.scalar_tensor_tensor(out=ypre_s[:, j * 512:(j + 1) * 512], in0=mod_s[:, 2 * D + j * 512:2 * D + (j + 1) * 512], scalar=1.0,
                                                               in1=PB[4 + j][0:P16, :], op0=ALU.add, op1=ALU.mult), reads=[mod_s, PB[4 + j]], writes=[ypre_s])
        k.op("dve", lambda v: v.scalar_tensor_tensor(out=ypre_s[:], in0=xs[:], scalar=ALPHA, in1=ypre_s[:], op0=ALU.mult, op1=ALU.add),
             reads=[xs, ypre_s], writes=[ypre_s])
        k.op("act", lambda a: a.activation(out=yo_s[:], in_=ypre_s[:], func=ACT.Identity, accum_out=st_s[:, 0:1]), reads=[ypre_s], writes=[yo_s, st_s])
        k.op("act", lambda a: a.activation(out=yo_s[:], in_=ypre_s[:], func=ACT.Square, accum_out=st_s[:, 1:2]), reads=[ypre_s], writes=[yo_s, st_s])
        k.op("dve", lambda v: v.tensor_scalar(out=st_s[:, 2:3], in0=st_s[:, 0:1], scalar1=1.0 / D, scalar2=None, op0=ALU.mult), reads=[st_s], writes=[st_s])
        k.op("dve", lambda v: v.tensor_tensor(out=st_s[:, 3:4], in0=st_s[:, 2:3], in1=st_s[:, 2:3], op=ALU.mult), reads=[st_s], writes=[st_s])
        k.op("dve", lambda v: v.scalar_tensor_tensor(out=st_s[:, 4:5], in0=st_s[:, 1:2], scalar=1.0 / D, in1=st_s[:, 3:4], op0=ALU.mult, op1=ALU.subtract),
             reads=[st_s], writes=[st_s])
        k.op("act", lambda a: a.activation(out=st_s[:, 5:6], in_=st_s[:, 4:5], func=ACT.Ln, bias=LN_EPS, scale=1.0), reads=[st_s], writes=[st_s])
        k.op("act", lambda a: a.activation(out=st_s[:, 5:6], in_=st_s[:, 5:6], func=ACT.Exp, scale=-0.5), reads=[st_s], writes=[st_s])
        k.op("dve", lambda v: v.scalar_tensor_tensor(out=st_s[:, 6:7], in0=st_s[:, 2:3], scalar=-1.0, in1=st_s[:, 5:6], op0=ALU.mult, op1=ALU.mult),
             reads=[st_s], writes=[st_s])
        k.op("act", lambda a: a.activation(out=yo_s[:], in_=ypre_s[:], func=ACT.Identity, bias=st_s[:, 6:7], scale=st_s[:, 5:6]), reads=[ypre_s, st_s], writes=[yo_s])
        k.op("dve", lambda v: v.tensor_tensor(out=yo_s[:], in0=yo_s[:], in1=lng_bc[0:P16, :], op=ALU.mult), reads=[yo_s, lng_bc], writes=[yo_s])
        k.op("dve", lambda v: v.tensor_tensor(out=yo_s[:], in0=yo_s[:], in1=lnb_bc[0:P16, :], op=ALU.add), reads=[yo_s, lnb_bc], writes=[yo_s])
        k.dma("sp", ys_d[:, :], yo_s[:], reads=[yo_s], semkey="st_smisc")
        k.barrier()
        k3s.close()

    k.barrier()
    k2.close()
    k1s.close()
    identb = k.sb("identb", [128, 128], BF16); Ub = k.sb("Ub", [128, 128], BF16)
    k.op("pool", lambda g: g.tensor_copy(out=identb[:], in_=ident[:]), reads=[ident], writes=[identb])
    k.op("pool", lambda g: g.tensor_copy(out=Ub[:], in_=U[:]), reads=[U], writes=[Ub])
    Xt = [k.sb("Xt%d" % i, [128, D]) for i in range(2)]
    Xb = k.sb("Xb", [128, D], BF16)
    hT = k.sb("hT", [128, 8, 128], BF16)
    pre = k.sb("pre", [128, 12, 131])
    k.op("pool", lambda g: g.memset(pre[:], 0.0), writes=[pre])
    cm = [k.sb("cm%d" % i, [128, 12, 128]) for i in range(2)]
    qkvs = cm[0]
    qkb2 = [k.sb("qkb%d" % i, [128, 12, 128], BF16) for i in range(2)]
    sqb = k.sb("sqb", [128, 8, 128], BF16)
    lnss = k.sb("lnss", [128, 8, 128])
    rn = lnss
    zAs = k.sb("zAs", [128, 512]); zBs2 = [k.sb("zBs%d" % i, [128, 512]) for i in range(2)]
    qrb2 = [k.sb("qrb%d" % i, [128, 512], BF16) for i in range(2)]; kr2 = [k.sb("kr%d" % i, [128, 128]) for i in range(2)]
    rt = [k.sb("rt%d" % i, [128, 8, 32]) for i in range(4)]
    vB = [k.sb("vB%d" % i, [128, 128], BF16) for i in range(3)]
    vB32 = k.sb("vB32", [128, 128])
    kTb = [k.sb("kTb%d" % i, [128, 128], BF16) for i in range(2)]
    k.op("pool", lambda g: g.memset(kTb[1][:], 0.0), writes=[kTb[1]])
    k.op("pool", lambda g: g.memset(vB[2][:], 0.0), writes=[vB[2]])
    qTb = k.sb("qTb", [128, 4, 128], BF16)
    bd = k.sb("bd", [128, 8]); bdt = k.sb("bdt", [128, 8])
    beta2 = [k.sb("beta%d" % i, [128, 4]) for i in range(2)]; negbeta2 = [k.sb("negbeta%d" % i, [128, 4]) for i in range(2)]
    gg2 = [k.sb("gg%d" % i, [128, 4]) for i in range(2)]
    gsp = k.sb("gsp", [128, 8], BF16); gtmp = k.sb("gtmp", [128, 4])
    gUh = k.sb("gUh", [128, 4, 128], BF16); gUl = k.sb("gUl", [128, 4, 128], BF16)
    Gc = k.sb("Gc", [128, 4]); negGc = k.sb("negGc", [128, 4]); eG = k.sb("eG", [128, 4]); beG = k.sb("beG", [128, 4])
    edG = k.sb("edG", [128, 4]); egl = k.sb("egl", [128, 4]); dG = k.sb("dG", [128, 4]); Glb = k.sb("Glb", [128, 4])
    tmp1 = k.sb("tmp1", [128, 4, 128]); tmp2 = k.sb("tmp2", [128, 4, 128])
    E1 = tmp1; E2 = tmp2
    eGbc = k.sb("eGbc", [128, 4, 128]); qdT = k.sb("qdT", [128, 4, 128], BF16)
    AqkT = k.sb("AqkT", [128, 4, 128], BF16)
    Y0f = k.sb("Y0f", [128, 4, 128])
    XRh = [[k.sb("XRh%d_%d" % (g_, i), [128, 2, 256], BF16) for i in range(2)] for g_ in range(2)]
    XRl = [[k.sb("XRl%d_%d" % (g_, i), [128, 2, 256], BF16) for i in range(2)] for g_ in range(2)]
    Yh = [[k.sb("Yh%d_%d" % (g_, i), [128, 2, 128], BF16) for i in range(2)] for g_ in range(2)]
    Yl = [[k.sb("Yl%d_%d" % (g_, i), [128, 2, 128], BF16) for i in range(2)] for g_ in range(2)]
    Rf = [k.sb("Rf%d" % g_, [128, 2, 128]) for g_ in range(2)]
    kbt = k.sb("kbt", [128, 4, 128], BF16); kdt = k.sb("kdt", [128, 4, 128], BF16); bvt = k.sb("bvt", [128, 4, 128], BF16)
    wT = [k.sb("wT%d" % g_, [128, 2, 128], BF16) for g_ in range(2)]
    ut = [k.sb("ut%d" % g_, [128, 2, 128]) for g_ in range(2)]
    uub = [k.sb("uub%d" % g_, [128, 2, 128], BF16) for g_ in range(2)]
    S = [k.sb("S%d" % g_, [128, 2, 128]) for g_ in range(2)]
    Sb = [k.sb("Sb%d" % g_, [128, 2, 128], BF16) for g_ in range(2)]
    for g_ in range(2):
        k.op("pool", lambda g, g_=g_: g.memset(S[g_][:], 0.0), writes=[S[g_]])
        k.op("pool", lambda g, g_=g_: g.memset(Sb[g_][:], 0.0), writes=[Sb[g_]])
    oss = [k.sb("oss%d" % g_, [128, 2]) for g_ in range(2)]; orr = [k.sb("orr%d" % g_, [128, 2]) for g_ in range(2)]
    nz = k.sb("nz", [128, 4, 128])
    mixb = k.sb("mixb", [128, D], BF16); obt = k.sb("obt", [128, 512])
    ojunk = [tmp1[:, 0, :], tmp2[:, 0, :]]
    mixT = k.sb("mixT", [128, 8, 128], BF16)
    SC = k.sb("SC", [128, 8, 256]); Pb = k.sb("Pb", [128, 8, 256], BF16)
    PTb = k.sb("PTb", [128, 16, 128], BF16)
    mx = k.sb("mx", [128, 8]); negm = k.sb("negm", [128, 8]); rs = k.sb("rs", [128, 8]); es_ = k.sb("es_", [128, 8])
    rden = k.sb("rden", [128, 8])
    SCf = SC[:].rearrange("p a b -> p (a b)")
    ypre = SCf[:, 0:D]
    st = k.sb("st", [128, 8]); negpad = k.sb("negpad", [128, 1])
    Yt = [SCf[:, D:2 * D]]

    def v3(ap, a):
        return ap.rearrange("p (a b) -> p a b", a=a)

    def pbf(pb):
        return pb[:, :].bitcast(BF16)

    def mm(out, lhsT, rhs, start, stop, reads, writes):
        ncol = int(np.prod(out.shape[1:]))
        k.op("pe", lambda p: p.matmul(out, lhsT=lhsT, rhs=rhs, start=start, stop=stop), reads=reads, writes=writes,
             cost=0.14 + ncol / 1200.0)

    def tr(out, in_, idn, reads, writes):
        k.op("pe", lambda p: p.transpose(out=out, in_=in_, identity=idn), reads=reads + [idn], writes=writes, cost=0.25)

    k.barrier()
    k.pe_inorder = True
    krb = k.sb("krb", [128, 128], BF16)
    gdone = {}

    def P1a(n):
        f_ = full(n); sp_ = swaprep(n)
        qkb = qkb2[n % 2]; beta = beta2[n % 2]; negbeta = negbeta2[n % 2]; gg = gg2[n % 2]
        c0 = 0 if f_ else 4
        X = Xt[n % 2]
        cs_t = cst[n % 2]
        zBs = zBs2[n % 2]; qrb = qrb2[n % 2]; kr = kr2[n % 2]
        vcur = vB[n % 3]
        k.dma("sp", X[:], x_d[n * 128:(n + 1) * 128, :], writes=[X], semkey="ld_" + X.name)
        if f_ or sp_:
            k.dma("sp", cs_t[:, 0, :], cosk_d[n * 128:(n + 1) * 128, :], writes=[cs_t], semkey="ld_" + cs_t.name)
            k.dma("sp", cs_t[:, 1, :], sink_d[n * 128:(n + 1) * 128, :], writes=[cs_t], semkey="ld_" + cs_t.name, nowaw=True)
        yield
        k.op("pool", lambda g: g.tensor_copy(out=Xb[:], in_=X[:]), reads=[X], writes=[Xb])
        P3b = pbf(PB[2])
        yield
        for kc in range(8):
            tr(P3b[:, kc * 128:(kc + 1) * 128], Xb[:, kc * 128:(kc + 1) * 128], identb[:], [Xb], [PB[2]])
        yield
        for kc in range(8):
            k.op("act" if kc % 2 == 0 else "dve",
                 (lambda a, kc=kc: a.activation(out=hT[:, kc, :], in_=P3b[:, kc * 128:(kc + 1) * 128], func=ACT.Identity,
                                                bias=mod_fm[:, kc:kc + 1], scale=mod_fm[:, 8 + kc:9 + kc])) if kc % 2 == 0 else
                 (lambda v, kc=kc: v.tensor_scalar(out=hT[:, kc, :], in0=P3b[:, kc * 128:(kc + 1) * 128], scalar1=mod_fm[:, 8 + kc:9 + kc],
                                                   scalar2=mod_fm[:, kc:kc + 1], op0=ALU.mult, op1=ALU.add)),
                 reads=[PB[2], mod_fm], writes=[hT])
            if kc % 2 == 1:
                yield
        pc = pre
        k.op("pool", lambda g: g.tensor_copy(out=pc[:, :, 0:3], in_=pc[:, :, 128:131]), reads=[pc], writes=[pc])
        yield
        for j in range(3):
            if j == 0 and not (f_ or sp_):
                continue
            pb = [PB[3], PB[2], PB[3]][j]
            for c4 in range(4):
                c = j * 4 + c4
                for kc in range(8):
                    mm(pb[:, c4 * 128:(c4 + 1) * 128], Wf[:, kc, c * 128:(c + 1) * 128], hT[:, kc, :], kc == 0, kc == 7, [Wf, hT], [pb])
                yield
            if j != 1:
                k.op("act", lambda a, j=j, pb=pb: a.activation(out=pc[:, j * 4:(j + 1) * 4, 3:131], in_=v3(pb[:, :], 4), func=ACT.Identity,
                                                               scale=valid_bc[:, n:n + 1]), reads=[pb, valid_bc], writes=[pc])
            else:
                k.op("dve", lambda v, j=j, pb=pb: v.tensor_scalar(out=pc[:, j * 4:(j + 1) * 4, 3:131], in0=v3(pb[:, :], 4), scalar1=valid_bc[:, n:n + 1],
                                                                  scalar2=None, op0=ALU.mult), reads=[pb, valid_bc], writes=[pc])
            yield
        nch = 12 - c0
        k.op("dve", lambda v: v.tensor_tensor(out=cm[0][:, c0:12, :], in0=pc[:, c0:12, 0:128], in1=bc(cwT[:, c0:12].rearrange("p (a b) -> p a b", b=1), [128, nch, 128]), op=ALU.mult),
             reads=[pc, cwT], writes=[cm[0]], cost=0.13 * nch)
        yield
        for j in range(1, 4):
            k.op("pool", lambda g, j=j: g.tensor_tensor(out=cm[1][:, c0:12, :], in0=pc[:, c0:12, j:j + 128], in1=bc(cwT[:, j * 12 + c0:(j + 1) * 12].rearrange("p (a b) -> p a b", b=1), [128, nch, 128]), op=ALU.mult),
                 reads=[pc, cwT], writes=[cm[1]], cost=0.25 * nch)
            yield
            k.op("dve", lambda v: v.tensor_tensor(out=cm[0][:, c0:12, :], in0=cm[0][:, c0:12, :], in1=cm[1][:, c0:12, :], op=ALU.add), reads=[cm[0], cm[1]], writes=[cm[0]],
                 cost=0.13 * nch)
            yield
        k.op("act", lambda a: a.activation(out=qkvs[:, c0:12, :], in_=cm[0][:, c0:12, :], func=ACT.Silu), reads=[cm[0]], writes=[qkvs], cost=0.12 * nch)
        yield
        offs = [(0, 512), (512, 512), (1024, 512), (1536, 264)]

        def tproj(j, pb):
            o, w_ = offs[j]
            for kc in range(8):
                mm(pb[:, 0:w_], hT[:, kc, :], Wt[:, kc, o:o + w_], kc == 0, kc == 7, [hT, Wt], [pb])
        PzA, PqB = PB[2], PB[3]
        if f_:
            tproj(0, PzA)
            yield
            k.op("act", lambda a: a.activation(out=zAs[:], in_=PzA[:, :], func=ACT.Silu), reads=[PzA], writes=[zAs])
            yield
            tproj(1, PqB)
            yield
            q3 = v3(PqB[:, :], 8)
            qr3 = v3(qrb[:], 8)
            cq = bc(cs_t[:, 0:1, :], [128, 8, 32]); sq_ = bc(cs_t[:, 1:2, :], [128, 8, 32])
            k.op("dve", lambda v: v.tensor_tensor(out=rt[0][:], in0=q3[:, :, 0:32], in1=cq, op=ALU.mult), reads=[PqB, cs_t], writes=[rt[0]])
            k.op("dve", lambda v: v.tensor_tensor(out=rt[1][:], in0=q3[:, :, 32:64], in1=sq_, op=ALU.mult), reads=[PqB, cs_t], writes=[rt[1]])
            yield
            k.op("dve", lambda v: v.tensor_tensor(out=rt[2][:], in0=q3[:, :, 32:64], in1=cq, op=ALU.mult), reads=[PqB, cs_t], writes=[rt[2]])
            k.op("dve", lambda v: v.tensor_tensor(out=rt[3][:], in0=q3[:, :, 0:32], in1=sq_, op=ALU.mult), reads=[PqB, cs_t], writes=[rt[3]])
            yield
            k.op("pool", lambda g: g.tensor_tensor(out=qr3[:, :, 0:32], in0=rt[0][:], in1=rt[1][:], op=ALU.subtract), reads=[rt[0], rt[1]], writes=[qrb])
            k.op("pool", lambda g: g.tensor_tensor(out=qr3[:, :, 32:64], in0=rt[2][:], in1=rt[3][:], op=ALU.add), reads=[rt[2], rt[3]], writes=[qrb])
            yield
        PzB, Pk = PB[2], PB[3]
        if f_:
            tproj(2, PzB)
            yield
            k.op("act", lambda a: a.activation(out=zBs[:], in_=PzB[:, :], func=ACT.Silu), reads=[PzB], writes=[zBs])
            yield
        tproj(3, Pk)
        yield
        if f_ or sp_:
            k3 = v3(Pk[:, 0:128], 2)
            kr3 = v3(kr[:], 2)
            ck = bc(cs_t[:, 0:1, :], [128, 2, 32]); sk_ = bc(cs_t[:, 1:2, :], [128, 2, 32])
            k.op("dve", lambda v: v.tensor_tensor(out=rt[0][:, 0:2, :], in0=k3[:, :, 0:32], in1=ck, op=ALU.mult), reads=[Pk, cs_t], writes=[rt[0]])
            k.op("dve", lambda v: v.tensor_tensor(out=rt[1][:, 0:2, :], in0=k3[:, :, 32:64], in1=sk_, op=ALU.mult), reads=[Pk, cs_t], writes=[rt[1]])
            yield
            k.op("dve", lambda v: v.tensor_tensor(out=rt[2][:, 0:2, :], in0=k3[:, :, 32:64], in1=ck, op=ALU.mult), reads=[Pk, cs_t], writes=[rt[2]])
            k.op("dve", lambda v: v.tensor_tensor(out=rt[3][:, 0:2, :], in0=k3[:, :, 0:32], in1=sk_, op=ALU.mult), reads=[Pk, cs_t], writes=[rt[3]])
            yield
            k.op("pool", lambda g: g.tensor_tensor(out=kr3[:, :, 0:32], in0=rt[0][:, 0:2, :], in1=rt[1][:, 0:2, :], op=ALU.subtract), reads=[rt[0], rt[1]], writes=[kr])
            k.op("pool", lambda g: g.tensor_tensor(out=kr3[:, :, 32:64], in0=rt[2][:, 0:2, :], in1=rt[3][:, 0:2, :], op=ALU.add), reads=[rt[2], rt[3]], writes=[kr])
            yield
            k.op("dve", lambda v: v.tensor_copy(out=vcur[:], in_=Pk[:, 128:256]), reads=[Pk], writes=[vcur])
            if n == NT - 1:
                k.op("dve", lambda v: v.tensor_copy(out=vB32[:], in_=Pk[:, 128:256]), reads=[Pk], writes=[vB32])
        k.op("dve", lambda v: v.tensor_copy(out=bd[:], in_=Pk[:, 256:264]), reads=[Pk], writes=[bd])
        yield
        j0 = 0 if f_ else 1
        k.op("pool", lambda g: g.tensor_tensor(out=sqb[:, c0:8, :], in0=qkvs[:, c0:8, :], in1=qkvs[:, c0:8, :], op=ALU.mult), reads=[qkvs], writes=[sqb],
             cost=0.25 * (8 - c0))
        yield
        for j in range(j0, 2):
            mm(PB[2 + j][:, :], onesb[:], sqb[:, j * 4:(j + 1) * 4, :].rearrange("p a b -> p (a b)"), True, True, [onesb, sqb], [PB[2 + j]])
        yield
        for j in range(j0, 2):
            k.op("act", lambda a, j=j: a.activation(out=lnss[:, j * 4:(j + 1) * 4, :].rearrange("p a b -> p (a b)"), in_=PB[2 + j][:, :],
                                                    func=ACT.Ln, bias=L2_EPS, scale=1.0), reads=[PB[2 + j]], writes=[lnss])
        yield
        if f_:
            k.op("act", lambda a: a.activation(out=rn[:, 0:4, :], in_=lnss[:, 0:4, :], func=ACT.Exp, bias=float(-0.5 * np.log(128.0)), scale=-0.5), reads=[lnss], writes=[rn])
        k.op("act", lambda a: a.activation(out=rn[:, 4:8, :], in_=lnss[:, 4:8, :], func=ACT.Exp, scale=-0.5), reads=[lnss], writes=[rn])
        yield
        k.op("dve", lambda v: v.tensor_tensor(out=qkvs[:, c0:8, :], in0=qkvs[:, c0:8, :], in1=rn[:, c0:8, :], op=ALU.mult), reads=[qkvs, rn], writes=[qkvs],
             cost=0.13 * (8 - c0))
        yield
        k.op("pool", lambda g: g.tensor_copy(out=qkb[:, c0:12, :], in_=qkvs[:, c0:12, :]), reads=[qkvs], writes=[qkb], cost=0.25 * nch)
        yield
        k.op("act", lambda a: a.activation(out=bdt[:, 0:4], in_=bd[:, 0:4], func=ACT.Exp, scale=-1.0), reads=[bd], writes=[bdt])
        k.op("dve", lambda v: v.tensor_scalar(out=bdt[:, 0:4], in0=bdt[:, 0:4], scalar1=1.0, scalar2=None, op0=ALU.add), reads=[bdt], writes=[bdt])
        k.op("dve", lambda v: v.reciprocal(out=beta[:], in_=bdt[:, 0:4]), reads=[bdt], writes=[beta])
        k.op("dve", lambda v: v.tensor_scalar(out=beta[:], in0=beta[:], scalar1=valid_bc[:, n:n + 1], scalar2=None, op0=ALU.mult), reads=[beta, valid_bc], writes=[beta])
        k.op("dve", lambda v: v.tensor_scalar(out=negbeta[:], in0=beta[:], scalar1=-1.0, scalar2=None, op0=ALU.mult), reads=[beta], writes=[negbeta])
        yield
        k.op("dve", lambda v: v.tensor_tensor(out=bd[:, 4:8], in0=bd[:, 4:8], in1=dtb_bc[:], op=ALU.add), reads=[bd, dtb_bc], writes=[bd])
        k.op("act", lambda a: a.activation(out=bdt[:, 4:8], in_=bd[:, 4:8], func=ACT.Exp), reads=[bd], writes=[bdt])
        k.op("act", lambda a: a.activation(out=bdt[:, 4:8], in_=bdt[:, 4:8], func=ACT.Ln, bias=1.0, scale=1.0), reads=[bdt], writes=[bdt])
        k.op("dve", lambda v: v.tensor_tensor(out=gg[:], in0=bdt[:, 4:8], in1=ea[:], op=ALU.mult), reads=[bdt, ea], writes=[gg])
        yield

    def P1b(n, pl=False):
        f_ = full(n)
        qkb = qkb2[n % 2]; beta = beta2[n % 2]; negbeta = negbeta2[n % 2]; gg = gg2[n % 2]
        Bsm, Bx0, Bkv = (PB[0], PB[0], PB[1]) if pl else (PB[1], PB[0], PB[1])
        yield
        if f_:
            k.op("pool", lambda g: g.tensor_tensor(out=nz[:], in0=v3(zAs[:], 4), in1=bc(norma_bc[:].rearrange("p (a b) -> p a b", a=1), [128, 4, 128]), op=ALU.mult),
                 reads=[zAs, norma_bc], writes=[nz])
        yield
        k.op("dve", lambda v: v.tensor_copy(out=gsp[:, 0:4], in_=gg[:]), reads=[gg], writes=[gsp])
        yield
        k.op("dve", lambda v: v.tensor_tensor(out=gtmp[:], in0=gg[:], in1=gsp[:, 0:4], op=ALU.subtract), reads=[gg, gsp], writes=[gtmp])
        yield
        k.op("dve", lambda v: v.tensor_copy(out=gsp[:, 4:8], in_=gtmp[:]), reads=[gtmp], writes=[gsp])
        Ub4 = bc(Ub[:].rearrange("p (a b) -> p a b", a=1), [128, 4, 128])
        yield
        k.op("pool", lambda g: g.tensor_tensor(out=gUh[:], in0=Ub4, in1=bc(gsp[:, 0:4].rearrange("p (a b) -> p a b", b=1), [128, 4, 128]), op=ALU.mult),
             reads=[Ub, gsp], writes=[gUh])
        yield
        k.op("pool", lambda g: g.tensor_tensor(out=gUl[:], in0=Ub4, in1=bc(gsp[:, 4:8].rearrange("p (a b) -> p a b", b=1), [128, 4, 128]), op=ALU.mult),
             reads=[Ub, gsp], writes=[gUl])
        PG, PK_, PQ = (PB[1], PB[0], PB[5]) if pl else (PB[3], PB[4], PB[5])
        yield
        mm(Bsm[:, 0:8], Ub[:], gsp[:], True, True, [Ub, gsp], [Bsm])
        yield
        mm(Bsm[:, 8:16], onesb[:], gsp[:], True, True, [onesb, gsp], [Bsm])
        yield
        mm(PG[:, :], onesb[:], gUh[:].rearrange("p a b -> p (a b)"), True, False, [onesb, gUh], [PG])
        yield
        mm(PG[:, :], onesb[:], gUl[:].rearrange("p a b -> p (a b)"), False, True, [onesb, gUl], [PG])
        yield
        k.op("dve", lambda v: v.tensor_copy(out=gtmp[:], in_=Bsm[:, 0:4]), reads=[Bsm], writes=[gtmp])
        yield
        k.op("dve", lambda v: v.tensor_tensor(out=Gc[:], in0=gtmp[:], in1=Bsm[:, 4:8], op=ALU.add), reads=[gtmp, Bsm], writes=[Gc])
        yield
        k.op("dve", lambda v: v.tensor_copy(out=gtmp[:], in_=Bsm[:, 8:12]), reads=[Bsm], writes=[gtmp])
        yield
        k.op("dve", lambda v: v.tensor_tensor(out=Glb[:], in0=gtmp[:], in1=Bsm[:, 12:16], op=ALU.add), reads=[gtmp, Bsm], writes=[Glb])
        yield
        k.op("dve", lambda v: v.tensor_scalar(out=negGc[:], in0=Gc[:], scalar1=-1.0, scalar2=None, op0=ALU.mult), reads=[Gc], writes=[negGc])
        yield
        k.op("dve", lambda v: v.tensor_tensor(out=dG[:], in0=Glb[:], in1=Gc[:], op=ALU.subtract), reads=[Glb, Gc], writes=[dG])
        yield
        k.op("act", lambda a: a.activation(out=eG[:], in_=Gc[:], func=ACT.Exp), reads=[Gc], writes=[eG])
        yield
        k.op("act", lambda a: a.activation(out=edG[:], in_=dG[:], func=ACT.Exp), reads=[dG], writes=[edG])
        yield
        yield
        k.op("dve", lambda v: v.tensor_tensor(out=beG[:], in0=eG[:], in1=beta[:], op=ALU.mult), reads=[eG, beta], writes=[beG])
        PG3 = v3(PG[:, :], 4)
        yield
        k.op("dve", lambda v: v.scalar_tensor_tensor(out=tmp1[:], in0=PG3, scalar=-1.0, in1=bc(negA[:].rearrange("p (a b) -> p a b", a=1), [128, 4, 128]),
                                                     op0=ALU.mult, op1=ALU.add), reads=[PG, negA], writes=[tmp1])
        yield
        if f_:
            k.op("dve", lambda v: v.tensor_tensor(out=tmp2[:], in0=PG3, in1=bc(negB[:].rearrange("p (a b) -> p a b", a=1), [128, 4, 128]), op=ALU.add),
                 reads=[PG, negB], writes=[tmp2])
            yield
            k.op("act", lambda a: a.activation(out=eGbc[:], in_=PG3, func=ACT.Exp), reads=[PG], writes=[eGbc])
        yield
        for h in range(4):
            k.op("act", lambda a, h=h: a.activation(out=E1[:, h, :], in_=tmp1[:, h, :], func=ACT.Exp, bias=Gc[:, h:h + 1], scale=1.0), reads=[tmp1, Gc], writes=[E1])
            if f_:
                k.op("act", lambda a, h=h: a.activation(out=E2[:, h, :], in_=tmp2[:, h, :], func=ACT.Exp, bias=negGc[:, h:h + 1], scale=1.0), reads=[tmp2, negGc], writes=[E2])
        yield
        for h in range(4):
            mm(PK_[:, h * 128:(h + 1) * 128], qkb[:, 4 + h, :], qkb[:, 4 + h, :], True, True, [qkb], [PK_])
        yield
        for h in range(4):
            if f_:
                mm(PQ[:, h * 128:(h + 1) * 128], qkb[:, 4 + h, :], qkb[:, h, :], True, True, [qkb], [PQ])
        yield
        for h in range(4):
            k.op("dve", lambda v, h=h: v.scalar_tensor_tensor(out=Y0f[:, h, :], in0=PK_[:, h * 128:(h + 1) * 128], scalar=negbeta[:, h:h + 1],
                                                               in1=E1[:, h, :], op0=ALU.mult, op1=ALU.mult), reads=[PK_, negbeta, E1], writes=[Y0f])
        yield
        if f_:
            k.op("dve", lambda v: v.tensor_tensor(out=AqkT[:], in0=v3(PQ[:, :], 4), in1=E2[:], op=ALU.mult), reads=[PQ, E2], writes=[AqkT])
            yield
            k.op("pool", lambda g: g.tensor_tensor(out=qdT[:], in0=qkvs[:, 0:4, :], in1=eGbc[:], op=ALU.mult), reads=[qkvs, eGbc], writes=[qdT])
        if pl:
            while not (gdone.get((n - 1, 0)) and gdone.get((n - 1, 1))):
                yield "wait"
        k.op("act", lambda a: a.activation(out=egl[:], in_=Glb[:], func=ACT.Exp), reads=[Glb], writes=[egl])
        P5b = pbf(Bx0)
        id2 = bc(ident[:].rearrange("p (a b) -> p a b", a=1), [128, 2, 128])
        for g_ in range(2):
            yield
            k.op("act", lambda a, g_=g_: a.copy(out=Yh[g_][0][:], in_=Y0f[:, 2 * g_:2 * g_ + 2, :]), reads=[Y0f], writes=[Yh[g_][0]])
            yield
            k.op("pool", lambda g, g_=g_: g.tensor_tensor(out=Yl[g_][0][:], in0=Y0f[:, 2 * g_:2 * g_ + 2, :], in1=Yh[g_][0][:], op=ALU.subtract),
                 reads=[Y0f, Yh[g_][0]], writes=[Yl[g_][0]])
            yield
            for hh in range(2):
                h = 2 * g_ + hh
                tr(P5b[:, h * 128:(h + 1) * 128], Yh[g_][0][:, hh, :], identb[:], [Yh[g_][0]], [Bx0])
                tr(P5b[:, (4 + h) * 128:(5 + h) * 128], Yl[g_][0][:, hh, :], identb[:], [Yl[g_][0]], [Bx0])
        for g_ in range(2):
            yield
            k.op("act", lambda a, g_=g_: a.copy(out=XRh[g_][0][:, :, 0:128], in_=v3(P5b[:, g_ * 256:(g_ + 1) * 256], 2)), reads=[Bx0], writes=[XRh[g_][0]])
            yield
            k.op("dve", lambda v, g_=g_: v.tensor_copy(out=XRl[g_][0][:, :, 0:128], in_=v3(P5b[:, 512 + g_ * 256:512 + (g_ + 1) * 256], 2)), reads=[Bx0], writes=[XRl[g_][0]])
            yield
            k.op("pool", lambda g, g_=g_: g.tensor_copy(out=Rf[g_][:], in_=id2), reads=[ident], writes=[Rf[g_]])
            k.op("pool", lambda g, g_=g_: g.tensor_copy(out=XRh[g_][0][:, :, 128:256], in_=id2), reads=[ident], writes=[XRh[g_][0]])
            k.op("pool", lambda g, g_=g_: g.memset(XRl[g_][0][:, :, 128:256], 0.0), writes=[XRl[g_][0]])
        P1b_ = pbf(Bkv)
        yield
        for h in range(4):
            tr(P1b_[:, h * 128:(h + 1) * 128], qkb[:, 4 + h, :], identb[:], [qkb], [Bkv])
        yield
        for h in range(4):
            tr(P1b_[:, (4 + h) * 128:(5 + h) * 128], qkb[:, 8 + h, :], identb[:], [qkb], [Bkv])
        yield
        for h in range(4):
            k.op("dve", lambda v, h=h: v.tensor_scalar(out=kbt[:, h, :], in0=P1b_[:, h * 128:(h + 1) * 128], scalar1=beG[:, h:h + 1], scalar2=None, op0=ALU.mult),
                 reads=[Bkv, beG], writes=[kbt])
            k.op("act", lambda a, h=h: a.activation(out=kdt[:, h, :], in_=P1b_[:, h * 128:(h + 1) * 128], func=ACT.Identity, scale=edG[:, h:h + 1]),
                 reads=[Bkv, edG], writes=[kdt])
            k.op("act", lambda a, h=h: a.activation(out=bvt[:, h, :], in_=P1b_[:, (4 + h) * 128:(5 + h) * 128], func=ACT.Identity, scale=beta[:, h:h + 1]),
                 reads=[Bkv, beta], writes=[bvt])
        yield

    def G(n, g_):
        BA, BB = PB[4 + g_], PB[6 + g_]
        for lev in range(7):
            cur = lev % 2; nxt = (lev + 1) % 2
            xh, xl, yh, yl = XRh[g_][cur], XRl[g_][cur], Yh[g_][cur], Yl[g_][cur]
            xhn, xln, yhn, yln = XRh[g_][nxt], XRl[g_][nxt], Yh[g_][nxt], Yl[g_][nxt]
            last = (lev == 6)
            yield
            for hh in range(2):
                if not last:
                    o_ = BA[:, hh * 256:(hh + 1) * 256]
                    r_h = xh[:, hh, :]; r_l = xl[:, hh, :]
                else:
                    o_ = BA[:, hh * 256 + 128:(hh + 1) * 256]
                    r_h = xh[:, hh, 128:256]; r_l = xl[:, hh, 128:256]
                mm(o_, yh[:, hh, :], r_h, True, False, [yh, xh], [BA])
                mm(o_, yh[:, hh, :], r_l, False, False, [yh, xl], [BA])
                mm(o_, yl[:, hh, :], r_h, False, True, [yl, xh], [BA])
            if not last:
                yield
                for hh in range(2):
                    o_ = BB[:, hh * 128:(hh + 1) * 128]
                    mm(o_, xh[:, hh, 0:128], yh[:, hh, :], True, False, [xh, yh], [BB])
                    mm(o_, xh[:, hh, 0:128], yl[:, hh, :], False, False, [xh, yl], [BB])
                    mm(o_, xl[:, hh, 0:128], yh[:, hh, :], False, True, [xl, yh], [BB])
            yield
            k.op("dve", lambda v: v.tensor_tensor(out=Rf[g_][:], in0=Rf[g_][:], in1=v3(BA[:, :], 2)[:, :, 128:256], op=ALU.add), reads=[Rf[g_], BA], writes=[Rf[g_]],
                 cost=0.3)
            yield
            k.op("act", lambda a: a.copy(out=xhn[:, :, 128:256], in_=Rf[g_][:]), reads=[Rf[g_]], writes=[xhn], cost=0.3)
            if not last:
                yield
                k.op("pool", lambda g: g.tensor_tensor(out=xln[:, :, 128:256], in0=Rf[g_][:], in1=xhn[:, :, 128:256], op=ALU.subtract), reads=[Rf[g_], xhn], writes=[xln],
                     cost=0.6)
                yield
                k.op("act", lambda a: a.copy(out=xhn[:, :, 0:128], in_=v3(BA[:, :], 2)[:, :, 0:128]), reads=[BA], writes=[xhn], cost=0.3)
                yield
                k.op("dve", lambda v: v.tensor_tensor(out=xln[:, :, 0:128], in0=v3(BA[:, :], 2)[:, :, 0:128], in1=xhn[:, :, 0:128], op=ALU.subtract),
                     reads=[BA, xhn], writes=[xln], cost=0.3)
                yield
                k.op("act", lambda a: a.copy(out=yhn[:], in_=v3(BB[:, 0:256], 2)), reads=[BB], writes=[yhn], cost=0.3)
                yield
                k.op("dve", lambda v: v.tensor_tensor(out=yln[:], in0=v3(BB[:, 0:256], 2), in1=yhn[:], op=ALU.subtract), reads=[BB, yhn], writes=[yln], cost=0.3)
        Rh = XRh[g_][1]
        yield
        for hh in range(2):
            h = 2 * g_ + hh
            mm(BA[:, hh * 128:(hh + 1) * 128], kbt[:, h, :], Rh[:, hh, 128:256], True, True, [kbt, Rh], [BA])
        for hh in range(2):
            h = 2 * g_ + hh
            mm(BB[:, hh * 128:(hh + 1) * 128], Rh[:, hh, 128:256], bvt[:, h, :], True, True, [Rh, bvt], [BB])
        yield
        k.op("act", lambda a: a.copy(out=wT[g_][:], in_=v3(BA[:, 0:256], 2)), reads=[BA], writes=[wT[g_]], cost=0.3)
        yield
        k.op("dve", lambda v: v.tensor_copy(out=ut[g_][:], in_=v3(BB[:, 0:256], 2)), reads=[BB], writes=[ut[g_]], cost=0.3)
        yield
        for hh in range(2):
            mm(BA[:, 256 + hh * 128:256 + (hh + 1) * 128], wT[g_][:, hh, :], Sb[g_][:, hh, :], True, True, [wT[g_], Sb[g_]], [BA])
        yield
        k.op("dve", lambda v: v.tensor_tensor(out=uub[g_][:], in0=ut[g_][:], in1=v3(BA[:, 256:512], 2), op=ALU.subtract), reads=[ut[g_], BA], writes=[uub[g_]], cost=0.3)
        yield
        for hh in range(2):
            h = 2 * g_ + hh
            if full(n):
                mm(BB[:, 256 + hh * 128:256 + (hh + 1) * 128], qdT[:, h, :], Sb[g_][:, hh, :], True, False, [qdT, Sb[g_]], [BB])
                mm(BB[:, 256 + hh * 128:256 + (hh + 1) * 128], AqkT[:, h, :], uub[g_][:, hh, :], False, True, [AqkT, uub[g_]], [BB])
        for hh in range(2):
            h = 2 * g_ + hh
            mm(BA[:, hh * 128:(hh + 1) * 128], kdt[:, h, :], uub[g_][:, hh, :], True, True, [kdt, uub[g_]], [BA])
        yield
        for hh in range(2):
            h = 2 * g_ + hh
            k.op("dve", lambda v, hh=hh, h=h: v.scalar_tensor_tensor(out=S[g_][:, hh, :], in0=S[g_][:, hh, :], scalar=egl[:, h:h + 1], in1=BA[:, hh * 128:(hh + 1) * 128],
                                                                   op0=ALU.mult, op1=ALU.add), reads=[S[g_], egl, BA], writes=[S[g_]], cost=0.2)
        yield
        k.op("pool", lambda g: g.tensor_copy(out=Sb[g_][:], in_=S[g_][:]), reads=[S[g_]], writes=[Sb[g_]], cost=0.5)
        yield
        gdone[(n, g_)] = True
        if not full(n):
            return
        for hh in range(2):
            k.op("act", lambda a, hh=hh: a.activation(out=ojunk[g_][:], in_=BB[:, 256 + hh * 128:256 + (hh + 1) * 128], func=ACT.Square, accum_out=oss[g_][:, hh:hh + 1]),
                 reads=[BB], writes=[ojunk[g_], oss[g_]], cost=0.25)
        yield
        k.op("act", lambda a: a.activation(out=orr[g_][:], in_=oss[g_][:], func=ACT.Ln, bias=RMS_EPS, scale=1.0 / 128.0), reads=[oss[g_]], writes=[orr[g_]], cost=0.2)
        k.op("act", lambda a: a.activation(out=orr[g_][:], in_=orr[g_][:], func=ACT.Exp, scale=-0.5), reads=[orr[g_]], writes=[orr[g_]], cost=0.2)
        yield
        for hh in range(2):
            h = 2 * g_ + hh
            k.op("dve", lambda v, hh=hh, h=h: v.scalar_tensor_tensor(out=mixb[:, h * 128:(h + 1) * 128], in0=BB[:, 256 + hh * 128:256 + (hh + 1) * 128], scalar=orr[g_][:, hh:hh + 1],
                                                                   in1=nz[:, h, :], op0=ALU.mult, op1=ALU.mult), reads=[BB, orr[g_], nz], writes=[mixb], cost=0.2)
        yield

    def W(n):
        vcur = vB[n % 3]; vprev = vB[(n + 2) % 3]
        zBs = zBs2[n % 2]; qrb = qrb2[n % 2]; kr = kr2[n % 2]
        kcur = kTb[n % 2]; kprev = kTb[(n + 1) % 2]
        if not (full(n) or swaprep(n)):
            return
        if swaprep(n):
            yield
            k.op("pool", lambda g: g.tensor_copy(out=krb[:], in_=kr[:]), reads=[kr], writes=[krb])
            yield
            tr(pbf(PB[0])[:, 512:640], krb[:], identb[:], [krb], [PB[0]])
            yield
            k.op("dve", lambda v: v.tensor_copy(out=kcur[:], in_=pbf(PB[0])[:, 512:640]), reads=[PB[0]], writes=[kcur])
            yield
            return
        P0b = pbf(PB[0])
        yield
        for c in range(4):
            tr(P0b[:, c * 128:(c + 1) * 128], qrb[:, c * 128:(c + 1) * 128], identb[:], [qrb], [PB[0]])
        yield
        k.op("act", lambda a: a.activation(out=qTb[:], in_=v3(P0b[:, 0:512], 4), func=ACT.Identity, scale=0.125), reads=[PB[0]], writes=[qTb])
        yield
        k.op("pool", lambda g: g.tensor_copy(out=krb[:], in_=kr[:]), reads=[kr], writes=[krb])
        yield
        tr(pbf(PB[0])[:, 512:640], krb[:], identb[:], [krb], [PB[0]])
        yield
        k.op("dve", lambda v: v.tensor_copy(out=kcur[:], in_=pbf(PB[0])[:, 512:640]), reads=[PB[0]], writes=[kcur])
        msk = swam
        rnd = 0
        for kv in range(2):
            for cp in range(2):
                pb = PB[1] if rnd % 2 == 0 else PB[0]
                rnd += 1
                yield
                for cc in range(2):
                    c = cp * 2 + cc
                    o = cc * 256
                    mm(pb[:, o:o + 128], qTb[64 * kv:64 * kv + 64, c, :], kprev[64 * kv:64 * kv + 64, :], True, True, [qTb, kprev], [pb])
                    mm(pb[:, o + 128:o + 256], qTb[64 * kv:64 * kv + 64, c, :], kcur[64 * kv:64 * kv + 64, :], True, True, [qTb, kcur], [pb])
                yield
                s0 = kv * 4 + cp * 2
                k.op("dve", lambda v, pb=pb, s0=s0: v.tensor_tensor(out=SC[:, s0:s0 + 2, :], in0=v3(pb[:, :], 2),
                                                                     in1=bc(msk[:].rearrange("p (a b) -> p a b", a=1), [128, 2, 256]), op=ALU.add),
                     reads=[pb, msk], writes=[SC])
        yield
        if n == F0:
            if F0 == 0:
                k.op("dve", lambda v: v.tensor_scalar(out=SC[:, :, 0:128], in0=SC[:, :, 0:128], scalar1=NEG, scalar2=None, op0=ALU.add), reads=[SC], writes=[SC])
            else:
                k.op("dve", lambda v: v.tensor_scalar(out=negpad[:], in0=valid_bc[:, F0 - 1:F0], scalar1=-1.0, scalar2=-NEG, op0=ALU.add, op1=ALU.mult),
                     reads=[valid_bc], writes=[negpad])
                k.op("dve", lambda v: v.tensor_scalar(out=SC[:, :, 0:128], in0=SC[:, :, 0:128], scalar1=negpad[:, 0:1], scalar2=None, op0=ALU.add),
                     reads=[SC, negpad], writes=[SC])
            yield
        k.op("dve", lambda v: v.tensor_reduce(out=mx[:], in_=SC[:], axis=AX.X, op=ALU.max), reads=[SC], writes=[mx])
        yield
        k.op("dve", lambda v: v.tensor_tensor(out=mx[:], in0=mx[:], in1=sinks_bc[:], op=ALU.max), reads=[mx, sinks_bc], writes=[mx])
        yield
        k.op("dve", lambda v: v.tensor_scalar(out=negm[:], in0=mx[:], scalar1=-1.0, scalar2=None, op0=ALU.mult), reads=[mx], writes=[negm])
        yield
        for s_ in range(8):
            k.op("act", lambda a, s_=s_: a.activation(out=Pb[:, s_, :], in_=SC[:, s_, :], func=ACT.Exp, bias=negm[:, s_:s_ + 1], scale=1.0,
                                                      accum_out=rs[:, s_:s_ + 1]), reads=[SC, negm], writes=[Pb, rs])
        yield
        k.op("dve", lambda v: v.tensor_tensor(out=es_[:], in0=sinks_bc[:], in1=mx[:], op=ALU.subtract), reads=[sinks_bc, mx], writes=[es_])
        yield
        k.op("act", lambda a: a.activation(out=es_[:], in_=es_[:], func=ACT.Exp), reads=[es_], writes=[es_])
        yield
        k.op("dve", lambda v: v.tensor_tensor(out=es_[:], in0=es_[:], in1=rs[:], op=ALU.add), reads=[es_, rs], writes=[es_])
        yield
        k.op("dve", lambda v: v.reciprocal(out=es_[:], in_=es_[:]), reads=[es_], writes=[es_])
        yield
        k.op("dve", lambda v: v.tensor_copy(out=rden[:].rearrange("p (c k) -> p k c", k=2), in_=es_[:].rearrange("p (k c) -> p k c", k=2)),
             reads=[es_], writes=[rden])
        PPT = [PB[1], PB[0]]
        yield
        for s_ in range(8):
            for half in range(2):
                i16 = s_ * 2 + half
                pb = PPT[i16 // 8]
                o = (i16 % 8) * 128
                tr(pbf(pb)[:, o:o + 128], Pb[:, s_, half * 128:(half + 1) * 128], identb[:], [Pb], [pb])
        yield
        k.op("act", lambda a: a.copy(out=PTb[:, 0:8, :], in_=v3(pbf(PPT[0]), 8)), reads=[PPT[0]], writes=[PTb])
        yield
        k.op("dve", lambda v: v.tensor_copy(out=PTb[:, 8:16, :], in_=v3(pbf(PPT[1]), 8)), reads=[PPT[1]], writes=[PTb])
        PV = PB[1]
        yield
        for c in range(4):
            for kv in range(2):
                slot = c * 2 + kv
                sp_ = kv * 4 + c
                mm(PV[:, slot * 64:(slot + 1) * 64], PTb[:, 2 * sp_, :], vprev[:, 64 * kv:64 * kv + 64], True, False, [PTb, vprev], [PV])
                mm(PV[:, slot * 64:(slot + 1) * 64], PTb[:, 2 * sp_ + 1, :], vcur[:, 64 * kv:64 * kv + 64], False, True, [PTb, vcur], [PV])
        yield
        k.op("dve", lambda v: v.tensor_tensor(out=v3(obt[:], 8), in0=v3(PV[:, :], 8), in1=bc(rden[:].rearrange("p (a b) -> p a b", b=1), [128, 8, 64]), op=ALU.mult),
             reads=[PV, rden], writes=[obt])
        yield
        k.op("pool", lambda g: g.tensor_tensor(out=mixb[:, 512:1024], in0=obt[:], in1=zBs[:], op=ALU.mult), reads=[obt, zBs], writes=[mixb])

        yield

    def E(n):
        X = Xt[n % 2]; pc = pre; kr = kr2[n % 2]
        if not full(n):
            return
        P3b = pbf(PB[6])
        yield
        for kc in range(8):
            tr(P3b[:, kc * 128:(kc + 1) * 128], mixb[:, kc * 128:(kc + 1) * 128], identb[:], [mixb], [PB[6]])
        yield
        k.op("act", lambda a: a.copy(out=mixT[:, 0:4, :], in_=v3(P3b[:, 0:512], 4)), reads=[PB[6]], writes=[mixT])
        yield
        k.op("dve", lambda v: v.tensor_copy(out=mixT[:, 4:8, :], in_=v3(P3b[:, 512:1024], 4)), reads=[PB[6]], writes=[mixT])
        yield
        for j in range(2):
            for kc in range(8):
                mm(PB[6 + j][:, :], mixT[:, kc, :], Wo[:, kc, j * 512:(j + 1) * 512], kc == 0, kc == 7, [mixT, Wo], [PB[6 + j]])
        yield
        for j in range(2):
            k.op("dve", lambda v, j=j: v.tensor_tensor(out=ypre[:, j * 512:(j + 1) * 512], in0=PB[6 + j][:, :], in1=g1bc[:, j * 512:(j + 1) * 512], op=ALU.mult),
                 reads=[PB[6 + j], g1bc], writes=[ypre])
        yield
        k.op("dve", lambda g: g.scalar_tensor_tensor(out=ypre[:], in0=X[:], scalar=ALPHA, in1=ypre[:], op0=ALU.mult, op1=ALU.add), reads=[X, ypre], writes=[ypre])
        Yo = Yt[0]
        yjunk = Yo
        yield
        k.op("act", lambda a: a.activation(out=yjunk[:], in_=ypre[:], func=ACT.Identity, accum_out=st[:, 0:1]), reads=[ypre], writes=[yjunk, st])
        yield
        k.op("act", lambda a: a.activation(out=yjunk[:], in_=ypre[:], func=ACT.Square, accum_out=st[:, 1:2]), reads=[ypre], writes=[yjunk, st])
        yield
        k.op("dve", lambda v: v.tensor_scalar(out=st[:, 2:3], in0=st[:, 0:1], scalar1=1.0 / D, scalar2=None, op0=ALU.mult), reads=[st], writes=[st])
        yield
        k.op("dve", lambda v: v.tensor_tensor(out=st[:, 3:4], in0=st[:, 2:3], in1=st[:, 2:3], op=ALU.mult), reads=[st], writes=[st])
        yield
        k.op("dve", lambda v: v.scalar_tensor_tensor(out=st[:, 4:5], in0=st[:, 1:2], scalar=1.0 / D, in1=st[:, 3:4], op0=ALU.mult, op1=ALU.subtract), reads=[st], writes=[st])
        yield
        k.op("act", lambda a: a.activation(out=st[:, 5:6], in_=st[:, 4:5], func=ACT.Ln, bias=LN_EPS, scale=1.0), reads=[st], writes=[st])
        yield
        k.op("act", lambda a: a.activation(out=st[:, 5:6], in_=st[:, 5:6], func=ACT.Exp, scale=-0.5), reads=[st], writes=[st])
        yield
        k.op("dve", lambda v: v.scalar_tensor_tensor(out=st[:, 6:7], in0=st[:, 2:3], scalar=-1.0, in1=st[:, 5:6], op0=ALU.mult, op1=ALU.mult), reads=[st], writes=[st])
        yield
        k.op("act", lambda a: a.activation(out=yjunk[:], in_=ypre[:], func=ACT.Identity, bias=st[:, 6:7], scale=st[:, 5:6]), reads=[ypre, st], writes=[yjunk])
        yield
        k.op("pool", lambda g: g.tensor_tensor(out=yjunk[:], in0=yjunk[:], in1=lng_bc[:], op=ALU.mult), reads=[yjunk, lng_bc], writes=[yjunk])
        yield
        k.op("dve", lambda v: v.tensor_tensor(out=Yo[:], in0=yjunk[:], in1=lnb_bc[:], op=ALU.add), reads=[yjunk, lnb_bc], writes=[Yo])
        yield
        k.dma("sp", y_d[(n - F0) * 128:(n - F0 + 1) * 128, :], Yo[:], reads=[Yo], semkey="st_" + Yo.name)

        if n == NT - 1:
            k.barrier()
            k.pe_inorder = False
            cpo_in = rt[0][:].rearrange("p a b -> p (a b)")[:, 0:36].rearrange("p (a b) -> p a b", a=3)
            cpo = obt[0:12, 0:384].rearrange("p (a b) -> p a b", a=3)
            k.op("dve", lambda v: v.tensor_copy(out=cpo_in[:], in_=pc[:, :, 128:131].rearrange("p c r -> p r c")), reads=[pc], writes=[cpo_in])
            for r_ in range(3):
                tr(PB[6][0:12, r_ * 128:(r_ + 1) * 128], cpo_in[:, r_, :], ident[:], [cpo_in], [PB[6]])
            k.op("dve", lambda v: v.tensor_copy(out=cpo[:].rearrange("p a b -> p (a b)"), in_=PB[6][0:12, 0:384]), reads=[PB[6]], writes=[cpo])
            k.dma("sp", convp_d.rearrange("r (c p) -> c r p", p=128), cpo[:], reads=[cpo], semkey="st_misc")
            for g_ in range(2):
                k.dma("sp", deltap_d[2 * g_:2 * g_ + 2].rearrange("h k v -> k h v"), S[g_][:], reads=[S[g_]], semkey="st_misc")
            k.dma("sp", swak_d[:, :], kr[:], reads=[kr], semkey="st_misc")
            k.dma("sp", swav_d[:, :], vB32[:], reads=[vB32], semkey="st_misc")

        yield

    def chain(*gens):
        for g_ in gens:
            for v_ in g_:
                yield v_

    def merge(*gens):
        gens = [[g_, "s%d" % i] for i, g_ in enumerate(gens) if g_ is not None]
        base = min(k.t_eng.values())
        for _, lab in gens:
            k.t_stream[lab] = base
        while gens:
            gens.sort(key=lambda x: k.t_stream[x[1]])
            g_, lab = gens[0]
            k.stream = lab
            try:
                r_ = next(g_)
            except StopIteration:
                gens.pop(0)
                continue
            if r_ == "wait":
                assert len(gens) > 1, "gated stream waits on nothing"
                k.t_stream[lab] = max(k.t_stream[x[1]] for x in gens[1:]) + 1e-3
        k.stream = None

    gdone[(-1, 0)] = True
    gdone[(-1, 1)] = True
    emittedA = set()

    def PA(n):
        if n >= NT or n in emittedA:
            return None
        emittedA.add(n)
        return P1a(n)

    merge(PA(0))
    if F0 > 0:
        merge(P1b(0, True), PA(1))
    else:
        merge(P1b(0))
    for n in range(NT):
        nxt = n + 1 < NT
        pl = nxt and (n + 1 < F0)
        if pl:
            merge(G(n, 0), G(n, 1), W(n), chain(*[g_ for g_ in (PA(n + 1),) if g_ is not None], P1b(n + 1, True)), PA(n + 2))
        else:
            merge(G(n, 0), G(n, 1), W(n), PA(n + 1))
        merge(E(n), P1b(n + 1) if (nxt and not pl) else None)

    k.finish("sp")
    if k.dry:
        return k.need_out
    nc._kb_nops = k.nops
    nc._kb_sig = dict(k.sig)
    nc._kb_cnt = {e: k.cnt[e] for e in k.eng}
    return nc


def rope_tables(pos):
    half = 32
    inv = (1.0 / (10000.0 ** (np.arange(half, dtype=np.float32) / np.float32(half)))).astype(np.float32)
    ang = pos.astype(np.float32)[:, None] * inv[None, :]
    return np.cos(ang).astype(np.float32), np.sin(ang).astype(np.float32)


def prep_shared(inputs):
    w_in = np.asarray(inputs["w_in"][0], np.float32)
    perm_cols = np.concatenate([np.arange(h * 64, (h + 1) * 64) for h in PERM])
    w_f = np.ascontiguousarray(w_in[:, 0:1536])
    zA = w_in[:, 1536:2048]
    beta = w_in[:, 2048:2052]
    dec = w_in[:, 2052:2056]
    qB = w_in[:, 2056:2568][:, perm_cols]
    kB = w_in[:, 2568:2696]
    vB = w_in[:, 2696:2824]
    zB = w_in[:, 2824:3336][:, perm_cols]
    w_t = np.ascontiguousarray(np.concatenate([zA, qB, zB, kB, vB, beta, dec], axis=1))
    w_out = np.asarray(inputs["w_out"][0], np.float32)
    w_o = np.ascontiguousarray(np.concatenate([w_out[0:512], w_out[512:1024][perm_cols]], axis=0))
    sh = {
        "w_ada": np.ascontiguousarray(inputs["w_ada"][0], np.float32),
        "b_ada": np.ascontiguousarray(inputs["b_ada"], np.float32).reshape(1, -1),
        "w_f": w_f, "w_t": w_t, "w_o": w_o,
        "conv_w": np.ascontiguousarray(inputs["conv_w"][0], np.float32),
        "a_log": np.ascontiguousarray(inputs["a_log"], np.float32).reshape(1, 4),
        "dt_bias": np.ascontiguousarray(inputs["dt_bias"], np.float32).reshape(1, 4),
        "norm_a": np.ascontiguousarray(inputs["norm_a"], np.float32).reshape(1, 128),
        "sinks_p": np.ascontiguousarray(np.asarray(inputs["sinks"], np.float32).reshape(8)).reshape(1, 8),
        "ln_g": np.ascontiguousarray(inputs["ln_g"], np.float32).reshape(1, D),
        "ln_b": np.ascontiguousarray(inputs["ln_b"], np.float32).reshape(1, D),
    }
    return sh


def prep_core(inputs, sh, core, ntiles=NTILES, do_sample=True, pad=None):
    b = core // 4
    q = core % 4
    if pad is None:
        pad = (3 - q) * NLOCAL
    real = ntiles - pad
    T = ntiles * 128
    m = dict(sh)
    x = np.zeros((T, D), np.float32)
    x[pad * 128:] = inputs["x_prompt"][b, :real * 128]
    m["x"] = x
    m["c"] = np.ascontiguousarray(inputs["c_prompt"][b], np.float32).reshape(1, D)
    valid = np.zeros((1, ntiles), np.float32)
    valid[0, pad:] = 1.0
    m["valid"] = valid
    pos = np.maximum(np.arange(T) - pad * 128, 0)
    cos, sin = rope_tables(pos)
    m["cosk"] = cos
    m["sink"] = sin
    if do_sample:
        sl = slice(core * NS, (core + 1) * NS)
        m["xs"] = np.ascontiguousarray(inputs["x_sample"][sl, 0], np.float32)
        m["cs"] = np.ascontiguousarray(inputs["c_sample"][sl], np.float32)
        m["s_conv"] = np.ascontiguousarray(inputs["state_conv"][0, sl], np.float32)
        m["s_delta"] = np.ascontiguousarray(inputs["state_delta"][0, sl], np.float32)
        m["s_k"] = np.ascontiguousarray(inputs["cache_swa_k"][0, sl], np.float32).reshape(NS, 128, 128)
        m["s_v"] = np.ascontiguousarray(inputs["cache_swa_v"][0, sl], np.float32).reshape(NS, 128, 128)
        cs_, ss_ = rope_tables(np.array([8192]))
        m["sinks_col"] = np.ascontiguousarray(np.tile(np.asarray(inputs["sinks"], np.float32).reshape(8), NS).reshape(128, 1))
        m["cos_s"] = cs_.reshape(1, 32)
        m["sin_s"] = ss_.reshape(1, 32)
    return m


_NC_CACHE = {}


DO_SAMPLE = True


def kernel(**inputs):
    if "nc" not in _NC_CACHE:
        _NC_CACHE["nc"] = build_program(NTILES, DO_SAMPLE)
    nc = _NC_CACHE["nc"]
    sh = prep_shared(inputs)
    in_maps = [prep_core(inputs, sh, c, NTILES, DO_SAMPLE) for c in range(8)]
    res = run_bass_kernel_spmd(nc, in_maps, core_ids=list(range(8))).results
    yp = np.stack([np.concatenate([res[b * 4 + q]["y"] for q in range(4)], 0) for b in range(2)], 0).astype(np.float32)
    conv_p = np.stack([res[3]["conv_p"], res[7]["conv_p"]], 0)[None].astype(np.float32)
    delta_p = np.stack([res[3]["delta_p"], res[7]["delta_p"]], 0)[None].astype(np.float32)
    swa_k_p = np.stack([res[3]["swa_k_p"], res[7]["swa_k_p"]], 0).reshape(1, 2, 128, 2, 64).astype(np.float32)
    swa_v_p = np.stack([res[3]["swa_v_p"], res[7]["swa_v_p"]], 0).reshape(1, 2, 128, 2, 64).astype(np.float32)
    if DO_SAMPLE:
        ys = np.concatenate([r["ys"] for r in res], 0).reshape(128, 1, D).astype(np.float32)
        conv_s = np.concatenate([r["conv_s"] for r in res], 0)[None].astype(np.float32)
        delta_s = np.concatenate([r["delta_s"] for r in res], 0)[None].astype(np.float32)
        swa_k_s = np.concatenate([r["swa_k_s"] for r in res], 0).reshape(1, 128, 128, 2, 64).astype(np.float32)
        swa_v_s = np.concatenate([r["swa_v_s"] for r in res], 0).reshape(1, 128, 128, 2, 64).astype(np.float32)
    else:
        ys = np.zeros((128, 1, D), np.float32)
        conv_s = np.zeros((1, 128, 3, 1536), np.float32)
        delta_s = np.zeros((1, 128, 4, 128, 128), np.float32)
        swa_k_s = np.zeros((1, 128, 128, 2, 64), np.float32)
        swa_v_s = np.zeros((1, 128, 128, 2, 64), np.float32)
    return (yp, ys, conv_p, delta_p, swa_k_p, swa_v_p, conv_s, delta_s, swa_k_s, swa_v_s)
```

```python
import contextlib
import numpy as np
import concourse.bass as bass
import concourse.mybir as mybir
from concourse.bass_utils import run_bass_kernel_spmd

F32 = mybir.dt.float32
BF16 = mybir.dt.bfloat16
ACT = mybir.ActivationFunctionType
ALU = mybir.AluOpType
AX = mybir.AxisListType

D = 1024
NTILES = 64
NS = 16
ALPHA = 2.0 ** 0.25
NEG = -1.0e30
LN_EPS = 1e-5
RMS_EPS = 1e-6
L2_EPS = 1e-6
WT_COLS = 1800
PERM = [0, 4, 1, 5, 2, 6, 3, 7]


class KB:
    def __init__(self, nc):
        self.nc = nc
        self.es = contextlib.ExitStack()
        self.eng = {"pe": nc.tensor, "dve": nc.vector, "act": nc.scalar, "pool": nc.gpsimd, "sp": nc.sync}
        self.sem = {}
        self.cnt = {}
        for e in self.eng:
            self.sem[e] = self.es.enter_context(nc.semaphore("sem_" + e))
            self.cnt[e] = 0
        self.waited = {}
        self.last_write = {}
        self.readers = {}
        self.ntensors = 0
        self.limit = None
        self.nops = 0
        self.pe_inorder = False
        self.t_eng = {e: 0.0 for e in self.eng}
        self.t_fin = {}
        self.stream = None
        self.t_stream = {}
        self.dry = False
        self.needed = None
        self.need_out = set()
        self.sig = {e: 0 for e in self.eng}
        self.sigval = {}

    def sb(self, name, shape, dt=F32):
        return self.es.enter_context(self.nc.sbuf_tensor(name, list(shape), dt))

    def ps(self, name, shape=(128, 512), dt=F32):
        return self.es.enter_context(self.nc.psum_tensor(name, list(shape), dt))

    def _deps(self, reads, writes, nowaw=False):
        deps = set()
        for t in reads:
            if t in self.last_write:
                deps.add(self.last_write[t])
        for t in writes:
            if t in self.last_write and not nowaw:
                deps.add(self.last_write[t])
            for r in self.readers.get(t, ()):
                deps.add(r)
        return deps

    def _semval(self, src, val):
        if src in self.eng and self.needed is not None:
            return self.sigval[(src, val)]
        return val

    def _wait(self, e, deps):
        for (src, val) in sorted(deps, key=lambda x: str(x)):
            if e == "pe" and src == "pe" and self.pe_inorder:
                continue
            if self.waited.get((e, src), 0) < val:
                if self.dry:
                    self.need_out.add((src, val))
                else:
                    self.eng[e].wait_ge(self.sem[src], self._semval(src, val))
                self.waited[(e, src)] = val

    def _record(self, key, reads, writes):
        for t in writes:
            self.last_write[t] = key
            self.readers[t] = set()
        for t in reads:
            if t not in writes:
                self.readers.setdefault(t, set()).add(key)

    COST = {"pe": 0.2, "dve": 0.45, "act": 0.5, "pool": 1.0, "sp": 0.1}

    def _model(self, e, key, deps, cost):
        t0 = self.t_eng.get(e, 0.0)
        for d in deps:
            t0 = max(t0, self.t_fin.get(d, 0.0) + (0.0 if d[0] == e else 0.15))
        t1 = t0 + cost
        self.t_eng[e] = t1
        self.t_fin[key] = t1
        if self.stream is not None:
            self.t_stream[self.stream] = max(self.t_stream.get(self.stream, 0.0), t1)

    def op(self, e, fn, reads=(), writes=(), cost=None):
        self.nops += 1
        if self.limit is not None and self.nops > self.limit:
            return
        reads = [r.name if hasattr(r, "name") else r for r in reads]
        writes = [w.name if hasattr(w, "name") else w for w in writes]
        writes = list(writes) + [r for r in reads if r.startswith("pb") and r not in writes]
        deps = self._deps(reads, writes)
        self._wait(e, deps)
        self.cnt[e] += 1
        self._model(e, (e, self.cnt[e]), deps, self.COST[e] if cost is None else cost)
        if not self.dry:
            inst = fn(self.eng[e])
            if self.needed is None:
                inst.then_inc(self.sem[e], 1)
            elif (e, self.cnt[e]) in self.needed:
                self.sig[e] += 1
                self.sigval[(e, self.cnt[e])] = self.sig[e]
                inst.then_inc(self.sem[e], 1)
        self._record((e, self.cnt[e]), reads, writes)

    def dma(self, e, out, in_, reads=(), writes=(), semkey=None, nowaw=False, **kw):
        reads = [r.name if hasattr(r, "name") else r for r in reads]
        writes = [w.name if hasattr(w, "name") else w for w in writes]
        self.nops += 1
        if self.limit is not None and self.nops > self.limit:
            return
        if semkey not in self.sem:
            self.sem[semkey] = self.es.enter_context(self.nc.semaphore("semd_" + str(semkey)))
            self.cnt[semkey] = 0
        deps = self._deps(reads, writes, nowaw)
        self._wait(e, deps)
        self.cnt[semkey] += 16
        self._model(e, (semkey, self.cnt[semkey]), deps, 2.0)
        if not self.dry:
            inst = self.eng[e].dma_start(out=out, in_=in_, **kw)
            inst.then_inc(self.sem[semkey], 16)
        self._record((semkey, self.cnt[semkey]), reads, writes)

    def barrier(self):
        for e in self.eng:
            for src, val in self.cnt.items():
                if val > 0 and self.waited.get((e, src), 0) < val:
                    if self.dry:
                        self.need_out.add((src, val))
                    else:
                        self.eng[e].wait_ge(self.sem[src], self._semval(src, val))
                    self.waited[(e, src)] = val
        self.last_write = {}
        self.readers = {}

    def finish(self, e="sp"):
        for src, val in self.cnt.items():
            if val > 0 and self.waited.get((e, src), 0) < val:
                if self.dry:
                    self.need_out.add((src, val))
                else:
                    self.eng[e].wait_ge(self.sem[src], self._semval(src, val))
                self.waited[(e, src)] = val
        self.es.close()


def bc(ap, shape):
    return ap.to_broadcast(list(shape))


GBIAS = 0.0
NLOCAL = 16


def build_program(ntiles=NTILES, do_sample=True, limit=None, nl=None):
    if nl is None:
        nl = NLOCAL if ntiles >= NLOCAL else ntiles
    plan = _build(ntiles, do_sample, limit, None, nl)
    return _build(ntiles, do_sample, limit, plan, nl)


def _build(ntiles, do_sample, limit, plan, NL):
    nc = bass.Bass("TRN2", target_bir_lowering=False)
    k = KB(nc)
    k.limit = limit
    if plan is None:
        k.dry = True
    else:
        k.needed = plan
    NT = ntiles
    T = NT * 128

    def din(name, shape):
        return nc.dram_tensor(name, list(shape), F32, kind="ExternalInput").ap()

    def dout(name, shape):
        return nc.dram_tensor(name, list(shape), F32, kind="ExternalOutput").ap()

    x_d = din("x", [T, D])
    c_d = din("c", [1, D])
    wada_d = din("w_ada", [D, 3 * D])
    bada_d = din("b_ada", [1, 3 * D])
    wf_d = din("w_f", [D, 1536])
    wt_d = din("w_t", [D, WT_COLS])
    wo_d = din("w_o", [D, D])
    convw_d = din("conv_w", [4, 1536])
    alog_d = din("a_log", [1, 4])
    dtb_d = din("dt_bias", [1, 4])
    norma_d = din("norm_a", [1, 128])
    sinks_d = din("sinks_p", [1, 8])
    lng_d = din("ln_g", [1, D])
    lnb_d = din("ln_b", [1, D])
    cosk_d = din("cosk", [T, 32])
    sink_d = din("sink", [T, 32])

    y_d = dout("y", [NL * 128, D])
    valid_d = din("valid", [1, NT])
    F0 = NT - NL

    def full(n):
        return n >= F0

    def swaprep(n):
        return n == F0 - 1
    convp_d = dout("conv_p", [3, 1536])
    deltap_d = dout("delta_p", [4, 128, 128])
    swak_d = dout("swa_k_p", [128, 128])
    swav_d = dout("swa_v_p", [128, 128])

    if do_sample:
        xs_d = din("xs", [NS, D])
        cs_d = din("cs", [NS, D])
        sconv_d = din("s_conv", [NS, 3, 1536])
        sdelta_d = din("s_delta", [NS, 4, 128, 128])
        sk_d = din("s_k", [NS, 128, 128])
        sv_d = din("s_v", [NS, 128, 128])
        coss_d = din("cos_s", [1, 32])
        sins_d = din("sin_s", [1, 32])
        ys_d = dout("ys", [NS, D])
        convs_d = dout("conv_s", [NS, 3, 1536])
        deltas_d = dout("delta_s", [NS, 4, 128, 128])
        swaks_d = dout("swa_k_s", [NS, 128, 128])
        swavs_d = dout("swa_v_s", [NS, 128, 128])

    ident = k.sb("ident", [128, 128])
    U = k.sb("U", [128, 128])
    ones = k.sb("ones", [128, 128])
    onesb = k.sb("onesb", [128, 128], BF16)
    negA = k.sb("negA", [128, 128])
    negB = k.sb("negB", [128, 128])
    swam = k.sb("swam", [128, 256])

    k.op("pool", lambda g: g.memset(ones[:], 1.0), writes=[ones])
    k.op("pool", lambda g: g.memset(onesb[:], 1.0), writes=[onesb])
    k.op("pool", lambda g: g.affine_select(out=ident[:], in_=ones[:], pattern=[[-1, 128]], compare_op=ALU.is_equal,
                                            fill=0.0, base=0, channel_multiplier=1), reads=[ones], writes=[ident])
    k.op("pool", lambda g: g.affine_select(out=U[:], in_=ones[:], pattern=[[1, 128]], compare_op=ALU.is_ge,
                                            fill=0.0, base=0, channel_multiplier=-1), reads=[ones], writes=[U])
    zer = k.sb("zer", [128, 256])
    k.op("pool", lambda g: g.memset(zer[:], 0.0), writes=[zer])
    k.op("pool", lambda g: g.affine_select(out=negA[:], in_=zer[:, 0:128], pattern=[[-1, 128]], compare_op=ALU.is_ge,
                                            fill=NEG, base=-1, channel_multiplier=1), reads=[zer], writes=[negA])
    k.op("pool", lambda g: g.affine_select(out=negB[:], in_=zer[:, 0:128], pattern=[[1, 128]], compare_op=ALU.is_ge,
                                            fill=NEG, base=0, channel_multiplier=-1), reads=[zer], writes=[negB])
    swamt = k.sb("swamt", [128, 256])
    k.op("pool", lambda g: g.affine_select(out=swamt[:], in_=zer[:], pattern=[[1, 256]], compare_op=ALU.is_ge,
                                            fill=NEG, base=0, channel_multiplier=-1), reads=[zer], writes=[swamt])
    k.op("pool", lambda g: g.affine_select(out=swam[:], in_=swamt[:], pattern=[[-1, 256]], compare_op=ALU.is_ge,
                                            fill=NEG, base=128, channel_multiplier=1), reads=[swamt], writes=[swam])

    PB = [k.ps("pb%d" % i) for i in range(8)]
    def load_bc(name, src, n, parts=128):
        t = k.sb(name, [parts, n])
        k.dma("sp", t[:], src.partition_broadcast(parts), writes=[t], semkey="ld_" + name)
        return t

    lng_bc = load_bc("lng_bc", lng_d[0], D)
    lnb_bc = load_bc("lnb_bc", lnb_d[0], D)
    norma_bc = load_bc("norma_bc", norma_d[0], 128)
    sinks_bc = load_bc("sinks_bc", sinks_d[0], 8)
    valid_bc = load_bc("valid_bc", valid_d[0], NT)
    alog_bc = load_bc("alog_bc", alog_d[0], 4)
    dtb_bc = load_bc("dtb_bc", dtb_d[0], 4)
    cwT = k.sb("cwT", [128, 48])
    bada_fm = k.sb("bada_fm", [128, 24])
    cT = k.sb("cT", [128, 8])
    rowst = k.sb("rowst", [80, 128])
    k.dma("sp", rowst[0:48, :], convw_d.rearrange("j (c p) -> (j c) p", p=128), writes=[rowst], semkey="ld_rowst", nowaw=True)
    k.dma("sp", rowst[48:72, :], bada_d[0].rearrange("(j p) -> j p", p=128), writes=[rowst], semkey="ld_rowst", nowaw=True)
    k.dma("sp", rowst[72:80, :], c_d[0].rearrange("(j p) -> j p", p=128), writes=[rowst], semkey="ld_rowst", nowaw=True)
    cst = [k.sb("cst%d" % i, [128, 2, 32]) for i in range(2)]
    k.op("pe", lambda p: p.transpose(out=PB[0][:, 0:80], in_=rowst[:, :], identity=ident[0:80, 0:80]), reads=[rowst, ident], writes=[PB[0]])
    k.op("dve", lambda v: v.tensor_copy(out=cwT[:], in_=PB[0][:, 0:48]), reads=[PB[0]], writes=[cwT])
    k.op("dve", lambda v: v.tensor_copy(out=bada_fm[:], in_=PB[0][:, 48:72]), reads=[PB[0]], writes=[bada_fm])
    k.op("dve", lambda v: v.tensor_copy(out=cT[:], in_=PB[0][:, 72:80]), reads=[PB[0]], writes=[cT])
    ea = k.sb("ea", [128, 4])
    k.op("act", lambda a: a.activation(out=ea[:], in_=alog_bc[:], func=ACT.Exp), reads=[alog_bc], writes=[ea])
    k.op("dve", lambda v: v.tensor_scalar(out=ea[:], in0=ea[:], scalar1=-1.0, scalar2=None, op0=ALU.mult),
         reads=[ea], writes=[ea])

    Wf = k.sb("Wf", [128, 8, 1536], BF16)
    Wt = k.sb("Wt", [128, 8, WT_COLS], BF16)
    Wo = k.sb("Wo", [128, 8, D], BF16)
    mod_fm = k.sb("mod_fm", [128, 16])
    g1bc = k.sb("g1bc", [128, D])
    k1s = contextlib.ExitStack()
    if do_sample:
        csT = k1s.enter_context(nc.sbuf_tensor("csT", [128, 8, NS], F32))
        mod_s = k1s.enter_context(nc.sbuf_tensor("mod_s", [NS, 3 * D], F32))
    k2 = contextlib.ExitStack()
    def sb2(name, shape, dt=F32):
        return k2.enter_context(nc.sbuf_tensor(name, list(shape), dt))
    bada_bc = sb2("bada_bc", [128, D])
    k.dma("sp", bada_bc[:], bada_d[0, 2 * D:3 * D].partition_broadcast(128), writes=[bada_bc], semkey="ld_bada_bc")
    si = 0
    cast_engs = ["dve", "pool", "act"]

    def load_cast(dst, src_d, ncols):
        for kc in range(8):
            k.dma("pool", dst[:, kc, :], src_d[kc * 128:(kc + 1) * 128, :], writes=[dst], semkey="ld_" + dst.name, nowaw=True)

    load_cast(Wf, wf_d, 1536)
    load_cast(Wt, wt_d, WT_COLS)
    load_cast(Wo, wo_d, D)


    wa = [sb2("wa%d" % i, [128, 8, 512]) for i in range(1)]
    c_bcT = sb2("c_bcT", [128, 8, 128])
    k.op("pool", lambda g: g.tensor_copy(out=c_bcT[:], in_=bc(cT[:].rearrange("p (a b) -> p a b", b=1), [128, 8, 128])),
         reads=[cT], writes=[c_bcT])
    if do_sample:
        cs_sb = sb2("cs_sb", [NS, D])
        k.dma("sp", cs_sb[:], cs_d[:, :], writes=[cs_sb], semkey="ld_cs")
        for kc in range(8):
            k.op("pe", lambda p, kc=kc: p.transpose(out=PB[0][:, kc * NS:(kc + 1) * NS], in_=cs_sb[:, kc * 128:(kc + 1) * 128],
                                                    identity=ident[0:NS, 0:NS]), reads=[cs_sb, ident], writes=[PB[0]])
        k.op("dve", lambda v: v.tensor_copy(out=csT[:].rearrange("p a b -> p (a b)"), in_=PB[0][:, 0:8 * NS]),
             reads=[PB[0]], writes=[csT])
        bada_s = sb2("bada_s", [NS, 3 * D])
        k.dma("sp", bada_s[:], bada_d[0].partition_broadcast(NS), writes=[bada_s], semkey="ld_bada_s")
    for j in range(6):
        w = wa[0]
        for kc in range(8):
            k.dma("sp", w[:, kc, :], wada_d[kc * 128:(kc + 1) * 128, j * 512:(j + 1) * 512], writes=[w],
                  semkey="ld_" + w.name, nowaw=True)
        if j < 4:
            for sub in range(4):
                col = j * 4 + sub
                for kc in range(8):
                    k.op("pe", lambda p, kc=kc, sub=sub, col=col, w=w: p.matmul(
                        PB[1][:, col:col + 1], lhsT=w[:, kc, sub * 128:(sub + 1) * 128], rhs=cT[:, kc:kc + 1],
                        start=(kc == 0), stop=(kc == 7)), reads=[w, cT], writes=[PB[1]])
        else:
            for kc in range(8):
                k.op("pe", lambda p, kc=kc, w=w: p.matmul(PB[2 + (j - 4)][:, :], lhsT=c_bcT[:, kc, :], rhs=w[:, kc, :],
                                                          start=(kc == 0), stop=(kc == 7)),
                     reads=[w, c_bcT], writes=[PB[2 + (j - 4)]])
        if do_sample:
            for kc in range(8):
                k.op("pe", lambda p, kc=kc, w=w: p.matmul(PB[4 + j % 2][0:NS, :], lhsT=csT[:, kc, :], rhs=w[:, kc, :],
                                                          start=(kc == 0), stop=(kc == 7)),
                     reads=[w, csT], writes=[PB[4 + j % 2]])
            k.op("dve", lambda v, j=j: v.tensor_tensor(out=mod_s[:, j * 512:(j + 1) * 512], in0=PB[4 + j % 2][0:NS, :],
                                                       in1=bada_s[:, j * 512:(j + 1) * 512], op=ALU.add),
                 reads=[PB[4 + j % 2], bada_s], writes=[mod_s])
    k.op("dve", lambda v: v.tensor_tensor(out=mod_fm[:], in0=PB[1][:, 0:16], in1=bada_fm[:, 0:16], op=ALU.add),
         reads=[PB[1], bada_fm], writes=[mod_fm])
    k.op("dve", lambda v: v.tensor_scalar(out=mod_fm[:, 8:16], in0=mod_fm[:, 8:16], scalar1=1.0, scalar2=None, op0=ALU.add),
         reads=[mod_fm], writes=[mod_fm])
    for j in range(2):
        k.op("dve", lambda v, j=j: v.scalar_tensor_tensor(out=g1bc[:, j * 512:(j + 1) * 512], in0=PB[2 + j][:, :], scalar=1.0,
                                                           in1=bada_bc[:, j * 512:(j + 1) * 512], op0=ALU.add, op1=ALU.add),
             reads=[PB[2 + j], bada_bc], writes=[g1bc])


    if do_sample:
        k.barrier()
        k2.close()
        k2 = contextlib.ExitStack()
        P16 = NS
        sinkcol_d = din("sinks_col", [128, 1])
        xs = sb2("xs_sb", [P16, D]); hs = sb2("hs", [P16, D]); mix_s = sb2("mix_s", [P16, D])
        hsT = sb2("hsT", [128, 8, P16], BF16)
        pqkv = sb2("pqkv", [P16, 1536])
        qkv_s = sb2("qkv_s", [P16, 12, 128])
        zAs_s = sb2("zAs_s", [P16, 512]); zBs_s = sb2("zBs_s", [P16, 512])
        qr_s = sb2("qr_s", [P16, 512]); kr_s = sb2("kr_s", [P16, 128]); v_s = sb2("v_s", [P16, 128])
        vsb = sb2("vsb", [P16, 128], BF16)
        bd_s = sb2("bd_s", [P16, 8]); bdt_s = sb2("bdt_s", [P16, 8])
        beta_s = sb2("beta_s", [P16, 4]); nbeta_s = sb2("nbeta_s", [P16, 4]); g_s = sb2("g_s", [P16, 4]); eg_s = sb2("eg_s", [P16, 4])
        cs16 = sb2("cs16", [P16, 2, 32])
        k.dma("sp", cs16[:, 0, :], coss_d[0].partition_broadcast(P16), writes=[cs16], semkey="ld_cs16", nowaw=True)
        k.dma("sp", cs16[:, 1, :], sins_d[0].partition_broadcast(P16), writes=[cs16], semkey="ld_cs16", nowaw=True)
        k.dma("sp", xs[:], xs_d[:, :], writes=[xs], semkey="ld_xs")
        k.op("dve", lambda v: v.scalar_tensor_tensor(out=hs[:], in0=mod_s[:, D:2 * D], scalar=1.0, in1=xs[:], op0=ALU.add, op1=ALU.mult),
             reads=[mod_s, xs], writes=[hs])
        k.op("dve", lambda v: v.tensor_tensor(out=hs[:], in0=hs[:], in1=mod_s[:, 0:D], op=ALU.add), reads=[hs, mod_s], writes=[hs])
        for kc in range(8):
            k.op("pe", lambda p, kc=kc: p.transpose(out=PB[0][:, kc * P16:(kc + 1) * P16], in_=hs[:, kc * 128:(kc + 1) * 128],
                                                    identity=ident[0:P16, 0:P16]), reads=[hs, ident], writes=[PB[0]])
        k.op("dve", lambda v: v.tensor_copy(out=hsT[:].rearrange("p a b -> p (a b)"), in_=PB[0][:, 0:8 * P16]), reads=[PB[0]], writes=[hsT])
        for j in range(3):
            for kc in range(8):
                k.op("pe", lambda p, j=j, kc=kc: p.matmul(PB[1 + j][0:P16, :], lhsT=hsT[:, kc, :], rhs=Wf[:, kc, j * 512:(j + 1) * 512],
                                                          start=(kc == 0), stop=(kc == 7)), reads=[hsT, Wf], writes=[PB[1 + j]])
        offs = [(0, 512), (512, 512), (1024, 512), (1536, 264)]
        for j, (o, w_) in enumerate(offs):
            for kc in range(8):
                k.op("pe", lambda p, j=j, o=o, w_=w_, kc=kc: p.matmul(PB[4 + j][0:P16, 0:w_], lhsT=hsT[:, kc, :], rhs=Wt[:, kc, o:o + w_],
                                                                      start=(kc == 0), stop=(kc == 7)), reads=[hsT, Wt], writes=[PB[4 + j]])
        for j in range(3):
            k.op("dve", lambda v, j=j: v.tensor_copy(out=pqkv[:, j * 512:(j + 1) * 512], in_=PB[1 + j][0:P16, :]), reads=[PB[1 + j]], writes=[pqkv])
        k.op("act", lambda a: a.activation(out=zAs_s[:], in_=PB[4][0:P16, :], func=ACT.Silu), reads=[PB[4]], writes=[zAs_s])
        k.op("act", lambda a: a.activation(out=zBs_s[:], in_=PB[6][0:P16, :], func=ACT.Silu), reads=[PB[6]], writes=[zBs_s])
        rts = [sb2("rts%d" % i, [P16, 8, 32]) for i in range(4)]
        q3 = PB[5][0:P16, :].rearrange("p (a b) -> p a b", a=8)
        qr3 = qr_s[:].rearrange("p (a b) -> p a b", a=8)
        cq = bc(cs16[:, 0:1, :], [P16, 8, 32]); sq_ = bc(cs16[:, 1:2, :], [P16, 8, 32])
        k.op("dve", lambda v: v.tensor_tensor(out=rts[0][:], in0=q3[:, :, 0:32], in1=cq, op=ALU.mult), reads=[PB[5], cs16], writes=[rts[0]])
        k.op("dve", lambda v: v.tensor_tensor(out=rts[1][:], in0=q3[:, :, 32:64], in1=sq_, op=ALU.mult), reads=[PB[5], cs16], writes=[rts[1]])
        k.op("dve", lambda v: v.tensor_tensor(out=rts[2][:], in0=q3[:, :, 32:64], in1=cq, op=ALU.mult), reads=[PB[5], cs16], writes=[rts[2]])
        k.op("dve", lambda v: v.tensor_tensor(out=rts[3][:], in0=q3[:, :, 0:32], in1=sq_, op=ALU.mult), reads=[PB[5], cs16], writes=[rts[3]])
        k.op("dve", lambda v: v.tensor_tensor(out=qr3[:, :, 0:32], in0=rts[0][:], in1=rts[1][:], op=ALU.subtract), reads=[rts[0], rts[1]], writes=[qr_s])
        k.op("dve", lambda v: v.tensor_tensor(out=qr3[:, :, 32:64], in0=rts[2][:], in1=rts[3][:], op=ALU.add), reads=[rts[2], rts[3]], writes=[qr_s])
        k.op("dve", lambda v: v.tensor_scalar(out=qr_s[:], in0=qr_s[:], scalar1=0.125, scalar2=None, op0=ALU.mult), reads=[qr_s], writes=[qr_s])
        k3 = PB[7][0:P16, 0:128].rearrange("p (a b) -> p a b", a=2)
        kr3 = kr_s[:].rearrange("p (a b) -> p a b", a=2)
        ck = bc(cs16[:, 0:1, :], [P16, 2, 32]); sk_ = bc(cs16[:, 1:2, :], [P16, 2, 32])
        k.op("dve", lambda v: v.tensor_tensor(out=rts[0][:, 0:2, :], in0=k3[:, :, 0:32], in1=ck, op=ALU.mult), reads=[PB[7], cs16], writes=[rts[0]])
        k.op("dve", lambda v: v.tensor_tensor(out=rts[1][:, 0:2, :], in0=k3[:, :, 32:64], in1=sk_, op=ALU.mult), reads=[PB[7], cs16], writes=[rts[1]])
        k.op("dve", lambda v: v.tensor_tensor(out=rts[2][:, 0:2, :], in0=k3[:, :, 32:64], in1=ck, op=ALU.mult), reads=[PB[7], cs16], writes=[rts[2]])
        k.op("dve", lambda v: v.tensor_tensor(out=rts[3][:, 0:2, :], in0=k3[:, :, 0:32], in1=sk_, op=ALU.mult), reads=[PB[7], cs16], writes=[rts[3]])
        k.op("dve", lambda v: v.tensor_tensor(out=kr3[:, :, 0:32], in0=rts[0][:, 0:2, :], in1=rts[1][:, 0:2, :], op=ALU.subtract), reads=[rts[0], rts[1]], writes=[kr_s])
        k.op("dve", lambda v: v.tensor_tensor(out=kr3[:, :, 32:64], in0=rts[2][:, 0:2, :], in1=rts[3][:, 0:2, :], op=ALU.add), reads=[rts[2], rts[3]], writes=[kr_s])
        k.op("dve", lambda v: v.tensor_copy(out=v_s[:], in_=PB[7][0:P16, 128:256]), reads=[PB[7]], writes=[v_s])
        k.op("dve", lambda v: v.tensor_copy(out=vsb[:], in_=PB[7][0:P16, 128:256]), reads=[PB[7]], writes=[vsb])
        k.op("dve", lambda v: v.tensor_copy(out=bd_s[:], in_=PB[7][0:P16, 256:264]), reads=[PB[7]], writes=[bd_s])
        k.op("act", lambda a: a.activation(out=bdt_s[:, 0:4], in_=bd_s[:, 0:4], func=ACT.Exp, scale=-1.0), reads=[bd_s], writes=[bdt_s])
        k.op("dve", lambda v: v.tensor_scalar(out=bdt_s[:, 0:4], in0=bdt_s[:, 0:4], scalar1=1.0, scalar2=None, op0=ALU.add), reads=[bdt_s], writes=[bdt_s])
        k.op("dve", lambda v: v.reciprocal(out=beta_s[:], in_=bdt_s[:, 0:4]), reads=[bdt_s], writes=[beta_s])
        k.op("dve", lambda v: v.tensor_scalar(out=nbeta_s[:], in0=beta_s[:], scalar1=-1.0, scalar2=None, op0=ALU.mult), reads=[beta_s], writes=[nbeta_s])
        k.op("dve", lambda v: v.tensor_tensor(out=bd_s[:, 4:8], in0=bd_s[:, 4:8], in1=dtb_bc[0:P16, :], op=ALU.add), reads=[bd_s, dtb_bc], writes=[bd_s])
        k.op("act", lambda a: a.activation(out=bdt_s[:, 4:8], in_=bd_s[:, 4:8], func=ACT.Exp), reads=[bd_s], writes=[bdt_s])
        k.op("act", lambda a: a.activation(out=bdt_s[:, 4:8], in_=bdt_s[:, 4:8], func=ACT.Ln, bias=1.0, scale=1.0), reads=[bdt_s], writes=[bdt_s])
        k.op("dve", lambda v: v.tensor_tensor(out=g_s[:], in0=bdt_s[:, 4:8], in1=ea[0:P16, :], op=ALU.mult), reads=[bdt_s, ea], writes=[g_s])
        k.op("act", lambda a: a.activation(out=eg_s[:], in_=g_s[:], func=ACT.Exp), reads=[g_s], writes=[eg_s])
        k3s = contextlib.ExitStack()
        def sb3(name, shape, dt=F32):
            return k3s.enter_context(nc.sbuf_tensor(name, list(shape), dt))
        xp4 = sb3("xp4", [P16, 4, 1536]); cwb = sb3("cwb", [P16, 4, 1536]); tmpc = xp4
        acc_s = sb3("acc_s", [P16, 1536])
        k.dma("sp", xp4[:, 0:3, :], sconv_d[:, :, :], writes=[xp4], semkey="ld_xp4", nowaw=True)
        k.dma("sp", cwb[:].rearrange("p a b -> p (a b)"), convw_d.rearrange("a b -> (a b)").partition_broadcast(P16), writes=[cwb], semkey="ld_cwb")
        k.op("act", lambda a: a.copy(out=xp4[:, 3, :], in_=pqkv[:]), reads=[pqkv], writes=[xp4])
        k.dma("sp", convs_d[:, :, :], xp4[:, 1:4, :], reads=[xp4], semkey="st_smisc")
        k.op("dve", lambda v: v.tensor_tensor(out=tmpc[:], in0=xp4[:], in1=cwb[:], op=ALU.mult), reads=[xp4, cwb], writes=[tmpc])
        k.op("dve", lambda v: v.tensor_reduce(out=acc_s[:], in_=tmpc[:].rearrange("p j c -> p c j"), axis=AX.X, op=ALU.add), reads=[tmpc], writes=[acc_s])
        k.op("act", lambda a: a.activation(out=qkv_s[:].rearrange("p a b -> p (a b)"), in_=acc_s[:], func=ACT.Silu), reads=[acc_s], writes=[qkv_s])
        sqs = sb3("sqs", [P16, 8, 128]); sss = sb3("sss", [P16, 8])
        k.op("dve", lambda v: v.tensor_tensor(out=sqs[:], in0=qkv_s[:, 0:8, :], in1=qkv_s[:, 0:8, :], op=ALU.mult), reads=[qkv_s], writes=[sqs])
        k.op("dve", lambda v: v.tensor_reduce(out=sss[:], in_=sqs[:], axis=AX.X, op=ALU.add), reads=[sqs], writes=[sss])
        k.op("act", lambda a: a.activation(out=sss[:], in_=sss[:], func=ACT.Ln, bias=L2_EPS, scale=1.0), reads=[sss], writes=[sss])
        k.op("act", lambda a: a.activation(out=sss[:, 0:4], in_=sss[:, 0:4], func=ACT.Exp, bias=float(-0.5 * np.log(128.0)), scale=-0.5), reads=[sss], writes=[sss])
        k.op("act", lambda a: a.activation(out=sss[:, 4:8], in_=sss[:, 4:8], func=ACT.Exp, scale=-0.5), reads=[sss], writes=[sss])
        k.op("dve", lambda v: v.tensor_tensor(out=qkv_s[:, 0:8, :], in0=qkv_s[:, 0:8, :], in1=bc(sss[:].rearrange("p (a b) -> p a b", b=1), [P16, 8, 128]), op=ALU.mult),
             reads=[qkv_s, sss], writes=[qkv_s])
        k.barrier()
        k3s.close()
        k3s = contextlib.ExitStack()
        Ssb = sb3("Ssb", [128, P16 * 4, 128])
        sdv = sdelta_d.rearrange("b h k v -> k (b h) v")
        for i4 in range(4):
            k.dma("sp", Ssb[:, i4 * 16:(i4 + 1) * 16, :], sdv[:, i4 * 16:(i4 + 1) * 16, :], writes=[Ssb], semkey="ld_Ssb", nowaw=True)
        qkT_s = sb3("qkT_s", [128, 8, P16])
        for c in range(8):
            k.op("pe", lambda p, c=c: p.transpose(out=PB[0][:, c * P16:(c + 1) * P16], in_=qkv_s[:, c, :], identity=ident[0:P16, 0:P16]),
                 reads=[qkv_s, ident], writes=[PB[0]])
        k.op("dve", lambda v: v.tensor_copy(out=qkT_s[:].rearrange("p a b -> p (a b)"), in_=PB[0][:, 0:8 * P16]), reads=[PB[0]], writes=[qkT_s])
        dmask = sb3("dmask", [P16, P16, 128])
        k.op("pool", lambda g: g.tensor_copy(out=dmask[:], in_=bc(ident[0:P16, 0:P16].rearrange("p (a b) -> p a b", b=1), [P16, P16, 128])),
             reads=[ident], writes=[dmask])
        egm = sb3("egm", [P16, P16, 4]); egbc = sb3("egbc", [128, P16 * 4])
        k.op("dve", lambda v: v.tensor_tensor(out=egm[:], in0=bc(eg_s[:].rearrange("p (a b) -> p a b", a=1), [P16, P16, 4]),
                                              in1=bc(ident[0:P16, 0:P16].rearrange("p (a b) -> p a b", b=1), [P16, P16, 4]), op=ALU.mult),
             reads=[eg_s, ident], writes=[egm])
        k.op("pe", lambda p: p.matmul(PB[1][:, 0:P16 * 4], lhsT=ones[0:P16, :], rhs=egm[:].rearrange("p a b -> p (a b)"), start=True, stop=True),
             reads=[ones, egm], writes=[PB[1]])
        k.op("dve", lambda v: v.tensor_copy(out=egbc[:], in_=PB[1][:, 0:P16 * 4]), reads=[PB[1]], writes=[egbc])
        pred = sb3("pred", [P16, 4, 128]); qS = sb3("qS", [P16, 4, 128]); tmpd = sb3("tmpd", [P16, P16, 128])
        dd = sb3("dd", [P16, 4, 128]); Dm = sb3("Dm", [P16, P16, 128]); o_s = sb3("o_s", [P16, 4, 128])
        qk_s = sb3("qk_s", [P16, 4]); qkt = sb3("qkt", [P16, 4, 128])
        k.op("dve", lambda v: v.tensor_tensor(out=qkt[:], in0=qkv_s[:, 0:4, :], in1=qkv_s[:, 4:8, :], op=ALU.mult), reads=[qkv_s], writes=[qkt])
        k.op("dve", lambda v: v.tensor_reduce(out=qk_s[:], in_=qkt[:], axis=AX.X, op=ALU.add), reads=[qkt], writes=[qk_s])
        for h in range(4):
            for which, dst in ((4, pred), (0, qS)):
                banks = [PB[2], PB[3], PB[4], PB[5]] if which == 4 else [PB[6], PB[7], PB[0], PB[1]]
                for b in range(P16):
                    pb = banks[b // 4]
                    k.op("pe", lambda p, b=b, pb=pb, which=which, h=h: p.matmul(pb[0:P16, (b % 4) * 128:(b % 4 + 1) * 128], lhsT=qkT_s[:, which + h, :],
                                                                               rhs=Ssb[:, b * 4 + h, :], start=True, stop=True),
                         reads=[qkT_s, Ssb], writes=[pb])
                for j in range(4):
                    k.op("dve", lambda v, j=j, banks=banks: v.tensor_tensor(out=tmpd[:, 4 * j:4 * j + 4, :], in0=banks[j][0:P16, :].rearrange("p (a b) -> p a b", a=4),
                                                                            in1=dmask[:, 4 * j:4 * j + 4, :], op=ALU.mult), reads=[banks[j], dmask], writes=[tmpd])
                k.op("dve", lambda v, dst=dst, h=h: v.tensor_reduce(out=dst[:, h, :], in_=tmpd[:].rearrange("p b v -> p v b"), axis=AX.X, op=ALU.add),
                     reads=[tmpd], writes=[dst])
            k.op("dve", lambda v, h=h: v.scalar_tensor_tensor(out=dd[:, h, :], in0=pred[:, h, :], scalar=eg_s[:, h:h + 1], in1=qkv_s[:, 8 + h, :],
                                                               op0=ALU.mult, op1=ALU.subtract), reads=[pred, eg_s, qkv_s], writes=[dd])
            k.op("dve", lambda v, h=h: v.tensor_scalar(out=dd[:, h, :], in0=dd[:, h, :], scalar1=nbeta_s[:, h:h + 1], scalar2=None, op0=ALU.mult),
                 reads=[dd, nbeta_s], writes=[dd])
            k.op("dve", lambda v, h=h: v.tensor_scalar(out=o_s[:, h, :], in0=dd[:, h, :], scalar1=qk_s[:, h:h + 1], scalar2=None, op0=ALU.mult),
                 reads=[dd, qk_s], writes=[o_s])
            k.op("dve", lambda v, h=h: v.scalar_tensor_tensor(out=o_s[:, h, :], in0=qS[:, h, :], scalar=eg_s[:, h:h + 1], in1=o_s[:, h, :],
                                                               op0=ALU.mult, op1=ALU.add), reads=[qS, eg_s, o_s], writes=[o_s])
            k.op("dve", lambda v, h=h: v.tensor_tensor(out=Dm[:], in0=bc(dd[:, h:h + 1, :], [P16, P16, 128]), in1=dmask[:], op=ALU.mult),
                 reads=[dd, dmask], writes=[Dm])
            banks = [PB[2], PB[3], PB[4], PB[5]]
            for b in range(P16):
                pb = banks[b // 4]
                k.op("pe", lambda p, b=b, pb=pb, h=h: p.matmul(pb[:, (b % 4) * 128:(b % 4 + 1) * 128], lhsT=qkv_s[:, 4 + h, :], rhs=Dm[:, b, :],
                                                               start=True, stop=True), reads=[qkv_s, Dm], writes=[pb])
            for b in range(P16):
                pb = banks[b // 4]
                k.op("dve", lambda v, b=b, pb=pb, h=h: v.scalar_tensor_tensor(out=Ssb[:, b * 4 + h, :], in0=Ssb[:, b * 4 + h, :], scalar=egbc[:, b * 4 + h:b * 4 + h + 1],
                                                                              in1=pb[:, (b % 4) * 128:(b % 4 + 1) * 128], op0=ALU.mult, op1=ALU.add),
                     reads=[Ssb, egbc, pb], writes=[Ssb])
        ddv = deltas_d.rearrange("b h k v -> k (b h) v")
        for i4 in range(4):
            k.dma(["sp", "act", "sp", "act"][i4], ddv[:, i4 * 16:(i4 + 1) * 16, :], Ssb[:, i4 * 16:(i4 + 1) * 16, :], reads=[Ssb], semkey="st_smisc")
        oss_s = sb3("oss_s", [P16, 4])
        k.op("dve", lambda v: v.tensor_tensor(out=qkt[:], in0=o_s[:], in1=o_s[:], op=ALU.mult), reads=[o_s], writes=[qkt])
        k.op("dve", lambda v: v.tensor_reduce(out=oss_s[:], in_=qkt[:], axis=AX.X, op=ALU.add), reads=[qkt], writes=[oss_s])
        k.op("act", lambda a: a.activation(out=oss_s[:], in_=oss_s[:], func=ACT.Ln, bias=RMS_EPS, scale=1.0 / 128.0), reads=[oss_s], writes=[oss_s])
        k.op("act", lambda a: a.activation(out=oss_s[:], in_=oss_s[:], func=ACT.Exp, scale=-0.5), reads=[oss_s], writes=[oss_s])
        k.op("dve", lambda v: v.tensor_tensor(out=o_s[:], in0=o_s[:], in1=bc(oss_s[:].rearrange("p (a b) -> p a b", b=1), [P16, 4, 128]), op=ALU.mult),
             reads=[o_s, oss_s], writes=[o_s])
        k.op("dve", lambda v: v.tensor_tensor(out=o_s[:], in0=o_s[:], in1=bc(norma_bc[0:P16, :].rearrange("p (a b) -> p a b", a=1), [P16, 4, 128]), op=ALU.mult),
             reads=[o_s, norma_bc], writes=[o_s])
        k.op("dve", lambda v: v.tensor_tensor(out=mix_s[:, 0:512], in0=o_s[:].rearrange("p a b -> p (a b)"), in1=zAs_s[:], op=ALU.mult),
             reads=[o_s, zAs_s], writes=[mix_s])
        k.barrier()
        k3s.close()
        k3s = contextlib.ExitStack()
        Kc = sb3("Kc", [128, P16, 128]); Vc = sb3("Vc", [128, P16, 128])
        KcT = sb3("KcT", [128, P16, 128], BF16); VcB = sb3("VcB", [128, P16, 128], BF16)
        k.dma("sp", Kc[:], sk_d.rearrange("b s c -> s b c"), writes=[Kc], semkey="ld_Kc")
        k.dma("act", Vc[:], sv_d.rearrange("b s c -> s b c"), writes=[Vc], semkey="ld_Vc")
        k.dma("sp", swaks_d[:, 0:127, :], sk_d[:, 1:128, :], semkey="st_smisc")
        k.dma("act", swavs_d[:, 0:127, :], sv_d[:, 1:128, :], semkey="st_smisc")
        k.dma("sp", swaks_d[:, 127, :], kr_s[:], reads=[kr_s], semkey="st_smisc")
        k.dma("sp", swavs_d[:, 127, :], v_s[:], reads=[v_s], semkey="st_smisc")
        k.op("pool", lambda g: g.tensor_copy(out=VcB[:], in_=Vc[:]), reads=[Vc], writes=[VcB])
        for b in range(P16):
            pb = [PB[2], PB[3], PB[4], PB[5]][b // 4]
            k.op("pe", lambda p, b=b, pb=pb: p.transpose(out=pb[:, (b % 4) * 128:(b % 4 + 1) * 128], in_=Kc[:, b, :], identity=ident[:]),
                 reads=[Kc, ident], writes=[pb])
        for j in range(4):
            pb = [PB[2], PB[3], PB[4], PB[5]][j]
            k.op("act", lambda a, j=j, pb=pb: a.copy(out=KcT[:, 4 * j:4 * j + 4, :], in_=pb[:, :].rearrange("p (a b) -> p a b", a=4)), reads=[pb], writes=[KcT])
        qT_s = sb3("qT_s", [128, 4, P16]); Aq = sb3("Aq", [128, P16, 2, 4], BF16)
        knT = sb3("knT", [128, P16], BF16); zBT = sb3("zBT", [128, 4, P16])
        for c in range(4):
            k.op("pe", lambda p, c=c: p.transpose(out=PB[6][:, c * P16:(c + 1) * P16], in_=qr_s[:, c * 128:(c + 1) * 128], identity=ident[0:P16, 0:P16]),
                 reads=[qr_s, ident], writes=[PB[6]])
        k.op("pe", lambda p: p.transpose(out=PB[6][:, 4 * P16:5 * P16], in_=kr_s[:], identity=ident[0:P16, 0:P16]), reads=[kr_s, ident], writes=[PB[6]])
        for c in range(4):
            k.op("pe", lambda p, c=c: p.transpose(out=PB[6][:, (5 + c) * P16:(6 + c) * P16], in_=zBs_s[:, c * 128:(c + 1) * 128], identity=ident[0:P16, 0:P16]),
                 reads=[zBs_s, ident], writes=[PB[6]])
        k.op("dve", lambda v: v.tensor_copy(out=qT_s[:].rearrange("p a b -> p (a b)"), in_=PB[6][:, 0:4 * P16]), reads=[PB[6]], writes=[qT_s])
        k.op("dve", lambda v: v.tensor_copy(out=knT[:], in_=PB[6][:, 4 * P16:5 * P16]), reads=[PB[6]], writes=[knT])
        k.op("dve", lambda v: v.tensor_copy(out=zBT[:].rearrange("p a b -> p (a b)"), in_=PB[6][:, 5 * P16:9 * P16]), reads=[PB[6]], writes=[zBT])
        k.op("pool", lambda g: g.memset(Aq[:], 0.0), writes=[Aq])
        k.op("dve", lambda v: v.tensor_copy(out=Aq[0:64, :, 0, :], in_=qT_s[0:64, :, :].rearrange("p c b -> p b c")), reads=[qT_s], writes=[Aq])
        k.op("dve", lambda v: v.tensor_copy(out=Aq[64:128, :, 1, :], in_=qT_s[64:128, :, :].rearrange("p c b -> p b c")), reads=[qT_s], writes=[Aq])
        for b in range(P16):
            k.op("pe", lambda p, b=b: p.matmul(PB[7][:, b * 8:(b + 1) * 8], lhsT=KcT[:, b, :], rhs=Aq[:, b, :, :].rearrange("p a b -> p (a b)"),
                                               start=True, stop=True), reads=[KcT, Aq], writes=[PB[7]])
        STs = sb3("STs", [128, 128])
        k.op("dve", lambda v: v.tensor_copy(out=STs[:], in_=PB[7][:, 0:128]), reads=[PB[7]], writes=[STs])
        k.op("pe", lambda p: p.transpose(out=PB[0][:, 0:128], in_=STs[:], identity=ident[:]), reads=[STs, ident], writes=[PB[0]])
        k.op("pe", lambda p: p.matmul(PB[0][:, 128:128 + P16], lhsT=Aq[:].rearrange("p a b c -> p (a b c)"), rhs=knT[:], start=True, stop=True),
             reads=[Aq, knT], writes=[PB[0]])
        M2 = sb3("M2", [128, P16]); M2t = sb3("M2t", [128, P16])
        k.op("pool", lambda g: g.affine_select(out=M2t[:], in_=ones[:, 0:P16], pattern=[[-8, P16]], compare_op=ALU.is_ge, fill=0.0, base=0, channel_multiplier=1),
             reads=[ones], writes=[M2t])
        k.op("pool", lambda g: g.affine_select(out=M2[:], in_=M2t[:], pattern=[[8, P16]], compare_op=ALU.is_ge, fill=0.0, base=7, channel_multiplier=-1),
             reads=[M2t], writes=[M2])
        sm = sb3("sm", [128, 16]); tmps = sb3("tmps", [128, P16])
        sinkcol = sb3("sinkcol", [128, 1])
        k.dma("sp", sinkcol[:], sinkcol_d[:, :], writes=[sinkcol], semkey="ld_sinkcol")
        k.op("dve", lambda v: v.tensor_tensor(out=tmps[:], in0=PB[0][:, 128:128 + P16], in1=M2[:], op=ALU.mult), reads=[PB[0], M2], writes=[tmps])
        k.op("dve", lambda v: v.tensor_reduce(out=sm[:, 0:1], in_=tmps[:], axis=AX.X, op=ALU.add), reads=[tmps], writes=[sm])
        k.op("dve", lambda v: v.tensor_reduce(out=sm[:, 1:2], in_=PB[0][:, 0:128], axis=AX.X, op=ALU.max), reads=[PB[0]], writes=[sm])
        k.op("dve", lambda v: v.tensor_tensor(out=sm[:, 1:2], in0=sm[:, 1:2], in1=sm[:, 0:1], op=ALU.max), reads=[sm], writes=[sm])
        k.op("dve", lambda v: v.tensor_tensor(out=sm[:, 1:2], in0=sm[:, 1:2], in1=sinkcol[:], op=ALU.max), reads=[sm, sinkcol], writes=[sm])
        k.op("dve", lambda v: v.tensor_scalar(out=sm[:, 2:3], in0=sm[:, 1:2], scalar1=-1.0, scalar2=None, op0=ALU.mult), reads=[sm], writes=[sm])
        Ps = sb3("Ps", [128, 128])
        k.op("act", lambda a: a.activation(out=Ps[:], in_=PB[0][:, 0:128], func=ACT.Exp, bias=sm[:, 2:3], scale=1.0, accum_out=sm[:, 3:4]),
             reads=[PB[0], sm], writes=[Ps, sm])
        k.op("act", lambda a: a.activation(out=sm[:, 4:5], in_=sm[:, 0:1], func=ACT.Exp, bias=sm[:, 2:3], scale=1.0), reads=[sm], writes=[sm])
        k.op("act", lambda a: a.activation(out=sm[:, 5:6], in_=sinkcol[:], func=ACT.Exp, bias=sm[:, 2:3], scale=1.0), reads=[sm, sinkcol], writes=[sm])
        k.op("dve", lambda v: v.tensor_tensor(out=sm[:, 6:7], in0=sm[:, 3:4], in1=sm[:, 4:5], op=ALU.add), reads=[sm], writes=[sm])
        k.op("dve", lambda v: v.tensor_tensor(out=sm[:, 6:7], in0=sm[:, 6:7], in1=sm[:, 5:6], op=ALU.add), reads=[sm], writes=[sm])
        k.op("dve", lambda v: v.reciprocal(out=sm[:, 7:8], in_=sm[:, 6:7]), reads=[sm], writes=[sm])
        k.op("dve", lambda v: v.tensor_scalar(out=Ps[:], in0=Ps[:], scalar1=sm[:, 7:8], scalar2=None, op0=ALU.mult), reads=[Ps, sm], writes=[Ps])
        k.op("dve", lambda v: v.tensor_tensor(out=sm[:, 8:9], in0=sm[:, 4:5], in1=sm[:, 7:8], op=ALU.mult), reads=[sm], writes=[sm])
        PsT = sb3("PsT", [128, 128], BF16); Wd = sb3("Wd", [128, P16]); Wn = sb3("Wn", [P16, 128], BF16)
        k.op("pe", lambda p: p.transpose(out=PB[1][:, 0:128], in_=Ps[:], identity=ident[:]), reads=[Ps, ident], writes=[PB[1]])
        k.op("act", lambda a: a.copy(out=PsT[:], in_=PB[1][:, 0:128]), reads=[PB[1]], writes=[PsT])
        k.op("dve", lambda v: v.tensor_scalar(out=Wd[:], in0=M2[:], scalar1=sm[:, 8:9], scalar2=None, op0=ALU.mult), reads=[M2, sm], writes=[Wd])
        k.op("pe", lambda p: p.transpose(out=PB[1][0:P16, 128:256], in_=Wd[:], identity=ident[:]), reads=[Wd, ident], writes=[PB[1]])
        k.op("act", lambda a: a.copy(out=Wn[:], in_=PB[1][0:P16, 128:256]), reads=[PB[1]], writes=[Wn])
        for b in range(P16):
            k.op("pe", lambda p, b=b: p.matmul(PB[2][:, b * 8:(b + 1) * 8], lhsT=VcB[:, b, :], rhs=PsT[:, b * 8:(b + 1) * 8], start=True, stop=False),
                 reads=[VcB, PsT], writes=[PB[2]])
            k.op("pe", lambda p, b=b: p.matmul(PB[2][:, b * 8:(b + 1) * 8], lhsT=vsb[:], rhs=Wn[:, b * 8:(b + 1) * 8], start=False, stop=True),
                 reads=[vsb, Wn], writes=[PB[2]])
        obT = sb3("obT", [128, 4, P16])
        OT4 = PB[2][:, 0:128].rearrange("p (b k c) -> p b k c", b=P16, k=2)
        k.op("dve", lambda v: v.tensor_copy(out=obT[0:64, :, :].rearrange("p c b -> p b c"), in_=OT4[0:64, :, 0, :]), reads=[PB[2]], writes=[obT])
        k.op("dve", lambda v: v.tensor_copy(out=obT[64:128, :, :].rearrange("p c b -> p b c"), in_=OT4[64:128, :, 1, :]), reads=[PB[2]], writes=[obT])
        mixT_s = sb3("mixT_s", [128, 8, P16], BF16)
        k.op("dve", lambda v: v.tensor_tensor(out=mixT_s[:, 4:8, :], in0=obT[:], in1=zBT[:], op=ALU.mult), reads=[obT, zBT], writes=[mixT_s])
        for c in range(4):
            k.op("pe", lambda p, c=c: p.transpose(out=PB[3][:, c * P16:(c + 1) * P16], in_=mix_s[:, c * 128:(c + 1) * 128], identity=ident[0:P16, 0:P16]),
                 reads=[mix_s, ident], writes=[PB[3]])
        k.op("dve", lambda v: v.tensor_copy(out=mixT_s[:, 0:4, :].rearrange("p a b -> p (a b)"), in_=PB[3][:, 0:4 * P16]), reads=[PB[3]], writes=[mixT_s])
        for j in range(2):
            for kc in range(8):
                k.op("pe", lambda p, j=j, kc=kc: p.matmul(PB[4 + j][0:P16, :], lhsT=mixT_s[:, kc, :], rhs=Wo[:, kc, j * 512:(j + 1) * 512],
                                                          start=(kc == 0), stop=(kc == 7)), reads=[mixT_s, Wo], writes=[PB[4 + j]])
        ypre_s = sb3("ypre_s", [P16, D]); yo_s = sb3("yo_s", [P16, D]); st_s = sb3("st_s", [P16, 8])
        for j in range(2):
            k.op("dve", lambda v, j=j: v.scalar_tensor_tensor(out=ypre_s[:, j * 512:(j + 1) * 512], in0=mod_s[:, 2 * D + j * 512:2 * D + (j + 1) * 512], scalar=1.0,
                                                               in1=PB[4 + j][0:P16, :], op0=ALU.add, op1=ALU.mult), reads=[mod_s, PB[4 + j]], writes=[ypre_s])
        k.op("dve", lambda v: v.scalar_tensor_tensor(out=ypre_s[:], in0=xs[:], scalar=ALPHA, in1=ypre_s[:], op0=ALU.mult, op1=ALU.add),
             reads=[xs, ypre_s], writes=[ypre_s])
        k.op("act", lambda a: a.activation(out=yo_s[:], in_=ypre_s[:], func=ACT.Identity, accum_out=st_s[:, 0:1]), reads=[ypre_s], writes=[yo_s, st_s])
        k.op("act", lambda a: a.activation(out=yo_s[:], in_=ypre_s[:], func=ACT.Square, accum_out=st_s[:, 1:2]), reads=[ypre_s], writes=[yo_s, st_s])
        k.op("dve", lambda v: v.tensor_scalar(out=st_s[:, 2:3], in0=st_s[:, 0:1], scalar1=1.0 / D, scalar2=None, op0=ALU.mult), reads=[st_s], writes=[st_s])
        k.op("dve", lambda v: v.tensor_tensor(out=st_s[:, 3:4], in0=st_s[:, 2:3], in1=st_s[:, 2:3], op=ALU.mult), reads=[st_s], writes=[st_s])
        k.op("dve", lambda v: v.scalar_tensor_tensor(out=st_s[:, 4:5], in0=st_s[:, 1:2], scalar=1.0 / D, in1=st_s[:, 3:4], op0=ALU.mult, op1=ALU.subtract),
             reads=[st_s], writes=[st_s])
        k.op("act", lambda a: a.activation(out=st_s[:, 5:6], in_=st_s[:, 4:5], func=ACT.Ln, bias=LN_EPS, scale=1.0), reads=[st_s], writes=[st_s])
        k.op("act", lambda a: a.activation(out=st_s[:, 5:6], in_=st_s[:, 5:6], func=ACT.Exp, scale=-0.5), reads=[st_s], writes=[st_s])
        k.op("dve", lambda v: v.scalar_tensor_tensor(out=st_s[:, 6:7], in0=st_s[:, 2:3], scalar=-1.0, in1=st_s[:, 5:6], op0=ALU.mult, op1=ALU.mult),
             reads=[st_s], writes=[st_s])
        k.op("act", lambda a: a.activation(out=yo_s[:], in_=ypre_s[:], func=ACT.Identity, bias=st_s[:, 6:7], scale=st_s[:, 5:6]), reads=[ypre_s, st_s], writes=[yo_s])
        k.op("dve", lambda v: v.tensor_tensor(out=yo_s[:], in0=yo_s[:], in1=lng_bc[0:P16, :], op=ALU.mult), reads=[yo_s, lng_bc], writes=[yo_s])
        k.op("dve", lambda v: v.tensor_tensor(out=yo_s[:], in0=yo_s[:], in1=lnb_bc[0:P16, :], op=ALU.add), reads=[yo_s, lnb_bc], writes=[yo_s])
        k.dma("sp", ys_d[:, :], yo_s[:], reads=[yo_s], semkey="st_smisc")
        k.barrier()
        k3s.close()

    k.barrier()
    k2.close()
    k1s.close()
    identb = k.sb("identb", [128, 128], BF16); Ub = k.sb("Ub", [128, 128], BF16)
    k.op("pool", lambda g: g.tensor_copy(out=identb[:], in_=ident[:]), reads=[ident], writes=[identb])
    k.op("pool", lambda g: g.tensor_copy(out=Ub[:], in_=U[:]), reads=[U], writes=[Ub])
    Xt = [k.sb("Xt%d" % i, [128, D]) for i in range(2)]
    Xb = k.sb("Xb", [128, D], BF16)
    hT = k.sb("hT", [128, 8, 128], BF16)
    pre = k.sb("pre", [128, 12, 131])
    k.op("pool", lambda g: g.memset(pre[:], 0.0), writes=[pre])
    cm = [k.sb("cm%d" % i, [128, 12, 128]) for i in range(2)]
    qkvs = cm[0]
    qkb2 = [k.sb("qkb%d" % i, [128, 12, 128], BF16) for i in range(2)]
    sqb = k.sb("sqb", [128, 8, 128], BF16)
    lnss = k.sb("lnss", [128, 8, 128])
    rn = lnss
    zAs = k.sb("zAs", [128, 512]); zBs2 = [k.sb("zBs%d" % i, [128, 512]) for i in range(2)]
    qrb2 = [k.sb("qrb%d" % i, [128, 512], BF16) for i in range(2)]; kr2 = [k.sb("kr%d" % i, [128, 128]) for i in range(2)]
    rt = [k.sb("rt%d" % i, [128, 8, 32]) for i in range(4)]
    vB = [k.sb("vB%d" % i, [128, 128], BF16) for i in range(3)]
    vB32 = k.sb("vB32", [128, 128])
    kTb = [k.sb("kTb%d" % i, [128, 128], BF16) for i in range(2)]
    k.op("pool", lambda g: g.memset(kTb[1][:], 0.0), writes=[kTb[1]])
    k.op("pool", lambda g: g.memset(vB[2][:], 0.0), writes=[vB[2]])
    qTb = k.sb("qTb", [128, 4, 128], BF16)
    bd = k.sb("bd", [128, 8]); bdt = k.sb("bdt", [128, 8])
    beta2 = [k.sb("beta%d" % i, [128, 4]) for i in range(2)]; negbeta2 = [k.sb("negbeta%d" % i, [128, 4]) for i in range(2)]
    gg2 = [k.sb("gg%d" % i, [128, 4]) for i in range(2)]
    gsp = k.sb("gsp", [128, 8], BF16); gtmp = k.sb("gtmp", [128, 4])
    gUh = k.sb("gUh", [128, 4, 128], BF16); gUl = k.sb("gUl", [128, 4, 128], BF16)
    Gc = k.sb("Gc", [128, 4]); negGc = k.sb("negGc", [128, 4]); eG = k.sb("eG", [128, 4]); beG = k.sb("beG", [128, 4])
    edG = k.sb("edG", [128, 4]); egl = k.sb("egl", [128, 4]); dG = k.sb("dG", [128, 4]); Glb = k.sb("Glb", [128, 4])
    tmp1 = k.sb("tmp1", [128, 4, 128]); tmp2 = k.sb("tmp2", [128, 4, 128])
    E1 = tmp1; E2 = tmp2
    eGbc = k.sb("eGbc", [128, 4, 128]); qdT = k.sb("qdT", [128, 4, 128], BF16)
    AqkT = k.sb("AqkT", [128, 4, 128], BF16)
    Y0f = k.sb("Y0f", [128, 4, 128])
    XRh = [[k.sb("XRh%d_%d" % (g_, i), [128, 2, 256], BF16) for i in range(2)] for g_ in range(2)]
    XRl = [[k.sb("XRl%d_%d" % (g_, i), [128, 2, 256], BF16) for i in range(2)] for g_ in range(2)]
    Yh = [[k.sb("Yh%d_%d" % (g_, i), [128, 2, 128], BF16) for i in range(2)] for g_ in range(2)]
    Yl = [[k.sb("Yl%d_%d" % (g_, i), [128, 2, 128], BF16) for i in range(2)] for g_ in range(2)]
    Rf = [k.sb("Rf%d" % g_, [128, 2, 128]) for g_ in range(2)]
    kbt = k.sb("kbt", [128, 4, 128], BF16); kdt = k.sb("kdt", [128, 4, 128], BF16); bvt = k.sb("bvt", [128, 4, 128], BF16)
    wT = [k.sb("wT%d" % g_, [128, 2, 128], BF16) for g_ in range(2)]
    ut = [k.sb("ut%d" % g_, [128, 2, 128]) for g_ in range(2)]
    uub = [k.sb("uub%d" % g_, [128, 2, 128], BF16) for g_ in range(2)]
    S = [k.sb("S%d" % g_, [128, 2, 128]) for g_ in range(2)]
    Sb = [k.sb("Sb%d" % g_, [128, 2, 128], BF16) for g_ in range(2)]
    for g_ in range(2):
        k.op("pool", lambda g, g_=g_: g.memset(S[g_][:], 0.0), writes=[S[g_]])
        k.op("pool", lambda g, g_=g_: g.memset(Sb[g_][:], 0.0), writes=[Sb[g_]])
    oss = [k.sb("oss%d" % g_, [128, 2]) for g_ in range(2)]; orr = [k.sb("orr%d" % g_, [128, 2]) for g_ in range(2)]
    nz = k.sb("nz", [128, 4, 128])
    mixb = k.sb("mixb", [128, D], BF16); obt = k.sb("obt", [128, 512])
    ojunk = [tmp1[:, 0, :], tmp2[:, 0, :]]
    mixT = k.sb("mixT", [128, 8, 128], BF16)
    SC = k.sb("SC", [128, 8, 256]); Pb = k.sb("Pb", [128, 8, 256], BF16)
    PTb = k.sb("PTb", [128, 16, 128], BF16)
    mx = k.sb("mx", [128, 8]); negm = k.sb("negm", [128, 8]); rs = k.sb("rs", [128, 8]); es_ = k.sb("es_", [128, 8])
    rden = k.sb("rden", [128, 8])
    SCf = SC[:].rearrange("p a b -> p (a b)")
    ypre = SCf[:, 0:D]
    st = k.sb("st", [128, 8]); negpad = k.sb("negpad", [128, 1])
    Yt = [SCf[:, D:2 * D]]

    def v3(ap, a):
        return ap.rearrange("p (a b) -> p a b", a=a)

    def pbf(pb):
        return pb[:, :].bitcast(BF16)

    def mm(out, lhsT, rhs, start, stop, reads, writes):
        ncol = int(np.prod(out.shape[1:]))
        k.op("pe", lambda p: p.matmul(out, lhsT=lhsT, rhs=rhs, start=start, stop=stop), reads=reads, writes=writes,
             cost=0.14 + ncol / 1200.0)

    def tr(out, in_, idn, reads, writes):
        k.op("pe", lambda p: p.transpose(out=out, in_=in_, identity=idn), reads=reads + [idn], writes=writes, cost=0.25)

    k.barrier()
    k.pe_inorder = True
    krb = k.sb("krb", [128, 128], BF16)
    gdone = {}

    def P1a(n):
        f_ = full(n); sp_ = swaprep(n)
        qkb = qkb2[n % 2]; beta = beta2[n % 2]; negbeta = negbeta2[n % 2]; gg = gg2[n % 2]
        c0 = 0 if f_ else 4
        X = Xt[n % 2]
        cs_t = cst[n % 2]
        zBs = zBs2[n % 2]; qrb = qrb2[n % 2]; kr = kr2[n % 2]
        vcur = vB[n % 3]
        k.dma("pool", Xb[:], x_d[n * 128:(n + 1) * 128, :], writes=[Xb], semkey="ld_Xb")
        if f_:
            k.dma("sp", X[:], x_d[n * 128:(n + 1) * 128, :], writes=[X], semkey="ld_" + X.name)
        if f_ or sp_:
            k.dma("sp", cs_t[:, 0, :], cosk_d[n * 128:(n + 1) * 128, :], writes=[cs_t], semkey="ld_" + cs_t.name)
            k.dma("sp", cs_t[:, 1, :], sink_d[n * 128:(n + 1) * 128, :], writes=[cs_t], semkey="ld_" + cs_t.name, nowaw=True)
        yield
        P3b = pbf(PB[2])
        yield
        for kc in range(8):
            tr(P3b[:, kc * 128:(kc + 1) * 128], Xb[:, kc * 128:(kc + 1) * 128], identb[:], [Xb], [PB[2]])
        yield
        for kc in range(8):
            k.op("act" if kc % 2 == 0 else "dve",
                 (lambda a, kc=kc: a.activation(out=hT[:, kc, :], in_=P3b[:, kc * 128:(kc + 1) * 128], func=ACT.Identity,
                                                bias=mod_fm[:, kc:kc + 1], scale=mod_fm[:, 8 + kc:9 + kc])) if kc % 2 == 0 else
                 (lambda v, kc=kc: v.tensor_scalar(out=hT[:, kc, :], in0=P3b[:, kc * 128:(kc + 1) * 128], scalar1=mod_fm[:, 8 + kc:9 + kc],
                                                   scalar2=mod_fm[:, kc:kc + 1], op0=ALU.mult, op1=ALU.add)),
                 reads=[PB[2], mod_fm], writes=[hT])
            if kc % 2 == 1:
                yield
        pc = pre
        k.op("pool", lambda g: g.tensor_copy(out=pc[:, :, 0:3], in_=pc[:, :, 128:131]), reads=[pc], writes=[pc])
        yield
        for j in range(3):
            if j == 0 and not (f_ or sp_):
                continue
            pb = [PB[3], PB[2], PB[3]][j]
            for c4 in range(4):
                c = j * 4 + c4
                for kc in range(8):
                    mm(pb[:, c4 * 128:(c4 + 1) * 128], Wf[:, kc, c * 128:(c + 1) * 128], hT[:, kc, :], kc == 0, kc == 7, [Wf, hT], [pb])
                yield
            if j != 1:
                k.op("act", lambda a, j=j, pb=pb: a.activation(out=pc[:, j * 4:(j + 1) * 4, 3:131], in_=v3(pb[:, :], 4), func=ACT.Identity,
                                                               scale=valid_bc[:, n:n + 1]), reads=[pb, valid_bc], writes=[pc])
            else:
                k.op("dve", lambda v, j=j, pb=pb: v.tensor_scalar(out=pc[:, j * 4:(j + 1) * 4, 3:131], in0=v3(pb[:, :], 4), scalar1=valid_bc[:, n:n + 1],
                                                                  scalar2=None, op0=ALU.mult), reads=[pb, valid_bc], writes=[pc])
            yield
        nch = 12 - c0
        k.op("dve", lambda v: v.tensor_tensor(out=cm[0][:, c0:12, :], in0=pc[:, c0:12, 0:128], in1=bc(cwT[:, c0:12].rearrange("p (a b) -> p a b", b=1), [128, nch, 128]), op=ALU.mult),
             reads=[pc, cwT], writes=[cm[0]], cost=0.13 * nch)
        yield
        for j in range(1, 4):
            k.op("pool", lambda g, j=j: g.tensor_tensor(out=cm[1][:, c0:12, :], in0=pc[:, c0:12, j:j + 128], in1=bc(cwT[:, j * 12 + c0:(j + 1) * 12].rearrange("p (a b) -> p a b", b=1), [128, nch, 128]), op=ALU.mult),
                 reads=[pc, cwT], writes=[cm[1]], cost=0.25 * nch)
            yield
            k.op("dve", lambda v: v.tensor_tensor(out=cm[0][:, c0:12, :], in0=cm[0][:, c0:12, :], in1=cm[1][:, c0:12, :], op=ALU.add), reads=[cm[0], cm[1]], writes=[cm[0]],
                 cost=0.13 * nch)
            yield
        k.op("act", lambda a: a.activation(out=qkvs[:, c0:12, :], in_=cm[0][:, c0:12, :], func=ACT.Silu), reads=[cm[0]], writes=[qkvs], cost=0.12 * nch)
        yield
        offs = [(0, 512), (512, 512), (1024, 512), (1536, 264)]

        def tproj(j, pb):
            o, w_ = offs[j]
            for kc in range(8):
                mm(pb[:, 0:w_], hT[:, kc, :], Wt[:, kc, o:o + w_], kc == 0, kc == 7, [hT, Wt], [pb])
        PzA, PqB = PB[2], PB[3]
        if f_:
            tproj(0, PzA)
            yield
            k.op("act", lambda a: a.activation(out=zAs[:], in_=PzA[:, :], func=ACT.Silu), reads=[PzA], writes=[zAs])
            yield
            tproj(1, PqB)
            yield
            q3 = v3(PqB[:, :], 8)
            qr3 = v3(qrb[:], 8)
            cq = bc(cs_t[:, 0:1, :], [128, 8, 32]); sq_ = bc(cs_t[:, 1:2, :], [128, 8, 32])
            k.op("dve", lambda v: v.tensor_tensor(out=rt[0][:], in0=q3[:, :, 0:32], in1=cq, op=ALU.mult), reads=[PqB, cs_t], writes=[rt[0]])
            k.op("dve", lambda v: v.tensor_tensor(out=rt[1][:], in0=q3[:, :, 32:64], in1=sq_, op=ALU.mult), reads=[PqB, cs_t], writes=[rt[1]])
            yield
            k.op("dve", lambda v: v.tensor_tensor(out=rt[2][:], in0=q3[:, :, 32:64], in1=cq, op=ALU.mult), reads=[PqB, cs_t], writes=[rt[2]])
            k.op("dve", lambda v: v.tensor_tensor(out=rt[3][:], in0=q3[:, :, 0:32], in1=sq_, op=ALU.mult), reads=[PqB, cs_t], writes=[rt[3]])
            yield
            k.op("pool", lambda g: g.tensor_tensor(out=qr3[:, :, 0:32], in0=rt[0][:], in1=rt[1][:], op=ALU.subtract), reads=[rt[0], rt[1]], writes=[qrb])
            k.op("pool", lambda g: g.tensor_tensor(out=qr3[:, :, 32:64], in0=rt[2][:], in1=rt[3][:], op=ALU.add), reads=[rt[2], rt[3]], writes=[qrb])
            yield
        PzB, Pk = PB[2], PB[3]
        if f_:
            tproj(2, PzB)
            yield
            k.op("act", lambda a: a.activation(out=zBs[:], in_=PzB[:, :], func=ACT.Silu), reads=[PzB], writes=[zBs])
            yield
        tproj(3, Pk)
        yield
        if f_ or sp_:
            k3 = v3(Pk[:, 0:128], 2)
            kr3 = v3(kr[:], 2)
            ck = bc(cs_t[:, 0:1, :], [128, 2, 32]); sk_ = bc(cs_t[:, 1:2, :], [128, 2, 32])
            k.op("dve", lambda v: v.tensor_tensor(out=rt[0][:, 0:2, :], in0=k3[:, :, 0:32], in1=ck, op=ALU.mult), reads=[Pk, cs_t], writes=[rt[0]])
            k.op("dve", lambda v: v.tensor_tensor(out=rt[1][:, 0:2, :], in0=k3[:, :, 32:64], in1=sk_, op=ALU.mult), reads=[Pk, cs_t], writes=[rt[1]])
            yield
            k.op("dve", lambda v: v.tensor_tensor(out=rt[2][:, 0:2, :], in0=k3[:, :, 32:64], in1=ck, op=ALU.mult), reads=[Pk, cs_t], writes=[rt[2]])
            k.op("dve", lambda v: v.tensor_tensor(out=rt[3][:, 0:2, :], in0=k3[:, :, 0:32], in1=sk_, op=ALU.mult), reads=[Pk, cs_t], writes=[rt[3]])
            yield
            k.op("pool", lambda g: g.tensor_tensor(out=kr3[:, :, 0:32], in0=rt[0][:, 0:2, :], in1=rt[1][:, 0:2, :], op=ALU.subtract), reads=[rt[0], rt[1]], writes=[kr])
            k.op("pool", lambda g: g.tensor_tensor(out=kr3[:, :, 32:64], in0=rt[2][:, 0:2, :], in1=rt[3][:, 0:2, :], op=ALU.add), reads=[rt[2], rt[3]], writes=[kr])
            yield
            k.op("dve", lambda v: v.tensor_copy(out=vcur[:], in_=Pk[:, 128:256]), reads=[Pk], writes=[vcur])
            if n == NT - 1:
                k.op("dve", lambda v: v.tensor_copy(out=vB32[:], in_=Pk[:, 128:256]), reads=[Pk], writes=[vB32])
        k.op("dve", lambda v: v.tensor_copy(out=bd[:], in_=Pk[:, 256:264]), reads=[Pk], writes=[bd])
        yield
        j0 = 0 if f_ else 1
        k.op("pool", lambda g: g.tensor_tensor(out=sqb[:, c0:8, :], in0=qkvs[:, c0:8, :], in1=qkvs[:, c0:8, :], op=ALU.mult), reads=[qkvs], writes=[sqb],
             cost=0.25 * (8 - c0))
        yield
        for j in range(j0, 2):
            mm(PB[2 + j][:, :], onesb[:], sqb[:, j * 4:(j + 1) * 4, :].rearrange("p a b -> p (a b)"), True, True, [onesb, sqb], [PB[2 + j]])
        yield
        for j in range(j0, 2):
            k.op("act", lambda a, j=j: a.activation(out=lnss[:, j * 4:(j + 1) * 4, :].rearrange("p a b -> p (a b)"), in_=PB[2 + j][:, :],
                                                    func=ACT.Ln, bias=L2_EPS, scale=1.0), reads=[PB[2 + j]], writes=[lnss])
        yield
        if f_:
            k.op("act", lambda a: a.activation(out=rn[:, 0:4, :], in_=lnss[:, 0:4, :], func=ACT.Exp, bias=float(-0.5 * np.log(128.0)), scale=-0.5), reads=[lnss], writes=[rn])
        k.op("act", lambda a: a.activation(out=rn[:, 4:8, :], in_=lnss[:, 4:8, :], func=ACT.Exp, scale=-0.5), reads=[lnss], writes=[rn])
        yield
        k.op("dve", lambda v: v.tensor_tensor(out=qkvs[:, c0:8, :], in0=qkvs[:, c0:8, :], in1=rn[:, c0:8, :], op=ALU.mult), reads=[qkvs, rn], writes=[qkvs],
             cost=0.13 * (8 - c0))
        yield
        k.op("pool", lambda g: g.tensor_copy(out=qkb[:, c0:12, :], in_=qkvs[:, c0:12, :]), reads=[qkvs], writes=[qkb], cost=0.25 * nch)
        yield
        k.op("act", lambda a: a.activation(out=bdt[:, 0:4], in_=bd[:, 0:4], func=ACT.Exp, scale=-1.0), reads=[bd], writes=[bdt])
        k.op("dve", lambda v: v.tensor_scalar(out=bdt[:, 0:4], in0=bdt[:, 0:4], scalar1=1.0, scalar2=None, op0=ALU.add), reads=[bdt], writes=[bdt])
        k.op("dve", lambda v: v.reciprocal(out=beta[:], in_=bdt[:, 0:4]), reads=[bdt], writes=[beta])
        k.op("dve", lambda v: v.tensor_scalar(out=beta[:], in0=beta[:], scalar1=valid_bc[:, n:n + 1], scalar2=None, op0=ALU.mult), reads=[beta, valid_bc], writes=[beta])
        k.op("dve", lambda v: v.tensor_scalar(out=negbeta[:], in0=beta[:], scalar1=-1.0, scalar2=None, op0=ALU.mult), reads=[beta], writes=[negbeta])
        yield
        k.op("dve", lambda v: v.tensor_tensor(out=bd[:, 4:8], in0=bd[:, 4:8], in1=dtb_bc[:], op=ALU.add), reads=[bd, dtb_bc], writes=[bd])
        k.op("act", lambda a: a.activation(out=bdt[:, 4:8], in_=bd[:, 4:8], func=ACT.Exp), reads=[bd], writes=[bdt])
        k.op("act", lambda a: a.activation(out=bdt[:, 4:8], in_=bdt[:, 4:8], func=ACT.Ln, bias=1.0, scale=1.0), reads=[bdt], writes=[bdt])
        k.op("dve", lambda v: v.tensor_tensor(out=gg[:], in0=bdt[:, 4:8], in1=ea[:], op=ALU.mult), reads=[bdt, ea], writes=[gg])
        yield

    def P1b(n, pl=False):
        f_ = full(n)
        qkb = qkb2[n % 2]; beta = beta2[n % 2]; negbeta = negbeta2[n % 2]; gg = gg2[n % 2]
        Bsm, Bx0, Bkv = (PB[0], PB[0], PB[1]) if pl else (PB[1], PB[0], PB[1])
        yield
        if f_:
            k.op("pool", lambda g: g.tensor_tensor(out=nz[:], in0=v3(zAs[:], 4), in1=bc(norma_bc[:].rearrange("p (a b) -> p a b", a=1), [128, 4, 128]), op=ALU.mult),
                 reads=[zAs, norma_bc], writes=[nz])
        yield
        k.op("dve", lambda v: v.tensor_copy(out=gsp[:, 0:4], in_=gg[:]), reads=[gg], writes=[gsp])
        yield
        k.op("dve", lambda v: v.tensor_tensor(out=gtmp[:], in0=gg[:], in1=gsp[:, 0:4], op=ALU.subtract), reads=[gg, gsp], writes=[gtmp])
        yield
        k.op("dve", lambda v: v.tensor_copy(out=gsp[:, 4:8], in_=gtmp[:]), reads=[gtmp], writes=[gsp])
        Ub4 = bc(Ub[:].rearrange("p (a b) -> p a b", a=1), [128, 4, 128])
        yield
        k.op("pool", lambda g: g.tensor_tensor(out=gUh[:], in0=Ub4, in1=bc(gsp[:, 0:4].rearrange("p (a b) -> p a b", b=1), [128, 4, 128]), op=ALU.mult),
             reads=[Ub, gsp], writes=[gUh])
        yield
        k.op("pool", lambda g: g.tensor_tensor(out=gUl[:], in0=Ub4, in1=bc(gsp[:, 4:8].rearrange("p (a b) -> p a b", b=1), [128, 4, 128]), op=ALU.mult),
             reads=[Ub, gsp], writes=[gUl])
        PG, PK_, PQ = (PB[1], PB[0], PB[5]) if pl else (PB[3], PB[4], PB[5])
        yield
        mm(Bsm[:, 0:8], Ub[:], gsp[:], True, True, [Ub, gsp], [Bsm])
        yield
        mm(Bsm[:, 8:16], onesb[:], gsp[:], True, True, [onesb, gsp], [Bsm])
        yield
        mm(PG[:, :], onesb[:], gUh[:].rearrange("p a b -> p (a b)"), True, False, [onesb, gUh], [PG])
        yield
        mm(PG[:, :], onesb[:], gUl[:].rearrange("p a b -> p (a b)"), False, True, [onesb, gUl], [PG])
        yield
        k.op("dve", lambda v: v.tensor_copy(out=gtmp[:], in_=Bsm[:, 0:4]), reads=[Bsm], writes=[gtmp])
        yield
        k.op("dve", lambda v: v.tensor_tensor(out=Gc[:], in0=gtmp[:], in1=Bsm[:, 4:8], op=ALU.add), reads=[gtmp, Bsm], writes=[Gc])
        yield
        k.op("dve", lambda v: v.tensor_copy(out=gtmp[:], in_=Bsm[:, 8:12]), reads=[Bsm], writes=[gtmp])
        yield
        k.op("dve", lambda v: v.tensor_tensor(out=Glb[:], in0=gtmp[:], in1=Bsm[:, 12:16], op=ALU.add), reads=[gtmp, Bsm], writes=[Glb])
        yield
        k.op("dve", lambda v: v.tensor_scalar(out=negGc[:], in0=Gc[:], scalar1=-1.0, scalar2=None, op0=ALU.mult), reads=[Gc], writes=[negGc])
        yield
        k.op("dve", lambda v: v.tensor_tensor(out=dG[:], in0=Glb[:], in1=Gc[:], op=ALU.subtract), reads=[Glb, Gc], writes=[dG])
        yield
        k.op("act", lambda a: a.activation(out=eG[:], in_=Gc[:], func=ACT.Exp), reads=[Gc], writes=[eG])
        yield
        k.op("act", lambda a: a.activation(out=edG[:], in_=dG[:], func=ACT.Exp), reads=[dG], writes=[edG])
        yield
        yield
        k.op("dve", lambda v: v.tensor_tensor(out=beG[:], in0=eG[:], in1=beta[:], op=ALU.mult), reads=[eG, beta], writes=[beG])
        PG3 = v3(PG[:, :], 4)
        yield
        k.op("dve", lambda v: v.scalar_tensor_tensor(out=tmp1[:], in0=PG3, scalar=-1.0, in1=bc(negA[:].rearrange("p (a b) -> p a b", a=1), [128, 4, 128]),
                                                     op0=ALU.mult, op1=ALU.add), reads=[PG, negA], writes=[tmp1])
        yield
        if f_:
            k.op("dve", lambda v: v.tensor_tensor(out=tmp2[:], in0=PG3, in1=bc(negB[:].rearrange("p (a b) -> p a b", a=1), [128, 4, 128]), op=ALU.add),
                 reads=[PG, negB], writes=[tmp2])
            yield
            k.op("act", lambda a: a.activation(out=eGbc[:], in_=PG3, func=ACT.Exp), reads=[PG], writes=[eGbc])
        yield
        for h in range(4):
            k.op("act", lambda a, h=h: a.activation(out=E1[:, h, :], in_=tmp1[:, h, :], func=ACT.Exp, bias=Gc[:, h:h + 1], scale=1.0), reads=[tmp1, Gc], writes=[E1])
            if f_:
                k.op("act", lambda a, h=h: a.activation(out=E2[:, h, :], in_=tmp2[:, h, :], func=ACT.Exp, bias=negGc[:, h:h + 1], scale=1.0), reads=[tmp2, negGc], writes=[E2])
        yield
        for h in range(4):
            mm(PK_[:, h * 128:(h + 1) * 128], qkb[:, 4 + h, :], qkb[:, 4 + h, :], True, True, [qkb], [PK_])
        yield
        for h in range(4):
            if f_:
                mm(PQ[:, h * 128:(h + 1) * 128], qkb[:, 4 + h, :], qkb[:, h, :], True, True, [qkb], [PQ])
        yield
        for h in range(4):
            k.op("dve", lambda v, h=h: v.scalar_tensor_tensor(out=Y0f[:, h, :], in0=PK_[:, h * 128:(h + 1) * 128], scalar=negbeta[:, h:h + 1],
                                                               in1=E1[:, h, :], op0=ALU.mult, op1=ALU.mult), reads=[PK_, negbeta, E1], writes=[Y0f])
        yield
        if f_:
            k.op("dve", lambda v: v.tensor_tensor(out=AqkT[:], in0=v3(PQ[:, :], 4), in1=E2[:], op=ALU.mult), reads=[PQ, E2], writes=[AqkT])
            yield
            k.op("pool", lambda g: g.tensor_tensor(out=qdT[:], in0=qkvs[:, 0:4, :], in1=eGbc[:], op=ALU.mult), reads=[qkvs, eGbc], writes=[qdT])
        if pl:
            while not (gdone.get((n - 1, 0)) and gdone.get((n - 1, 1))):
                yield "wait"
        k.op("act", lambda a: a.activation(out=egl[:], in_=Glb[:], func=ACT.Exp), reads=[Glb], writes=[egl])
        P5b = pbf(Bx0)
        id2 = bc(ident[:].rearrange("p (a b) -> p a b", a=1), [128, 2, 128])
        for g_ in range(2):
            yield
            k.op("act", lambda a, g_=g_: a.copy(out=Yh[g_][0][:], in_=Y0f[:, 2 * g_:2 * g_ + 2, :]), reads=[Y0f], writes=[Yh[g_][0]])
            yield
            k.op("pool", lambda g, g_=g_: g.tensor_tensor(out=Yl[g_][0][:], in0=Y0f[:, 2 * g_:2 * g_ + 2, :], in1=Yh[g_][0][:], op=ALU.subtract),
                 reads=[Y0f, Yh[g_][0]], writes=[Yl[g_][0]])
            yield
            for hh in range(2):
                h = 2 * g_ + hh
                tr(P5b[:, h * 128:(h + 1) * 128], Yh[g_][0][:, hh, :], identb[:], [Yh[g_][0]], [Bx0])
                tr(P5b[:, (4 + h) * 128:(5 + h) * 128], Yl[g_][0][:, hh, :], identb[:], [Yl[g_][0]], [Bx0])
        for g_ in range(2):
            yield
            k.op("act", lambda a, g_=g_: a.copy(out=XRh[g_][0][:, :, 0:128], in_=v3(P5b[:, g_ * 256:(g_ + 1) * 256], 2)), reads=[Bx0], writes=[XRh[g_][0]])
            yield
            k.op("dve", lambda v, g_=g_: v.tensor_copy(out=XRl[g_][0][:, :, 0:128], in_=v3(P5b[:, 512 + g_ * 256:512 + (g_ + 1) * 256], 2)), reads=[Bx0], writes=[XRl[g_][0]])
            yield
            k.op("pool", lambda g, g_=g_: g.tensor_copy(out=Rf[g_][:], in_=id2), reads=[ident], writes=[Rf[g_]])
            k.op("pool", lambda g, g_=g_: g.tensor_copy(out=XRh[g_][0][:, :, 128:256], in_=id2), reads=[ident], writes=[XRh[g_][0]])
            k.op("pool", lambda g, g_=g_: g.memset(XRl[g_][0][:, :, 128:256], 0.0), writes=[XRl[g_][0]])
        P1b_ = pbf(Bkv)
        yield
        for h in range(4):
            tr(P1b_[:, h * 128:(h + 1) * 128], qkb[:, 4 + h, :], identb[:], [qkb], [Bkv])
        yield
        for h in range(4):
            tr(P1b_[:, (4 + h) * 128:(5 + h) * 128], qkb[:, 8 + h, :], identb[:], [qkb], [Bkv])
        yield
        for h in range(4):
            k.op("dve", lambda v, h=h: v.tensor_scalar(out=kbt[:, h, :], in0=P1b_[:, h * 128:(h + 1) * 128], scalar1=beG[:, h:h + 1], scalar2=None, op0=ALU.mult),
                 reads=[Bkv, beG], writes=[kbt])
            k.op("act", lambda a, h=h: a.activation(out=kdt[:, h, :], in_=P1b_[:, h * 128:(h + 1) * 128], func=ACT.Identity, scale=edG[:, h:h + 1]),
                 reads=[Bkv, edG], writes=[kdt])
            k.op("act", lambda a, h=h: a.activation(out=bvt[:, h, :], in_=P1b_[:, (4 + h) * 128:(5 + h) * 128], func=ACT.Identity, scale=beta[:, h:h + 1]),
                 reads=[Bkv, beta], writes=[bvt])
        yield

    def G(n, g_):
        BA, BB = PB[4 + g_], PB[6 + g_]
        for lev in range(7):
            cur = lev % 2; nxt = (lev + 1) % 2
            xh, xl, yh, yl = XRh[g_][cur], XRl[g_][cur], Yh[g_][cur], Yl[g_][cur]
            xhn, xln, yhn, yln = XRh[g_][nxt], XRl[g_][nxt], Yh[g_][nxt], Yl[g_][nxt]
            last = (lev == 6)
            yield
            for hh in range(2):
                if not last:
                    o_ = BA[:, hh * 256:(hh + 1) * 256]
                    r_h = xh[:, hh, :]; r_l = xl[:, hh, :]
                else:
                    o_ = BA[:, hh * 256 + 128:(hh + 1) * 256]
                    r_h = xh[:, hh, 128:256]; r_l = xl[:, hh, 128:256]
                mm(o_, yh[:, hh, :], r_h, True, False, [yh, xh], [BA])
                mm(o_, yh[:, hh, :], r_l, False, False, [yh, xl], [BA])
                mm(o_, yl[:, hh, :], r_h, False, True, [yl, xh], [BA])
            if not last:
                yield
                for hh in range(2):
                    o_ = BB[:, hh * 128:(hh + 1) * 128]
                    mm(o_, xh[:, hh, 0:128], yh[:, hh, :], True, False, [xh, yh], [BB])
                    mm(o_, xh[:, hh, 0:128], yl[:, hh, :], False, False, [xh, yl], [BB])
                    mm(o_, xl[:, hh, 0:128], yh[:, hh, :], False, True, [xl, yh], [BB])
            yield
            k.op("dve", lambda v: v.tensor_tensor(out=Rf[g_][:], in0=Rf[g_][:], in1=v3(BA[:, :], 2)[:, :, 128:256], op=ALU.add), reads=[Rf[g_], BA], writes=[Rf[g_]],
                 cost=0.3)
            yield
            k.op("act", lambda a: a.copy(out=xhn[:, :, 128:256], in_=Rf[g_][:]), reads=[Rf[g_]], writes=[xhn], cost=0.3)
            if not last:
                yield
                k.op("pool", lambda g: g.tensor_tensor(out=xln[:, :, 128:256], in0=Rf[g_][:], in1=xhn[:, :, 128:256], op=ALU.subtract), reads=[Rf[g_], xhn], writes=[xln],
                     cost=0.6)
                yield
                k.op("act", lambda a: a.copy(out=xhn[:, :, 0:128], in_=v3(BA[:, :], 2)[:, :, 0:128]), reads=[BA], writes=[xhn], cost=0.3)
                yield
                k.op("dve", lambda v: v.tensor_tensor(out=xln[:, :, 0:128], in0=v3(BA[:, :], 2)[:, :, 0:128], in1=xhn[:, :, 0:128], op=ALU.subtract),
                     reads=[BA, xhn], writes=[xln], cost=0.3)
                yield
                k.op("act", lambda a: a.copy(out=yhn[:], in_=v3(BB[:, 0:256], 2)), reads=[BB], writes=[yhn], cost=0.3)
                yield
                k.op("dve", lambda v: v.tensor_tensor(out=yln[:], in0=v3(BB[:, 0:256], 2), in1=yhn[:], op=ALU.subtract), reads=[BB, yhn], writes=[yln], cost=0.3)
        Rh = XRh[g_][1]
        yield
        for hh in range(2):
            h = 2 * g_ + hh
            mm(BA[:, hh * 128:(hh + 1) * 128], kbt[:, h, :], Rh[:, hh, 128:256], True, True, [kbt, Rh], [BA])
        for hh in range(2):
            h = 2 * g_ + hh
            mm(BB[:, hh * 128:(hh + 1) * 128], Rh[:, hh, 128:256], bvt[:, h, :], True, True, [Rh, bvt], [BB])
        yield
        k.op("act", lambda a: a.copy(out=wT[g_][:], in_=v3(BA[:, 0:256], 2)), reads=[BA], writes=[wT[g_]], cost=0.3)
        yield
        k.op("dve", lambda v: v.tensor_copy(out=ut[g_][:], in_=v3(BB[:, 0:256], 2)), reads=[BB], writes=[ut[g_]], cost=0.3)
        yield
        for hh in range(2):
            mm(BA[:, 256 + hh * 128:256 + (hh + 1) * 128], wT[g_][:, hh, :], Sb[g_][:, hh, :], True, True, [wT[g_], Sb[g_]], [BA])
        yield
        k.op("dve", lambda v: v.tensor_tensor(out=uub[g_][:], in0=ut[g_][:], in1=v3(BA[:, 256:512], 2), op=ALU.subtract), reads=[ut[g_], BA], writes=[uub[g_]], cost=0.3)
        yield
        for hh in range(2):
            h = 2 * g_ + hh
            if full(n):
                mm(BB[:, 256 + hh * 128:256 + (hh + 1) * 128], qdT[:, h, :], Sb[g_][:, hh, :], True, False, [qdT, Sb[g_]], [BB])
                mm(BB[:, 256 + hh * 128:256 + (hh + 1) * 128], AqkT[:, h, :], uub[g_][:, hh, :], False, True, [AqkT, uub[g_]], [BB])
        for hh in range(2):
            h = 2 * g_ + hh
            mm(BA[:, hh * 128:(hh + 1) * 128], kdt[:, h, :], uub[g_][:, hh, :], True, True, [kdt, uub[g_]], [BA])
        yield
        for hh in range(2):
            h = 2 * g_ + hh
            k.op("dve", lambda v, hh=hh, h=h: v.scalar_tensor_tensor(out=S[g_][:, hh, :], in0=S[g_][:, hh, :], scalar=egl[:, h:h + 1], in1=BA[:, hh * 128:(hh + 1) * 128],
                                                                   op0=ALU.mult, op1=ALU.add), reads=[S[g_], egl, BA], writes=[S[g_]], cost=0.2)
        yield
        k.op("pool", lambda g: g.tensor_copy(out=Sb[g_][:], in_=S[g_][:]), reads=[S[g_]], writes=[Sb[g_]], cost=0.5)
        yield
        gdone[(n, g_)] = True
        if not full(n):
            return
        for hh in range(2):
            k.op("act", lambda a, hh=hh: a.activation(out=ojunk[g_][:], in_=BB[:, 256 + hh * 128:256 + (hh + 1) * 128], func=ACT.Square, accum_out=oss[g_][:, hh:hh + 1]),
                 reads=[BB], writes=[ojunk[g_], oss[g_]], cost=0.25)
        yield
        k.op("act", lambda a: a.activation(out=orr[g_][:], in_=oss[g_][:], func=ACT.Ln, bias=RMS_EPS, scale=1.0 / 128.0), reads=[oss[g_]], writes=[orr[g_]], cost=0.2)
        k.op("act", lambda a: a.activation(out=orr[g_][:], in_=orr[g_][:], func=ACT.Exp, scale=-0.5), reads=[orr[g_]], writes=[orr[g_]], cost=0.2)
        yield
        for hh in range(2):
            h = 2 * g_ + hh
            k.op("dve", lambda v, hh=hh, h=h: v.scalar_tensor_tensor(out=mixb[:, h * 128:(h + 1) * 128], in0=BB[:, 256 + hh * 128:256 + (hh + 1) * 128], scalar=orr[g_][:, hh:hh + 1],
                                                                   in1=nz[:, h, :], op0=ALU.mult, op1=ALU.mult), reads=[BB, orr[g_], nz], writes=[mixb], cost=0.2)
        yield

    def W(n):
        vcur = vB[n % 3]; vprev = vB[(n + 2) % 3]
        zBs = zBs2[n % 2]; qrb = qrb2[n % 2]; kr = kr2[n % 2]
        kcur = kTb[n % 2]; kprev = kTb[(n + 1) % 2]
        if not (full(n) or swaprep(n)):
            return
        if swaprep(n):
            yield
            k.op("pool", lambda g: g.tensor_copy(out=krb[:], in_=kr[:]), reads=[kr], writes=[krb])
            yield
            tr(pbf(PB[0])[:, 512:640], krb[:], identb[:], [krb], [PB[0]])
            yield
            k.op("dve", lambda v: v.tensor_copy(out=kcur[:], in_=pbf(PB[0])[:, 512:640]), reads=[PB[0]], writes=[kcur])
            yield
            return
        P0b = pbf(PB[0])
        yield
        for c in range(4):
            tr(P0b[:, c * 128:(c + 1) * 128], qrb[:, c * 128:(c + 1) * 128], identb[:], [qrb], [PB[0]])
        yield
        k.op("act", lambda a: a.activation(out=qTb[:], in_=v3(P0b[:, 0:512], 4), func=ACT.Identity, scale=0.125), reads=[PB[0]], writes=[qTb])
        yield
        k.op("pool", lambda g: g.tensor_copy(out=krb[:], in_=kr[:]), reads=[kr], writes=[krb])
        yield
        tr(pbf(PB[0])[:, 512:640], krb[:], identb[:], [krb], [PB[0]])
        yield
        k.op("dve", lambda v: v.tensor_copy(out=kcur[:], in_=pbf(PB[0])[:, 512:640]), reads=[PB[0]], writes=[kcur])
        msk = swam
        rnd = 0
        for kv in range(2):
            for cp in range(2):
                pb = PB[1] if rnd % 2 == 0 else PB[0]
                rnd += 1
                yield
                for cc in range(2):
                    c = cp * 2 + cc
                    o = cc * 256
                    mm(pb[:, o:o + 128], qTb[64 * kv:64 * kv + 64, c, :], kprev[64 * kv:64 * kv + 64, :], True, True, [qTb, kprev], [pb])
                    mm(pb[:, o + 128:o + 256], qTb[64 * kv:64 * kv + 64, c, :], kcur[64 * kv:64 * kv + 64, :], True, True, [qTb, kcur], [pb])
                yield
                s0 = kv * 4 + cp * 2
                k.op("dve", lambda v, pb=pb, s0=s0: v.tensor_tensor(out=SC[:, s0:s0 + 2, :], in0=v3(pb[:, :], 2),
                                                                     in1=bc(msk[:].rearrange("p (a b) -> p a b", a=1), [128, 2, 256]), op=ALU.add),
                     reads=[pb, msk], writes=[SC])
        yield
        if n == F0:
            if F0 == 0:
                k.op("dve", lambda v: v.tensor_scalar(out=SC[:, :, 0:128], in0=SC[:, :, 0:128], scalar1=NEG, scalar2=None, op0=ALU.add), reads=[SC], writes=[SC])
            else:
                k.op("dve", lambda v: v.tensor_scalar(out=negpad[:], in0=valid_bc[:, F0 - 1:F0], scalar1=-1.0, scalar2=-NEG, op0=ALU.add, op1=ALU.mult),
                     reads=[valid_bc], writes=[negpad])
                k.op("dve", lambda v: v.tensor_scalar(out=SC[:, :, 0:128], in0=SC[:, :, 0:128], scalar1=negpad[:, 0:1], scalar2=None, op0=ALU.add),
                     reads=[SC, negpad], writes=[SC])
            yield
        k.op("dve", lambda v: v.tensor_reduce(out=mx[:], in_=SC[:], axis=AX.X, op=ALU.max), reads=[SC], writes=[mx])
        yield
        k.op("dve", lambda v: v.tensor_tensor(out=mx[:], in0=mx[:], in1=sinks_bc[:], op=ALU.max), reads=[mx, sinks_bc], writes=[mx])
        yield
        k.op("dve", lambda v: v.tensor_scalar(out=negm[:], in0=mx[:], scalar1=-1.0, scalar2=None, op0=ALU.mult), reads=[mx], writes=[negm])
        yield
        for s_ in range(8):
            k.op("act", lambda a, s_=s_: a.activation(out=Pb[:, s_, :], in_=SC[:, s_, :], func=ACT.Exp, bias=negm[:, s_:s_ + 1], scale=1.0,
                                                      accum_out=rs[:, s_:s_ + 1]), reads=[SC, negm], writes=[Pb, rs])
        yield
        k.op("dve", lambda v: v.tensor_tensor(out=es_[:], in0=sinks_bc[:], in1=mx[:], op=ALU.subtract), reads=[sinks_bc, mx], writes=[es_])
        yield
        k.op("act", lambda a: a.activation(out=es_[:], in_=es_[:], func=ACT.Exp), reads=[es_], writes=[es_])
        yield
        k.op("dve", lambda v: v.tensor_tensor(out=es_[:], in0=es_[:], in1=rs[:], op=ALU.add), reads=[es_, rs], writes=[es_])
        yield
        k.op("dve", lambda v: v.reciprocal(out=es_[:], in_=es_[:]), reads=[es_], writes=[es_])
        yield
        k.op("dve", lambda v: v.tensor_copy(out=rden[:].rearrange("p (c k) -> p k c", k=2), in_=es_[:].rearrange("p (k c) -> p k c", k=2)),
             reads=[es_], writes=[rden])
        PPT = [PB[1], PB[0]]
        yield
        for s_ in range(8):
            for half in range(2):
                i16 = s_ * 2 + half
                pb = PPT[i16 // 8]
                o = (i16 % 8) * 128
                tr(pbf(pb)[:, o:o + 128], Pb[:, s_, half * 128:(half + 1) * 128], identb[:], [Pb], [pb])
        yield
        k.op("act", lambda a: a.copy(out=PTb[:, 0:8, :], in_=v3(pbf(PPT[0]), 8)), reads=[PPT[0]], writes=[PTb])
        yield
        k.op("dve", lambda v: v.tensor_copy(out=PTb[:, 8:16, :], in_=v3(pbf(PPT[1]), 8)), reads=[PPT[1]], writes=[PTb])
        PV = PB[1]
        yield
        for c in range(4):
            for kv in range(2):
                slot = c * 2 + kv
                sp_ = kv * 4 + c
                mm(PV[:, slot * 64:(slot + 1) * 64], PTb[:, 2 * sp_, :], vprev[:, 64 * kv:64 * kv + 64], True, False, [PTb, vprev], [PV])
                mm(PV[:, slot * 64:(slot + 1) * 64], PTb[:, 2 * sp_ + 1, :], vcur[:, 64 * kv:64 * kv + 64], False, True, [PTb, vcur], [PV])
        yield
        k.op("dve", lambda v: v.tensor_tensor(out=v3(obt[:], 8), in0=v3(PV[:, :], 8), in1=bc(rden[:].rearrange("p (a b) -> p a b", b=1), [128, 8, 64]), op=ALU.mult),
             reads=[PV, rden], writes=[obt])
        yield
        k.op("pool", lambda g: g.tensor_tensor(out=mixb[:, 512:1024], in0=obt[:], in1=zBs[:], op=ALU.mult), reads=[obt, zBs], writes=[mixb])

        yield

    def E(n):
        X = Xt[n % 2]; pc = pre; kr = kr2[n % 2]
        if not full(n):
            return
        P3b = pbf(PB[6])
        yield
        for kc in range(8):
            tr(P3b[:, kc * 128:(kc + 1) * 128], mixb[:, kc * 128:(kc + 1) * 128], identb[:], [mixb], [PB[6]])
        yield
        k.op("act", lambda a: a.copy(out=mixT[:, 0:4, :], in_=v3(P3b[:, 0:512], 4)), reads=[PB[6]], writes=[mixT])
        yield
        k.op("dve", lambda v: v.tensor_copy(out=mixT[:, 4:8, :], in_=v3(P3b[:, 512:1024], 4)), reads=[PB[6]], writes=[mixT])
        yield
        for j in range(2):
            for kc in range(8):
                mm(PB[6 + j][:, :], mixT[:, kc, :], Wo[:, kc, j * 512:(j + 1) * 512], kc == 0, kc == 7, [mixT, Wo], [PB[6 + j]])
        yield
        for j in range(2):
            k.op("dve", lambda v, j=j: v.tensor_tensor(out=ypre[:, j * 512:(j + 1) * 512], in0=PB[6 + j][:, :], in1=g1bc[:, j * 512:(j + 1) * 512], op=ALU.mult),
                 reads=[PB[6 + j], g1bc], writes=[ypre])
        yield
        k.op("dve", lambda g: g.scalar_tensor_tensor(out=ypre[:], in0=X[:], scalar=ALPHA, in1=ypre[:], op0=ALU.mult, op1=ALU.add), reads=[X, ypre], writes=[ypre])
        Yo = Yt[0]
        yjunk = Yo
        yield
        k.op("act", lambda a: a.activation(out=yjunk[:], in_=ypre[:], func=ACT.Identity, accum_out=st[:, 0:1]), reads=[ypre], writes=[yjunk, st])
        yield
        k.op("act", lambda a: a.activation(out=yjunk[:], in_=ypre[:], func=ACT.Square, accum_out=st[:, 1:2]), reads=[ypre], writes=[yjunk, st])
        yield
        k.op("dve", lambda v: v.tensor_scalar(out=st[:, 2:3], in0=st[:, 0:1], scalar1=1.0 / D, scalar2=None, op0=ALU.mult), reads=[st], writes=[st])
        yield
        k.op("dve", lambda v: v.tensor_tensor(out=st[:, 3:4], in0=st[:, 2:3], in1=st[:, 2:3], op=ALU.mult), reads=[st], writes=[st])
        yield
        k.op("dve", lambda v: v.scalar_tensor_tensor(out=st[:, 4:5], in0=st[:, 1:2], scalar=1.0 / D, in1=st[:, 3:4], op0=ALU.mult, op1=ALU.subtract), reads=[st], writes=[st])
        yield
        k.op("act", lambda a: a.activation(out=st[:, 5:6], in_=st[:, 4:5], func=ACT.Ln, bias=LN_EPS, scale=1.0), reads=[st], writes=[st])
        yield
        k.op("act", lambda a: a.activation(out=st[:, 5:6], in_=st[:, 5:6], func=ACT.Exp, scale=-0.5), reads=[st], writes=[st])
        yield
        k.op("dve", lambda v: v.scalar_tensor_tensor(out=st[:, 6:7], in0=st[:, 2:3], scalar=-1.0, in1=st[:, 5:6], op0=ALU.mult, op1=ALU.mult), reads=[st], writes=[st])
        yield
        k.op("act", lambda a: a.activation(out=yjunk[:], in_=ypre[:], func=ACT.Identity, bias=st[:, 6:7], scale=st[:, 5:6]), reads=[ypre, st], writes=[yjunk])
        yield
        k.op("pool", lambda g: g.tensor_tensor(out=yjunk[:], in0=yjunk[:], in1=lng_bc[:], op=ALU.mult), reads=[yjunk, lng_bc], writes=[yjunk])
        yield
        k.op("dve", lambda v: v.tensor_tensor(out=Yo[:], in0=yjunk[:], in1=lnb_bc[:], op=ALU.add), reads=[yjunk, lnb_bc], writes=[Yo])
        yield
        k.dma("sp", y_d[(n - F0) * 128:(n - F0 + 1) * 128, :], Yo[:], reads=[Yo], semkey="st_" + Yo.name)

        if n == NT - 1:
            k.barrier()
            k.pe_inorder = False
            cpo_in = rt[0][:].rearrange("p a b -> p (a b)")[:, 0:36].rearrange("p (a b) -> p a b", a=3)
            cpo = obt[0:12, 0:384].rearrange("p (a b) -> p a b", a=3)
            k.op("dve", lambda v: v.tensor_copy(out=cpo_in[:], in_=pc[:, :, 128:131].rearrange("p c r -> p r c")), reads=[pc], writes=[cpo_in])
            for r_ in range(3):
                tr(PB[6][0:12, r_ * 128:(r_ + 1) * 128], cpo_in[:, r_, :], ident[:], [cpo_in], [PB[6]])
            k.op("dve", lambda v: v.tensor_copy(out=cpo[:].rearrange("p a b -> p (a b)"), in_=PB[6][0:12, 0:384]), reads=[PB[6]], writes=[cpo])
            k.dma("sp", convp_d.rearrange("r (c p) -> c r p", p=128), cpo[:], reads=[cpo], semkey="st_misc")
            for g_ in range(2):
                k.dma("sp", deltap_d[2 * g_:2 * g_ + 2].rearrange("h k v -> k h v"), S[g_][:], reads=[S[g_]], semkey="st_misc")
            k.dma("sp", swak_d[:, :], kr[:], reads=[kr], semkey="st_misc")
            k.dma("sp", swav_d[:, :], vB32[:], reads=[vB32], semkey="st_misc")

        yield

    def chain(*gens):
        for g_ in gens:
            for v_ in g_:
                yield v_

    def merge(*gens):
        gens = [[g_, "s%d" % i] for i, g_ in enumerate(gens) if g_ is not None]
        base = min(k.t_eng.values())
        for _, lab in gens:
            k.t_stream[lab] = base
        while gens:
            gens.sort(key=lambda x: k.t_stream[x[1]] - (GBIAS if x[1] in ("s0", "s1") else 0.0))
            g_, lab = gens[0]
            k.stream = lab
            try:
                r_ = next(g_)
            except StopIteration:
                gens.pop(0)
                continue
            if r_ == "wait":
                assert len(gens) > 1, "gated stream waits on nothing"
                k.t_stream[lab] = max(k.t_stream[x[1]] for x in gens[1:]) + 1e-3
        k.stream = None

    gdone[(-1, 0)] = True
    gdone[(-1, 1)] = True
    emittedA = set()

    def PA(n):
        if n >= NT or n in emittedA:
            return None
        emittedA.add(n)
        return P1a(n)

    merge(PA(0))
    if F0 > 0:
        merge(P1b(0, True), PA(1))
    else:
        merge(P1b(0))
    for n in range(NT):
        nxt = n + 1 < NT
        pl = nxt and (n + 1 < F0)
        if pl:
            merge(G(n, 0), G(n, 1), W(n), chain(*[g_ for g_ in (PA(n + 1),) if g_ is not None], P1b(n + 1, True)), PA(n + 2))
        else:
            merge(G(n, 0), G(n, 1), W(n), PA(n + 1))
        merge(E(n), P1b(n + 1) if (nxt and not pl) else None)

    k.finish("sp")
    if k.dry:
        return k.need_out
    nc._kb_nops = k.nops
    nc._kb_sig = dict(k.sig)
    nc._kb_cnt = {e: k.cnt[e] for e in k.eng}
    return nc


def rope_tables(pos):
    half = 32
    inv = (1.0 / (10000.0 ** (np.arange(half, dtype=np.float32) / np.float32(half)))).astype(np.float32)
    ang = pos.astype(np.float32)[:, None] * inv[None, :]
    return np.cos(ang).astype(np.float32), np.sin(ang).astype(np.float32)


def prep_shared(inputs):
    w_in = np.asarray(inputs["w_in"][0], np.float32)
    perm_cols = np.concatenate([np.arange(h * 64, (h + 1) * 64) for h in PERM])
    w_f = np.ascontiguousarray(w_in[:, 0:1536])
    zA = w_in[:, 1536:2048]
    beta = w_in[:, 2048:2052]
    dec = w_in[:, 2052:2056]
    qB = w_in[:, 2056:2568][:, perm_cols]
    kB = w_in[:, 2568:2696]
    vB = w_in[:, 2696:2824]
    zB = w_in[:, 2824:3336][:, perm_cols]
    w_t = np.ascontiguousarray(np.concatenate([zA, qB, zB, kB, vB, beta, dec], axis=1))
    w_out = np.asarray(inputs["w_out"][0], np.float32)
    w_o = np.ascontiguousarray(np.concatenate([w_out[0:512], w_out[512:1024][perm_cols]], axis=0))
    sh = {
        "w_ada": np.ascontiguousarray(inputs["w_ada"][0], np.float32),
        "b_ada": np.ascontiguousarray(inputs["b_ada"], np.float32).reshape(1, -1),
        "w_f": w_f, "w_t": w_t, "w_o": w_o,
        "conv_w": np.ascontiguousarray(inputs["conv_w"][0], np.float32),
        "a_log": np.ascontiguousarray(inputs["a_log"], np.float32).reshape(1, 4),
        "dt_bias": np.ascontiguousarray(inputs["dt_bias"], np.float32).reshape(1, 4),
        "norm_a": np.ascontiguousarray(inputs["norm_a"], np.float32).reshape(1, 128),
        "sinks_p": np.ascontiguousarray(np.asarray(inputs["sinks"], np.float32).reshape(8)).reshape(1, 8),
        "ln_g": np.ascontiguousarray(inputs["ln_g"], np.float32).reshape(1, D),
        "ln_b": np.ascontiguousarray(inputs["ln_b"], np.float32).reshape(1, D),
    }
    return sh


def prep_core(inputs, sh, core, ntiles=NTILES, do_sample=True, pad=None):
    b = core // 4
    q = core % 4
    if pad is None:
        pad = (3 - q) * NLOCAL
    real = ntiles - pad
    T = ntiles * 128
    m = dict(sh)
    x = np.zeros((T, D), np.float32)
    x[pad * 128:] = inputs["x_prompt"][b, :real * 128]
    m["x"] = x
    m["c"] = np.ascontiguousarray(inputs["c_prompt"][b], np.float32).reshape(1, D)
    valid = np.zeros((1, ntiles), np.float32)
    valid[0, pad:] = 1.0
    m["valid"] = valid
    pos = np.maximum(np.arange(T) - pad * 128, 0)
    cos, sin = rope_tables(pos)
    m["cosk"] = cos
    m["sink"] = sin
    if do_sample:
        sl = slice(core * NS, (core + 1) * NS)
        m["xs"] = np.ascontiguousarray(inputs["x_sample"][sl, 0], np.float32)
        m["cs"] = np.ascontiguousarray(inputs["c_sample"][sl], np.float32)
        m["s_conv"] = np.ascontiguousarray(inputs["state_conv"][0, sl], np.float32)
        m["s_delta"] = np.ascontiguousarray(inputs["state_delta"][0, sl], np.float32)
        m["s_k"] = np.ascontiguousarray(inputs["cache_swa_k"][0, sl], np.float32).reshape(NS, 128, 128)
        m["s_v"] = np.ascontiguousarray(inputs["cache_swa_v"][0, sl], np.float32).reshape(NS, 128, 128)
        cs_, ss_ = rope_tables(np.array([8192]))
        m["sinks_col"] = np.ascontiguousarray(np.tile(np.asarray(inputs["sinks"], np.float32).reshape(8), NS).reshape(128, 1))
        m["cos_s"] = cs_.reshape(1, 32)
        m["sin_s"] = ss_.reshape(1, 32)
    return m


_NC_CACHE = {}


DO_SAMPLE = True


def kernel(**inputs):
    if "nc" not in _NC_CACHE:
        _NC_CACHE["nc"] = build_program(NTILES, DO_SAMPLE)
    nc = _NC_CACHE["nc"]
    sh = prep_shared(inputs)
    in_maps = [prep_core(inputs, sh, c, NTILES, DO_SAMPLE) for c in range(8)]
    res = run_bass_kernel_spmd(nc, in_maps, core_ids=list(range(8))).results
    yp = np.stack([np.concatenate([res[b * 4 + q]["y"] for q in range(4)], 0) for b in range(2)], 0).astype(np.float32)
    conv_p = np.stack([res[3]["conv_p"], res[7]["conv_p"]], 0)[None].astype(np.float32)
    delta_p = np.stack([res[3]["delta_p"], res[7]["delta_p"]], 0)[None].astype(np.float32)
    swa_k_p = np.stack([res[3]["swa_k_p"], res[7]["swa_k_p"]], 0).reshape(1, 2, 128, 2, 64).astype(np.float32)
    swa_v_p = np.stack([res[3]["swa_v_p"], res[7]["swa_v_p"]], 0).reshape(1, 2, 128, 2, 64).astype(np.float32)
    if DO_SAMPLE:
        ys = np.concatenate([r["ys"] for r in res], 0).reshape(128, 1, D).astype(np.float32)
        conv_s = np.concatenate([r["conv_s"] for r in res], 0)[None].astype(np.float32)
        delta_s = np.concatenate([r["delta_s"] for r in res], 0)[None].astype(np.float32)
        swa_k_s = np.concatenate([r["swa_k_s"] for r in res], 0).reshape(1, 128, 128, 2, 64).astype(np.float32)
        swa_v_s = np.concatenate([r["swa_v_s"] for r in res], 0).reshape(1, 128, 128, 2, 64).astype(np.float32)
    else:
        ys = np.zeros((128, 1, D), np.float32)
        conv_s = np.zeros((1, 128, 3, 1536), np.float32)
        delta_s = np.zeros((1, 128, 4, 128, 128), np.float32)
        swa_k_s = np.zeros((1, 128, 128, 2, 64), np.float32)
        swa_v_s = np.zeros((1, 128, 128, 2, 64), np.float32)
    return (yp, ys, conv_p, delta_p, swa_k_p, swa_v_p, conv_s, delta_s, swa_k_s, swa_v_s)
```
